# Optimizing a Trainium2 kernel written in Bass

```python
import math
import jax, jax.numpy as jnp
from jax import lax
import numpy as np

D_MODEL = 1024
BATCH = 32
SEQ = 2048
DEPTH = 2
DEC_BATCH = 16
DEC_SEQ = 64
PAST_LEN = 2048

CHUNK = 64
N_META = 16
RMS_EPS = 1e-6
SSD_HEADS = 16
SSD_HEAD_DIM = 64
SSD_GROUPS = 2
SSD_HPG = SSD_HEADS // SSD_GROUPS
SSD_STATE = 64
SSD_WIDTH = SSD_HEADS * SSD_HEAD_DIM
CONV_W = 4
CONV_DIM = SSD_WIDTH + 2 * SSD_GROUPS * SSD_STATE
S5_WIDTH = D_MODEL // 2
S5_GROUP_CH = 16
S5_GROUPS = S5_WIDTH // S5_GROUP_CH
S5_STATE = 64
SB_HEADS = 8
SB_HEAD_DIM = 64
SB_WIDTH = SB_HEADS * SB_HEAD_DIM
SB_BLOCK = 128
SB_SCALE = 1.0 / math.sqrt(SB_HEAD_DIM)
N_BRANCH = 3
D_FF = 4 * D_MODEL
OFF_Z = 0
OFF_XBC = OFF_Z + SSD_WIDTH
OFF_DT = OFF_XBC + CONV_DIM
OFF_U = OFF_DT + SSD_HEADS
OFF_Q = OFF_U + S5_WIDTH
OFF_K = OFF_Q + SB_WIDTH
OFF_V = OFF_K + SB_WIDTH
OFF_GATE = OFF_V + SB_WIDTH
IN_COLS = OFF_GATE + N_BRANCH * D_MODEL

kernel_name = 'hybrid_ssd_s5_stickbreak_stream_step'


def rmsnorm(x, g):
    xf = x.astype(jnp.float32)
    return (xf * lax.rsqrt(jnp.mean(xf * xf, axis=-1, keepdims=True) + RMS_EPS)).astype(x.dtype) * g


def causal_dwconv(x, hist, w, b):
    xp = jnp.concatenate([hist.astype(x.dtype), x], axis=1)
    T = x.shape[1]
    y = b
    for k in range(CONV_W):
        y = y + xp[:, k:k + T] * w[k]
    return y, xp[:, -(CONV_W - 1):]


def ssd_scan(x, dt, a, bm, cm, s0):
    bsz, T = x.shape[:2]
    nc = T // CHUNK
    blk = lambda t: t.reshape((bsz, nc, CHUNK) + t.shape[2:])
    x, dt, bm, cm = blk(x), blk(dt), blk(bm), blk(cm)
    a_cum = jnp.cumsum(dt * a, axis=2)
    seg = a_cum[:, :, :, None] - a_cum[:, :, None]
    causal = jnp.tril(jnp.ones((CHUNK, CHUNK), bool))[:, :, None, None]
    decay = jnp.exp(jnp.where(causal, seg, -jnp.inf))
    cb = jnp.einsum('bcign,bcjgn->bcijg', cm, bm)
    y_diag = jnp.einsum('bcijgh,bcjghp->bcighp', cb[..., None] * decay * dt[:, :, None], x)
    to_end = jnp.exp(a_cum[:, :, -1:] - a_cum) * dt
    blk_states = jnp.einsum('bcjgn,bcjgh,bcjghp->bcghpn', bm, to_end, x)
    blk_decay = jnp.exp(a_cum[:, :, -1])

    def step(s, inp):
        dec, st = inp
        return dec[..., None, None] * s + st, s

    s_final, s_in = lax.scan(step, s0.astype(jnp.float32),
                             (jnp.moveaxis(blk_decay, 1, 0), jnp.moveaxis(blk_states, 1, 0)))
    s_in = jnp.moveaxis(s_in, 0, 1)
    y_off = jnp.einsum('bcign,bcghpn,bcigh->bcighp', cm, s_in, jnp.exp(a_cum))
    return (y_diag + y_off).reshape((bsz, T) + x.shape[3:]), s_final


def ssd_branch(h, w_z, w_xbc, w_dt, conv_hist, s0, conv_w, conv_b, dt_bias, a_log, d_skip, norm_w, front):
    bsz, T, _ = h.shape
    z = h @ w_z
    xbc, conv_new = causal_dwconv(h @ w_xbc, conv_hist, conv_w, conv_b)
    xbc = jax.nn.silu(xbc)
    n_bc = SSD_GROUPS * SSD_STATE
    xs = xbc[..., :SSD_WIDTH].reshape(bsz, T, SSD_GROUPS, SSD_HPG, SSD_HEAD_DIM)
    bm = xbc[..., SSD_WIDTH:SSD_WIDTH + n_bc].reshape(bsz, T, SSD_GROUPS, SSD_STATE)
    cm = xbc[..., SSD_WIDTH + n_bc:].reshape(bsz, T, SSD_GROUPS, SSD_STATE)
    dt = jax.nn.softplus((h @ w_dt + dt_bias).astype(jnp.float32)).reshape(bsz, T, SSD_GROUPS, SSD_HPG)
    a = -jnp.exp(a_log.astype(jnp.float32)).reshape(SSD_GROUPS, SSD_HPG)
    back = (-(front + T)) % CHUNK
    pad = lambda t: jnp.pad(t, [(0, 0), (front, back)] + [(0, 0)] * (t.ndim - 2))
    y, s_new = ssd_scan(pad(xs), pad(dt), a, pad(bm), pad(cm), s0)
    y = y[:, front:front + T] + d_skip.reshape(SSD_GROUPS, SSD_HPG)[:, :, None] * xs
    y = y.reshape(bsz, T, SSD_WIDTH) * jax.nn.silu(z)
    gw = SSD_WIDTH // SSD_GROUPS
    y = rmsnorm(y.reshape(bsz, T, SSD_GROUPS, gw), norm_w.reshape(SSD_GROUPS, gw)).reshape(bsz, T, SSD_WIDTH)
    return y.astype(h.dtype), conv_new, s_new


def _complex_affine_combine(e1, e2):
    ar1, ai1, br1, bi1 = e1
    ar2, ai2, br2, bi2 = e2
    return (ar1 * ar2 - ai1 * ai2, ar1 * ai2 + ai1 * ar2,
            ar2 * br1 - ai2 * bi1 + br2, ar2 * bi1 + ai2 * br1 + bi2)


def s5_branch(u, s_re0, s_im0, lam_re, lam_im, log_step, b_re, b_im, c_re, c_im, d_skip):
    f32 = jnp.float32
    bsz, T, _ = u.shape
    ug = u.reshape(bsz, T, S5_GROUPS, S5_GROUP_CH).astype(f32)
    lr, li = lam_re.astype(f32), lam_im.astype(f32)
    step = jnp.exp(log_step.astype(f32))[:, None]
    mag = jnp.exp(lr * step)
    ab_re, ab_im = mag * jnp.cos(li * step), mag * jnp.sin(li * step)
    den = lr * lr + li * li
    nr = ab_re - 1.0
    f_re = (nr * lr + ab_im * li) / den
    f_im = (ab_im * lr - nr * li) / den
    br, bi = b_re.astype(f32), b_im.astype(f32)
    bb_re = f_re[..., None] * br - f_im[..., None] * bi
    bb_im = f_re[..., None] * bi + f_im[..., None] * br
    bu_re = jnp.einsum('btgh,gph->btgp', ug, bb_re)
    bu_im = jnp.einsum('btgh,gph->btgp', ug, bb_im)
    sr, si = s_re0.astype(f32), s_im0.astype(f32)
    bu_re = bu_re.at[:, 0].add(ab_re * sr - ab_im * si)
    bu_im = bu_im.at[:, 0].add(ab_re * si + ab_im * sr)
    a_re = jnp.broadcast_to(ab_re, (1, T) + ab_re.shape)
    a_im = jnp.broadcast_to(ab_im, (1, T) + ab_im.shape)
    _, _, x_re, x_im = lax.associative_scan(_complex_affine_combine, (a_re, a_im, bu_re, bu_im), axis=1)
    y = (jnp.einsum('btgp,ghp->btgh', x_re, c_re.astype(f32))
         - jnp.einsum('btgp,ghp->btgh', x_im, c_im.astype(f32)) + d_skip * ug)
    return y.reshape(bsz, T, S5_WIDTH).astype(u.dtype), x_re[:, -1], x_im[:, -1]


def stick_breaking(q, k, v):
    tq, tk = q.shape[1], k.shape[1]
    n_hist = tk - tq
    outs = []
    for start in range(0, tq, SB_BLOCK):
        end = min(start + SB_BLOCK, tq)
        kend = n_hist + end
        z = jnp.einsum('bqhd,bkhd->bhqk', q[:, start:end], k[:, :kend]).astype(jnp.float32) * SB_SCALE
        q_pos = n_hist + jnp.arange(start, end)
        k_pos = jnp.arange(kend)
        mask = k_pos[None, :] < q_pos[:, None]
        log_keep = jnp.where(mask, jax.nn.log_sigmoid(-z), 0.0)
        later = lax.cumsum(log_keep, axis=3, reverse=True) - log_keep
        w = jnp.where(mask, jnp.exp(jax.nn.log_sigmoid(z) + later), 0.0)
        outs.append(jnp.einsum('bhqk,bkhd->bqhd', w.astype(v.dtype), v[:, :kend]))
    return jnp.concatenate(outs, axis=1)


def trunk_layer(x, k_hist, v_hist, conv_hist, ssd_s0, s5_re0, s5_im0, front, lp):
    bsz, T, _ = x.shape
    w_in = lp['w_in']
    col = lambda off, n: w_in[:, off:off + n]
    h = rmsnorm(x, lp['norm_mix'])
    y_a, conv_new, ssd_new = ssd_branch(h, col(OFF_Z, SSD_WIDTH), col(OFF_XBC, CONV_DIM), col(OFF_DT, SSD_HEADS),
                                        conv_hist, ssd_s0, lp['conv_w'], lp['conv_b'], lp['dt_bias'],
                                        lp['a_log'], lp['d_ssd'], lp['norm_ssd'], front)
    y_b, s5_re, s5_im = s5_branch(h @ col(OFF_U, S5_WIDTH), s5_re0, s5_im0, lp['lam_re'], lp['lam_im'],
                                  lp['log_step'], lp['b_re'], lp['b_im'], lp['c_re'], lp['c_im'], lp['d_s5'])
    glu = jax.nn.gelu(y_b) @ lp['w_glu']
    out_b = glu[..., :D_MODEL] * jax.nn.sigmoid(glu[..., D_MODEL:])
    heads = lambda t: t.reshape(bsz, T, SB_HEADS, SB_HEAD_DIM)
    q = rmsnorm(heads(h @ col(OFF_Q, SB_WIDTH)), lp['q_norm'])
    k = rmsnorm(heads(h @ col(OFF_K, SB_WIDTH)), lp['k_norm'])
    v = heads(h @ col(OFF_V, SB_WIDTH))
    o_c = stick_breaking(q, jnp.concatenate([k_hist.astype(k.dtype), k], axis=1),
                         jnp.concatenate([v_hist.astype(v.dtype), v], axis=1))
    gates = jax.nn.sigmoid(h @ col(OFF_GATE, N_BRANCH * D_MODEL)).reshape(bsz, T, N_BRANCH, D_MODEL)
    mix = (gates[:, :, 0] * (y_a @ lp['w_lift_a']) + gates[:, :, 1] * out_b
           + gates[:, :, 2] * (o_c.reshape(bsz, T, SB_WIDTH) @ lp['w_lift_c']))
    x = x + mix @ lp['w_out']
    h2 = rmsnorm(x, lp['norm_ffn'])
    x = x + jnp.square(jax.nn.relu(h2 @ lp['w_up'])) @ lp['w_down']
    return x, k, v, conv_new, ssd_new, s5_re, s5_im


def setup_inputs(seed: int = 0) -> dict:
    key = jax.random.key(seed)
    ks = jax.random.split(key, 40)
    f32 = jnp.float32
    L = DEPTH

    def nrm(i, shape, scale):
        return jax.random.normal(ks[i], shape, f32) * scale

    def unif(i, shape, lo, hi):
        return jax.random.uniform(ks[i], shape, f32, lo, hi)

    dt0 = jnp.exp(unif(13, (L, SSD_HEADS), math.log(1e-3), math.log(1e-1)))
    n = jnp.arange(S5_STATE, dtype=f32)
    return {
        'x_prompt': nrm(0, (BATCH, SEQ, D_MODEL), 1.0),
        'x_sample': nrm(1, (DEC_BATCH, DEC_SEQ, D_MODEL), 1.0),
        'cache_k': nrm(2, (L, DEC_BATCH, PAST_LEN, SB_HEADS, SB_HEAD_DIM), 1.0),
        'cache_v': nrm(3, (L, DEC_BATCH, PAST_LEN, SB_HEADS, SB_HEAD_DIM), 1.0),
        'state_conv': nrm(4, (L, DEC_BATCH, CONV_W - 1, CONV_DIM), 1.0),
        'state_ssd': nrm(5, (L, DEC_BATCH, SSD_GROUPS, SSD_HPG, SSD_HEAD_DIM, SSD_STATE), 0.1),
        'state_s5_re': nrm(6, (L, DEC_BATCH, S5_GROUPS, S5_STATE), 0.1),
        'state_s5_im': nrm(7, (L, DEC_BATCH, S5_GROUPS, S5_STATE), 0.1),
        'meta_tokens': nrm(8, (N_META, D_MODEL), 1.0),
        'norm_mix': 1.0 + nrm(9, (L, D_MODEL), 0.02),
        'w_in': nrm(10, (L, D_MODEL, IN_COLS), D_MODEL ** -0.5),
        'conv_w': nrm(11, (L, CONV_W, CONV_DIM), CONV_W ** -0.5),
        'conv_b': nrm(12, (L, CONV_DIM), 0.01),
        'dt_bias': dt0 + jnp.log(-jnp.expm1(-dt0)),
        'a_log': jnp.log(unif(14, (L, SSD_HEADS), 1.0, 16.0)),
        'd_ssd': 1.0 + nrm(15, (L, SSD_HEADS), 0.02),
        'norm_ssd': 1.0 + nrm(16, (L, SSD_WIDTH), 0.02),
        'lam_re': -0.5 + nrm(17, (L, S5_GROUPS, S5_STATE), 0.01),
        'lam_im': math.pi * n + nrm(18, (L, S5_GROUPS, S5_STATE), 0.01),
        'log_step': unif(19, (L, S5_GROUPS), math.log(1e-3), math.log(1e-1)),
        'b_re': nrm(20, (L, S5_GROUPS, S5_STATE, S5_GROUP_CH), (2 * S5_GROUP_CH) ** -0.5),
        'b_im': nrm(21, (L, S5_GROUPS, S5_STATE, S5_GROUP_CH), (2 * S5_GROUP_CH) ** -0.5),
        'c_re': nrm(22, (L, S5_GROUPS, S5_GROUP_CH, S5_STATE), (2 * S5_STATE) ** -0.5),
        'c_im': nrm(23, (L, S5_GROUPS, S5_GROUP_CH, S5_STATE), (2 * S5_STATE) ** -0.5),
        'd_s5': nrm(24, (L, S5_GROUPS, S5_GROUP_CH), 1.0),
        'w_glu': nrm(25, (L, S5_WIDTH, 2 * D_MODEL), S5_WIDTH ** -0.5),
        'q_norm': 1.0 + nrm(26, (L, SB_HEAD_DIM), 0.02),
        'k_norm': 1.0 + nrm(27, (L, SB_HEAD_DIM), 0.02),
        'w_lift_a': nrm(28, (L, SSD_WIDTH, D_MODEL), SSD_WIDTH ** -0.5),
        'w_lift_c': nrm(29, (L, SB_WIDTH, D_MODEL), SB_WIDTH ** -0.5),
        'w_out': nrm(30, (L, D_MODEL, D_MODEL), D_MODEL ** -0.5),
        'norm_ffn': 1.0 + nrm(31, (L, D_MODEL), 0.02),
        'w_up': nrm(32, (L, D_MODEL, D_FF), D_MODEL ** -0.5),
        'w_down': nrm(33, (L, D_FF, D_MODEL), D_FF ** -0.5),
    }


def reference(x_prompt, x_sample, cache_k, cache_v, state_conv, state_ssd, state_s5_re, state_s5_im,
              meta_tokens, norm_mix, w_in, conv_w, conv_b, dt_bias, a_log, d_ssd, norm_ssd,
              lam_re, lam_im, log_step, b_re, b_im, c_re, c_im, d_s5, w_glu, q_norm, k_norm,
              w_lift_a, w_lift_c, w_out, norm_ffn, w_up, w_down):
    bp = x_prompt.shape[0]
    dtype = x_prompt.dtype
    xp = jnp.concatenate([jnp.broadcast_to(meta_tokens.astype(dtype)[None], (bp, N_META, D_MODEL)), x_prompt], axis=1)
    xs = x_sample
    front_p = (-N_META) % CHUNK
    front_s = PAST_LEN % CHUNK
    kv_empty = jnp.zeros((bp, 0, SB_HEADS, SB_HEAD_DIM), dtype)
    conv_p0 = jnp.zeros((bp, CONV_W - 1, CONV_DIM), dtype)
    ssd_p0 = jnp.zeros((bp, SSD_GROUPS, SSD_HPG, SSD_HEAD_DIM, SSD_STATE), jnp.float32)
    s5_p0 = jnp.zeros((bp, S5_GROUPS, S5_STATE), jnp.float32)
    outs_p, outs_s = [], []
    for l in range(DEPTH):
        lp = {'norm_mix': norm_mix[l], 'w_in': w_in[l], 'conv_w': conv_w[l], 'conv_b': conv_b[l],
              'dt_bias': dt_bias[l], 'a_log': a_log[l], 'd_ssd': d_ssd[l], 'norm_ssd': norm_ssd[l],
              'lam_re': lam_re[l], 'lam_im': lam_im[l], 'log_step': log_step[l], 'b_re': b_re[l],
              'b_im': b_im[l], 'c_re': c_re[l], 'c_im': c_im[l], 'd_s5': d_s5[l], 'w_glu': w_glu[l],
              'q_norm': q_norm[l], 'k_norm': k_norm[l], 'w_lift_a': w_lift_a[l], 'w_lift_c': w_lift_c[l],
              'w_out': w_out[l], 'norm_ffn': norm_ffn[l], 'w_up': w_up[l], 'w_down': w_down[l]}
        xp, *st_p = trunk_layer(xp, kv_empty, kv_empty, conv_p0, ssd_p0, s5_p0, s5_p0, front_p, lp)
        xs, *st_s = trunk_layer(xs, cache_k[l], cache_v[l], state_conv[l], state_ssd[l],
                                state_s5_re[l], state_s5_im[l], front_s, lp)
        outs_p.append(st_p)
        outs_s.append(st_s)
    stk = lambda outs, i: jnp.stack([o[i] for o in outs], axis=0)
    return (xp[:, N_META:], xs,
            stk(outs_p, 0), stk(outs_p, 1), stk(outs_p, 2), stk(outs_p, 3), stk(outs_p, 4), stk(outs_p, 5),
            stk(outs_s, 0), stk(outs_s, 1), stk(outs_s, 2), stk(outs_s, 3), stk(outs_s, 4), stk(outs_s, 5))
```

```python
import math
import bisect
import numpy as np
import concourse.bass as bass
import concourse.mybir as mybir
from concourse.bass_utils import run_bass_kernel_spmd

F32 = mybir.dt.float32
F32R = mybir.dt.float32r
BF16 = mybir.dt.bfloat16
AF = mybir.ActivationFunctionType
ALU = mybir.AluOpType
AX = mybir.AxisListType

D = 1024
NMETA = 16
SEQ = 2048
TP = NMETA + SEQ
PAST = 2048
DSEQ = 64
NKMAX = 2112
INC = 7440
OFF_Z, OFF_XBC, OFF_DT, OFF_U, OFF_Q, OFF_K, OFF_V, OFF_G = 0, 1024, 2304, 2320, 2832, 3344, 3856, 4368
EPS = 1e-6
NT = 256
QS = 64
NEG = -30000.0
PCOL = {}
_c = 0
for _nm, _w in [("nmix", 8), ("nffn", 8), ("nssd", 8), ("cw", 40), ("cb", 10), ("d5", 4), ("dtb", 1), ("alog", 1),
                ("dsk", 8), ("qn", 1), ("kn", 1), ("lre", 16), ("lim", 16), ("lst", 16)]:
    PCOL[_nm] = (_c, _w)
    _c += _w
NPAR = _c


def pack_params(inp):
    par = np.zeros((2, 128, NPAR), np.float32)
    g = lambda n: np.asarray(inp[n], np.float32)

    def put(l, nm, arr):
        c0, w = PCOL[nm]
        par[l, :, c0:c0 + w] = arr

    for l in range(2):
        put(l, "nmix", g("norm_mix")[l].reshape(8, 128).T)
        put(l, "nffn", g("norm_ffn")[l].reshape(8, 128).T)
        put(l, "nssd", g("norm_ssd")[l].reshape(8, 128).T)
        cw = g("conv_w")[l].reshape(4, 10, 128)
        put(l, "cw", cw.transpose(2, 0, 1).reshape(128, 40))
        put(l, "cb", g("conv_b")[l].reshape(10, 128).T)
        put(l, "d5", g("d_s5")[l].reshape(4, 128).T)
        for nm, src in (("dtb", "dt_bias"), ("alog", "a_log")):
            col = np.zeros((128, 1), np.float32)
            col[0:16, 0] = g(src)[l]
            col[32:48, 0] = g(src)[l]
            put(l, nm, col)
        dsk = np.zeros((128, 8), np.float32)
        d = g("d_ssd")[l]
        for hh in range(2):
            dsk[64 * hh:64 * hh + 64, :] = d[hh::2][None, :]
        put(l, "dsk", dsk)
        put(l, "qn", np.tile(g("q_norm")[l], 2)[:, None])
        put(l, "kn", np.tile(g("k_norm")[l], 2)[:, None])
        for nm, src in (("lre", "lam_re"), ("lim", "lam_im")):
            a = g(src)[l].reshape(16, 2, 64)
            put(l, nm, a.transpose(1, 2, 0).reshape(128, 16))
        ls = g("log_step")[l].reshape(16, 2)
        put(l, "lst", np.repeat(ls.T[:, None, :], 64, axis=1).reshape(128, 16))
    return par
DTSZ = {F32: 4, F32R: 4, BF16: 2}


class Tile:
    def __init__(self, h, space, lo, hi):
        self.h, self.space, self.lo, self.hi = h, space, lo, hi

    def __getitem__(self, k):
        return self.h[k]

    @property
    def all(self):
        return (self.space, self.lo, self.hi)

    def r(self, lo, hi):
        return (self.space, self.lo + lo, self.lo + hi)

    def c(self, i, n=1):
        return (self.space, self.lo + i * self.cb, self.lo + (i + n) * self.cb)


def _reg(x):
    return x.all if isinstance(x, Tile) else x


class Sched:
    ENG = ["pe", "act", "dve", "pool", "sp"]

    def __init__(self, nc):
        self.nc = nc
        self.ops = []
        self.segs = {}
        self.off = 16512
        self.cnt = 0
        self.stream_n = {}
        self.psb = []

    def sb(self, shape, dtype, name="t"):
        nb = int(np.prod(shape[1:])) * DTSZ[dtype]
        off = (self.off + 31) // 32 * 32
        self.cnt += 1
        h = self.nc.alloc_sbuf_tensor_at(f"{name}{self.cnt}", list(shape), dtype, offset=off)
        self.off = off + nb
        self.peak = max(getattr(self, "peak", 0), self.off)
        assert self.off <= 16512 + 208000, ("sbuf overflow", name, self.off)
        t = Tile(h, "sb", off, off + nb)
        t.cb = (int(np.prod(shape[2:])) if len(shape) > 2 else 1) * DTSZ[dtype]
        return t

    def mark(self):
        return self.off

    def reset(self, m):
        self.off = m

    def _access(self, opi, reg, write, deps):
        space, lo, hi = reg
        if space not in self.segs:
            self.segs[space] = ([0], [[None, {}]])
        starts, data = self.segs[space]
        for b in (lo, hi):
            i = bisect.bisect_right(starts, b) - 1
            if starts[i] != b:
                starts.insert(i + 1, b)
                data.insert(i + 1, [data[i][0], dict(data[i][1])])
        i = bisect.bisect_left(starts, lo)
        while i < len(starts) and starts[i] < hi:
            w, rd = data[i]
            if w is not None and w != opi:
                if not write:
                    deps[w] = "raw"
                elif deps.get(w) != "raw":
                    deps[w] = "waw"
            if write:
                for r_ in rd.values():
                    if r_ != opi and r_ not in deps:
                        deps[r_] = "war"
                data[i][0] = opi
                data[i][1] = {}
            else:
                key = self.ops[opi]["rk"]
                rd[key] = opi
            i += 1

    def op(self, eng, fn, reads=(), writes=(), stream=None):
        opi = len(self.ops)
        o = {"eng": eng, "fn": fn, "stream": stream, "deps": {}, "sig": False}
        if stream is not None:
            o["rk"] = "dma:" + stream
        else:
            o["rk"] = eng
        self.ops.append(o)
        deps = {}
        for r_ in reads:
            self._access(opi, _reg(r_), False, deps)
        for w_ in writes:
            rg_ = _reg(w_)
            if eng == "pe" and rg_[0] == "ps":
                rg_ = ("ps", rg_[1] // 2048 * 2048, (rg_[2] + 2047) // 2048 * 2048)
            self._access(opi, rg_, True, deps)
        res = {}
        for d, kind in deps.items():
            od = self.ops[d]
            if od["stream"] is not None:
                res[d] = ("s", od["stream"], 16 * self.stream_n[od["stream"]])
            else:
                if od["eng"] == eng and stream is None:
                    if eng == "pe":
                        continue
                res[d] = ("e", od["eng"], None)
                od["sig"] = True
        lw = self.__dict__.setdefault("last_waiter", {})
        if stream is not None and stream in lw:
            w = lw[stream]
            ow = self.ops[w]
            if ow["eng"] != eng and w not in res:
                if ow["stream"] is not None:
                    res[w] = ("s", ow["stream"], 16 * self.stream_n[ow["stream"]])
                else:
                    res[w] = ("e", ow["eng"], None)
                    ow["sig"] = True
        for d, (k, key, val) in res.items():
            if k == "s":
                lw[key] = opi
        o["deps"] = res
        if stream is not None:
            self.stream_n[stream] = self.stream_n.get(stream, 0) + 1
            o["sval"] = 16 * self.stream_n[stream]
        return opi

    def dma(self, q, out, in_, reads, writes, stream, **kw):
        return self.op(q, lambda e: e.dma_start(out=out, in_=in_, **kw), reads, writes, stream=stream)

    def emit(self, final_streams):
        nc = self.nc
        counts = {e: 0 for e in self.ENG}
        for o in self.ops:
            if o["stream"] is None and o["sig"]:
                counts[o["eng"]] += 1
                o["sval"] = counts[o["eng"]]
        from contextlib import ExitStack
        with ExitStack() as es:
            esem = {e: es.enter_context(nc.semaphore("e_" + e)) for e in ["pe", "act", "dve", "pool"]}
            ssem = {s: es.enter_context(nc.semaphore("s_" + s)) for s in self.stream_n}
            block = es.enter_context(nc.Block())
            per = {e: [o for o in self.ops if o["eng"] == e] for e in self.ENG}

            def run(ename, e):
                known = {}
                for o in per[ename]:
                    for d, (k, key, val) in o["deps"].items():
                        if k == "s":
                            sem, v = ssem[key], val
                        else:
                            sem, v = esem[key], self.ops[d]["sval"]
                        kk = (k, key)
                        if known.get(kk, 0) >= v:
                            continue
                        known[kk] = v
                        e.wait_ge(sem, v)
                    ins = o["fn"](e)
                    if o["stream"] is not None:
                        ins.then_inc(ssem[o["stream"]], 16)
                    elif o["sig"]:
                        ins.then_inc(esem[ename], 1)
                if ename == "sp":
                    for s in final_streams:
                        if s in self.stream_n:
                            e.wait_ge(ssem[s], 16 * self.stream_n[s])

            @block.tensor
            def _(e):
                run("pe", e)

            @block.scalar
            def _(e):
                run("act", e)

            @block.vector
            def _(e):
                run("dve", e)

            @block.gpsimd
            def _(e):
                run("pool", e)

            @block.sync
            def _(e):
                run("sp", e)


def build(cfg=None):
    cfg = cfg or {}
    n_ptiles = cfg.get("n_ptiles", SEQ // NT)
    n_pseq = cfg.get("n_pseq", 4)
    n_sseq = cfg.get("n_sseq", 2)
    nc = bass.Bass("TRN2", target_bir_lowering=False)
    S = Sched(nc)

    def din(name, shape):
        return nc.dram_tensor(name, list(shape), F32, kind="ExternalInput").ap()

    def dout(name, shape):
        return nc.dram_tensor(name, list(shape), F32, kind="ExternalOutput").ap()

    I = {}
    I["x_prompt"] = din("x_prompt", [4, SEQ, D])
    I["x_sample"] = din("x_sample", [2, DSEQ, D])
    I["cache_k"] = din("cache_k", [2, 2, PAST, 512])
    I["cache_v"] = din("cache_v", [2, 2, PAST, 512])
    I["state_conv"] = din("state_conv", [2, 2, 128, 30])
    I["state_ssd"] = din("state_ssd", [2, 2, 2, 64, 512])
    I["state_s5_re"] = din("state_s5_re", [2, 2, 128, 16])
    I["state_s5_im"] = din("state_s5_im", [2, 2, 128, 16])
    I["meta_tokens"] = din("meta_tokens", [NMETA, D])
    for nm, sh in [("norm_mix", [2, D]), ("w_in", [2, D, INC]), ("conv_w", [2, 4, 1280]), ("conv_b", [2, 1280]),
                   ("dt_bias", [2, 16]), ("a_log", [2, 16]), ("d_ssd", [2, 16]), ("norm_ssd", [2, D]),
                   ("lam_re", [2, 32, 64]), ("lam_im", [2, 32, 64]), ("log_step", [2, 32]),
                   ("s5bc", [2, 4, 128, 2048]), ("d_s5", [2, 512]), ("w_glu", [2, 512, 2048]),
                   ("q_norm", [2, 64]), ("k_norm", [2, 64]), ("w_lift_a", [2, D, D]), ("w_lift_c", [2, 512, D]),
                   ("w_out", [2, D, D]), ("norm_ffn", [2, D]), ("w_up", [2, D, 4096]), ("w_down", [2, 4096, D])]:
        I[nm] = din(nm, sh)
    O = {}
    O["y_prompt"] = dout("y_prompt", [4, SEQ, D])
    O["y_sample"] = dout("y_sample", [2, DSEQ, D])
    O["k_prompt"] = dout("k_prompt", [2, 4, TP, 512])
    O["v_prompt"] = dout("v_prompt", [2, 4, TP, 512])
    O["conv_prompt"] = dout("conv_prompt", [2, 4, 128, 30])
    O["ssd_prompt"] = dout("ssd_prompt", [2, 4, 2, 64, 512])
    O["s5re_prompt"] = dout("s5re_prompt", [2, 4, 128, 16])
    O["s5im_prompt"] = dout("s5im_prompt", [2, 4, 128, 16])
    O["k_sample"] = dout("k_sample", [2, 2, DSEQ, 512])
    O["v_sample"] = dout("v_sample", [2, 2, DSEQ, 512])
    O["conv_sample"] = dout("conv_sample", [2, 2, 128, 30])
    O["ssd_sample"] = dout("ssd_sample", [2, 2, 2, 64, 512])
    O["s5re_sample"] = dout("s5re_sample", [2, 2, 128, 16])
    O["s5im_sample"] = dout("s5im_sample", [2, 2, 128, 16])
    kTs = nc.dram_tensor("kT_scr", [6, 2, 512, NKMAX], BF16, kind="Internal").ap()
    vhs = nc.dram_tensor("vh_scr", [6, 2, NKMAX, 512], BF16, kind="Internal").ap()

    PS = []
    for b in range(8):
        h = nc.alloc_psum_tensor(f"psb{b}", [128, 512], F32)
        PS.append(Tile(h, "ps", b * 2048, (b + 1) * 2048))

    def V(fn, reads, writes):
        return S.op("dve", fn, reads, writes)

    def A(fn, reads, writes):
        return S.op("act", fn, reads, writes)

    def G(fn, reads, writes):
        return S.op("pool", fn, reads, writes)

    def MM(out, lhsT, rhs, start, stop, reads, writes, **kw):
        return S.op("pe", lambda e: e.matmul(out, lhsT=lhsT, rhs=rhs, start=start, stop=stop, **kw), reads, writes)

    def TR(out, in_, ident, reads, writes):
        return S.op("pe", lambda e: e.matmul(out, lhsT=in_, rhs=ident, start=True, stop=True), reads, writes)

    def act(out, in_, func, reads, writes, bias=None, scale=None):
        kw = {}
        if bias is not None:
            kw["bias"] = bias
        if scale is not None:
            kw["scale"] = scale
        return A(lambda e: e.activation(out=out, in_=in_, func=func, **kw), reads, writes)

    def ldpar(out, in_, writes):
        return S.dma("sp", out, in_, [], writes, "par", allow_slow_non_contiguous=True)

    iota_pc = S.sb([128, 256], F32, "iota")
    ident_f = S.sb([128, 128], F32, "identf")
    ident_b = S.sb([128, 128], BF16, "identb")
    ones_b = S.sb([128, 128], BF16, "onesb")
    bd64_b = S.sb([128, 128], BF16, "bd64")
    tri_r = S.sb([128, 128], F32R, "tri")
    ones_r = S.sb([128, 128], F32R, "onesr")
    iota_t = S.sb([128, QS + 1], F32, "iotat")
    ones_f = S.sb([128, 128], F32, "onesf")
    masks = {}
    G(lambda e: e.iota(iota_pc[:, :], [[-1, 256]], base=0, channel_multiplier=1,
                       allow_small_or_imprecise_dtypes=True), [], [iota_pc])
    G(lambda e: e.iota(iota_t[:, :], [[1, QS + 1]], base=0, channel_multiplier=0,
                       allow_small_or_imprecise_dtypes=True), [], [iota_t])
    V(lambda e: e.tensor_single_scalar(out=ident_f[:, :], in_=iota_pc[:, 0:128], scalar=0.0, op=ALU.is_equal),
      [iota_pc], [ident_f])
    V(lambda e: e.tensor_copy(out=ident_b[:, :], in_=ident_f[:, :]), [ident_f], [ident_b])
    V(lambda e: e.memset(ones_b[:, :], 1.0), [], [ones_b])
    V(lambda e: e.memset(bd64_b[:, :], 0.0), [], [bd64_b])
    V(lambda e: e.memset(bd64_b[0:64, 0:64], 1.0), [], [bd64_b])
    V(lambda e: e.memset(bd64_b[64:128, 64:128], 1.0), [], [bd64_b])
    V(lambda e: e.tensor_scalar(out=tri_r[:, :], in0=iota_pc[:, 0:128], scalar1=0.0, scalar2=-8.0,
                                op0=ALU.is_ge, op1=ALU.mult), [iota_pc], [tri_r])
    V(lambda e: e.memset(ones_f[:, :], 1.0), [], [ones_f])
    V(lambda e: e.tensor_copy(out=ones_r[:, :], in_=ones_f[:, :]), [ones_f], [ones_r])
    for off in (16, -112, -240, 0):
        m = S.sb([128, 256], F32, "mask")
        V(lambda e, m=m, off=off: e.tensor_single_scalar(out=m[:, :], in_=iota_pc[:, :], scalar=float(off),
                                                         op=ALU.is_lt), [iota_pc], [m])
        masks[off] = m

    P = []
    stage = cfg.get('stage', 99)
    I["par"] = din("par", [2, 128, NPAR])
    for l in range(2):
        pt = S.sb([128, NPAR], F32, "par")
        S.dma("sp", pt[:, :], I["par"][l], [], [pt], "par")
        p = {"_t": pt}
        for nm, (c0, w) in PCOL.items():
            p[nm] = (pt, c0, w)
        P.append(p)

    def pc(l, nm, j=0, rows=slice(0, 128)):
        pt, c0, w = P[l][nm]
        return pt[rows, c0 + j:c0 + j + 1]

    def pv(l, nm):
        pt, c0, w = P[l][nm]
        return pt[:, c0:c0 + w]

    TWO_PI = 2.0 * math.pi
    MAGIC = 12582912.0
    S5TAB = []
    for l in range(2):
        S5TAB.append((S.sb([128, 16, QS + 1], F32, "cosT"), S.sb([128, 16, QS + 1], F32, "sinT"),
                      S.sb([128, 16, QS], F32, "tbr"), S.sb([128, 16, QS], F32, "tbi"),
                      S.sb([128, 16, QS], F32, "magT")))
    S5L = [(S.sb([128, 16], F32, "lre"), S.sb([128, 16], F32, "lim"), S.sb([128, 16], F32, "lst")) for l in range(2)]
    ARENA0 = S.mark()
    S5SCR = [S.sb([128, 16], F32, "s5s") for _ in range(8)] + [S.sb([128, 16, QS + 1], F32, "s5w") for _ in range(3)]
    for l in range(2 if stage >= 2 else 0):
        p = P[l]
        lre, lim, lst = S5L[l]
        for dst, nm in ((lre, "lre"), (lim, "lim"), (lst, "lst")):
            V(lambda e, dst=dst, nm=nm, l=l: e.tensor_copy(out=dst[:, :], in_=pv(l, nm)), [P[l]["_t"]], [dst])
        step, th, mag, t0_, t1_, t2_, fre, fim, ang, w1, w2 = S5SCR
        cosT, sinT, tbr, tbi, magT = S5TAB[l]
        p.update(cosT=cosT, sinT=sinT, tbr=tbr, tbi=tbi, magT=magT, mag=mag)
        act(step[:, :], lst[:, :], AF.Exp, [lst], [step])
        V(lambda e, th=th, lim=lim, step=step: e.tensor_tensor(out=th[:, :], in0=lim[:, :], in1=step[:, :], op=ALU.mult),
          [lim, step], [th])
        V(lambda e, t0_=t0_, lre=lre, step=step: e.tensor_tensor(out=t0_[:, :], in0=lre[:, :], in1=step[:, :], op=ALU.mult),
          [lre, step], [t0_])
        act(mag[:, :], t0_[:, :], AF.Exp, [t0_], [mag])
        V(lambda e, ang=ang, th=th: e.tensor_tensor(
            out=ang[:, :, :], in0=iota_t[:, :].unsqueeze(1).broadcast_to([128, 16, QS + 1]),
            in1=th[:, :].unsqueeze(2).broadcast_to([128, 16, QS + 1]), op=ALU.mult), [iota_t, th], [ang])
        for which, outT in (("sin", sinT), ("cos", cosT)):
            addc = 0.0 if which == "sin" else 0.25
            V(lambda e, ang=ang, w1=w1, addc=addc: e.tensor_scalar(
                out=w1[:, :, :], in0=ang[:, :, :], scalar1=1.0 / TWO_PI, scalar2=addc, op0=ALU.mult, op1=ALU.add),
              [ang], [w1])
            V(lambda e, w1=w1, w2=w2: e.tensor_scalar(out=w2[:, :, :], in0=w1[:, :, :], scalar1=MAGIC, scalar2=None,
                                                     op0=ALU.add), [w1], [w2])
            V(lambda e, w2=w2: e.tensor_scalar(out=w2[:, :, :], in0=w2[:, :, :], scalar1=-MAGIC, scalar2=None,
                                               op0=ALU.add), [w2], [w2])
            V(lambda e, w1=w1, w2=w2: e.tensor_tensor(out=w1[:, :, :], in0=w1[:, :, :], in1=w2[:, :, :],
                                                     op=ALU.subtract), [w1, w2], [w1])
            V(lambda e, w1=w1: e.tensor_scalar(out=w1[:, :, :], in0=w1[:, :, :], scalar1=-0.4999, scalar2=0.4999,
                                               op0=ALU.max, op1=ALU.min), [w1], [w1])
            act(outT[:, :, :], w1[:, :, :], AF.Sin, [w1], [outT], scale=TWO_PI)
        abr, abi = t1_, t2_
        V(lambda e, abr=abr, cosT=cosT, mag=mag: e.tensor_tensor(out=abr[:, :], in0=cosT[:, :, 1], in1=mag[:, :], op=ALU.mult),
          [cosT, mag], [abr])
        V(lambda e, abi=abi, sinT=sinT, mag=mag: e.tensor_tensor(out=abi[:, :], in0=sinT[:, :, 1], in1=mag[:, :], op=ALU.mult),
          [sinT, mag], [abi])
        V(lambda e, abr=abr: e.tensor_scalar(out=abr[:, :], in0=abr[:, :], scalar1=-1.0, scalar2=None, op0=ALU.add),
          [abr], [abr])
        den = step
        V(lambda e, den=den, lre=lre: e.tensor_tensor(out=den[:, :], in0=lre[:, :], in1=lre[:, :], op=ALU.mult), [lre], [den])
        V(lambda e, t0_=t0_, lim=lim: e.tensor_tensor(out=t0_[:, :], in0=lim[:, :], in1=lim[:, :], op=ALU.mult), [lim], [t0_])
        V(lambda e, den=den, t0_=t0_: e.tensor_tensor(out=den[:, :], in0=den[:, :], in1=t0_[:, :], op=ALU.add), [den, t0_], [den])
        V(lambda e, den=den: e.reciprocal(out=den[:, :], in_=den[:, :]), [den], [den])
        V(lambda e, fre=fre, abr=abr, lre=lre: e.tensor_tensor(out=fre[:, :], in0=abr[:, :], in1=lre[:, :], op=ALU.mult), [abr, lre], [fre])
        V(lambda e, t0_=t0_, abi=abi, lim=lim: e.tensor_tensor(out=t0_[:, :], in0=abi[:, :], in1=lim[:, :], op=ALU.mult), [abi, lim], [t0_])
        V(lambda e, fre=fre, t0_=t0_: e.tensor_tensor(out=fre[:, :], in0=fre[:, :], in1=t0_[:, :], op=ALU.add), [fre, t0_], [fre])
        V(lambda e, fre=fre, den=den: e.tensor_tensor(out=fre[:, :], in0=fre[:, :], in1=den[:, :], op=ALU.mult), [fre, den], [fre])
        V(lambda e, fim=fim, abi=abi, lre=lre: e.tensor_tensor(out=fim[:, :], in0=abi[:, :], in1=lre[:, :], op=ALU.mult), [abi, lre], [fim])
        V(lambda e, t0_=t0_, abr=abr, lim=lim: e.tensor_tensor(out=t0_[:, :], in0=abr[:, :], in1=lim[:, :], op=ALU.mult), [abr, lim], [t0_])
        V(lambda e, fim=fim, t0_=t0_: e.tensor_tensor(out=fim[:, :], in0=fim[:, :], in1=t0_[:, :], op=ALU.subtract), [fim, t0_], [fim])
        V(lambda e, fim=fim, den=den: e.tensor_tensor(out=fim[:, :], in0=fim[:, :], in1=den[:, :], op=ALU.mult), [fim, den], [fim])
        frb = lambda f: f[:, :].unsqueeze(2).broadcast_to([128, 16, QS])
        V(lambda e, w1=w1, cosT=cosT, fre=fre: e.tensor_tensor(out=w1[:, :, 0:QS], in0=cosT[:, :, 0:QS], in1=frb(fre), op=ALU.mult), [cosT, fre], [w1])
        V(lambda e, w2=w2, sinT=sinT, fim=fim: e.tensor_tensor(out=w2[:, :, 0:QS], in0=sinT[:, :, 0:QS], in1=frb(fim), op=ALU.mult), [sinT, fim], [w2])
        V(lambda e, tbr=tbr, w1=w1, w2=w2: e.tensor_tensor(out=tbr[:, :, :], in0=w1[:, :, 0:QS], in1=w2[:, :, 0:QS], op=ALU.add), [w1, w2], [tbr])
        V(lambda e, w1=w1, cosT=cosT, fim=fim: e.tensor_tensor(out=w1[:, :, 0:QS], in0=cosT[:, :, 0:QS], in1=frb(fim), op=ALU.mult), [cosT, fim], [w1])
        V(lambda e, w2=w2, sinT=sinT, fre=fre: e.tensor_tensor(out=w2[:, :, 0:QS], in0=sinT[:, :, 0:QS], in1=frb(fre), op=ALU.mult), [sinT, fre], [w2])
        V(lambda e, tbi=tbi, w1=w1, w2=w2: e.tensor_tensor(out=tbi[:, :, :], in0=w1[:, :, 0:QS], in1=w2[:, :, 0:QS], op=ALU.subtract), [w1, w2], [tbi])
        V(lambda e, magT=magT, mag=mag: e.tensor_tensor(
            out=magT[:, :, :], in0=ones_f[:, 0:QS].unsqueeze(1).broadcast_to([128, 16, QS]),
            in1=mag[:, :].unsqueeze(2).broadcast_to([128, 16, QS]), op=ALU.mult), [ones_f, mag], [magT])
    WB = {}
    OQ = cfg.get("oq", "pool")
    _wi = 0
    for nm, shp in [("w_in", [2, D, INC]), ("w_glu", [2, 512, 2048]), ("w_lift_a", [2, D, D]), ("w_lift_c", [2, 512, D]),
                    ("w_out", [2, D, D]), ("w_up", [2, D, 4096]), ("w_down", [2, 4096, D])]:
        WB[nm] = nc.dram_tensor(nm + "_bf", shp, BF16, kind="Internal").ap()
        rows, cols = shp[1], shp[2]
        chunk = max(128, ((1 << 20) // cols) // 128 * 128)
        for l in range(2):
            for r0 in range(0, rows, chunk):
                r1 = min(rows, r0 + chunk)
                S.dma("pool", WB[nm][l, r0:r1, :], I[nm][l, r0:r1, :], [], [("wbf", _wi, _wi + 1)], "wcast")
                _wi += 1
    upto = cfg.get("upto", 99)
    epsT = S.sb([128, 1], F32, "eps")
    V(lambda e: e.memset(epsT[:, :], EPS), [], [epsT])
    oneT = S.sb([128, 1], F32, "one")
    V(lambda e: e.memset(oneT[:, :], 1.0), [], [oneT])
    WDT = []
    for l in range(2):
        w_ = S.sb([128, 8, 48], BF16, "WDT")
        V(lambda e, w_=w_: e.memset(w_[:, :, :], 0.0), [], [w_])
        for c0 in (0, 32):
            S.dma("pool", w_[:, :, c0:c0 + 16], I["w_in"][l][:, OFF_DT:OFF_DT + 16].rearrange("(kc p) n -> p kc n", p=128),
                  [], [w_], "wdt")
        WDT.append(w_)

    def tt(out, in0, in1, op, reads, writes, eng="dve"):
        return S.op(eng, lambda e: e.tensor_tensor(out=out, in0=in0, in1=in1, op=op), reads, writes)

    def ts(out, in0, s1, s2, op0, op1, reads, writes, eng="dve"):
        if op1 is None:
            return S.op(eng, lambda e: e.tensor_scalar(out=out, in0=in0, scalar1=s1, scalar2=None, op0=op0), reads, writes)
        return S.op(eng, lambda e: e.tensor_scalar(out=out, in0=in0, scalar1=s1, scalar2=s2, op0=op0, op1=op1), reads, writes)

    def stt(out, in0, scalar, in1, op0, op1, reads, writes):
        return V(lambda e: e.scalar_tensor_tensor(out=out, in0=in0, scalar=scalar, in1=in1, op0=op0, op1=op1), reads, writes)

    def cp(out, in_, reads, writes, eng="dve"):
        return S.op(eng, lambda e: e.tensor_copy(out=out, in_=in_), reads, writes)

    def scan(out, d0, d1, init, reads, writes):
        return V(lambda e: e.tensor_tensor_scan(out=out, data0=d0, data1=d1, initial=init, op0=ALU.mult, op1=ALU.add), reads, writes)

    NWS = cfg.get('nws', 4)
    WSL = [S.sb([128, 4096], BF16, "wslot") for _ in range(NWS)]
    wctr = [0]

    def load_w(src_ap, kc, ncols, prt=128):
        i = wctr[0] % NWS
        wctr[0] += 1
        wt = WSL[i]
        view = wt[0:prt, 0:kc * ncols].rearrange("p (kc n) -> p kc n", n=ncols)
        v = src_ap.rearrange("(kc p) n -> p kc n", p=prt)
        step = max(1, kc // 4)
        for k0 in range(0, kc, step):
            S.dma("sp", view[:, k0:k0 + step, :], v[:, k0:k0 + step, :], [("wbf", 0, 100000)],
                  [wt.r(k0 * ncols * 2, (k0 + step) * ncols * 2)], f"w{i}")
        return wt, view

    ones48 = S.sb([48, 64], F32, "ones48")
    V(lambda e: e.memset(ones48[:, :], 1.0), [], [ones48])
    LT = []
    for i in range(2):
        t = S.sb([128, 128], F32, "LT")
        V(lambda e, t=t: e.memset(t[:, :], 0.0), [], [t])
        V(lambda e, t=t: e.memset(t[0:16, :], 1.0), [], [t])
        cp(t[64:128, 0:64], ident_f[64:128, 64:128], [ident_f], [t])
        LT.append(t)
    RF = []
    for g in range(2):
        t = S.sb([128, 8, 64], F32, "RF")
        V(lambda e, t=t: e.memset(t[:, :, :], 0.0), [], [t])
        for h in range(8):
            if True:
                r = 8 * g + h
                pass
        RF.append(t)
    DEL = []
    for g in range(2):
        d_ = S.sb([48, 8], F32, "DEL")
        io = S.sb([48, 8], F32, "DELi")
        G(lambda e, io=io: e.iota(io[:, :], [[-1, 8]], base=0, channel_multiplier=1, allow_small_or_imprecise_dtypes=True), [], [io])
        ts(d_[0:32, :], io[0:32, :], float(8 * g), None, ALU.is_equal, None, [io], [d_])
        ts(d_[32:48, :], io[32:48, :], float(32 + 8 * g), None, ALU.is_equal, None, [io], [d_])
        DEL.append(d_)
        cp(RF[g][32:48, :, :], d_[32:48, :].unsqueeze(2).broadcast_to([16, 8, 64]), [d_], [RF[g]])
        ts(RF[g][64:128, :, :], iota_pc[64:128, 0:64].unsqueeze(1).broadcast_to([64, 8, 64]), 64.0, NEG, ALU.is_gt, ALU.mult,
           [iota_pc], [RF[g]])
    SHIFT = S.sb([128, 128], BF16, "shift")
    V(lambda e: e.memset(SHIFT[:, :], 0.0), [], [SHIFT])
    cp(SHIFT[0:64, 64:128], ident_b[0:64, 0:64], [ident_b], [SHIFT])
    cp(SHIFT[64:128, 64:128], ident_b[64:128, 64:128], [ident_b], [SHIFT])
    BTOK = S.sb([64, 128], BF16, "btok")
    V(lambda e: e.memset(BTOK[:, :], 0.0), [], [BTOK])
    ST_, XS_, CH_, XR_, XI_, STm, CHm, XRm, XIm = [], [], [], [], [], [], [], [], []
    for l in range(2):
        ST_.append(S.sb([128, 2, 512], F32, "ST"))
        XS_.append(S.sb([128, 2, 4, 256], BF16, "XS"))
        CH_.append(S.sb([128, 10, 3], F32, "CH"))
        XR_.append(S.sb([128, 16], F32, "XR"))
        XI_.append(S.sb([128, 16], F32, "XI"))
        STm.append(S.sb([128, 2, 512], F32, "STm"))
        CHm.append(S.sb([128, 10, 3], F32, "CHm"))
        XRm.append(S.sb([128, 16], F32, "XRm"))
        XIm.append(S.sb([128, 16], F32, "XIm"))
    acol = []
    for l in range(2):
        a_ = S.sb([48, 1], F32, "acol")
        act(a_[:, :], pc(l, "alog", 0, slice(0, 48)), AF.Exp, [P[l]["_t"]], [a_])
        ts(a_[0:32, :], a_[0:32, :], -1.0, None, ALU.mult, None, [a_], [a_])
        acol.append(a_)

    def st_to_xs(l):
        for g in range(2):
            src = ST_[l][64:128, g, :].rearrange("p (pr eo d) -> p pr eo d", pr=4, eo=2)
            dst = XS_[l][64:128, g, :, :].rearrange("p pr (blk d) -> p pr blk d", d=64)[:, :, 1::2, :]
            cp(dst, src, [ST_[l].c(g)], [XS_[l].c(g)])

    xT = S.sb([128, 8, NT], F32, "xT")
    mixT = S.sb([128, 8, NT], F32, "mixT")
    hT = S.sb([128, 8, NT], BF16, "hT")
    ARENA = S.mark()
    seqs = [dict(kind="meta", b=0, T=NMETA, slot=0)]
    for b in range(n_pseq):
        seqs.append(dict(kind="prompt", b=b, T=n_ptiles * NT, slot=b))
    for b in range(n_sseq):
        seqs.append(dict(kind="sample", b=b, T=DSEQ, slot=4 + b))
    if stage < 3:
        seqs = []
    only = cfg.get('only', ['meta', 'prompt', 'sample'])
    seqs = [q for q in seqs if q['kind'] in only]

    def rmsnorm_to(dst_bf, src_f32, ncol_name, l, N):
        mk = S.mark()
        sqb = S.sb([128, 8, N], BF16, "sqb")
        rstd = S.sb([128, N], F32, "rstd")
        for c in range(8):
            act(sqb[:, c, :], src_f32[:, c, 0:N], AF.Square, [src_f32.c(c)], [sqb.c(c)])
        bank = PS[4]
        for c in range(8):
            MM(bank[:, 0:N], ones_b[:, :], sqb[:, c, :], c == 0, c == 7, [ones_b, sqb.c(c)], [bank.r(0, 4 * N)])
        act(rstd[:, :], bank[:, 0:N], AF.Sqrt, [bank.r(0, 4 * N), epsT], [rstd], bias=epsT[:, 0:1], scale=1.0 / D)
        V(lambda e, o_=rstd[:, :]: e.reciprocal(out=o_, in_=o_), [rstd], [rstd])
        for c in range(8):
            stt(dst_bf[:, c, 0:N], src_f32[:, c, 0:N], pc(l, ncol_name, c), rstd[:, :], ALU.mult, ALU.mult,
                [src_f32.c(c), P[l]["_t"], rstd], [dst_bf.c(c)])
        S.reset(mk)

    def proj_fm(l, wsrc, kc, c0, ncols, rhs_fn, rhs_reads, N, evac, prt=128, bankset=(0, 1)):
        mi = 0
        for g0 in range(0, ncols, 512):
            gw = min(512, ncols - g0)
            wt, view = load_w(wsrc[:, c0 + g0:c0 + g0 + gw], kc, gw, prt)
            for m0 in range(0, gw, 128):
                mw = min(128, gw - m0)
                bank = PS[bankset[mi % len(bankset)]]
                for k in range(kc):
                    MM(bank[0:mw, 0:N], view[:, k, m0:m0 + mw], rhs_fn(k), k == 0, k == kc - 1,
                       [wt] + rhs_reads(k), [bank.r(0, 4 * N)])
                evac(mi, bank, mw)
                mi += 1

    for sq in seqs:
        kind, b, slot = sq["kind"], sq["b"], sq["slot"]
        tiles = [(t0, min(NT, sq["T"] - t0)) for t0 in range(0, sq["T"], NT)]
        for l in range(2):
            if kind == "prompt" and upto < 3:
                continue
            if kind == "meta":
                for t in (ST_[l], CH_[l], XR_[l], XI_[l], XS_[l]):
                    V(lambda e, a_=t.h[:]: e.memset(a_, 0.0), [], [t])
            elif kind == "prompt":
                for dst, src in ((ST_[l], STm[l]), (CH_[l], CHm[l]), (XR_[l], XRm[l]), (XI_[l], XIm[l])):
                    cp(dst.h[:], src.h[:], [src], [dst])
                st_to_xs(l)
            else:
                S.dma(OQ, CH_[l][:, :, :].rearrange("p c k -> p (c k)"), I["state_conv"][l, b], [], [CH_[l]], "stin")
                for g in range(2):
                    S.dma(OQ, ST_[l][64:128, g, :], I["state_ssd"][l, b, g], [], [ST_[l].c(g)], "stin")
                S.dma(OQ, XR_[l][:, :], I["state_s5_re"][l, b], [], [XR_[l]], "stin")
                S.dma(OQ, XI_[l][:, :], I["state_s5_im"][l, b], [], [XI_[l]], "stin")
                st_to_xs(l)
                S.reset(ARENA)
                S.dma("pool", vhs[slot, l, 0:PAST, :], I["cache_v"][l, b], [], [("vh%d_%d" % (slot, l), 0, PAST)], "vpre")
                CK = [S.sb([128, 512], F32, "CK") for _ in range(2)]
                KTP = [S.sb([128, 4, 128], BF16, "KTP") for _ in range(2)]
                for tb in range(PAST // 128):
                    ck, kt = CK[tb % 2], KTP[tb % 2]
                    S.dma(OQ, ck[:, :], I["cache_k"][l, b, 128 * tb:128 * tb + 128, :], [], [ck], "ckl%d" % (tb % 2))
                    bank = PS[7]
                    for m in range(4):
                        TR(bank[:, 128 * m:128 * m + 128], ck[:, 128 * m:128 * m + 128], ident_f[:, :], [ck, ident_f],
                           [bank.r(512 * m, 512 * m + 512)])
                    act(kt[:, :, :], bank[:, :].rearrange("p (m t) -> p m t", t=128), AF.Copy, [bank], [kt])
                    S.dma(OQ, kTs[slot, l].rearrange("(m p) t -> p m t", p=128)[:, :, 128 * tb:128 * tb + 128], kt[:, :, :],
                          [kt], [("kT%d_%d" % (slot, l), 128 * tb, 128 * tb + 128)], "kpre%d" % (tb % 2))
        for (t0, N) in tiles:
            S.reset(ARENA)
            pos0 = {"meta": 0, "prompt": NMETA + t0, "sample": PAST}[kind]
            nbk = (N + 127) // 128
            blk = [(tb, min(128, N - 128 * tb)) for tb in range(nbk)]
            if kind == "meta":
                xsrc = I["meta_tokens"]
            elif kind == "prompt":
                xsrc = I["x_prompt"][b, t0:t0 + N, :]
            else:
                xsrc = I["x_sample"][b, t0:t0 + N, :]
            mk0 = S.mark()
            x_tm = S.sb([128, nbk, D], F32, "x_tm")
            for tb, nt in blk:
                S.dma(OQ, x_tm[0:nt, tb, :], xsrc[128 * tb:128 * tb + nt, :], [], [x_tm.c(tb)], "xin")
            for c in range(8):
                bank = PS[c % 4]
                for tb, nt in blk:
                    TR(bank[:, 128 * tb:128 * tb + nt], x_tm[0:nt, tb, 128 * c:128 * c + 128], ident_f[0:nt, 0:nt],
                       [x_tm.c(tb), ident_f], [bank.r(512 * tb, 512 * tb + 4 * nt)])
                act(xT[:, c, 0:N], bank[:, 0:N], AF.Copy, [bank.r(0, 4 * N)], [xT.c(c)])
            S.reset(mk0)
            for l in range(2):
                S.reset(ARENA)
                rmsnorm_to(hT, xT, "nmix", l, N)
                hrd = lambda k: [hT.c(k)]
                hrhs = lambda k, N=N: hT[:, k, 0:N]
                if kind == "meta":
                    kdst = [O["k_prompt"][l, bb, 0:NMETA, :] for bb in range(n_pseq)]
                    vdst = [O["v_prompt"][l, bb, 0:NMETA, :] for bb in range(n_pseq)]
                    slots = list(range(n_pseq))
                elif kind == "prompt":
                    kdst = [O["k_prompt"][l, b, NMETA + t0:NMETA + t0 + N, :]]
                    vdst = [O["v_prompt"][l, b, NMETA + t0:NMETA + t0 + N, :]]
                    slots = [slot]
                else:
                    kdst = [O["k_sample"][l, b, t0:t0 + N, :]]
                    vdst = [O["v_sample"][l, b, t0:t0 + N, :]]
                    slots = [slot]
                mkA = S.mark()
                knT = S.sb([128, 4, N], F32, "knT")
                knb = S.sb([128, 4, N], BF16, "knb")
                qnb = S.sb([128, 4, N], BF16, "qnb")
                ksq = S.sb([128, N], BF16, "ksq")
                rk = S.sb([128, N], F32, "rk")

                def qk_evac(dst32, dstbf, gname):
                    def ev(m, bank, mw):
                        kd = cfg.get("kd", 99)
                        if kd < 1:
                            return
                        act(ksq[:, :], bank[:, 0:N], AF.Square, [bank.r(0, 4 * N)], [ksq])
                        if kd < 2:
                            return
                        b2 = PS[2 + m % 2]
                        MM(b2[:, 0:N], bd64_b[:, :], ksq[:, :], True, True, [bd64_b, ksq], [b2.r(0, 4 * N)])
                        if kd < 3:
                            return
                        act(rk[:, :], b2[:, 0:N], AF.Sqrt, [b2.r(0, 4 * N), epsT], [rk], bias=epsT[:, 0:1], scale=1.0 / 64)
                        if kd < 4:
                            return
                        V(lambda e, o_=rk[:, :]: e.reciprocal(out=o_, in_=o_), [rk], [rk])
                        if kd < 5:
                            return
                        if dst32 is not None:
                            stt(dst32[:, m, :], bank[:, 0:N], pc(l, gname), rk[:, :], ALU.mult, ALU.mult,
                                [bank.r(0, 4 * N), P[l]["_t"], rk], [dst32.c(m)])
                            cp(dstbf[:, m, :], dst32[:, m, :], [dst32.c(m)], [dstbf.c(m)], eng="pool")
                        else:
                            stt(dstbf[:, m, :], bank[:, 0:N], pc(l, gname), rk[:, :], ALU.mult, ALU.mult,
                                [bank.r(0, 4 * N), P[l]["_t"], rk], [dstbf.c(m)])
                    return ev
                proj_fm(l, WB["w_in"][l], 8, OFF_K, 512, hrhs, hrd, N, qk_evac(knT, knb, "kn"))
                proj_fm(l, WB["w_in"][l], 8, OFF_Q, 512, hrhs, hrd, N, qk_evac(None, qnb, "qn"))
                if cfg.get("kd", 99) < 7:
                    continue
                for sl in slots:
                    S.dma(OQ, kTs[sl, l].rearrange("(m p) t -> p m t", p=128)[:, :, pos0:pos0 + N], knb[:, :, :],
                          [knb], [("kT%d_%d" % (sl, l), pos0, pos0 + N)], "ktw")
                if cfg.get("kd", 99) < 8:
                    continue
                k_tm = S.sb([128, nbk, 512], F32, "k_tm")
                for tb, nt in blk:
                    bank = PS[5]
                    for m in range(4):
                        TR(bank[0:nt, 128 * m:128 * m + 128], knT[:, m, 128 * tb:128 * tb + nt], ident_f[:, :],
                           [knT.c(m), ident_f], [bank.r(512 * m, 512 * m + 512)])
                    act(k_tm[0:nt, tb, :], bank[0:nt, :], AF.Copy, [bank], [k_tm.c(tb)])
                    for dst in kdst:
                        S.dma(OQ, dst[128 * tb:128 * tb + nt, :], k_tm[0:nt, tb, :], [k_tm.c(tb)], [], "kout")
                if cfg.get("kd", 99) < 9:
                    continue
                wt, wv = load_w(WB["w_in"][l][:, OFF_V:OFF_V + 512], 8, 512)
                v_tm = S.sb([128, nbk, 512], F32, "v_tm")
                v_bf = S.sb([128, nbk, 512], BF16, "v_bf")
                for tb, nt in blk:
                    bank = PS[6]
                    for kc in range(8):
                        MM(bank[0:nt, :], hT[:, kc, 128 * tb:128 * tb + nt], wv[:, kc, :], kc == 0, kc == 7,
                           [wt, hT.c(kc)], [bank])
                    act(v_tm[0:nt, tb, :], bank[0:nt, :], AF.Copy, [bank], [v_tm.c(tb)])
                    for dst in vdst:
                        S.dma(OQ, dst[128 * tb:128 * tb + nt, :], v_tm[0:nt, tb, :], [v_tm.c(tb)], [], "vout")
                    if cfg.get("kd", 99) < 10:
                        continue
                    cp(v_bf[0:nt, tb, :], v_tm[0:nt, tb, :], [v_tm.c(tb)], [v_bf.c(tb)], eng="pool")
                    if cfg.get("vh", 2) < 2:
                        continue
                    for sl in slots:
                        S.dma(OQ, vhs[sl, l, pos0 + 128 * tb:pos0 + 128 * tb + nt, :], v_bf[0:nt, tb, :],
                              [v_bf.c(tb)], [("vh%d_%d" % (sl, l), pos0 + 128 * tb, pos0 + 128 * tb + nt)], "vhw")
                if upto < 2:
                    continue
                GT = S.sb([128, 8, N], F32, "GT")
                tmpA = S.sb([128, N], F32, "tmpA")

                def gates(i):
                    proj_fm(l, WB["w_in"][l], 8, OFF_G + 1024 * i, 1024, hrhs, hrd, N,
                            lambda m, bank, mw: act(GT[:, m, :], bank[:, 0:N], AF.Sigmoid, [bank.r(0, 4 * N)], [GT.c(m)]))

                def mix_evac(first):
                    def ev(m, bank, mw):
                        if first:
                            tt(mixT[:, m, 0:N], GT[:, m, :], bank[:, 0:N], ALU.mult, [GT.c(m), bank.r(0, 4 * N)], [mixT.c(m)])
                        else:
                            tt(tmpA[:, :], GT[:, m, :], bank[:, 0:N], ALU.mult, [GT.c(m), bank.r(0, 4 * N)], [tmpA])
                            tt(mixT[:, m, 0:N], mixT[:, m, 0:N], tmpA[:, :], ALU.add, [mixT.c(m), tmpA], [mixT.c(m)])
                    return ev
                nk = pos0 + N
                nb = (nk + 127) // 128
                slot_r = slots[0]
                OC = S.sb([64, 8, N], BF16, "OC")
                RS = S.sb([128, N], F32, "RS")
                EX = S.sb([128, N], F32, "EX")
                LPs = [S.sb([128, N], F32, "LP") for _ in range(2)]
                ARG = S.sb([128, N], F32, "ARG")
                WTs = [S.sb([128, N], BF16, "WT") for _ in range(2)]
                KTb = [S.sb([128, NKMAX], BF16, "KT") for _ in range(2)]
                VBb = [S.sb([128, 17, 128], BF16, "VB") for _ in range(2)]
                nfull, rem = nk // 128, nk % 128
                bctr = 0
                for pr in range(4):
                    KT, VB = KTb[pr % 2], VBb[pr % 2]
                    kname, vname = "kT%d_%d" % (slot_r, l), "vh%d_%d" % (slot_r, l)
                    S.dma(OQ, KT[:, 0:nk], kTs[slot_r, l, 128 * pr:128 * pr + 128, 0:nk], [(kname, 0, nk)], [KT], "ktl%d" % (pr % 2))
                    if nfull:
                        S.dma(OQ, VB[:, 0:nfull, :],
                              vhs[slot_r, l, 0:128 * nfull, 128 * pr:128 * pr + 128].rearrange("(b p) c -> p b c", p=128),
                              [(vname, 0, 128 * nfull)], [VB.r(0, nfull * 256)], "vbl%d" % (pr % 2))
                    if rem:
                        S.dma(OQ, VB[0:rem, nfull, :], vhs[slot_r, l, 128 * nfull:nk, 128 * pr:128 * pr + 128],
                              [(vname, 128 * nfull, nk)], [VB.r(nfull * 256, nfull * 256 + 256)], "vbl%d" % (pr % 2))
                    for hh in range(2):
                        h = 2 * pr + hh
                        R = slice(64 * hh, 64 * hh + 64)
                        V(lambda e, a_=RS[:, :]: e.memset(a_, 0.0), [], [RS])
                        PO = PS[3]
                        first = True
                        for bI in reversed(range(nb)):
                            kb = min(128, nk - 128 * bI)
                            PZ, P2 = PS[bctr % 2], PS[2]
                            LP, WT = LPs[bctr % 2], WTs[bctr % 2]
                            bctr += 1
                            MM(PZ[0:kb, 0:N], KT[R, 128 * bI:128 * bI + kb], qnb[R, pr, 0:N], True, True,
                               [KT, qnb.c(pr)], [PZ.r(0, 4 * N)])
                            act(EX[0:kb, :], PZ[0:kb, 0:N], AF.Exp, [PZ.r(0, 4 * N)], [EX], scale=0.125)
                            act(LP[0:kb, :].bitcast(F32R), EX[0:kb, :], AF.Ln, [EX, oneT], [LP], bias=oneT[0:kb, 0:1])
                            diag = (128 * bI + kb - 1 >= pos0)
                            if diag:
                                mt = masks[pos0 - 128 * bI]
                                tt(LP[0:kb, :].bitcast(F32R), LP[0:kb, :], mt[0:kb, 0:N], ALU.mult, [LP, mt], [LP])
                            MM(PZ[0:kb, 0:N], tri_r[0:kb, 0:kb], LP[0:kb, :].bitcast(F32R), False, True,
                               [tri_r, LP], [PZ.r(0, 4 * N)], skip_group_check=True)
                            stt(ARG[0:kb, :], PZ[0:kb, 0:N], 0.125, RS[0:kb, :], ALU.mult, ALU.subtract,
                                [PZ.r(0, 4 * N), RS], [ARG])
                            act(WT[0:kb, :], ARG[0:kb, :], AF.Exp, [ARG], [WT])
                            if diag:
                                tt(WT[0:kb, :], WT[0:kb, :], mt[0:kb, 0:N], ALU.mult, [WT, mt], [WT])
                            if bI > 0:
                                MM(P2[:, 0:N], ones_r[0:kb, :], LP[0:kb, :].bitcast(F32R), True, True, [ones_r, LP], [P2.r(0, 4 * N)])
                                tt(RS[:, :], RS[:, :], P2[:, 0:N], ALU.add, [RS, P2.r(0, 4 * N)], [RS])
                            MM(PO[0:64, 0:N], VB[0:kb, bI, 64 * hh:64 * hh + 64], WT[0:kb, :], first, bI == 0,
                               [VB.r(bI * 256, bI * 256 + 256), WT], [PO.r(0, 4 * N)])
                            first = False
                        act(OC[:, h, :], PO[0:64, 0:N], AF.Copy, [PO.r(0, 4 * N)], [OC.c(h)])
                gates(2)
                proj_fm(l, WB["w_lift_c"][l], 8, 0, 1024, lambda h_: OC[:, h_, :], lambda h_: [OC.c(h_)], N, mix_evac(True), prt=64)
                if upto < 3:
                    continue
                S.reset(mkA)
                GT = S.sb([128, 8, N], F32, "GT")
                tmpA = S.sb([128, N], F32, "tmpA")
                Q = min(64, N)
                nch = N // Q
                xp = S.sb([128, 10, N + 3], F32, "xp")
                XBC = S.sb([128, 10, N], BF16, "XBC")
                yT = S.sb([128, 8, N], F32, "yT")
                cp(xp[:, :, 0:3], CH_[l][:, :, :], [CH_[l]], [xp])
                proj_fm(l, WB["w_in"][l], 8, OFF_XBC, 1280, hrhs, hrd, N,
                        lambda m, bank, mw: act(xp[:, m, 3:3 + N], bank[:, 0:N], AF.Copy, [bank.r(0, 4 * N)], [xp.c(m)]))
                cp(CH_[l][:, :, :], xp[:, :, N:N + 3], [xp], [CH_[l]])
                for c in range(10):
                    ts(tmpA[:, :], xp[:, c, 0:N], pc(l, "cw", c), pc(l, "cb", c), ALU.mult, ALU.add, [xp.c(c), P[l]["_t"]], [tmpA])
                    for k in range(1, 4):
                        stt(tmpA[:, :], xp[:, c, k:k + N], pc(l, "cw", 10 * k + c), tmpA[:, :], ALU.mult, ALU.add,
                            [xp.c(c), P[l]["_t"], tmpA], [tmpA])
                    act(XBC[:, c, :], tmpA[:, :], AF.Silu, [tmpA], [XBC.c(c)])
                if cfg.get("sd", 99) < 2:
                    continue
                PD = PS[2]
                for k in range(8):
                    MM(PD[0:48, 0:N], WDT[l][:, k, :], hT[:, k, 0:N], k == 0, k == 7, [WDT[l], hT.c(k)], [PD.r(0, 4 * N)])
                dtT = S.sb([48, N], F32, "dtT")
                dA = S.sb([48, N], F32, "dA")
                AC = S.sb([16, N], F32, "AC")
                act(dtT[:, :], PD[0:48, 0:N], AF.Exp, [PD.r(0, 4 * N), P[l]["_t"]], [dtT], bias=pc(l, "dtb", 0, slice(0, 48)))
                act(dtT[:, :], dtT[:, :], AF.Ln, [dtT, oneT], [dtT], bias=oneT[0:48, 0:1])
                ts(dA[:, :], dtT[:, :], acol[l][:, 0:1], None, ALU.mult, None, [dtT, acol[l]], [dA])
                if cfg.get("sd", 99) < 3:
                    continue
                Ets = [S.sb([128, 8, Q], F32, "Et") for _ in range(2)]
                Wts = [S.sb([128, 8, Q], BF16, "Wt") for _ in range(2)]
                if Q < 64:
                    for w__ in Wts:
                        V(lambda e, a_=w__[:, :, :]: e.memset(a_, 0.0), [], [w__])
                dt_tm = S.sb([64, 16], F32, "dt_tm")
                coef = S.sb([64, 8], F32, "coef")
                XW = S.sb([64, 8, 64], BF16, "XW")
                ectr = 0
                for ci in range(nch):
                    cs = slice(ci * Q, ci * Q + Q)
                    LTt = LT[ci % 2]
                    scan(AC[0:16, cs], ones48[0:16, 0:Q], dA[0:16, cs], 0.0, [ones48, dA], [AC])
                    scan(LTt[32:48, 0:Q], ones48[32:48, 0:Q], dA[32:48, cs], 0.0, [ones48, dA], [LTt])
                    PT = PS[7]
                    MM(PT[0:Q, 0:16], dtT[0:16, cs], ident_f[0:16, 0:16], True, True, [dtT, ident_f], [PT.r(0, 64)])
                    act(dt_tm[0:Q, :], PT[0:Q, 0:16], AF.Copy, [PT.r(0, 64)], [dt_tm])
                    if cfg.get("sd", 99) < 4:
                        continue
                    PY = PS[3]
                    for g in range(2):
                        GR = slice(64 * g, 64 * g + 64)
                        Et, Wt = Ets[ectr % 2], Wts[ectr % 2]
                        ectr += 1
                        tt(RF[g][0:16, :, 0:Q], AC[0:16, cs].unsqueeze(1).broadcast_to([16, 8, Q]),
                           DEL[g][0:16, :].unsqueeze(2).broadcast_to([16, 8, Q]), ALU.mult, [AC, DEL[g]], [RF[g]])
                        PE_ = PS[4]
                        pev = PE_[:, 0:8 * Q].rearrange("p (h i) -> p h i", i=Q)
                        MM(pev, LTt[:, :], RF[g][:, :, 0:Q], True, True, [LTt, RF[g]], [PE_])
                        act(Et[:, :, :], pev, AF.Exp, [PE_], [Et])
                        if cfg.get("sd", 99) < 5:
                            continue
                        PG = PS[5]
                        MM(PG[0:Q, 0:Q], XBC[GR, 8, cs], XBC[GR, 9, cs], True, True, [XBC.c(8), XBC.c(9)], [PG.r(0, 256)])
                        tt(Wt[0:Q, :, :], Et[0:Q, :, :], PG[0:Q, 0:Q].unsqueeze(1).broadcast_to([Q, 8, Q]), ALU.mult,
                           [Et, PG.r(0, 256)], [Wt])
                        MM(PG[:, 128:128 + Q], SHIFT[GR, :], XBC[GR, 9, cs], True, True, [SHIFT, XBC.c(9)], [PG.r(512, 768)])
                        tt(Wt[64:128, :, :], Et[64:128, :, :], PG[64:128, 128:128 + Q].unsqueeze(1).broadcast_to([64, 8, Q]), ALU.mult,
                           [Et, PG.r(512, 768)], [Wt])
                        if cfg.get("sd", 99) < 6:
                            continue
                        PX = PS[6]
                        for pr in range(4):
                            MM(PX[0:Q, 128 * pr:128 * pr + 128], XBC[:, 4 * g + pr, cs], ident_b[:, :], True, True,
                               [XBC.c(4 * g + pr), ident_b], [PX.r(512 * pr, 512 * pr + 512)])
                        dstX = XS_[l][0:Q, g, :, :].rearrange("q pr (blk d) -> q pr blk d", d=64)[:, :, 1::2, :]
                        tt(dstX, PX[0:Q, :].rearrange("q (pr eo d) -> q pr eo d", pr=4, eo=2),
                           dt_tm[0:Q, 8 * g:8 * g + 8].rearrange("q (pr eo) -> q pr eo", eo=2).unsqueeze(3).broadcast_to([Q, 4, 2, 64]),
                           ALU.mult, [PX, dt_tm], [XS_[l].c(g)])
                        if cfg.get("sd", 99) < 7:
                            continue
                        for pr in range(4):
                            k = 4 * g + pr
                            for eo in range(2):
                                win = slice(64 + 64 * eo, 192 + 64 * eo)
                                MM(PY[:, k * Q:k * Q + Q], XS_[l][:, g, pr, win], Wt[:, 2 * pr + eo, :], eo == 0, eo == 1,
                                   [XS_[l].c(g), Wt], [PY.r(4 * k * Q, 4 * k * Q + 4 * Q)])
                        if cfg.get("sd", 99) < 8:
                            continue
                        tt(coef[0:Q, :], dt_tm[0:Q, 8 * g:8 * g + 8], Et[0:Q, :, Q - 1], ALU.mult, [dt_tm, Et], [coef])
                        tt(XW[0:Q, :, :], PX[0:Q, :].rearrange("q (h d) -> q h d", d=64),
                           coef[0:Q, :].unsqueeze(2).broadcast_to([Q, 8, 64]), ALU.mult, [PX, coef], [XW])
                        MM(PG[0:Q, 64:128], XBC[GR, 8, cs], ident_b[GR, 64 * g:64 * g + 64], True, True,
                           [XBC.c(8), ident_b], [PG.r(256, 512)])
                        cp(BTOK[0:Q, 64:128], PG[0:Q, 64:128], [PG.r(256, 512)], [BTOK])
                        PSt = PS[2]
                        MM(PSt[:, :], BTOK[0:Q, :], XW[0:Q, :, :], True, True, [BTOK, XW], [PSt])
                        stv = ST_[l][64:128, g, :].rearrange("p (h d) -> p h d", d=64)
                        tt(stv, stv, Et[64:128, :, Q - 1].unsqueeze(2).broadcast_to([64, 8, 64]), ALU.mult, [ST_[l].c(g), Et], [ST_[l].c(g)])
                        tt(ST_[l][64:128, g, :], ST_[l][64:128, g, :], PSt[64:128, :], ALU.add, [ST_[l].c(g), PSt], [ST_[l].c(g)])
                        src = ST_[l][64:128, g, :].rearrange("p (pr eo d) -> p pr eo d", pr=4, eo=2)
                        dst = XS_[l][64:128, g, :, :].rearrange("p pr (blk d) -> p pr blk d", d=64)[:, :, 1::2, :]
                        cp(dst, src, [ST_[l].c(g)], [XS_[l].c(g)])
                    if cfg.get("sd", 99) < 9:
                        continue
                    act(yT[:, :, cs], PY[:, 0:8 * Q].rearrange("p (k i) -> p k i", i=Q), AF.Copy, [PY], [yT])
                if cfg.get("sd", 99) < 10:
                    continue
                for k in range(8):
                    stt(yT[:, k, :], XBC[:, k, :], pc(l, "dsk", k), yT[:, k, :], ALU.mult, ALU.add, [XBC.c(k), P[l]["_t"], yT.c(k)], [yT.c(k)])

                def z_evac(m, bank, mw):
                    act(tmpA[:, :], bank[:, 0:N], AF.Silu, [bank.r(0, 4 * N)], [tmpA])
                    tt(yT[:, m, :], yT[:, m, :], tmpA[:, :], ALU.mult, [yT.c(m), tmpA], [yT.c(m)])
                proj_fm(l, WB["w_in"][l], 8, OFF_Z, 1024, hrhs, hrd, N, z_evac)
                ysq = S.sb([128, 8, N], BF16, "ysq")
                YN = S.sb([128, 8, N], BF16, "YN")
                rg = S.sb([128, N], F32, "rg")
                for k in range(8):
                    act(ysq[:, k, :], yT[:, k, :], AF.Square, [yT.c(k)], [ysq.c(k)])
                for g in range(2):
                    bank = PS[4]
                    for k in range(4 * g, 4 * g + 4):
                        MM(bank[:, 0:N], ones_b[:, :], ysq[:, k, :], k == 4 * g, k == 4 * g + 3, [ones_b, ysq.c(k)], [bank.r(0, 4 * N)])
                    act(rg[:, :], bank[:, 0:N], AF.Sqrt, [bank.r(0, 4 * N), epsT], [rg], bias=epsT[:, 0:1], scale=1.0 / 512)
                    V(lambda e, o_=rg[:, :]: e.reciprocal(out=o_, in_=o_), [rg], [rg])
                    for k in range(4 * g, 4 * g + 4):
                        stt(YN[:, k, :], yT[:, k, :], pc(l, "nssd", k), rg[:, :], ALU.mult, ALU.mult, [yT.c(k), P[l]["_t"], rg], [YN.c(k)])
                gates(0)
                proj_fm(l, WB["w_lift_a"][l], 8, 0, 1024, lambda k_: YN[:, k_, :], lambda k_: [YN.c(k_)], N, mix_evac(False))
                if upto < 4:
                    continue
                S.reset(mkA)
                GT = S.sb([128, 8, N], F32, "GT")
                tmpA = S.sb([128, N], F32, "tmpA")
                BC = S.sb([128, 4, 16, 128], BF16, "BC")
                for i in range(4):
                    S.dma("pool", BC[:, i, :, :].rearrange("p c n -> p (c n)"), I["s5bc"][l, i], [], [BC.c(i)], "bcl")
                uT = S.sb([128, 4, N], BF16, "uT")
                u32 = S.sb([128, 4, N], F32, "u32")

                def u_evac(m, bank, mw):
                    act(u32[:, m, :], bank[:, 0:N], AF.Copy, [bank.r(0, 4 * N)], [u32.c(m)])
                    cp(uT[:, m, :], u32[:, m, :], [u32.c(m)], [uT.c(m)], eng="pool")
                proj_fm(l, WB["w_in"][l], 8, OFF_U, 512, hrhs, hrd, N, u_evac)
                A_ = S.sb([128, 8, QS], F32, "A_")
                B_ = S.sb([128, 8, QS], F32, "B_")
                C_ = S.sb([128, 8, QS], F32, "C_")
                D_ = S.sb([128, 8, QS], F32, "D_")
                xrb = S.sb([128, 8, QS], BF16, "xrb")
                xib = S.sb([128, 8, QS], BF16, "xib")
                WIr = S.sb([128, 8], F32, "WIr")
                WIi = S.sb([128, 8], F32, "WIi")
                t8 = S.sb([128, 8], F32, "t8")
                cosT, sinT, tbr, tbi, magT = S5TAB[l]
                XR, XI = XR_[l], XI_[l]
                nsub = (N + QS - 1) // QS
                for s_ in range(nsub):
                    ncol = min(QS, N - s_ * QS)
                    cs = slice(s_ * QS, s_ * QS + ncol)
                    for hf in range(2):
                        ch = slice(8 * hf, 8 * hf + 8)
                        Pre, Pim = PS[0], PS[1]
                        for cc in range(8):
                            c = 8 * hf + cc
                            MM(Pre[:, cc * QS:cc * QS + ncol], BC[:, 0, c, :], uT[:, c // 4, cs], True, True,
                               [BC.c(0), uT.c(c // 4)], [Pre.r(4 * cc * QS, 4 * cc * QS + 4 * ncol)])
                            MM(Pim[:, cc * QS:cc * QS + ncol], BC[:, 1, c, :], uT[:, c // 4, cs], True, True,
                               [BC.c(1), uT.c(c // 4)], [Pim.r(4 * cc * QS, 4 * cc * QS + 4 * ncol)])
                        PreV = Pre[:, :].rearrange("p (c t) -> p c t", t=QS)[:, :, 0:ncol]
                        PimV = Pim[:, :].rearrange("p (c t) -> p c t", t=QS)[:, :, 0:ncol]
                        a_, b_, c_, d_ = A_[:, :, 0:ncol], B_[:, :, 0:ncol], C_[:, :, 0:ncol], D_[:, :, 0:ncol]
                        tr_, ti_ = tbr[:, ch, 0:ncol], tbi[:, ch, 0:ncol]
                        tt(a_, PreV, tr_, ALU.mult, [Pre, tbr], [A_])
                        tt(b_, PimV, ti_, ALU.mult, [Pim, tbi], [B_])
                        tt(a_, a_, b_, ALU.subtract, [A_, B_], [A_])
                        tt(b_, PreV, ti_, ALU.mult, [Pre, tbi], [B_])
                        tt(c_, PimV, tr_, ALU.mult, [Pim, tbr], [C_])
                        tt(b_, b_, c_, ALU.add, [B_, C_], [B_])
                        cos1, sin1 = cosT[:, ch, 1], sinT[:, ch, 1]
                        tt(WIr[:, :], cos1, XR[:, ch], ALU.mult, [cosT, XR], [WIr])
                        tt(t8[:, :], sin1, XI[:, ch], ALU.mult, [sinT, XI], [t8])
                        tt(WIr[:, :], WIr[:, :], t8[:, :], ALU.subtract, [WIr, t8], [WIr])
                        tt(WIi[:, :], sin1, XR[:, ch], ALU.mult, [sinT, XR], [WIi])
                        tt(t8[:, :], cos1, XI[:, ch], ALU.mult, [cosT, XI], [t8])
                        tt(WIi[:, :], WIi[:, :], t8[:, :], ALU.add, [WIi, t8], [WIi])
                        for cc in range(8):
                            c = 8 * hf + cc
                            scan(C_[:, cc, 0:ncol], magT[:, c, 0:ncol], A_[:, cc, 0:ncol], WIr[:, cc:cc + 1], [magT, A_, WIr], [C_.c(cc)])
                            scan(D_[:, cc, 0:ncol], magT[:, c, 0:ncol], B_[:, cc, 0:ncol], WIi[:, cc:cc + 1], [magT, B_, WIi], [D_.c(cc)])
                        cosL, sinL = cosT[:, ch, ncol - 1], sinT[:, ch, ncol - 1]
                        wrl, wil = C_[:, :, ncol - 1], D_[:, :, ncol - 1]
                        tt(XR[:, ch], cosL, wrl, ALU.mult, [cosT, C_], [XR])
                        tt(t8[:, :], sinL, wil, ALU.mult, [sinT, D_], [t8])
                        tt(XR[:, ch], XR[:, ch], t8[:, :], ALU.subtract, [XR, t8], [XR])
                        tt(XI[:, ch], sinL, wrl, ALU.mult, [sinT, C_], [XI])
                        tt(t8[:, :], cosL, wil, ALU.mult, [cosT, D_], [t8])
                        tt(XI[:, ch], XI[:, ch], t8[:, :], ALU.add, [XI, t8], [XI])
                        cos_, sin_ = cosT[:, ch, 0:ncol], sinT[:, ch, 0:ncol]
                        tt(a_, c_, cos_, ALU.mult, [C_, cosT], [A_])
                        tt(b_, d_, sin_, ALU.mult, [D_, sinT], [B_])
                        tt(xrb[:, :, 0:ncol], a_, b_, ALU.subtract, [A_, B_], [xrb])
                        tt(a_, c_, sin_, ALU.mult, [C_, sinT], [A_])
                        tt(b_, d_, cos_, ALU.mult, [D_, cosT], [B_])
                        stt(xib[:, :, 0:ncol], a_, -1.0, b_, ALU.mult, ALU.subtract, [A_, B_], [xib])
                        for cc in range(8):
                            c = 8 * hf + cc
                            j, m_ = c // 4, c % 4
                            bank = PS[6 + j // 2]
                            col0 = (j % 2) * N + s_ * QS
                            MM(bank[:, col0:col0 + ncol], BC[:, 2, c, :], xrb[:, cc, 0:ncol], m_ == 0, False,
                               [BC.c(2), xrb], [bank.r(4 * col0, 4 * col0 + 4 * ncol)])
                            MM(bank[:, col0:col0 + ncol], BC[:, 3, c, :], xib[:, cc, 0:ncol], False, m_ == 3,
                               [BC.c(3), xib], [bank.r(4 * col0, 4 * col0 + 4 * ncol)])
                GB = S.sb([128, 4, N], BF16, "GB")
                t5 = S.sb([128, N], F32, "t5")
                t6 = S.sb([128, N], F32, "t6")
                for j in range(4):
                    bank = PS[6 + j // 2]
                    col0 = (j % 2) * N
                    stt(t5[:, :], u32[:, j, :], pc(l, "d5", j), bank[:, col0:col0 + N], ALU.mult, ALU.add,
                        [u32.c(j), P[l]["_t"], bank.r(4 * col0, 4 * col0 + 4 * N)], [t5])
                    tt(t6[:, :], t5[:, :], t5[:, :], ALU.mult, [t5], [t6])
                    ts(t6[:, :], t6[:, :], 0.044715, 1.0, ALU.mult, ALU.add, [t6], [t6])
                    tt(t6[:, :], t6[:, :], t5[:, :], ALU.mult, [t6, t5], [t6])
                    act(t6[:, :], t6[:, :], AF.Sigmoid, [t6], [t6], scale=1.5957691216057308)
                    tt(GB[:, j, :], t5[:, :], t6[:, :], ALU.mult, [t5, t6], [GB.c(j)])
                gates(1)
                for half in range(2):
                    wtA, vA = load_w(WB["w_glu"][l][:, 512 * half:512 * half + 512], 4, 512)
                    wtB, vB = load_w(WB["w_glu"][l][:, 1024 + 512 * half:1024 + 512 * half + 512], 4, 512)
                    for mm_ in range(4):
                        m = 4 * half + mm_
                        P1, P2 = PS[0], PS[1]
                        for j in range(4):
                            MM(P1[:, 0:N], vA[:, j, 128 * mm_:128 * mm_ + 128], GB[:, j, :], j == 0, j == 3, [wtA, GB.c(j)], [P1.r(0, 4 * N)])
                        for j in range(4):
                            MM(P2[:, 0:N], vB[:, j, 128 * mm_:128 * mm_ + 128], GB[:, j, :], j == 0, j == 3, [wtB, GB.c(j)], [P2.r(0, 4 * N)])
                        act(t5[:, :], P2[:, 0:N], AF.Sigmoid, [P2.r(0, 4 * N)], [t5])
                        tt(t5[:, :], t5[:, :], P1[:, 0:N], ALU.mult, [t5, P1.r(0, 4 * N)], [t5])
                        tt(t5[:, :], t5[:, :], GT[:, m, :], ALU.mult, [t5, GT.c(m)], [t5])
                        tt(mixT[:, m, 0:N], mixT[:, m, 0:N], t5[:, :], ALU.add, [mixT.c(m), t5], [mixT.c(m)])
                if upto < 5:
                    continue
                S.reset(mkA)
                mixb = S.sb([128, 8, N], BF16, "mixb")
                for c in range(8):
                    cp(mixb[:, c, :], mixT[:, c, 0:N], [mixT.c(c)], [mixb.c(c)], eng="pool")

                def res_evac(m, bank, mw):
                    tt(xT[:, m, 0:N], xT[:, m, 0:N], bank[:, 0:N], ALU.add, [xT.c(m), bank.r(0, 4 * N)], [xT.c(m)])
                proj_fm(l, WB["w_out"][l], 8, 0, 1024, lambda k_: mixb[:, k_, :], lambda k_: [mixb.c(k_)], N, res_evac)
                rmsnorm_to(hT, xT, "nffn", l, N)
                AT = S.sb([128, 32, N], BF16, "AT")
                t5 = S.sb([128, N], F32, "t5")

                def up_evac(m, bank, mw):
                    act(t5[:, :], bank[:, 0:N], AF.Relu, [bank.r(0, 4 * N)], [t5])
                    tt(AT[:, m, :], t5[:, :], t5[:, :], ALU.mult, [t5], [AT.c(m)])
                proj_fm(l, WB["w_up"][l], 8, 0, 4096, hrhs, hrd, N, up_evac)
                for m in range(8):
                    wt, vw = load_w(WB["w_down"][l][:, 128 * m:128 * m + 128], 32, 128)
                    bank = PS[m % 2]
                    for k in range(32):
                        MM(bank[:, 0:N], vw[:, k, :], AT[:, k, :], k == 0, k == 31, [wt, AT.c(k)], [bank.r(0, 4 * N)])
                    res_evac(m, bank, 128)
            if upto >= 5 and kind != "meta":
                S.reset(ARENA)
                ydst = O["y_prompt"][b, t0:t0 + N, :] if kind == "prompt" else O["y_sample"][b, t0:t0 + N, :]
                y_tm = S.sb([128, nbk, D], F32, "y_tm")
                for tb, nt in blk:
                    for hc in range(2):
                        bank = PS[5 + hc]
                        for c4 in range(4):
                            c = 4 * hc + c4
                            TR(bank[0:nt, 128 * c4:128 * c4 + 128], xT[:, c, 128 * tb:128 * tb + nt], ident_f[:, :],
                               [xT.c(c), ident_f], [bank.r(512 * c4, 512 * c4 + 512)])
                        act(y_tm[0:nt, tb, 512 * hc:512 * hc + 512], bank[0:nt, :], AF.Copy, [bank], [y_tm.c(tb)])
                    S.dma(OQ, ydst[128 * tb:128 * tb + nt, :], y_tm[0:nt, tb, :], [y_tm.c(tb)], [], "yout")
        if upto >= 3:
            for l in range(2):
                if kind == "meta":
                    for dst, src in ((STm[l], ST_[l]), (CHm[l], CH_[l]), (XRm[l], XR_[l]), (XIm[l], XI_[l])):
                        cp(dst.h[:], src.h[:], [src], [dst])
                else:
                    sfx = "prompt" if kind == "prompt" else "sample"
                    S.dma(OQ, O["conv_" + sfx][l, b], CH_[l][:, :, :].rearrange("p c k -> p (c k)"), [CH_[l]], [], "stout")
                    for g in range(2):
                        S.dma(OQ, O["ssd_" + sfx][l, b, g], ST_[l][64:128, g, :], [ST_[l].c(g)], [], "stout")
                    S.dma(OQ, O["s5re_" + sfx][l, b], XR_[l][:, :], [XR_[l]], [], "stout")
                    S.dma(OQ, O["s5im_" + sfx][l, b], XI_[l][:, :], [XI_[l]], [], "stout")
    OUT_STREAMS[:] = ["kout", "vout", "yout", "stout"]
    S.emit(OUT_STREAMS)
    return nc


OUT_STREAMS = []

_NC_CACHE = {}


def pack_s5bc(inp):
    out = np.zeros((2, 4, 128, 16, 128), np.float32)
    for l in range(2):
        for i, nm in enumerate(("b_re", "b_im")):
            b = np.asarray(inp[nm], np.float32)[l]
            for c in range(16):
                m = c % 4
                for two in range(2):
                    g = 2 * c + two
                    r0 = 32 * m + 16 * two
                    out[l, i, r0:r0 + 16, c, 64 * two:64 * two + 64] = b[g].T
        for i, nm in enumerate(("c_re", "c_im")):
            cc = np.asarray(inp[nm], np.float32)[l]
            for c in range(16):
                m = c % 4
                for two in range(2):
                    g = 2 * c + two
                    c0 = 32 * m + 16 * two
                    out[l, 2 + i, 64 * two:64 * two + 64, c, c0:c0 + 16] = cc[g].T
    return out.reshape(2, 4, 128, 2048)


def kernel(**inp):
    cfg = inp.pop("_cfg", None)
    key = repr(cfg)
    if key not in _NC_CACHE:
        _NC_CACHE[key] = build(cfg)
    nc = _NC_CACHE[key]
    f = lambda a: np.ascontiguousarray(np.asarray(a, dtype=np.float32))
    wnames = ["meta_tokens", "norm_mix", "w_in", "conv_w", "conv_b", "dt_bias", "a_log", "d_ssd", "norm_ssd",
              "lam_re", "lam_im", "log_step", "w_glu", "q_norm", "k_norm",
              "w_lift_a", "w_lift_c", "w_out", "norm_ffn", "w_up", "w_down"]
    shared = {n: f(inp[n]) for n in wnames}
    shared["d_s5"] = f(inp["d_s5"]).reshape(2, 512)
    shared["par"] = pack_params(inp)
    shared["s5bc"] = pack_s5bc(inp)
    in_maps = []
    for c in range(8):
        m = dict(shared)
        m["x_prompt"] = f(inp["x_prompt"][4 * c:4 * c + 4])
        m["x_sample"] = f(inp["x_sample"][2 * c:2 * c + 2])
        m["cache_k"] = f(inp["cache_k"][:, 2 * c:2 * c + 2]).reshape(2, 2, PAST, 512)
        m["cache_v"] = f(inp["cache_v"][:, 2 * c:2 * c + 2]).reshape(2, 2, PAST, 512)
        sc = f(inp["state_conv"][:, 2 * c:2 * c + 2]).reshape(2, 2, 3, 10, 128)
        m["state_conv"] = np.ascontiguousarray(sc.transpose(0, 1, 4, 3, 2)).reshape(2, 2, 128, 30)
        ss = f(inp["state_ssd"][:, 2 * c:2 * c + 2])
        m["state_ssd"] = np.ascontiguousarray(ss.transpose(0, 1, 2, 5, 3, 4)).reshape(2, 2, 2, 64, 512)
        for nm in ("state_s5_re", "state_s5_im"):
            a5 = f(inp[nm][:, 2 * c:2 * c + 2]).reshape(2, 2, 16, 2, 64)
            m[nm] = np.ascontiguousarray(a5.transpose(0, 1, 3, 4, 2)).reshape(2, 2, 128, 16)
        in_maps.append(m)
    ncores = (cfg or {}).get('ncores', 8)
    res = run_bass_kernel_spmd(nc, in_maps[:ncores], core_ids=list(range(ncores)))
    R = list(res.results) + [res.results[0]] * (8 - ncores)
    cat = lambda k, ax: np.concatenate([np.asarray(R[c][k], dtype=np.float32) for c in range(8)], axis=ax)
    y_p = cat("y_prompt", 0)
    y_s = cat("y_sample", 0)
    k_p = cat("k_prompt", 1).reshape(2, 32, TP, 8, 64)
    v_p = cat("v_prompt", 1).reshape(2, 32, TP, 8, 64)
    unconv = lambda a: np.ascontiguousarray(a.reshape(2, -1, 128, 10, 3).transpose(0, 1, 4, 3, 2)).reshape(2, -1, 3, 1280)
    unssd = lambda a: np.ascontiguousarray(a.reshape(2, -1, 2, 64, 8, 64).transpose(0, 1, 2, 4, 5, 3))
    uns5 = lambda a: np.ascontiguousarray(a.reshape(2, -1, 2, 64, 16).transpose(0, 1, 4, 2, 3)).reshape(2, -1, 32, 64)
    conv_p = unconv(cat("conv_prompt", 1))
    ssd_p = unssd(cat("ssd_prompt", 1))
    s5r_p = uns5(cat("s5re_prompt", 1))
    s5i_p = uns5(cat("s5im_prompt", 1))
    k_s = cat("k_sample", 1).reshape(2, 16, DSEQ, 8, 64)
    v_s = cat("v_sample", 1).reshape(2, 16, DSEQ, 8, 64)
    conv_s = unconv(cat("conv_sample", 1))
    ssd_s = unssd(cat("ssd_sample", 1))
    s5r_s = uns5(cat("s5re_sample", 1))
    s5i_s = uns5(cat("s5im_sample", 1))
    return (y_p, y_s, k_p, v_p, conv_p, ssd_p, s5r_p, s5i_p, k_s, v_s, conv_s, ssd_s, s5r_s, s5i_s)
```

```python
import math
import bisect
import numpy as np
import concourse.bass as bass
import concourse.mybir as mybir
from concourse.bass_utils import run_bass_kernel_spmd

F32 = mybir.dt.float32
F32R = mybir.dt.float32r
BF16 = mybir.dt.bfloat16
AF = mybir.ActivationFunctionType
ALU = mybir.AluOpType
AX = mybir.AxisListType

D = 1024
NMETA = 16
SEQ = 2048
TP = NMETA + SEQ
PAST = 2048
DSEQ = 64
NKMAX = 2112
INC = 7440
OFF_Z, OFF_XBC, OFF_DT, OFF_U, OFF_Q, OFF_K, OFF_V, OFF_G = 0, 1024, 2304, 2320, 2832, 3344, 3856, 4368
EPS = 1e-6
NT = 256
QS = 64
NEG = -30000.0
PCOL = {}
_c = 0
for _nm, _w in [("nmix", 8), ("nffn", 8), ("nssd", 8), ("cw", 40), ("cb", 10), ("d5", 4), ("dtb", 1), ("alog", 1),
                ("dsk", 8), ("qn", 1), ("kn", 1), ("lre", 16), ("lim", 16), ("lst", 16)]:
    PCOL[_nm] = (_c, _w)
    _c += _w
NPAR = _c


def pack_params(inp):
    par = np.zeros((2, 128, NPAR), np.float32)
    g = lambda n: np.asarray(inp[n], np.float32)

    def put(l, nm, arr):
        c0, w = PCOL[nm]
        par[l, :, c0:c0 + w] = arr

    for l in range(2):
        put(l, "nmix", g("norm_mix")[l].reshape(8, 128).T)
        put(l, "nffn", g("norm_ffn")[l].reshape(8, 128).T)
        put(l, "nssd", g("norm_ssd")[l].reshape(8, 128).T)
        cw = g("conv_w")[l].reshape(4, 10, 128)
        put(l, "cw", cw.transpose(2, 0, 1).reshape(128, 40))
        put(l, "cb", g("conv_b")[l].reshape(10, 128).T)
        put(l, "d5", g("d_s5")[l].reshape(4, 128).T)
        for nm, src in (("dtb", "dt_bias"), ("alog", "a_log")):
            col = np.zeros((128, 1), np.float32)
            col[0:16, 0] = g(src)[l]
            col[32:48, 0] = g(src)[l]
            put(l, nm, col)
        dsk = np.zeros((128, 8), np.float32)
        d = g("d_ssd")[l]
        for hh in range(2):
            dsk[64 * hh:64 * hh + 64, :] = d[hh::2][None, :]
        put(l, "dsk", dsk)
        put(l, "qn", np.tile(g("q_norm")[l], 2)[:, None])
        put(l, "kn", np.tile(g("k_norm")[l], 2)[:, None])
        for nm, src in (("lre", "lam_re"), ("lim", "lam_im")):
            a = g(src)[l].reshape(16, 2, 64)
            put(l, nm, a.transpose(1, 2, 0).reshape(128, 16))
        ls = g("log_step")[l].reshape(16, 2)
        put(l, "lst", np.repeat(ls.T[:, None, :], 64, axis=1).reshape(128, 16))
    return par
DTSZ = {F32: 4, F32R: 4, BF16: 2}


class Tile:
    def __init__(self, h, space, lo, hi):
        self.h, self.space, self.lo, self.hi = h, space, lo, hi

    def __getitem__(self, k):
        return self.h[k]

    @property
    def all(self):
        return (self.space, self.lo, self.hi)

    def r(self, lo, hi):
        return (self.space, self.lo + lo, self.lo + hi)

    def c(self, i, n=1):
        return (self.space, self.lo + i * self.cb, self.lo + (i + n) * self.cb)


def _reg(x):
    return x.all if isinstance(x, Tile) else x


class Sched:
    ENG = ["pe", "act", "dve", "pool", "sp"]

    def __init__(self, nc):
        self.nc = nc
        self.ops = []
        self.segs = {}
        self.off = 16512
        self.cnt = 0
        self.stream_n = {}
        self.psb = []

    def sb(self, shape, dtype, name="t"):
        nb = int(np.prod(shape[1:])) * DTSZ[dtype]
        off = (self.off + 31) // 32 * 32
        self.cnt += 1
        h = self.nc.alloc_sbuf_tensor_at(f"{name}{self.cnt}", list(shape), dtype, offset=off)
        self.off = off + nb
        self.peak = max(getattr(self, "peak", 0), self.off)
        assert self.off <= 16512 + 208000, ("sbuf overflow", name, self.off)
        t = Tile(h, "sb", off, off + nb)
        t.cb = (int(np.prod(shape[2:])) if len(shape) > 2 else 1) * DTSZ[dtype]
        return t

    def mark(self):
        return self.off

    def reset(self, m):
        self.off = m

    def _access(self, opi, reg, write, deps, norecord=False):
        space, lo, hi = reg
        if space not in self.segs:
            self.segs[space] = ([0], [[None, {}]])
        starts, data = self.segs[space]
        for b in (lo, hi):
            i = bisect.bisect_right(starts, b) - 1
            if starts[i] != b:
                starts.insert(i + 1, b)
                data.insert(i + 1, [data[i][0], dict(data[i][1])])
        i = bisect.bisect_left(starts, lo)
        while i < len(starts) and starts[i] < hi:
            w, rd = data[i]
            if w is not None and w != opi:
                if not write:
                    deps[w] = "raw"
                elif deps.get(w) != "raw":
                    deps[w] = "waw"
            if write:
                for r_ in rd.values():
                    if r_ != opi and r_ not in deps:
                        deps[r_] = "war"
                data[i][0] = opi
                data[i][1] = {}
            elif not norecord:
                key = self.ops[opi]["rk"]
                rd[key] = opi
            i += 1

    def op(self, eng, fn, reads=(), writes=(), stream=None):
        opi = len(self.ops)
        o = {"eng": eng, "fn": fn, "stream": stream, "deps": {}, "sig": False}
        if stream is not None:
            o["rk"] = "dma:" + stream
        else:
            o["rk"] = eng
        self.ops.append(o)
        deps = {}
        wregs = [_reg(w_) for w_ in writes]
        for r_ in reads:
            rr = _reg(r_)
            inplace = any(w[0] == rr[0] and w[1] < rr[2] and rr[1] < w[2] for w in wregs)
            self._access(opi, rr, False, deps, norecord=inplace)
        for w_ in writes:
            rg_ = _reg(w_)
            if eng == "pe" and rg_[0] == "ps":
                rg_ = ("ps", rg_[1] // 2048 * 2048, (rg_[2] + 2047) // 2048 * 2048)
            self._access(opi, rg_, True, deps)
        res = {}
        for d, kind in deps.items():
            od = self.ops[d]
            if od["stream"] is not None:
                res[d] = ("s", od["stream"], 16 * self.stream_n[od["stream"]])
            else:
                if od["eng"] == eng and stream is None:
                    if eng == "pe":
                        continue
                res[d] = ("e", od["eng"], None)
                od["sig"] = True
        lw = self.__dict__.setdefault("last_waiter", {})
        if stream is not None and stream in lw:
            w = lw[stream]
            ow = self.ops[w]
            if ow["eng"] != eng and w not in res:
                if ow["stream"] is not None:
                    res[w] = ("s", ow["stream"], 16 * self.stream_n[ow["stream"]])
                else:
                    res[w] = ("e", ow["eng"], None)
                    ow["sig"] = True
        for d, (k, key, val) in res.items():
            if k == "s":
                lw[key] = opi
        o["deps"] = res
        if stream is not None:
            self.stream_n[stream] = self.stream_n.get(stream, 0) + 1
            o["sval"] = 16 * self.stream_n[stream]
        return opi

    def dma(self, q, out, in_, reads, writes, stream, **kw):
        return self.op(q, lambda e: e.dma_start(out=out, in_=in_, **kw), reads, writes, stream=stream)

    def emit(self, final_streams):
        nc = self.nc
        counts = {e: 0 for e in self.ENG}
        for o in self.ops:
            if o["stream"] is None and o["sig"]:
                counts[o["eng"]] += 1
                o["sval"] = counts[o["eng"]]
        from contextlib import ExitStack
        with ExitStack() as es:
            esem = {e: es.enter_context(nc.semaphore("e_" + e)) for e in ["pe", "act", "dve", "pool"]}
            ssem = {s: es.enter_context(nc.semaphore("s_" + s)) for s in self.stream_n}
            block = es.enter_context(nc.Block())
            per = {e: [o for o in self.ops if o["eng"] == e] for e in self.ENG}

            def run(ename, e):
                known = {}
                for o in per[ename]:
                    for d, (k, key, val) in o["deps"].items():
                        if k == "s":
                            sem, v = ssem[key], val
                        else:
                            sem, v = esem[key], self.ops[d]["sval"]
                        kk = (k, key)
                        if known.get(kk, 0) >= v:
                            continue
                        known[kk] = v
                        e.wait_ge(sem, v)
                    ins = o["fn"](e)
                    if o["stream"] is not None:
                        ins.then_inc(ssem[o["stream"]], 16)
                    elif o["sig"]:
                        ins.then_inc(esem[ename], 1)
                if ename == "sp":
                    for s in final_streams:
                        if s in self.stream_n:
                            e.wait_ge(ssem[s], 16 * self.stream_n[s])

            @block.tensor
            def _(e):
                run("pe", e)

            @block.scalar
            def _(e):
                run("act", e)

            @block.vector
            def _(e):
                run("dve", e)

            @block.gpsimd
            def _(e):
                run("pool", e)

            @block.sync
            def _(e):
                run("sp", e)


def build(cfg=None):
    cfg = cfg or {}
    n_ptiles = cfg.get("n_ptiles", SEQ // NT)
    n_pseq = cfg.get("n_pseq", 4)
    n_sseq = cfg.get("n_sseq", 2)
    nc = bass.Bass("TRN2", target_bir_lowering=False)
    S = Sched(nc)

    def din(name, shape):
        return nc.dram_tensor(name, list(shape), F32, kind="ExternalInput").ap()

    def dout(name, shape):
        return nc.dram_tensor(name, list(shape), F32, kind="ExternalOutput").ap()

    I = {}
    I["x_prompt"] = din("x_prompt", [4, SEQ, D])
    I["x_sample"] = din("x_sample", [2, DSEQ, D])
    I["cache_k"] = din("cache_k", [2, 2, PAST, 512])
    I["cache_v"] = din("cache_v", [2, 2, PAST, 512])
    I["state_conv"] = din("state_conv", [2, 2, 128, 30])
    I["state_ssd"] = din("state_ssd", [2, 2, 2, 64, 512])
    I["state_s5_re"] = din("state_s5_re", [2, 2, 128, 16])
    I["state_s5_im"] = din("state_s5_im", [2, 2, 128, 16])
    I["meta_tokens"] = din("meta_tokens", [NMETA, D])
    for nm, sh in [("norm_mix", [2, D]), ("w_in", [2, D, INC]), ("conv_w", [2, 4, 1280]), ("conv_b", [2, 1280]),
                   ("dt_bias", [2, 16]), ("a_log", [2, 16]), ("d_ssd", [2, 16]), ("norm_ssd", [2, D]),
                   ("lam_re", [2, 32, 64]), ("lam_im", [2, 32, 64]), ("log_step", [2, 32]),
                   ("s5bc", [2, 4, 128, 2048]), ("d_s5", [2, 512]), ("w_glu", [2, 512, 2048]),
                   ("q_norm", [2, 64]), ("k_norm", [2, 64]), ("w_lift_a", [2, D, D]), ("w_lift_c", [2, 512, D]),
                   ("w_out", [2, D, D]), ("norm_ffn", [2, D]), ("w_up", [2, D, 4096]), ("w_down", [2, 4096, D])]:
        I[nm] = din(nm, sh)
    O = {}
    O["y_prompt"] = dout("y_prompt", [4, SEQ, D])
    O["y_sample"] = dout("y_sample", [2, DSEQ, D])
    O["k_prompt"] = dout("k_prompt", [2, 4, TP, 512])
    O["v_prompt"] = dout("v_prompt", [2, 4, TP, 512])
    O["conv_prompt"] = dout("conv_prompt", [2, 4, 128, 30])
    O["ssd_prompt"] = dout("ssd_prompt", [2, 4, 2, 64, 512])
    O["s5re_prompt"] = dout("s5re_prompt", [2, 4, 128, 16])
    O["s5im_prompt"] = dout("s5im_prompt", [2, 4, 128, 16])
    O["k_sample"] = dout("k_sample", [2, 2, DSEQ, 512])
    O["v_sample"] = dout("v_sample", [2, 2, DSEQ, 512])
    O["conv_sample"] = dout("conv_sample", [2, 2, 128, 30])
    O["ssd_sample"] = dout("ssd_sample", [2, 2, 2, 64, 512])
    O["s5re_sample"] = dout("s5re_sample", [2, 2, 128, 16])
    O["s5im_sample"] = dout("s5im_sample", [2, 2, 128, 16])
    kTs = nc.dram_tensor("kT_scr", [6, 2, 512, NKMAX], BF16, kind="Internal").ap()
    vhs = nc.dram_tensor("vh_scr", [6, 2, NKMAX, 512], BF16, kind="Internal").ap()

    PS = []
    for b in range(8):
        h = nc.alloc_psum_tensor(f"psb{b}", [128, 512], F32)
        PS.append(Tile(h, "ps", b * 2048, (b + 1) * 2048))

    def V(fn, reads, writes):
        return S.op("dve", fn, reads, writes)

    def A(fn, reads, writes):
        return S.op("act", fn, reads, writes)

    def G(fn, reads, writes):
        return S.op("pool", fn, reads, writes)

    def MM(out, lhsT, rhs, start, stop, reads, writes, **kw):
        return S.op("pe", lambda e: e.matmul(out, lhsT=lhsT, rhs=rhs, start=start, stop=stop, **kw), reads, writes)

    def TR(out, in_, ident, reads, writes):
        return S.op("pe", lambda e: e.matmul(out, lhsT=in_, rhs=ident, start=True, stop=True), reads, writes)

    def act(out, in_, func, reads, writes, bias=None, scale=None):
        kw = {}
        if bias is not None:
            kw["bias"] = bias
        if scale is not None:
            kw["scale"] = scale
        return A(lambda e: e.activation(out=out, in_=in_, func=func, **kw), reads, writes)

    def ldpar(out, in_, writes):
        return S.dma("sp", out, in_, [], writes, "par", allow_slow_non_contiguous=True)

    iota_pc = S.sb([128, 256], F32, "iota")
    ident_f = S.sb([128, 128], F32, "identf")
    ident_b = S.sb([128, 128], BF16, "identb")
    ones_b = S.sb([128, 128], BF16, "onesb")
    bd64_b = S.sb([128, 128], BF16, "bd64")
    tri_r = S.sb([128, 128], F32R, "tri")
    ones_r = S.sb([128, 128], F32R, "onesr")
    iota_t = S.sb([128, QS + 1], F32, "iotat")
    ones_f = S.sb([128, 128], F32, "onesf")
    masks = {}
    G(lambda e: e.iota(iota_pc[:, :], [[-1, 256]], base=0, channel_multiplier=1,
                       allow_small_or_imprecise_dtypes=True), [], [iota_pc])
    G(lambda e: e.iota(iota_t[:, :], [[1, QS + 1]], base=0, channel_multiplier=0,
                       allow_small_or_imprecise_dtypes=True), [], [iota_t])
    V(lambda e: e.tensor_single_scalar(out=ident_f[:, :], in_=iota_pc[:, 0:128], scalar=0.0, op=ALU.is_equal),
      [iota_pc], [ident_f])
    V(lambda e: e.tensor_copy(out=ident_b[:, :], in_=ident_f[:, :]), [ident_f], [ident_b])
    V(lambda e: e.memset(ones_b[:, :], 1.0), [], [ones_b])
    V(lambda e: e.memset(bd64_b[:, :], 0.0), [], [bd64_b])
    V(lambda e: e.memset(bd64_b[0:64, 0:64], 1.0), [], [bd64_b])
    V(lambda e: e.memset(bd64_b[64:128, 64:128], 1.0), [], [bd64_b])
    V(lambda e: e.tensor_scalar(out=tri_r[:, :], in0=iota_pc[:, 0:128], scalar1=0.0, scalar2=-8.0,
                                op0=ALU.is_ge, op1=ALU.mult), [iota_pc], [tri_r])
    V(lambda e: e.memset(ones_f[:, :], 1.0), [], [ones_f])
    V(lambda e: e.tensor_copy(out=ones_r[:, :], in_=ones_f[:, :]), [ones_f], [ones_r])
    for off in (16, -112, -240, 0):
        m = S.sb([128, 256], F32, "mask")
        V(lambda e, m=m, off=off: e.tensor_single_scalar(out=m[:, :], in_=iota_pc[:, :], scalar=float(off),
                                                         op=ALU.is_lt), [iota_pc], [m])
        masks[off] = m

    P = []
    stage = cfg.get('stage', 99)
    I["par"] = din("par", [2, 128, NPAR])
    for l in range(2):
        pt = S.sb([128, NPAR], F32, "par")
        S.dma("sp", pt[:, :], I["par"][l], [], [pt], "par")
        p = {"_t": pt}
        for nm, (c0, w) in PCOL.items():
            p[nm] = (pt, c0, w)
        P.append(p)

    def pc(l, nm, j=0, rows=slice(0, 128)):
        pt, c0, w = P[l][nm]
        return pt[rows, c0 + j:c0 + j + 1]

    def pv(l, nm):
        pt, c0, w = P[l][nm]
        return pt[:, c0:c0 + w]

    TWO_PI = 2.0 * math.pi
    MAGIC = 12582912.0
    S5TAB = []
    for l in range(2):
        S5TAB.append((S.sb([128, 16, QS + 1], F32, "cosT"), S.sb([128, 16, QS + 1], F32, "sinT"),
                      S.sb([128, 16, QS], F32, "tbr"), S.sb([128, 16, QS], F32, "tbi"),
                      S.sb([128, 16, QS], F32, "magT")))
    S5L = [(S.sb([128, 16], F32, "lre"), S.sb([128, 16], F32, "lim"), S.sb([128, 16], F32, "lst")) for l in range(2)]
    ARENA0 = S.mark()
    S5SCR = [S.sb([128, 16], F32, "s5s") for _ in range(8)] + [S.sb([128, 16, QS + 1], F32, "s5w") for _ in range(3)]
    for l in range(2 if stage >= 2 else 0):
        p = P[l]
        lre, lim, lst = S5L[l]
        for dst, nm in ((lre, "lre"), (lim, "lim"), (lst, "lst")):
            V(lambda e, dst=dst, nm=nm, l=l: e.tensor_copy(out=dst[:, :], in_=pv(l, nm)), [P[l]["_t"]], [dst])
        step, th, mag, t0_, t1_, t2_, fre, fim, ang, w1, w2 = S5SCR
        cosT, sinT, tbr, tbi, magT = S5TAB[l]
        p.update(cosT=cosT, sinT=sinT, tbr=tbr, tbi=tbi, magT=magT, mag=mag)
        act(step[:, :], lst[:, :], AF.Exp, [lst], [step])
        V(lambda e, th=th, lim=lim, step=step: e.tensor_tensor(out=th[:, :], in0=lim[:, :], in1=step[:, :], op=ALU.mult),
          [lim, step], [th])
        V(lambda e, t0_=t0_, lre=lre, step=step: e.tensor_tensor(out=t0_[:, :], in0=lre[:, :], in1=step[:, :], op=ALU.mult),
          [lre, step], [t0_])
        act(mag[:, :], t0_[:, :], AF.Exp, [t0_], [mag])
        V(lambda e, ang=ang, th=th: e.tensor_tensor(
            out=ang[:, :, :], in0=iota_t[:, :].unsqueeze(1).broadcast_to([128, 16, QS + 1]),
            in1=th[:, :].unsqueeze(2).broadcast_to([128, 16, QS + 1]), op=ALU.mult), [iota_t, th], [ang])
        for which, outT in (("sin", sinT), ("cos", cosT)):
            addc = 0.0 if which == "sin" else 0.25
            V(lambda e, ang=ang, w1=w1, addc=addc: e.tensor_scalar(
                out=w1[:, :, :], in0=ang[:, :, :], scalar1=1.0 / TWO_PI, scalar2=addc, op0=ALU.mult, op1=ALU.add),
              [ang], [w1])
            V(lambda e, w1=w1, w2=w2: e.tensor_scalar(out=w2[:, :, :], in0=w1[:, :, :], scalar1=MAGIC, scalar2=None,
                                                     op0=ALU.add), [w1], [w2])
            V(lambda e, w2=w2: e.tensor_scalar(out=w2[:, :, :], in0=w2[:, :, :], scalar1=-MAGIC, scalar2=None,
                                               op0=ALU.add), [w2], [w2])
            V(lambda e, w1=w1, w2=w2: e.tensor_tensor(out=w1[:, :, :], in0=w1[:, :, :], in1=w2[:, :, :],
                                                     op=ALU.subtract), [w1, w2], [w1])
            V(lambda e, w1=w1: e.tensor_scalar(out=w1[:, :, :], in0=w1[:, :, :], scalar1=-0.4999, scalar2=0.4999,
                                               op0=ALU.max, op1=ALU.min), [w1], [w1])
            act(outT[:, :, :], w1[:, :, :], AF.Sin, [w1], [outT], scale=TWO_PI)
        abr, abi = t1_, t2_
        V(lambda e, abr=abr, cosT=cosT, mag=mag: e.tensor_tensor(out=abr[:, :], in0=cosT[:, :, 1], in1=mag[:, :], op=ALU.mult),
          [cosT, mag], [abr])
        V(lambda e, abi=abi, sinT=sinT, mag=mag: e.tensor_tensor(out=abi[:, :], in0=sinT[:, :, 1], in1=mag[:, :], op=ALU.mult),
          [sinT, mag], [abi])
        V(lambda e, abr=abr: e.tensor_scalar(out=abr[:, :], in0=abr[:, :], scalar1=-1.0, scalar2=None, op0=ALU.add),
          [abr], [abr])
        den = step
        V(lambda e, den=den, lre=lre: e.tensor_tensor(out=den[:, :], in0=lre[:, :], in1=lre[:, :], op=ALU.mult), [lre], [den])
        V(lambda e, t0_=t0_, lim=lim: e.tensor_tensor(out=t0_[:, :], in0=lim[:, :], in1=lim[:, :], op=ALU.mult), [lim], [t0_])
        V(lambda e, den=den, t0_=t0_: e.tensor_tensor(out=den[:, :], in0=den[:, :], in1=t0_[:, :], op=ALU.add), [den, t0_], [den])
        V(lambda e, den=den: e.reciprocal(out=den[:, :], in_=den[:, :]), [den], [den])
        V(lambda e, fre=fre, abr=abr, lre=lre: e.tensor_tensor(out=fre[:, :], in0=abr[:, :], in1=lre[:, :], op=ALU.mult), [abr, lre], [fre])
        V(lambda e, t0_=t0_, abi=abi, lim=lim: e.tensor_tensor(out=t0_[:, :], in0=abi[:, :], in1=lim[:, :], op=ALU.mult), [abi, lim], [t0_])
        V(lambda e, fre=fre, t0_=t0_: e.tensor_tensor(out=fre[:, :], in0=fre[:, :], in1=t0_[:, :], op=ALU.add), [fre, t0_], [fre])
        V(lambda e, fre=fre, den=den: e.tensor_tensor(out=fre[:, :], in0=fre[:, :], in1=den[:, :], op=ALU.mult), [fre, den], [fre])
        V(lambda e, fim=fim, abi=abi, lre=lre: e.tensor_tensor(out=fim[:, :], in0=abi[:, :], in1=lre[:, :], op=ALU.mult), [abi, lre], [fim])
        V(lambda e, t0_=t0_, abr=abr, lim=lim: e.tensor_tensor(out=t0_[:, :], in0=abr[:, :], in1=lim[:, :], op=ALU.mult), [abr, lim], [t0_])
        V(lambda e, fim=fim, t0_=t0_: e.tensor_tensor(out=fim[:, :], in0=fim[:, :], in1=t0_[:, :], op=ALU.subtract), [fim, t0_], [fim])
        V(lambda e, fim=fim, den=den: e.tensor_tensor(out=fim[:, :], in0=fim[:, :], in1=den[:, :], op=ALU.mult), [fim, den], [fim])
        frb = lambda f: f[:, :].unsqueeze(2).broadcast_to([128, 16, QS])
        V(lambda e, w1=w1, cosT=cosT, fre=fre: e.tensor_tensor(out=w1[:, :, 0:QS], in0=cosT[:, :, 0:QS], in1=frb(fre), op=ALU.mult), [cosT, fre], [w1])
        V(lambda e, w2=w2, sinT=sinT, fim=fim: e.tensor_tensor(out=w2[:, :, 0:QS], in0=sinT[:, :, 0:QS], in1=frb(fim), op=ALU.mult), [sinT, fim], [w2])
        V(lambda e, tbr=tbr, w1=w1, w2=w2: e.tensor_tensor(out=tbr[:, :, :], in0=w1[:, :, 0:QS], in1=w2[:, :, 0:QS], op=ALU.add), [w1, w2], [tbr])
        V(lambda e, w1=w1, cosT=cosT, fim=fim: e.tensor_tensor(out=w1[:, :, 0:QS], in0=cosT[:, :, 0:QS], in1=frb(fim), op=ALU.mult), [cosT, fim], [w1])
        V(lambda e, w2=w2, sinT=sinT, fre=fre: e.tensor_tensor(out=w2[:, :, 0:QS], in0=sinT[:, :, 0:QS], in1=frb(fre), op=ALU.mult), [sinT, fre], [w2])
        V(lambda e, tbi=tbi, w1=w1, w2=w2: e.tensor_tensor(out=tbi[:, :, :], in0=w1[:, :, 0:QS], in1=w2[:, :, 0:QS], op=ALU.subtract), [w1, w2], [tbi])
        V(lambda e, magT=magT, mag=mag: e.tensor_tensor(
            out=magT[:, :, :], in0=ones_f[:, 0:QS].unsqueeze(1).broadcast_to([128, 16, QS]),
            in1=mag[:, :].unsqueeze(2).broadcast_to([128, 16, QS]), op=ALU.mult), [ones_f, mag], [magT])
    WB = {}
    OQ = cfg.get("oq", "pool")
    _wi = 0
    for nm, shp in [("w_in", [2, D, INC]), ("w_glu", [2, 512, 2048]), ("w_lift_a", [2, D, D]), ("w_lift_c", [2, 512, D]),
                    ("w_out", [2, D, D]), ("w_up", [2, D, 4096]), ("w_down", [2, 4096, D])]:
        WB[nm] = nc.dram_tensor(nm + "_bf", shp, BF16, kind="Internal").ap()
        rows, cols = shp[1], shp[2]
        chunk = max(128, ((1 << 20) // cols) // 128 * 128)
        for l in range(2):
            for r0 in range(0, rows, chunk):
                r1 = min(rows, r0 + chunk)
                S.dma("pool", WB[nm][l, r0:r1, :], I[nm][l, r0:r1, :], [], [("wbf", _wi, _wi + 1)], "wcast")
                _wi += 1
    upto = cfg.get("upto", 99)
    epsT = S.sb([128, 1], F32, "eps")
    V(lambda e: e.memset(epsT[:, :], EPS), [], [epsT])
    oneT = S.sb([128, 1], F32, "one")
    V(lambda e: e.memset(oneT[:, :], 1.0), [], [oneT])
    WDT = []
    for l in range(2):
        w_ = S.sb([128, 8, 48], BF16, "WDT")
        V(lambda e, w_=w_: e.memset(w_[:, :, :], 0.0), [], [w_])
        for c0 in (0, 32):
            S.dma("pool", w_[:, :, c0:c0 + 16], I["w_in"][l][:, OFF_DT:OFF_DT + 16].rearrange("(kc p) n -> p kc n", p=128),
                  [], [w_], "wdt")
        WDT.append(w_)

    def tt(out, in0, in1, op, reads, writes, eng="dve"):
        return S.op(eng, lambda e: e.tensor_tensor(out=out, in0=in0, in1=in1, op=op), reads, writes)

    def ts(out, in0, s1, s2, op0, op1, reads, writes, eng="dve"):
        if op1 is None:
            return S.op(eng, lambda e: e.tensor_scalar(out=out, in0=in0, scalar1=s1, scalar2=None, op0=op0), reads, writes)
        return S.op(eng, lambda e: e.tensor_scalar(out=out, in0=in0, scalar1=s1, scalar2=s2, op0=op0, op1=op1), reads, writes)

    def stt(out, in0, scalar, in1, op0, op1, reads, writes):
        return V(lambda e: e.scalar_tensor_tensor(out=out, in0=in0, scalar=scalar, in1=in1, op0=op0, op1=op1), reads, writes)

    def cp(out, in_, reads, writes, eng="dve"):
        return S.op(eng, lambda e: e.tensor_copy(out=out, in_=in_), reads, writes)

    def scan(out, d0, d1, init, reads, writes):
        return V(lambda e: e.tensor_tensor_scan(out=out, data0=d0, data1=d1, initial=init, op0=ALU.mult, op1=ALU.add), reads, writes)

    NWS = cfg.get('nws', 4)
    WSL = [S.sb([128, 4096], BF16, "wslot") for _ in range(NWS)]
    wctr = [0]

    def load_w(src_ap, kc, ncols, prt=128):
        i = wctr[0] % NWS
        wctr[0] += 1
        wt = WSL[i]
        view = wt[0:prt, 0:kc * ncols].rearrange("p (kc n) -> p kc n", n=ncols)
        v = src_ap.rearrange("(kc p) n -> p kc n", p=prt)
        step = max(1, kc // 4)
        for k0 in range(0, kc, step):
            S.dma("sp", view[:, k0:k0 + step, :], v[:, k0:k0 + step, :], [("wbf", 0, 100000)],
                  [wt.r(k0 * ncols * 2, (k0 + step) * ncols * 2)], f"w{i}")
        return wt, view

    ones48 = S.sb([48, 64], F32, "ones48")
    V(lambda e: e.memset(ones48[:, :], 1.0), [], [ones48])
    LT = []
    for i in range(2):
        t = S.sb([128, 128], F32, "LT")
        V(lambda e, t=t: e.memset(t[:, :], 0.0), [], [t])
        V(lambda e, t=t: e.memset(t[0:16, :], 1.0), [], [t])
        cp(t[64:128, 0:64], ident_f[64:128, 64:128], [ident_f], [t])
        LT.append(t)
    RF = []
    for g in range(2):
        t = S.sb([128, 8, 64], F32, "RF")
        V(lambda e, t=t: e.memset(t[:, :, :], 0.0), [], [t])
        for h in range(8):
            if True:
                r = 8 * g + h
                pass
        RF.append(t)
    DEL = []
    for g in range(2):
        d_ = S.sb([48, 8], F32, "DEL")
        io = S.sb([48, 8], F32, "DELi")
        G(lambda e, io=io: e.iota(io[:, :], [[-1, 8]], base=0, channel_multiplier=1, allow_small_or_imprecise_dtypes=True), [], [io])
        ts(d_[0:32, :], io[0:32, :], float(8 * g), None, ALU.is_equal, None, [io], [d_])
        ts(d_[32:48, :], io[32:48, :], float(32 + 8 * g), None, ALU.is_equal, None, [io], [d_])
        DEL.append(d_)
        cp(RF[g][32:48, :, :], d_[32:48, :].unsqueeze(2).broadcast_to([16, 8, 64]), [d_], [RF[g]])
        ts(RF[g][64:128, :, :], iota_pc[64:128, 0:64].unsqueeze(1).broadcast_to([64, 8, 64]), 64.0, NEG, ALU.is_gt, ALU.mult,
           [iota_pc], [RF[g]])
    SHIFT = S.sb([128, 128], BF16, "shift")
    V(lambda e: e.memset(SHIFT[:, :], 0.0), [], [SHIFT])
    cp(SHIFT[0:64, 64:128], ident_b[0:64, 0:64], [ident_b], [SHIFT])
    cp(SHIFT[64:128, 64:128], ident_b[64:128, 64:128], [ident_b], [SHIFT])
    BTOK = S.sb([64, 128], BF16, "btok")
    V(lambda e: e.memset(BTOK[:, :], 0.0), [], [BTOK])
    ST_, XS_, CH_, XR_, XI_, STm, CHm, XRm, XIm = [], [], [], [], [], [], [], [], []
    for l in range(2):
        ST_.append(S.sb([128, 2, 512], F32, "ST"))
        XS_.append(S.sb([128, 2, 4, 256], BF16, "XS"))
        CH_.append(S.sb([128, 10, 3], F32, "CH"))
        XR_.append(S.sb([128, 16], F32, "XR"))
        XI_.append(S.sb([128, 16], F32, "XI"))
        STm.append(S.sb([128, 2, 512], F32, "STm"))
        CHm.append(S.sb([128, 10, 3], F32, "CHm"))
        XRm.append(S.sb([128, 16], F32, "XRm"))
        XIm.append(S.sb([128, 16], F32, "XIm"))
    acol = []
    for l in range(2):
        a_ = S.sb([48, 1], F32, "acol")
        act(a_[:, :], pc(l, "alog", 0, slice(0, 48)), AF.Exp, [P[l]["_t"]], [a_])
        ts(a_[0:32, :], a_[0:32, :], -1.0, None, ALU.mult, None, [a_], [a_])
        acol.append(a_)

    def st_to_xs(l):
        for g in range(2):
            src = ST_[l][64:128, g, :].rearrange("p (pr eo d) -> p pr eo d", pr=4, eo=2)
            dst = XS_[l][64:128, g, :, :].rearrange("p pr (blk d) -> p pr blk d", d=64)[:, :, 1::2, :]
            cp(dst, src, [ST_[l].c(g)], [XS_[l].c(g)])

    xT = S.sb([128, 8, NT], F32, "xT")
    mixT = S.sb([128, 8, NT], F32, "mixT")
    hT = S.sb([128, 8, NT], BF16, "hT")
    ARENA = S.mark()
    seqs = [dict(kind="meta", b=0, T=NMETA, slot=0)]
    for b in range(n_pseq):
        seqs.append(dict(kind="prompt", b=b, T=n_ptiles * NT, slot=b))
    for b in range(n_sseq):
        seqs.append(dict(kind="sample", b=b, T=DSEQ, slot=4 + b))
    if stage < 3:
        seqs = []
    only = cfg.get('only', ['meta', 'prompt', 'sample'])
    seqs = [q for q in seqs if q['kind'] in only]

    def rmsnorm_to(dst_bf, src_f32, ncol_name, l, N):
        mk = S.mark()
        sqb = S.sb([128, 8, N], BF16, "sqb")
        rstd = S.sb([128, N], F32, "rstd")
        for c in range(8):
            act(sqb[:, c, :], src_f32[:, c, 0:N], AF.Square, [src_f32.c(c)], [sqb.c(c)])
        bank = PS[4]
        for c in range(8):
            MM(bank[:, 0:N], ones_b[:, :], sqb[:, c, :], c == 0, c == 7, [ones_b, sqb.c(c)], [bank.r(0, 4 * N)])
        act(rstd[:, :], bank[:, 0:N], AF.Sqrt, [bank.r(0, 4 * N), epsT], [rstd], bias=epsT[:, 0:1], scale=1.0 / D)
        V(lambda e, o_=rstd[:, :]: e.reciprocal(out=o_, in_=o_), [rstd], [rstd])
        for c in range(8):
            stt(dst_bf[:, c, 0:N], src_f32[:, c, 0:N], pc(l, ncol_name, c), rstd[:, :], ALU.mult, ALU.mult,
                [src_f32.c(c), P[l]["_t"], rstd], [dst_bf.c(c)])
        S.reset(mk)

    def proj_fm(l, wsrc, kc, c0, ncols, rhs_fn, rhs_reads, N, evac, prt=128, bankset=(0, 1)):
        mi = 0
        for g0 in range(0, ncols, 512):
            gw = min(512, ncols - g0)
            wt, view = load_w(wsrc[:, c0 + g0:c0 + g0 + gw], kc, gw, prt)
            for m0 in range(0, gw, 128):
                mw = min(128, gw - m0)
                bank = PS[bankset[mi % len(bankset)]]
                for k in range(kc):
                    MM(bank[0:mw, 0:N], view[:, k, m0:m0 + mw], rhs_fn(k), k == 0, k == kc - 1,
                       [wt] + rhs_reads(k), [bank.r(0, 4 * N)])
                evac(mi, bank, mw)
                mi += 1

    for sq in seqs:
        kind, b, slot = sq["kind"], sq["b"], sq["slot"]
        tiles = [(t0, min(NT, sq["T"] - t0)) for t0 in range(0, sq["T"], NT)]
        for l in range(2):
            if kind == "prompt" and upto < 3:
                continue
            if kind == "meta":
                for t in (ST_[l], CH_[l], XR_[l], XI_[l], XS_[l]):
                    V(lambda e, a_=t.h[:]: e.memset(a_, 0.0), [], [t])
            elif kind == "prompt":
                for dst, src in ((ST_[l], STm[l]), (CH_[l], CHm[l]), (XR_[l], XRm[l]), (XI_[l], XIm[l])):
                    cp(dst.h[:], src.h[:], [src], [dst])
                st_to_xs(l)
            else:
                S.dma(OQ, CH_[l][:, :, :].rearrange("p c k -> p (c k)"), I["state_conv"][l, b], [], [CH_[l]], "stin")
                for g in range(2):
                    S.dma(OQ, ST_[l][64:128, g, :], I["state_ssd"][l, b, g], [], [ST_[l].c(g)], "stin")
                S.dma(OQ, XR_[l][:, :], I["state_s5_re"][l, b], [], [XR_[l]], "stin")
                S.dma(OQ, XI_[l][:, :], I["state_s5_im"][l, b], [], [XI_[l]], "stin")
                st_to_xs(l)
                S.reset(ARENA)
                S.dma("pool", vhs[slot, l, 0:PAST, :], I["cache_v"][l, b], [], [("vh%d_%d" % (slot, l), 0, PAST)], "vpre")
                CK = [S.sb([128, 512], F32, "CK") for _ in range(2)]
                KTP = [S.sb([128, 4, 128], BF16, "KTP") for _ in range(2)]
                for tb in range(PAST // 128):
                    ck, kt = CK[tb % 2], KTP[tb % 2]
                    S.dma(OQ, ck[:, :], I["cache_k"][l, b, 128 * tb:128 * tb + 128, :], [], [ck], "ckl%d" % (tb % 2))
                    bank = PS[7]
                    for m in range(4):
                        TR(bank[:, 128 * m:128 * m + 128], ck[:, 128 * m:128 * m + 128], ident_f[:, :], [ck, ident_f],
                           [bank.r(512 * m, 512 * m + 512)])
                    act(kt[:, :, :], bank[:, :].rearrange("p (m t) -> p m t", t=128), AF.Copy, [bank], [kt])
                    S.dma(OQ, kTs[slot, l].rearrange("(m p) t -> p m t", p=128)[:, :, 128 * tb:128 * tb + 128], kt[:, :, :],
                          [kt], [("kT%d_%d" % (slot, l), 128 * tb, 128 * tb + 128)], "kpre%d" % (tb % 2))
        for (t0, N) in tiles:
            S.reset(ARENA)
            pos0 = {"meta": 0, "prompt": NMETA + t0, "sample": PAST}[kind]
            nbk = (N + 127) // 128
            blk = [(tb, min(128, N - 128 * tb)) for tb in range(nbk)]
            if kind == "meta":
                xsrc = I["meta_tokens"]
            elif kind == "prompt":
                xsrc = I["x_prompt"][b, t0:t0 + N, :]
            else:
                xsrc = I["x_sample"][b, t0:t0 + N, :]
            mk0 = S.mark()
            x_tm = S.sb([128, nbk, D], F32, "x_tm")
            for tb, nt in blk:
                S.dma(OQ, x_tm[0:nt, tb, :], xsrc[128 * tb:128 * tb + nt, :], [], [x_tm.c(tb)], "xin")
            for c in range(8):
                bank = PS[c % 4]
                for tb, nt in blk:
                    TR(bank[:, 128 * tb:128 * tb + nt], x_tm[0:nt, tb, 128 * c:128 * c + 128], ident_f[0:nt, 0:nt],
                       [x_tm.c(tb), ident_f], [bank.r(512 * tb, 512 * tb + 4 * nt)])
                act(xT[:, c, 0:N], bank[:, 0:N], AF.Copy, [bank.r(0, 4 * N)], [xT.c(c)])
            S.reset(mk0)
            for l in range(2):
                S.reset(ARENA)
                rmsnorm_to(hT, xT, "nmix", l, N)
                hrd = lambda k: [hT.c(k)]
                hrhs = lambda k, N=N: hT[:, k, 0:N]
                if kind == "meta":
                    kdst = [O["k_prompt"][l, bb, 0:NMETA, :] for bb in range(n_pseq)]
                    vdst = [O["v_prompt"][l, bb, 0:NMETA, :] for bb in range(n_pseq)]
                    slots = list(range(n_pseq))
                elif kind == "prompt":
                    kdst = [O["k_prompt"][l, b, NMETA + t0:NMETA + t0 + N, :]]
                    vdst = [O["v_prompt"][l, b, NMETA + t0:NMETA + t0 + N, :]]
                    slots = [slot]
                else:
                    kdst = [O["k_sample"][l, b, t0:t0 + N, :]]
                    vdst = [O["v_sample"][l, b, t0:t0 + N, :]]
                    slots = [slot]
                mkA = S.mark()
                knT = S.sb([128, 4, N], F32, "knT")
                knb = S.sb([128, 4, N], BF16, "knb")
                qnb = S.sb([128, 4, N], BF16, "qnb")
                ksq = S.sb([128, N], BF16, "ksq")
                rk = S.sb([128, N], F32, "rk")

                def qk_evac(dst32, dstbf, gname):
                    def ev(m, bank, mw):
                        kd = cfg.get("kd", 99)
                        if kd < 1:
                            return
                        act(ksq[:, :], bank[:, 0:N], AF.Square, [bank.r(0, 4 * N)], [ksq])
                        if kd < 2:
                            return
                        b2 = PS[2 + m % 2]
                        MM(b2[:, 0:N], bd64_b[:, :], ksq[:, :], True, True, [bd64_b, ksq], [b2.r(0, 4 * N)])
                        if kd < 3:
                            return
                        act(rk[:, :], b2[:, 0:N], AF.Sqrt, [b2.r(0, 4 * N), epsT], [rk], bias=epsT[:, 0:1], scale=1.0 / 64)
                        if kd < 4:
                            return
                        V(lambda e, o_=rk[:, :]: e.reciprocal(out=o_, in_=o_), [rk], [rk])
                        if kd < 5:
                            return
                        if dst32 is not None:
                            stt(dst32[:, m, :], bank[:, 0:N], pc(l, gname), rk[:, :], ALU.mult, ALU.mult,
                                [bank.r(0, 4 * N), P[l]["_t"], rk], [dst32.c(m)])
                            cp(dstbf[:, m, :], dst32[:, m, :], [dst32.c(m)], [dstbf.c(m)], eng="pool")
                        else:
                            stt(dstbf[:, m, :], bank[:, 0:N], pc(l, gname), rk[:, :], ALU.mult, ALU.mult,
                                [bank.r(0, 4 * N), P[l]["_t"], rk], [dstbf.c(m)])
                    return ev
                proj_fm(l, WB["w_in"][l], 8, OFF_K, 512, hrhs, hrd, N, qk_evac(knT, knb, "kn"))
                proj_fm(l, WB["w_in"][l], 8, OFF_Q, 512, hrhs, hrd, N, qk_evac(None, qnb, "qn"))
                if cfg.get("kd", 99) < 7:
                    continue
                for sl in slots:
                    S.dma(OQ, kTs[sl, l].rearrange("(m p) t -> p m t", p=128)[:, :, pos0:pos0 + N], knb[:, :, :],
                          [knb], [("kT%d_%d" % (sl, l), pos0, pos0 + N)], "ktw")
                if cfg.get("kd", 99) < 8:
                    continue
                k_tm = S.sb([128, nbk, 512], F32, "k_tm")
                for tb, nt in blk:
                    bank = PS[5]
                    for m in range(4):
                        TR(bank[0:nt, 128 * m:128 * m + 128], knT[:, m, 128 * tb:128 * tb + nt], ident_f[:, :],
                           [knT.c(m), ident_f], [bank.r(512 * m, 512 * m + 512)])
                    act(k_tm[0:nt, tb, :], bank[0:nt, :], AF.Copy, [bank], [k_tm.c(tb)])
                    for dst in kdst:
                        S.dma(OQ, dst[128 * tb:128 * tb + nt, :], k_tm[0:nt, tb, :], [k_tm.c(tb)], [], "kout")
                if cfg.get("kd", 99) < 9:
                    continue
                wt, wv = load_w(WB["w_in"][l][:, OFF_V:OFF_V + 512], 8, 512)
                v_tm = S.sb([128, nbk, 512], F32, "v_tm")
                v_bf = S.sb([128, nbk, 512], BF16, "v_bf")
                for tb, nt in blk:
                    bank = PS[6]
                    for kc in range(8):
                        MM(bank[0:nt, :], hT[:, kc, 128 * tb:128 * tb + nt], wv[:, kc, :], kc == 0, kc == 7,
                           [wt, hT.c(kc)], [bank])
                    act(v_tm[0:nt, tb, :], bank[0:nt, :], AF.Copy, [bank], [v_tm.c(tb)])
                    for dst in vdst:
                        S.dma(OQ, dst[128 * tb:128 * tb + nt, :], v_tm[0:nt, tb, :], [v_tm.c(tb)], [], "vout")
                    if cfg.get("kd", 99) < 10:
                        continue
                    cp(v_bf[0:nt, tb, :], v_tm[0:nt, tb, :], [v_tm.c(tb)], [v_bf.c(tb)], eng="pool")
                    if cfg.get("vh", 2) < 2:
                        continue
                    for sl in slots:
                        S.dma(OQ, vhs[sl, l, pos0 + 128 * tb:pos0 + 128 * tb + nt, :], v_bf[0:nt, tb, :],
                              [v_bf.c(tb)], [("vh%d_%d" % (sl, l), pos0 + 128 * tb, pos0 + 128 * tb + nt)], "vhw")
                if upto < 2:
                    continue
                def gates(i):
                    proj_fm(l, WB["w_in"][l], 8, OFF_G + 1024 * i, 1024, hrhs, hrd, N,
                            lambda m, bank, mw: act(GT[:, m, :], bank[:, 0:N], AF.Sigmoid, [bank.r(0, 4 * N)], [GT.c(m)]))

                def mix_evac(first):
                    def ev(m, bank, mw):
                        if first:
                            tt(mixT[:, m, 0:N], GT[:, m, :], bank[:, 0:N], ALU.mult, [GT.c(m), bank.r(0, 4 * N)], [mixT.c(m)])
                        else:
                            tt(tmpA[:, :], GT[:, m, :], bank[:, 0:N], ALU.mult, [GT.c(m), bank.r(0, 4 * N)], [tmpA])
                            tt(mixT[:, m, 0:N], mixT[:, m, 0:N], tmpA[:, :], ALU.add, [mixT.c(m), tmpA], [mixT.c(m)])
                    return ev
                nk = pos0 + N
                nb = (nk + 127) // 128
                slot_r = slots[0]
                OC = S.sb([64, 8, N], BF16, "OC")
                mkC = S.mark()
                RS = S.sb([128, N], F32, "RS")
                EXs = [S.sb([128, N], F32, "EX") for _ in range(2)]
                ARGs = [S.sb([128, N], F32, "ARG") for _ in range(2)]
                LPs = [S.sb([128, N], F32, "LP") for _ in range(2)]
                WTs = [S.sb([128, N], BF16, "WT") for _ in range(2)]
                KTb = [S.sb([128, NKMAX], BF16, "KT") for _ in range(2)]
                VBb = [S.sb([128, 17, 128], BF16, "VB") for _ in range(2)]
                nfull, rem = nk // 128, nk % 128
                bctr = 0
                for pr in range(4):
                    KT, VB = KTb[pr % 2], VBb[pr % 2]
                    kname, vname = "kT%d_%d" % (slot_r, l), "vh%d_%d" % (slot_r, l)
                    S.dma(OQ, KT[:, 0:nk], kTs[slot_r, l, 128 * pr:128 * pr + 128, 0:nk], [(kname, 0, nk)], [KT], "ktl%d" % (pr % 2))
                    if nfull:
                        S.dma(OQ, VB[:, 0:nfull, :],
                              vhs[slot_r, l, 0:128 * nfull, 128 * pr:128 * pr + 128].rearrange("(b p) c -> p b c", p=128),
                              [(vname, 0, 128 * nfull)], [VB.r(0, nfull * 256)], "vbl%d" % (pr % 2))
                    if rem:
                        S.dma(OQ, VB[0:rem, nfull, :], vhs[slot_r, l, 128 * nfull:nk, 128 * pr:128 * pr + 128],
                              [(vname, 128 * nfull, nk)], [VB.r(nfull * 256, nfull * 256 + 256)], "vbl%d" % (pr % 2))
                    for hh in range(2):
                        h = 2 * pr + hh
                        R = slice(64 * hh, 64 * hh + 64)
                        V(lambda e, a_=RS[:, :]: e.memset(a_, 0.0), [], [RS])
                        PO = PS[4]
                        blocks = list(reversed(range(nb)))

                        def stage1(bI, i):
                            kb = min(128, nk - 128 * bI)
                            PZ, P2 = PS[i % 2], PS[2 + i % 2]
                            LP, EXb = LPs[i % 2], EXs[i % 2]
                            MM(PZ[0:kb, 0:N], KT[R, 128 * bI:128 * bI + kb], qnb[R, pr, 0:N], True, True,
                               [KT, qnb.c(pr)], [PZ.r(0, 4 * N)])
                            act(EXb[0:kb, :], PZ[0:kb, 0:N], AF.Exp, [PZ.r(0, 4 * N)], [EXb], scale=0.125)
                            act(LP[0:kb, :].bitcast(F32R), EXb[0:kb, :], AF.Ln, [EXb, oneT], [LP], bias=oneT[0:kb, 0:1])
                            if 128 * bI + kb - 1 >= pos0:
                                mt = masks[pos0 - 128 * bI]
                                tt(LP[0:kb, :].bitcast(F32R), LP[0:kb, :], mt[0:kb, 0:N], ALU.mult, [LP, mt], [LP])
                            MM(PZ[0:kb, 0:N], tri_r[0:kb, 0:kb], LP[0:kb, :].bitcast(F32R), False, True,
                               [tri_r, LP], [PZ.r(0, 4 * N)], skip_group_check=True)
                            if bI > 0:
                                MM(P2[:, 0:N], ones_r[0:kb, :], LP[0:kb, :].bitcast(F32R), True, True, [ones_r, LP], [P2.r(0, 4 * N)])

                        def stage2(bI, i):
                            kb = min(128, nk - 128 * bI)
                            PZ, P2 = PS[i % 2], PS[2 + i % 2]
                            WT, AG = WTs[i % 2], ARGs[i % 2]
                            stt(AG[0:kb, :], PZ[0:kb, 0:N], 0.125, RS[0:kb, :], ALU.mult, ALU.subtract,
                                [PZ.r(0, 4 * N), RS], [AG])
                            if bI > 0:
                                tt(RS[:, :], RS[:, :], P2[:, 0:N], ALU.add, [RS, P2.r(0, 4 * N)], [RS])
                            act(WT[0:kb, :], AG[0:kb, :], AF.Exp, [AG], [WT])
                            if 128 * bI + kb - 1 >= pos0:
                                mt = masks[pos0 - 128 * bI]
                                tt(WT[0:kb, :], WT[0:kb, :], mt[0:kb, 0:N], ALU.mult, [WT, mt], [WT])
                            MM(PO[0:64, 0:N], VB[0:kb, bI, 64 * hh:64 * hh + 64], WT[0:kb, :], i == 0, bI == 0,
                               [VB.r(bI * 256, bI * 256 + 256), WT], [PO.r(0, 4 * N)])
                        for i, bI in enumerate(blocks):
                            stage1(bI, i)
                            if i >= 1:
                                stage2(blocks[i - 1], i - 1)
                        stage2(blocks[-1], len(blocks) - 1)
                        act(OC[:, h, :], PO[0:64, 0:N], AF.Copy, [PO.r(0, 4 * N)], [OC.c(h)])
                S.reset(mkC)
                GT = S.sb([128, 8, N], F32, "GT")
                tmpA = S.sb([128, N], F32, "tmpA")
                gates(2)
                proj_fm(l, WB["w_lift_c"][l], 8, 0, 1024, lambda h_: OC[:, h_, :], lambda h_: [OC.c(h_)], N, mix_evac(True), prt=64)
                if upto < 3:
                    continue
                S.reset(mkA)
                GT = S.sb([128, 8, N], F32, "GT")
                tmpA = S.sb([128, N], F32, "tmpA")
                Q = min(64, N)
                nch = N // Q
                xp = S.sb([128, 10, N + 3], F32, "xp")
                XBC = S.sb([128, 10, N], BF16, "XBC")
                yT = S.sb([128, 8, N], F32, "yT")
                cp(xp[:, :, 0:3], CH_[l][:, :, :], [CH_[l]], [xp])
                proj_fm(l, WB["w_in"][l], 8, OFF_XBC, 1280, hrhs, hrd, N,
                        lambda m, bank, mw: act(xp[:, m, 3:3 + N], bank[:, 0:N], AF.Copy, [bank.r(0, 4 * N)], [xp.c(m)]))
                cp(CH_[l][:, :, :], xp[:, :, N:N + 3], [xp], [CH_[l]])
                for c in range(10):
                    ts(tmpA[:, :], xp[:, c, 0:N], pc(l, "cw", c), pc(l, "cb", c), ALU.mult, ALU.add, [xp.c(c), P[l]["_t"]], [tmpA])
                    for k in range(1, 4):
                        stt(tmpA[:, :], xp[:, c, k:k + N], pc(l, "cw", 10 * k + c), tmpA[:, :], ALU.mult, ALU.add,
                            [xp.c(c), P[l]["_t"], tmpA], [tmpA])
                    act(XBC[:, c, :], tmpA[:, :], AF.Silu, [tmpA], [XBC.c(c)])
                if cfg.get("sd", 99) < 2:
                    continue
                PD = PS[2]
                for k in range(8):
                    MM(PD[0:48, 0:N], WDT[l][:, k, :], hT[:, k, 0:N], k == 0, k == 7, [WDT[l], hT.c(k)], [PD.r(0, 4 * N)])
                dtT = S.sb([48, N], F32, "dtT")
                dA = S.sb([48, N], F32, "dA")
                AC = S.sb([16, N], F32, "AC")
                act(dtT[:, :], PD[0:48, 0:N], AF.Exp, [PD.r(0, 4 * N), P[l]["_t"]], [dtT], bias=pc(l, "dtb", 0, slice(0, 48)))
                act(dtT[:, :], dtT[:, :], AF.Ln, [dtT, oneT], [dtT], bias=oneT[0:48, 0:1])
                ts(dA[:, :], dtT[:, :], acol[l][:, 0:1], None, ALU.mult, None, [dtT, acol[l]], [dA])
                if cfg.get("sd", 99) < 3:
                    continue
                Ets = [S.sb([128, 8, Q], F32, "Et") for _ in range(2)]
                Wts = [S.sb([128, 8, Q], BF16, "Wt") for _ in range(2)]
                if Q < 64:
                    for w__ in Wts:
                        V(lambda e, a_=w__[:, :, :]: e.memset(a_, 0.0), [], [w__])
                dt_tm = S.sb([64, 16], F32, "dt_tm")
                coef = S.sb([64, 8], F32, "coef")
                XW = S.sb([64, 8, 64], BF16, "XW")
                ectr = 0
                for ci in range(nch):
                    cs = slice(ci * Q, ci * Q + Q)
                    LTt = LT[ci % 2]
                    scan(AC[0:16, cs], ones48[0:16, 0:Q], dA[0:16, cs], 0.0, [ones48, dA], [AC])
                    scan(LTt[32:48, 0:Q], ones48[32:48, 0:Q], dA[32:48, cs], 0.0, [ones48, dA], [LTt])
                    PT = PS[7]
                    MM(PT[0:Q, 0:16], dtT[0:16, cs], ident_f[0:16, 0:16], True, True, [dtT, ident_f], [PT.r(0, 64)])
                    act(dt_tm[0:Q, :], PT[0:Q, 0:16], AF.Copy, [PT.r(0, 64)], [dt_tm])
                    if cfg.get("sd", 99) < 4:
                        continue
                    PY = PS[3]
                    for g in range(2):
                        GR = slice(64 * g, 64 * g + 64)
                        Et, Wt = Ets[ectr % 2], Wts[ectr % 2]
                        ectr += 1
                        tt(RF[g][0:16, :, 0:Q], AC[0:16, cs].unsqueeze(1).broadcast_to([16, 8, Q]),
                           DEL[g][0:16, :].unsqueeze(2).broadcast_to([16, 8, Q]), ALU.mult, [AC, DEL[g]], [RF[g]])
                        PE_ = PS[4]
                        pev = PE_[:, 0:8 * Q].rearrange("p (h i) -> p h i", i=Q)
                        MM(pev, LTt[:, :], RF[g][:, :, 0:Q], True, True, [LTt, RF[g]], [PE_])
                        act(Et[:, :, :], pev, AF.Exp, [PE_], [Et])
                        if cfg.get("sd", 99) < 5:
                            continue
                        PG = PS[5]
                        MM(PG[0:Q, 0:Q], XBC[GR, 8, cs], XBC[GR, 9, cs], True, True, [XBC.c(8), XBC.c(9)], [PG.r(0, 256)])
                        tt(Wt[0:Q, :, :], Et[0:Q, :, :], PG[0:Q, 0:Q].unsqueeze(1).broadcast_to([Q, 8, Q]), ALU.mult,
                           [Et, PG.r(0, 256)], [Wt])
                        MM(PG[:, 128:128 + Q], SHIFT[GR, :], XBC[GR, 9, cs], True, True, [SHIFT, XBC.c(9)], [PG.r(512, 768)])
                        tt(Wt[64:128, :, :], Et[64:128, :, :], PG[64:128, 128:128 + Q].unsqueeze(1).broadcast_to([64, 8, Q]), ALU.mult,
                           [Et, PG.r(512, 768)], [Wt])
                        if cfg.get("sd", 99) < 6:
                            continue
                        PX = PS[6]
                        for pr in range(4):
                            MM(PX[0:Q, 128 * pr:128 * pr + 128], XBC[:, 4 * g + pr, cs], ident_b[:, :], True, True,
                               [XBC.c(4 * g + pr), ident_b], [PX.r(512 * pr, 512 * pr + 512)])
                        dstX = XS_[l][0:Q, g, :, :].rearrange("q pr (blk d) -> q pr blk d", d=64)[:, :, 1::2, :]
                        tt(dstX, PX[0:Q, :].rearrange("q (pr eo d) -> q pr eo d", pr=4, eo=2),
                           dt_tm[0:Q, 8 * g:8 * g + 8].rearrange("q (pr eo) -> q pr eo", eo=2).unsqueeze(3).broadcast_to([Q, 4, 2, 64]),
                           ALU.mult, [PX, dt_tm], [XS_[l].c(g)])
                        if cfg.get("sd", 99) < 7:
                            continue
                        for pr in range(4):
                            k = 4 * g + pr
                            for eo in range(2):
                                win = slice(64 + 64 * eo, 192 + 64 * eo)
                                MM(PY[:, k * Q:k * Q + Q], XS_[l][:, g, pr, win], Wt[:, 2 * pr + eo, :], eo == 0, eo == 1,
                                   [XS_[l].c(g), Wt], [PY.r(4 * k * Q, 4 * k * Q + 4 * Q)])
                        if cfg.get("sd", 99) < 8:
                            continue
                        tt(coef[0:Q, :], dt_tm[0:Q, 8 * g:8 * g + 8], Et[0:Q, :, Q - 1], ALU.mult, [dt_tm, Et], [coef])
                        tt(XW[0:Q, :, :], PX[0:Q, :].rearrange("q (h d) -> q h d", d=64),
                           coef[0:Q, :].unsqueeze(2).broadcast_to([Q, 8, 64]), ALU.mult, [PX, coef], [XW])
                        MM(PG[0:Q, 64:128], XBC[GR, 8, cs], ident_b[GR, 64 * g:64 * g + 64], True, True,
                           [XBC.c(8), ident_b], [PG.r(256, 512)])
                        cp(BTOK[0:Q, 64:128], PG[0:Q, 64:128], [PG.r(256, 512)], [BTOK])
                        PSt = PS[2]
                        MM(PSt[:, :], BTOK[0:Q, :], XW[0:Q, :, :], True, True, [BTOK, XW], [PSt])
                        stv = ST_[l][64:128, g, :].rearrange("p (h d) -> p h d", d=64)
                        tt(stv, stv, Et[64:128, :, Q - 1].unsqueeze(2).broadcast_to([64, 8, 64]), ALU.mult, [ST_[l].c(g), Et], [ST_[l].c(g)])
                        tt(ST_[l][64:128, g, :], ST_[l][64:128, g, :], PSt[64:128, :], ALU.add, [ST_[l].c(g), PSt], [ST_[l].c(g)])
                        src = ST_[l][64:128, g, :].rearrange("p (pr eo d) -> p pr eo d", pr=4, eo=2)
                        dst = XS_[l][64:128, g, :, :].rearrange("p pr (blk d) -> p pr blk d", d=64)[:, :, 1::2, :]
                        cp(dst, src, [ST_[l].c(g)], [XS_[l].c(g)])
                    if cfg.get("sd", 99) < 9:
                        continue
                    act(yT[:, :, cs], PY[:, 0:8 * Q].rearrange("p (k i) -> p k i", i=Q), AF.Copy, [PY], [yT])
                if cfg.get("sd", 99) < 10:
                    continue
                for k in range(8):
                    stt(yT[:, k, :], XBC[:, k, :], pc(l, "dsk", k), yT[:, k, :], ALU.mult, ALU.add, [XBC.c(k), P[l]["_t"], yT.c(k)], [yT.c(k)])

                def z_evac(m, bank, mw):
                    act(tmpA[:, :], bank[:, 0:N], AF.Silu, [bank.r(0, 4 * N)], [tmpA])
                    tt(yT[:, m, :], yT[:, m, :], tmpA[:, :], ALU.mult, [yT.c(m), tmpA], [yT.c(m)])
                proj_fm(l, WB["w_in"][l], 8, OFF_Z, 1024, hrhs, hrd, N, z_evac)
                ysq = S.sb([128, 8, N], BF16, "ysq")
                YN = S.sb([128, 8, N], BF16, "YN")
                rg = S.sb([128, N], F32, "rg")
                for k in range(8):
                    act(ysq[:, k, :], yT[:, k, :], AF.Square, [yT.c(k)], [ysq.c(k)])
                for g in range(2):
                    bank = PS[4]
                    for k in range(4 * g, 4 * g + 4):
                        MM(bank[:, 0:N], ones_b[:, :], ysq[:, k, :], k == 4 * g, k == 4 * g + 3, [ones_b, ysq.c(k)], [bank.r(0, 4 * N)])
                    act(rg[:, :], bank[:, 0:N], AF.Sqrt, [bank.r(0, 4 * N), epsT], [rg], bias=epsT[:, 0:1], scale=1.0 / 512)
                    V(lambda e, o_=rg[:, :]: e.reciprocal(out=o_, in_=o_), [rg], [rg])
                    for k in range(4 * g, 4 * g + 4):
                        stt(YN[:, k, :], yT[:, k, :], pc(l, "nssd", k), rg[:, :], ALU.mult, ALU.mult, [yT.c(k), P[l]["_t"], rg], [YN.c(k)])
                gates(0)
                proj_fm(l, WB["w_lift_a"][l], 8, 0, 1024, lambda k_: YN[:, k_, :], lambda k_: [YN.c(k_)], N, mix_evac(False))
                if upto < 4:
                    continue
                S.reset(mkA)
                GT = S.sb([128, 8, N], F32, "GT")
                tmpA = S.sb([128, N], F32, "tmpA")
                BC = S.sb([128, 4, 16, 128], BF16, "BC")
                for i in range(4):
                    S.dma("pool", BC[:, i, :, :].rearrange("p c n -> p (c n)"), I["s5bc"][l, i], [], [BC.c(i)], "bcl")
                uT = S.sb([128, 4, N], BF16, "uT")
                u32 = S.sb([128, 4, N], F32, "u32")

                def u_evac(m, bank, mw):
                    act(u32[:, m, :], bank[:, 0:N], AF.Copy, [bank.r(0, 4 * N)], [u32.c(m)])
                    cp(uT[:, m, :], u32[:, m, :], [u32.c(m)], [uT.c(m)], eng="pool")
                proj_fm(l, WB["w_in"][l], 8, OFF_U, 512, hrhs, hrd, N, u_evac)
                A_ = S.sb([128, 8, QS], F32, "A_")
                B_ = S.sb([128, 8, QS], F32, "B_")
                C_ = S.sb([128, 8, QS], F32, "C_")
                D_ = S.sb([128, 8, QS], F32, "D_")
                xrb = S.sb([128, 8, QS], BF16, "xrb")
                xib = S.sb([128, 8, QS], BF16, "xib")
                WIr = S.sb([128, 8], F32, "WIr")
                WIi = S.sb([128, 8], F32, "WIi")
                t8 = S.sb([128, 8], F32, "t8")
                cosT, sinT, tbr, tbi, magT = S5TAB[l]
                XR, XI = XR_[l], XI_[l]
                nsub = (N + QS - 1) // QS
                for s_ in range(nsub):
                    ncol = min(QS, N - s_ * QS)
                    cs = slice(s_ * QS, s_ * QS + ncol)
                    for hf in range(2):
                        ch = slice(8 * hf, 8 * hf + 8)
                        Pre, Pim = PS[0], PS[1]
                        for cc in range(8):
                            c = 8 * hf + cc
                            MM(Pre[:, cc * QS:cc * QS + ncol], BC[:, 0, c, :], uT[:, c // 4, cs], True, True,
                               [BC.c(0), uT.c(c // 4)], [Pre.r(4 * cc * QS, 4 * cc * QS + 4 * ncol)])
                            MM(Pim[:, cc * QS:cc * QS + ncol], BC[:, 1, c, :], uT[:, c // 4, cs], True, True,
                               [BC.c(1), uT.c(c // 4)], [Pim.r(4 * cc * QS, 4 * cc * QS + 4 * ncol)])
                        PreV = Pre[:, :].rearrange("p (c t) -> p c t", t=QS)[:, :, 0:ncol]
                        PimV = Pim[:, :].rearrange("p (c t) -> p c t", t=QS)[:, :, 0:ncol]
                        a_, b_, c_, d_ = A_[:, :, 0:ncol], B_[:, :, 0:ncol], C_[:, :, 0:ncol], D_[:, :, 0:ncol]
                        tr_, ti_ = tbr[:, ch, 0:ncol], tbi[:, ch, 0:ncol]
                        tt(a_, PreV, tr_, ALU.mult, [Pre, tbr], [A_])
                        tt(b_, PimV, ti_, ALU.mult, [Pim, tbi], [B_])
                        tt(a_, a_, b_, ALU.subtract, [A_, B_], [A_])
                        tt(b_, PreV, ti_, ALU.mult, [Pre, tbi], [B_])
                        tt(c_, PimV, tr_, ALU.mult, [Pim, tbr], [C_])
                        tt(b_, b_, c_, ALU.add, [B_, C_], [B_])
                        cos1, sin1 = cosT[:, ch, 1], sinT[:, ch, 1]
                        tt(WIr[:, :], cos1, XR[:, ch], ALU.mult, [cosT, XR], [WIr])
                        tt(t8[:, :], sin1, XI[:, ch], ALU.mult, [sinT, XI], [t8])
                        tt(WIr[:, :], WIr[:, :], t8[:, :], ALU.subtract, [WIr, t8], [WIr])
                        tt(WIi[:, :], sin1, XR[:, ch], ALU.mult, [sinT, XR], [WIi])
                        tt(t8[:, :], cos1, XI[:, ch], ALU.mult, [cosT, XI], [t8])
                        tt(WIi[:, :], WIi[:, :], t8[:, :], ALU.add, [WIi, t8], [WIi])
                        for cc in range(8):
                            c = 8 * hf + cc
                            scan(C_[:, cc, 0:ncol], magT[:, c, 0:ncol], A_[:, cc, 0:ncol], WIr[:, cc:cc + 1], [magT, A_, WIr], [C_.c(cc)])
                            scan(D_[:, cc, 0:ncol], magT[:, c, 0:ncol], B_[:, cc, 0:ncol], WIi[:, cc:cc + 1], [magT, B_, WIi], [D_.c(cc)])
                        cosL, sinL = cosT[:, ch, ncol - 1], sinT[:, ch, ncol - 1]
                        wrl, wil = C_[:, :, ncol - 1], D_[:, :, ncol - 1]
                        tt(XR[:, ch], cosL, wrl, ALU.mult, [cosT, C_], [XR])
                        tt(t8[:, :], sinL, wil, ALU.mult, [sinT, D_], [t8])
                        tt(XR[:, ch], XR[:, ch], t8[:, :], ALU.subtract, [XR, t8], [XR])
                        tt(XI[:, ch], sinL, wrl, ALU.mult, [sinT, C_], [XI])
                        tt(t8[:, :], cosL, wil, ALU.mult, [cosT, D_], [t8])
                        tt(XI[:, ch], XI[:, ch], t8[:, :], ALU.add, [XI, t8], [XI])
                        cos_, sin_ = cosT[:, ch, 0:ncol], sinT[:, ch, 0:ncol]
                        tt(a_, c_, cos_, ALU.mult, [C_, cosT], [A_])
                        tt(b_, d_, sin_, ALU.mult, [D_, sinT], [B_])
                        tt(xrb[:, :, 0:ncol], a_, b_, ALU.subtract, [A_, B_], [xrb])
                        tt(a_, c_, sin_, ALU.mult, [C_, sinT], [A_])
                        tt(b_, d_, cos_, ALU.mult, [D_, cosT], [B_])
                        stt(xib[:, :, 0:ncol], a_, -1.0, b_, ALU.mult, ALU.subtract, [A_, B_], [xib])
                        for cc in range(8):
                            c = 8 * hf + cc
                            j, m_ = c // 4, c % 4
                            bank = PS[6 + j // 2]
                            col0 = (j % 2) * N + s_ * QS
                            MM(bank[:, col0:col0 + ncol], BC[:, 2, c, :], xrb[:, cc, 0:ncol], m_ == 0, False,
                               [BC.c(2), xrb], [bank.r(4 * col0, 4 * col0 + 4 * ncol)])
                            MM(bank[:, col0:col0 + ncol], BC[:, 3, c, :], xib[:, cc, 0:ncol], False, m_ == 3,
                               [BC.c(3), xib], [bank.r(4 * col0, 4 * col0 + 4 * ncol)])
                GB = S.sb([128, 4, N], BF16, "GB")
                t5 = S.sb([128, N], F32, "t5")
                t6 = S.sb([128, N], F32, "t6")
                for j in range(4):
                    bank = PS[6 + j // 2]
                    col0 = (j % 2) * N
                    stt(t5[:, :], u32[:, j, :], pc(l, "d5", j), bank[:, col0:col0 + N], ALU.mult, ALU.add,
                        [u32.c(j), P[l]["_t"], bank.r(4 * col0, 4 * col0 + 4 * N)], [t5])
                    tt(t6[:, :], t5[:, :], t5[:, :], ALU.mult, [t5], [t6])
                    ts(t6[:, :], t6[:, :], 0.044715, 1.0, ALU.mult, ALU.add, [t6], [t6])
                    tt(t6[:, :], t6[:, :], t5[:, :], ALU.mult, [t6, t5], [t6])
                    act(t6[:, :], t6[:, :], AF.Sigmoid, [t6], [t6], scale=1.5957691216057308)
                    tt(GB[:, j, :], t5[:, :], t6[:, :], ALU.mult, [t5, t6], [GB.c(j)])
                gates(1)
                for half in range(2):
                    wtA, vA = load_w(WB["w_glu"][l][:, 512 * half:512 * half + 512], 4, 512)
                    wtB, vB = load_w(WB["w_glu"][l][:, 1024 + 512 * half:1024 + 512 * half + 512], 4, 512)
                    for mm_ in range(4):
                        m = 4 * half + mm_
                        P1, P2 = PS[0], PS[1]
                        for j in range(4):
                            MM(P1[:, 0:N], vA[:, j, 128 * mm_:128 * mm_ + 128], GB[:, j, :], j == 0, j == 3, [wtA, GB.c(j)], [P1.r(0, 4 * N)])
                        for j in range(4):
                            MM(P2[:, 0:N], vB[:, j, 128 * mm_:128 * mm_ + 128], GB[:, j, :], j == 0, j == 3, [wtB, GB.c(j)], [P2.r(0, 4 * N)])
                        act(t5[:, :], P2[:, 0:N], AF.Sigmoid, [P2.r(0, 4 * N)], [t5])
                        tt(t5[:, :], t5[:, :], P1[:, 0:N], ALU.mult, [t5, P1.r(0, 4 * N)], [t5])
                        tt(t5[:, :], t5[:, :], GT[:, m, :], ALU.mult, [t5, GT.c(m)], [t5])
                        tt(mixT[:, m, 0:N], mixT[:, m, 0:N], t5[:, :], ALU.add, [mixT.c(m), t5], [mixT.c(m)])
                if upto < 5:
                    continue
                S.reset(mkA)
                mixb = S.sb([128, 8, N], BF16, "mixb")
                for c in range(8):
                    cp(mixb[:, c, :], mixT[:, c, 0:N], [mixT.c(c)], [mixb.c(c)], eng="pool")

                def res_evac(m, bank, mw):
                    tt(xT[:, m, 0:N], xT[:, m, 0:N], bank[:, 0:N], ALU.add, [xT.c(m), bank.r(0, 4 * N)], [xT.c(m)])
                proj_fm(l, WB["w_out"][l], 8, 0, 1024, lambda k_: mixb[:, k_, :], lambda k_: [mixb.c(k_)], N, res_evac)
                rmsnorm_to(hT, xT, "nffn", l, N)
                AT = S.sb([128, 32, N], BF16, "AT")
                t5 = S.sb([128, N], F32, "t5")

                def up_evac(m, bank, mw):
                    act(t5[:, :], bank[:, 0:N], AF.Relu, [bank.r(0, 4 * N)], [t5])
                    tt(AT[:, m, :], t5[:, :], t5[:, :], ALU.mult, [t5], [AT.c(m)])
                proj_fm(l, WB["w_up"][l], 8, 0, 4096, hrhs, hrd, N, up_evac)
                for m in range(8):
                    wt, vw = load_w(WB["w_down"][l][:, 128 * m:128 * m + 128], 32, 128)
                    bank = PS[m % 2]
                    for k in range(32):
                        MM(bank[:, 0:N], vw[:, k, :], AT[:, k, :], k == 0, k == 31, [wt, AT.c(k)], [bank.r(0, 4 * N)])
                    res_evac(m, bank, 128)
            if upto >= 5 and kind != "meta":
                S.reset(ARENA)
                ydst = O["y_prompt"][b, t0:t0 + N, :] if kind == "prompt" else O["y_sample"][b, t0:t0 + N, :]
                y_tm = S.sb([128, nbk, D], F32, "y_tm")
                for tb, nt in blk:
                    for hc in range(2):
                        bank = PS[5 + hc]
                        for c4 in range(4):
                            c = 4 * hc + c4
                            TR(bank[0:nt, 128 * c4:128 * c4 + 128], xT[:, c, 128 * tb:128 * tb + nt], ident_f[:, :],
                               [xT.c(c), ident_f], [bank.r(512 * c4, 512 * c4 + 512)])
                        act(y_tm[0:nt, tb, 512 * hc:512 * hc + 512], bank[0:nt, :], AF.Copy, [bank], [y_tm.c(tb)])
                    S.dma(OQ, ydst[128 * tb:128 * tb + nt, :], y_tm[0:nt, tb, :], [y_tm.c(tb)], [], "yout")
        if upto >= 3:
            for l in range(2):
                if kind == "meta":
                    for dst, src in ((STm[l], ST_[l]), (CHm[l], CH_[l]), (XRm[l], XR_[l]), (XIm[l], XI_[l])):
                        cp(dst.h[:], src.h[:], [src], [dst])
                else:
                    sfx = "prompt" if kind == "prompt" else "sample"
                    S.dma(OQ, O["conv_" + sfx][l, b], CH_[l][:, :, :].rearrange("p c k -> p (c k)"), [CH_[l]], [], "stout")
                    for g in range(2):
                        S.dma(OQ, O["ssd_" + sfx][l, b, g], ST_[l][64:128, g, :], [ST_[l].c(g)], [], "stout")
                    S.dma(OQ, O["s5re_" + sfx][l, b], XR_[l][:, :], [XR_[l]], [], "stout")
                    S.dma(OQ, O["s5im_" + sfx][l, b], XI_[l][:, :], [XI_[l]], [], "stout")
    OUT_STREAMS[:] = ["kout", "vout", "yout", "stout"]
    S.emit(OUT_STREAMS)
    return nc


OUT_STREAMS = []

_NC_CACHE = {}


def pack_s5bc(inp):
    out = np.zeros((2, 4, 128, 16, 128), np.float32)
    for l in range(2):
        for i, nm in enumerate(("b_re", "b_im")):
            b = np.asarray(inp[nm], np.float32)[l]
            for c in range(16):
                m = c % 4
                for two in range(2):
                    g = 2 * c + two
                    r0 = 32 * m + 16 * two
                    out[l, i, r0:r0 + 16, c, 64 * two:64 * two + 64] = b[g].T
        for i, nm in enumerate(("c_re", "c_im")):
            cc = np.asarray(inp[nm], np.float32)[l]
            for c in range(16):
                m = c % 4
                for two in range(2):
                    g = 2 * c + two
                    c0 = 32 * m + 16 * two
                    out[l, 2 + i, 64 * two:64 * two + 64, c, c0:c0 + 16] = cc[g].T
    return out.reshape(2, 4, 128, 2048)


def kernel(**inp):
    cfg = inp.pop("_cfg", None)
    key = repr(cfg)
    if key not in _NC_CACHE:
        _NC_CACHE[key] = build(cfg)
    nc = _NC_CACHE[key]
    f = lambda a: np.ascontiguousarray(np.asarray(a, dtype=np.float32))
    wnames = ["meta_tokens", "norm_mix", "w_in", "conv_w", "conv_b", "dt_bias", "a_log", "d_ssd", "norm_ssd",
              "lam_re", "lam_im", "log_step", "w_glu", "q_norm", "k_norm",
              "w_lift_a", "w_lift_c", "w_out", "norm_ffn", "w_up", "w_down"]
    shared = {n: f(inp[n]) for n in wnames}
    shared["d_s5"] = f(inp["d_s5"]).reshape(2, 512)
    shared["par"] = pack_params(inp)
    shared["s5bc"] = pack_s5bc(inp)
    in_maps = []
    for c in range(8):
        m = dict(shared)
        m["x_prompt"] = f(inp["x_prompt"][4 * c:4 * c + 4])
        m["x_sample"] = f(inp["x_sample"][2 * c:2 * c + 2])
        m["cache_k"] = f(inp["cache_k"][:, 2 * c:2 * c + 2]).reshape(2, 2, PAST, 512)
        m["cache_v"] = f(inp["cache_v"][:, 2 * c:2 * c + 2]).reshape(2, 2, PAST, 512)
        sc = f(inp["state_conv"][:, 2 * c:2 * c + 2]).reshape(2, 2, 3, 10, 128)
        m["state_conv"] = np.ascontiguousarray(sc.transpose(0, 1, 4, 3, 2)).reshape(2, 2, 128, 30)
        ss = f(inp["state_ssd"][:, 2 * c:2 * c + 2])
        m["state_ssd"] = np.ascontiguousarray(ss.transpose(0, 1, 2, 5, 3, 4)).reshape(2, 2, 2, 64, 512)
        for nm in ("state_s5_re", "state_s5_im"):
            a5 = f(inp[nm][:, 2 * c:2 * c + 2]).reshape(2, 2, 16, 2, 64)
            m[nm] = np.ascontiguousarray(a5.transpose(0, 1, 3, 4, 2)).reshape(2, 2, 128, 16)
        in_maps.append(m)
    ncores = (cfg or {}).get('ncores', 8)
    res = run_bass_kernel_spmd(nc, in_maps[:ncores], core_ids=list(range(ncores)))
    R = list(res.results) + [res.results[0]] * (8 - ncores)
    cat = lambda k, ax: np.concatenate([np.asarray(R[c][k], dtype=np.float32) for c in range(8)], axis=ax)
    y_p = cat("y_prompt", 0)
    y_s = cat("y_sample", 0)
    k_p = cat("k_prompt", 1).reshape(2, 32, TP, 8, 64)
    v_p = cat("v_prompt", 1).reshape(2, 32, TP, 8, 64)
    unconv = lambda a: np.ascontiguousarray(a.reshape(2, -1, 128, 10, 3).transpose(0, 1, 4, 3, 2)).reshape(2, -1, 3, 1280)
    unssd = lambda a: np.ascontiguousarray(a.reshape(2, -1, 2, 64, 8, 64).transpose(0, 1, 2, 4, 5, 3))
    uns5 = lambda a: np.ascontiguousarray(a.reshape(2, -1, 2, 64, 16).transpose(0, 1, 4, 2, 3)).reshape(2, -1, 32, 64)
    conv_p = unconv(cat("conv_prompt", 1))
    ssd_p = unssd(cat("ssd_prompt", 1))
    s5r_p = uns5(cat("s5re_prompt", 1))
    s5i_p = uns5(cat("s5im_prompt", 1))
    k_s = cat("k_sample", 1).reshape(2, 16, DSEQ, 8, 64)
    v_s = cat("v_sample", 1).reshape(2, 16, DSEQ, 8, 64)
    conv_s = unconv(cat("conv_sample", 1))
    ssd_s = unssd(cat("ssd_sample", 1))
    s5r_s = uns5(cat("s5re_sample", 1))
    s5i_s = uns5(cat("s5im_sample", 1))
    return (y_p, y_s, k_p, v_p, conv_p, ssd_p, s5r_p, s5i_p, k_s, v_s, conv_s, ssd_s, s5r_s, s5i_s)
```

```python
import math
import bisect
import numpy as np
import concourse.bass as bass
import concourse.mybir as mybir
from concourse.bass_utils import run_bass_kernel_spmd

F32 = mybir.dt.float32
F32R = mybir.dt.float32r
BF16 = mybir.dt.bfloat16
AF = mybir.ActivationFunctionType
ALU = mybir.AluOpType
AX = mybir.AxisListType

D = 1024
NMETA = 16
SEQ = 2048
TP = NMETA + SEQ
PAST = 2048
DSEQ = 64
NKMAX = 2112
INC = 7440
OFF_Z, OFF_XBC, OFF_DT, OFF_U, OFF_Q, OFF_K, OFF_V, OFF_G = 0, 1024, 2304, 2320, 2832, 3344, 3856, 4368
EPS = 1e-6
NT = 256
QS = 64
NEG = -30000.0
PCOL = {}
_c = 0
for _nm, _w in [("nmix", 8), ("nffn", 8), ("nssd", 8), ("cw", 40), ("cb", 10), ("d5", 4), ("dtb", 1), ("alog", 1),
                ("dsk", 8), ("qn", 1), ("kn", 1), ("lre", 16), ("lim", 16), ("lst", 16)]:
    PCOL[_nm] = (_c, _w)
    _c += _w
NPAR = _c


def pack_params(inp):
    par = np.zeros((2, 128, NPAR), np.float32)
    g = lambda n: np.asarray(inp[n], np.float32)

    def put(l, nm, arr):
        c0, w = PCOL[nm]
        par[l, :, c0:c0 + w] = arr

    for l in range(2):
        put(l, "nmix", g("norm_mix")[l].reshape(8, 128).T)
        put(l, "nffn", g("norm_ffn")[l].reshape(8, 128).T)
        put(l, "nssd", g("norm_ssd")[l].reshape(8, 128).T)
        cw = g("conv_w")[l].reshape(4, 10, 128)
        put(l, "cw", cw.transpose(2, 0, 1).reshape(128, 40))
        put(l, "cb", g("conv_b")[l].reshape(10, 128).T)
        put(l, "d5", g("d_s5")[l].reshape(4, 128).T)
        for nm, src in (("dtb", "dt_bias"), ("alog", "a_log")):
            col = np.zeros((128, 1), np.float32)
            col[0:16, 0] = g(src)[l]
            col[32:48, 0] = g(src)[l]
            put(l, nm, col)
        dsk = np.zeros((128, 8), np.float32)
        d = g("d_ssd")[l]
        for hh in range(2):
            dsk[64 * hh:64 * hh + 64, :] = d[hh::2][None, :]
        put(l, "dsk", dsk)
        put(l, "qn", np.tile(g("q_norm")[l], 2)[:, None])
        put(l, "kn", np.tile(g("k_norm")[l], 2)[:, None])
        for nm, src in (("lre", "lam_re"), ("lim", "lam_im")):
            a = g(src)[l].reshape(16, 2, 64)
            put(l, nm, a.transpose(1, 2, 0).reshape(128, 16))
        ls = g("log_step")[l].reshape(16, 2)
        put(l, "lst", np.repeat(ls.T[:, None, :], 64, axis=1).reshape(128, 16))
    return par
DTSZ = {F32: 4, F32R: 4, BF16: 2}


class Tile:
    def __init__(self, h, space, lo, hi):
        self.h, self.space, self.lo, self.hi = h, space, lo, hi

    def __getitem__(self, k):
        return self.h[k]

    @property
    def all(self):
        return (self.space, self.lo, self.hi)

    def r(self, lo, hi):
        return (self.space, self.lo + lo, self.lo + hi)

    def c(self, i, n=1):
        return (self.space, self.lo + i * self.cb, self.lo + (i + n) * self.cb)


def _reg(x):
    return x.all if isinstance(x, Tile) else x


class Sched:
    ENG = ["pe", "act", "dve", "pool", "sp"]

    def __init__(self, nc):
        self.nc = nc
        self.ops = []
        self.segs = {}
        self.off = 16512
        self.cnt = 0
        self.stream_n = {}
        self.psb = []

    def sb(self, shape, dtype, name="t"):
        nb = int(np.prod(shape[1:])) * DTSZ[dtype]
        off = (self.off + 31) // 32 * 32
        self.cnt += 1
        h = self.nc.alloc_sbuf_tensor_at(f"{name}{self.cnt}", list(shape), dtype, offset=off)
        self.off = off + nb
        self.peak = max(getattr(self, "peak", 0), self.off)
        assert self.off <= 16512 + 208000, ("sbuf overflow", name, self.off)
        t = Tile(h, "sb", off, off + nb)
        t.cb = (int(np.prod(shape[2:])) if len(shape) > 2 else 1) * DTSZ[dtype]
        return t

    def mark(self):
        return self.off

    def reset(self, m):
        self.off = m

    def _access(self, opi, reg, write, deps, norecord=False):
        space, lo, hi = reg
        if space not in self.segs:
            self.segs[space] = ([0], [[None, {}]])
        starts, data = self.segs[space]
        for b in (lo, hi):
            i = bisect.bisect_right(starts, b) - 1
            if starts[i] != b:
                starts.insert(i + 1, b)
                data.insert(i + 1, [data[i][0], dict(data[i][1])])
        i = bisect.bisect_left(starts, lo)
        while i < len(starts) and starts[i] < hi:
            w, rd = data[i]
            if w is not None and w != opi:
                if not write:
                    deps[w] = "raw"
                elif deps.get(w) != "raw":
                    deps[w] = "waw"
            if write:
                for r_ in rd.values():
                    if r_ != opi and r_ not in deps:
                        deps[r_] = "war"
                data[i][0] = opi
                data[i][1] = {}
            elif not norecord:
                key = self.ops[opi]["rk"]
                rd[key] = opi
            i += 1

    def op(self, eng, fn, reads=(), writes=(), stream=None):
        opi = len(self.ops)
        o = {"eng": eng, "fn": fn, "stream": stream, "deps": {}, "sig": False}
        if stream is not None:
            o["rk"] = "dma:" + stream
        else:
            o["rk"] = eng
        self.ops.append(o)
        deps = {}
        wregs = [_reg(w_) for w_ in writes]
        for r_ in reads:
            rr = _reg(r_)
            inplace = any(w[0] == rr[0] and w[1] < rr[2] and rr[1] < w[2] for w in wregs)
            self._access(opi, rr, False, deps, norecord=inplace)
        for w_ in writes:
            rg_ = _reg(w_)
            if eng == "pe" and rg_[0] == "ps":
                rg_ = ("ps", rg_[1] // 2048 * 2048, (rg_[2] + 2047) // 2048 * 2048)
            self._access(opi, rg_, True, deps)
        res = {}
        for d, kind in deps.items():
            od = self.ops[d]
            if od["stream"] is not None:
                res[d] = ("s", od["stream"], 16 * self.stream_n[od["stream"]])
            else:
                if od["eng"] == eng and stream is None:
                    if eng == "pe":
                        continue
                res[d] = ("e", od["eng"], None)
                od["sig"] = True
        lw = self.__dict__.setdefault("last_waiter", {})
        if stream is not None and stream in lw:
            w = lw[stream]
            ow = self.ops[w]
            if ow["eng"] != eng and w not in res:
                if ow["stream"] is not None:
                    res[w] = ("s", ow["stream"], 16 * self.stream_n[ow["stream"]])
                else:
                    res[w] = ("e", ow["eng"], None)
                    ow["sig"] = True
        for d, (k, key, val) in res.items():
            if k == "s":
                lw[key] = opi
        o["deps"] = res
        if stream is not None:
            self.stream_n[stream] = self.stream_n.get(stream, 0) + 1
            o["sval"] = 16 * self.stream_n[stream]
        return opi

    def dma(self, q, out, in_, reads, writes, stream, **kw):
        return self.op(q, lambda e: e.dma_start(out=out, in_=in_, **kw), reads, writes, stream=stream)

    def emit(self, final_streams):
        nc = self.nc
        counts = {e: 0 for e in self.ENG}
        for o in self.ops:
            if o["stream"] is None and o["sig"]:
                counts[o["eng"]] += 1
                o["sval"] = counts[o["eng"]]
        from contextlib import ExitStack
        with ExitStack() as es:
            esem = {e: es.enter_context(nc.semaphore("e_" + e)) for e in ["pe", "act", "dve", "pool"]}
            ssem = {s: es.enter_context(nc.semaphore("s_" + s)) for s in self.stream_n}
            block = es.enter_context(nc.Block())
            per = {e: [o for o in self.ops if o["eng"] == e] for e in self.ENG}

            def run(ename, e):
                known = {}
                for o in per[ename]:
                    for d, (k, key, val) in o["deps"].items():
                        if k == "s":
                            sem, v = ssem[key], val
                        else:
                            sem, v = esem[key], self.ops[d]["sval"]
                        kk = (k, key)
                        if known.get(kk, 0) >= v:
                            continue
                        known[kk] = v
                        e.wait_ge(sem, v)
                    ins = o["fn"](e)
                    if o["stream"] is not None:
                        ins.then_inc(ssem[o["stream"]], 16)
                    elif o["sig"]:
                        ins.then_inc(esem[ename], 1)
                if ename == "sp":
                    for s in final_streams:
                        if s in self.stream_n:
                            e.wait_ge(ssem[s], 16 * self.stream_n[s])

            @block.tensor
            def _(e):
                run("pe", e)

            @block.scalar
            def _(e):
                run("act", e)

            @block.vector
            def _(e):
                run("dve", e)

            @block.gpsimd
            def _(e):
                run("pool", e)

            @block.sync
            def _(e):
                run("sp", e)


def build(cfg=None):
    cfg = cfg or {}
    n_ptiles = cfg.get("n_ptiles", SEQ // NT)
    n_pseq = cfg.get("n_pseq", 4)
    n_sseq = cfg.get("n_sseq", 2)
    nc = bass.Bass("TRN2", target_bir_lowering=False)
    S = Sched(nc)

    def din(name, shape):
        return nc.dram_tensor(name, list(shape), F32, kind="ExternalInput").ap()

    def dout(name, shape):
        return nc.dram_tensor(name, list(shape), F32, kind="ExternalOutput").ap()

    I = {}
    I["x_prompt"] = din("x_prompt", [4, SEQ, D])
    I["x_sample"] = din("x_sample", [2, DSEQ, D])
    I["cache_k"] = din("cache_k", [2, 2, PAST, 512])
    I["cache_v"] = din("cache_v", [2, 2, PAST, 512])
    I["state_conv"] = din("state_conv", [2, 2, 128, 30])
    I["state_ssd"] = din("state_ssd", [2, 2, 2, 64, 512])
    I["state_s5_re"] = din("state_s5_re", [2, 2, 128, 16])
    I["state_s5_im"] = din("state_s5_im", [2, 2, 128, 16])
    I["meta_tokens"] = din("meta_tokens", [NMETA, D])
    for nm, sh in [("norm_mix", [2, D]), ("w_in", [2, D, INC]), ("conv_w", [2, 4, 1280]), ("conv_b", [2, 1280]),
                   ("dt_bias", [2, 16]), ("a_log", [2, 16]), ("d_ssd", [2, 16]), ("norm_ssd", [2, D]),
                   ("lam_re", [2, 32, 64]), ("lam_im", [2, 32, 64]), ("log_step", [2, 32]),
                   ("s5bc", [2, 4, 128, 2048]), ("d_s5", [2, 512]), ("w_glu", [2, 512, 2048]),
                   ("q_norm", [2, 64]), ("k_norm", [2, 64]), ("w_lift_a", [2, D, D]), ("w_lift_c", [2, 512, D]),
                   ("w_out", [2, D, D]), ("norm_ffn", [2, D]), ("w_up", [2, D, 4096]), ("w_down", [2, 4096, D])]:
        I[nm] = din(nm, sh)
    O = {}
    O["y_prompt"] = dout("y_prompt", [4, SEQ, D])
    O["y_sample"] = dout("y_sample", [2, DSEQ, D])
    O["k_prompt"] = dout("k_prompt", [2, 4, TP, 512])
    O["v_prompt"] = dout("v_prompt", [2, 4, TP, 512])
    O["conv_prompt"] = dout("conv_prompt", [2, 4, 128, 30])
    O["ssd_prompt"] = dout("ssd_prompt", [2, 4, 2, 64, 512])
    O["s5re_prompt"] = dout("s5re_prompt", [2, 4, 128, 16])
    O["s5im_prompt"] = dout("s5im_prompt", [2, 4, 128, 16])
    O["k_sample"] = dout("k_sample", [2, 2, DSEQ, 512])
    O["v_sample"] = dout("v_sample", [2, 2, DSEQ, 512])
    O["conv_sample"] = dout("conv_sample", [2, 2, 128, 30])
    O["ssd_sample"] = dout("ssd_sample", [2, 2, 2, 64, 512])
    O["s5re_sample"] = dout("s5re_sample", [2, 2, 128, 16])
    O["s5im_sample"] = dout("s5im_sample", [2, 2, 128, 16])
    kTs = nc.dram_tensor("kT_scr", [6, 2, 512, NKMAX], BF16, kind="Internal").ap()
    vhs = nc.dram_tensor("vh_scr", [6, 2, NKMAX, 512], BF16, kind="Internal").ap()

    PS = []
    for b in range(8):
        h = nc.alloc_psum_tensor(f"psb{b}", [128, 512], F32)
        PS.append(Tile(h, "ps", b * 2048, (b + 1) * 2048))

    def V(fn, reads, writes):
        return S.op("dve", fn, reads, writes)

    def A(fn, reads, writes):
        return S.op("act", fn, reads, writes)

    def G(fn, reads, writes):
        return S.op("pool", fn, reads, writes)

    def MM(out, lhsT, rhs, start, stop, reads, writes, **kw):
        return S.op("pe", lambda e: e.matmul(out, lhsT=lhsT, rhs=rhs, start=start, stop=stop, **kw), reads, writes)

    def TR(out, in_, ident, reads, writes):
        return S.op("pe", lambda e: e.matmul(out, lhsT=in_, rhs=ident, start=True, stop=True), reads, writes)

    def act(out, in_, func, reads, writes, bias=None, scale=None):
        kw = {}
        if bias is not None:
            kw["bias"] = bias
        if scale is not None:
            kw["scale"] = scale
        return A(lambda e: e.activation(out=out, in_=in_, func=func, **kw), reads, writes)

    def ldpar(out, in_, writes):
        return S.dma("sp", out, in_, [], writes, "par", allow_slow_non_contiguous=True)

    iota_pc = S.sb([128, 256], F32, "iota")
    ident_f = S.sb([128, 128], F32, "identf")
    ident_b = S.sb([128, 128], BF16, "identb")
    ones_b = S.sb([128, 128], BF16, "onesb")
    bd64_b = S.sb([128, 128], BF16, "bd64")
    tri_r = S.sb([128, 128], F32R, "tri")
    ones_r = S.sb([128, 128], F32R, "onesr")
    iota_t = S.sb([128, QS + 1], F32, "iotat")
    ones_f = S.sb([128, 128], F32, "onesf")
    masks = {}
    G(lambda e: e.iota(iota_pc[:, :], [[-1, 256]], base=0, channel_multiplier=1,
                       allow_small_or_imprecise_dtypes=True), [], [iota_pc])
    G(lambda e: e.iota(iota_t[:, :], [[1, QS + 1]], base=0, channel_multiplier=0,
                       allow_small_or_imprecise_dtypes=True), [], [iota_t])
    V(lambda e: e.tensor_single_scalar(out=ident_f[:, :], in_=iota_pc[:, 0:128], scalar=0.0, op=ALU.is_equal),
      [iota_pc], [ident_f])
    V(lambda e: e.tensor_copy(out=ident_b[:, :], in_=ident_f[:, :]), [ident_f], [ident_b])
    V(lambda e: e.memset(ones_b[:, :], 1.0), [], [ones_b])
    V(lambda e: e.memset(bd64_b[:, :], 0.0), [], [bd64_b])
    V(lambda e: e.memset(bd64_b[0:64, 0:64], 1.0), [], [bd64_b])
    V(lambda e: e.memset(bd64_b[64:128, 64:128], 1.0), [], [bd64_b])
    V(lambda e: e.tensor_scalar(out=tri_r[:, :], in0=iota_pc[:, 0:128], scalar1=0.0, scalar2=-8.0,
                                op0=ALU.is_ge, op1=ALU.mult), [iota_pc], [tri_r])
    V(lambda e: e.memset(ones_f[:, :], 1.0), [], [ones_f])
    V(lambda e: e.tensor_copy(out=ones_r[:, :], in_=ones_f[:, :]), [ones_f], [ones_r])
    for off in (16, -112, -240, 0):
        m = S.sb([128, 256], F32, "mask")
        V(lambda e, m=m, off=off: e.tensor_single_scalar(out=m[:, :], in_=iota_pc[:, :], scalar=float(off),
                                                         op=ALU.is_lt), [iota_pc], [m])
        masks[off] = m

    P = []
    stage = cfg.get('stage', 99)
    I["par"] = din("par", [2, 128, NPAR])
    for l in range(2):
        pt = S.sb([128, NPAR], F32, "par")
        S.dma("sp", pt[:, :], I["par"][l], [], [pt], "par")
        p = {"_t": pt}
        for nm, (c0, w) in PCOL.items():
            p[nm] = (pt, c0, w)
        P.append(p)

    def pc(l, nm, j=0, rows=slice(0, 128)):
        pt, c0, w = P[l][nm]
        return pt[rows, c0 + j:c0 + j + 1]

    def pv(l, nm):
        pt, c0, w = P[l][nm]
        return pt[:, c0:c0 + w]

    TWO_PI = 2.0 * math.pi
    MAGIC = 12582912.0
    S5TAB = []
    for l in range(2):
        S5TAB.append((S.sb([128, 16, QS + 1], F32, "cosT"), S.sb([128, 16, QS + 1], F32, "sinT"),
                      S.sb([128, 16, QS], F32, "tbr"), S.sb([128, 16, QS], F32, "tbi"),
                      S.sb([128, 16, QS], F32, "magT")))
    S5L = [(S.sb([128, 16], F32, "lre"), S.sb([128, 16], F32, "lim"), S.sb([128, 16], F32, "lst")) for l in range(2)]
    ARENA0 = S.mark()
    S5SCR = [S.sb([128, 16], F32, "s5s") for _ in range(8)] + [S.sb([128, 16, QS + 1], F32, "s5w") for _ in range(3)]
    for l in range(2 if stage >= 2 else 0):
        p = P[l]
        lre, lim, lst = S5L[l]
        for dst, nm in ((lre, "lre"), (lim, "lim"), (lst, "lst")):
            V(lambda e, dst=dst, nm=nm, l=l: e.tensor_copy(out=dst[:, :], in_=pv(l, nm)), [P[l]["_t"]], [dst])
        step, th, mag, t0_, t1_, t2_, fre, fim, ang, w1, w2 = S5SCR
        cosT, sinT, tbr, tbi, magT = S5TAB[l]
        p.update(cosT=cosT, sinT=sinT, tbr=tbr, tbi=tbi, magT=magT, mag=mag)
        act(step[:, :], lst[:, :], AF.Exp, [lst], [step])
        V(lambda e, th=th, lim=lim, step=step: e.tensor_tensor(out=th[:, :], in0=lim[:, :], in1=step[:, :], op=ALU.mult),
          [lim, step], [th])
        V(lambda e, t0_=t0_, lre=lre, step=step: e.tensor_tensor(out=t0_[:, :], in0=lre[:, :], in1=step[:, :], op=ALU.mult),
          [lre, step], [t0_])
        act(mag[:, :], t0_[:, :], AF.Exp, [t0_], [mag])
        V(lambda e, ang=ang, th=th: e.tensor_tensor(
            out=ang[:, :, :], in0=iota_t[:, :].unsqueeze(1).broadcast_to([128, 16, QS + 1]),
            in1=th[:, :].unsqueeze(2).broadcast_to([128, 16, QS + 1]), op=ALU.mult), [iota_t, th], [ang])
        for which, outT in (("sin", sinT), ("cos", cosT)):
            addc = 0.0 if which == "sin" else 0.25
            V(lambda e, ang=ang, w1=w1, addc=addc: e.tensor_scalar(
                out=w1[:, :, :], in0=ang[:, :, :], scalar1=1.0 / TWO_PI, scalar2=addc, op0=ALU.mult, op1=ALU.add),
              [ang], [w1])
            V(lambda e, w1=w1, w2=w2: e.tensor_scalar(out=w2[:, :, :], in0=w1[:, :, :], scalar1=MAGIC, scalar2=None,
                                                     op0=ALU.add), [w1], [w2])
            V(lambda e, w2=w2: e.tensor_scalar(out=w2[:, :, :], in0=w2[:, :, :], scalar1=-MAGIC, scalar2=None,
                                               op0=ALU.add), [w2], [w2])
            V(lambda e, w1=w1, w2=w2: e.tensor_tensor(out=w1[:, :, :], in0=w1[:, :, :], in1=w2[:, :, :],
                                                     op=ALU.subtract), [w1, w2], [w1])
            V(lambda e, w1=w1: e.tensor_scalar(out=w1[:, :, :], in0=w1[:, :, :], scalar1=-0.4999, scalar2=0.4999,
                                               op0=ALU.max, op1=ALU.min), [w1], [w1])
            act(outT[:, :, :], w1[:, :, :], AF.Sin, [w1], [outT], scale=TWO_PI)
        abr, abi = t1_, t2_
        V(lambda e, abr=abr, cosT=cosT, mag=mag: e.tensor_tensor(out=abr[:, :], in0=cosT[:, :, 1], in1=mag[:, :], op=ALU.mult),
          [cosT, mag], [abr])
        V(lambda e, abi=abi, sinT=sinT, mag=mag: e.tensor_tensor(out=abi[:, :], in0=sinT[:, :, 1], in1=mag[:, :], op=ALU.mult),
          [sinT, mag], [abi])
        V(lambda e, abr=abr: e.tensor_scalar(out=abr[:, :], in0=abr[:, :], scalar1=-1.0, scalar2=None, op0=ALU.add),
          [abr], [abr])
        den = step
        V(lambda e, den=den, lre=lre: e.tensor_tensor(out=den[:, :], in0=lre[:, :], in1=lre[:, :], op=ALU.mult), [lre], [den])
        V(lambda e, t0_=t0_, lim=lim: e.tensor_tensor(out=t0_[:, :], in0=lim[:, :], in1=lim[:, :], op=ALU.mult), [lim], [t0_])
        V(lambda e, den=den, t0_=t0_: e.tensor_tensor(out=den[:, :], in0=den[:, :], in1=t0_[:, :], op=ALU.add), [den, t0_], [den])
        V(lambda e, den=den: e.reciprocal(out=den[:, :], in_=den[:, :]), [den], [den])
        V(lambda e, fre=fre, abr=abr, lre=lre: e.tensor_tensor(out=fre[:, :], in0=abr[:, :], in1=lre[:, :], op=ALU.mult), [abr, lre], [fre])
        V(lambda e, t0_=t0_, abi=abi, lim=lim: e.tensor_tensor(out=t0_[:, :], in0=abi[:, :], in1=lim[:, :], op=ALU.mult), [abi, lim], [t0_])
        V(lambda e, fre=fre, t0_=t0_: e.tensor_tensor(out=fre[:, :], in0=fre[:, :], in1=t0_[:, :], op=ALU.add), [fre, t0_], [fre])
        V(lambda e, fre=fre, den=den: e.tensor_tensor(out=fre[:, :], in0=fre[:, :], in1=den[:, :], op=ALU.mult), [fre, den], [fre])
        V(lambda e, fim=fim, abi=abi, lre=lre: e.tensor_tensor(out=fim[:, :], in0=abi[:, :], in1=lre[:, :], op=ALU.mult), [abi, lre], [fim])
        V(lambda e, t0_=t0_, abr=abr, lim=lim: e.tensor_tensor(out=t0_[:, :], in0=abr[:, :], in1=lim[:, :], op=ALU.mult), [abr, lim], [t0_])
        V(lambda e, fim=fim, t0_=t0_: e.tensor_tensor(out=fim[:, :], in0=fim[:, :], in1=t0_[:, :], op=ALU.subtract), [fim, t0_], [fim])
        V(lambda e, fim=fim, den=den: e.tensor_tensor(out=fim[:, :], in0=fim[:, :], in1=den[:, :], op=ALU.mult), [fim, den], [fim])
        frb = lambda f: f[:, :].unsqueeze(2).broadcast_to([128, 16, QS])
        V(lambda e, w1=w1, cosT=cosT, fre=fre: e.tensor_tensor(out=w1[:, :, 0:QS], in0=cosT[:, :, 0:QS], in1=frb(fre), op=ALU.mult), [cosT, fre], [w1])
        V(lambda e, w2=w2, sinT=sinT, fim=fim: e.tensor_tensor(out=w2[:, :, 0:QS], in0=sinT[:, :, 0:QS], in1=frb(fim), op=ALU.mult), [sinT, fim], [w2])
        V(lambda e, tbr=tbr, w1=w1, w2=w2: e.tensor_tensor(out=tbr[:, :, :], in0=w1[:, :, 0:QS], in1=w2[:, :, 0:QS], op=ALU.add), [w1, w2], [tbr])
        V(lambda e, w1=w1, cosT=cosT, fim=fim: e.tensor_tensor(out=w1[:, :, 0:QS], in0=cosT[:, :, 0:QS], in1=frb(fim), op=ALU.mult), [cosT, fim], [w1])
        V(lambda e, w2=w2, sinT=sinT, fre=fre: e.tensor_tensor(out=w2[:, :, 0:QS], in0=sinT[:, :, 0:QS], in1=frb(fre), op=ALU.mult), [sinT, fre], [w2])
        V(lambda e, tbi=tbi, w1=w1, w2=w2: e.tensor_tensor(out=tbi[:, :, :], in0=w1[:, :, 0:QS], in1=w2[:, :, 0:QS], op=ALU.subtract), [w1, w2], [tbi])
        V(lambda e, magT=magT, mag=mag: e.tensor_tensor(
            out=magT[:, :, :], in0=ones_f[:, 0:QS].unsqueeze(1).broadcast_to([128, 16, QS]),
            in1=mag[:, :].unsqueeze(2).broadcast_to([128, 16, QS]), op=ALU.mult), [ones_f, mag], [magT])
    OQ = cfg.get("oq", "pool")
    WG = {}
    upto = cfg.get("upto", 99)
    epsT = S.sb([128, 1], F32, "eps")
    V(lambda e: e.memset(epsT[:, :], EPS), [], [epsT])
    oneT = S.sb([128, 1], F32, "one")
    V(lambda e: e.memset(oneT[:, :], 1.0), [], [oneT])
    WDT = []
    for l in range(2):
        w_ = S.sb([128, 8, 48], BF16, "WDT")
        V(lambda e, w_=w_: e.memset(w_[:, :, :], 0.0), [], [w_])
        for c0 in (0, 32):
            S.dma("pool", w_[:, :, c0:c0 + 16], I["w_in"][l][:, OFF_DT:OFF_DT + 16].rearrange("(kc p) n -> p kc n", p=128),
                  [], [w_], "wdt")
        WDT.append(w_)

    def tt(out, in0, in1, op, reads, writes, eng="dve"):
        return S.op(eng, lambda e: e.tensor_tensor(out=out, in0=in0, in1=in1, op=op), reads, writes)

    def ts(out, in0, s1, s2, op0, op1, reads, writes, eng="dve"):
        if op1 is None:
            return S.op(eng, lambda e: e.tensor_scalar(out=out, in0=in0, scalar1=s1, scalar2=None, op0=op0), reads, writes)
        return S.op(eng, lambda e: e.tensor_scalar(out=out, in0=in0, scalar1=s1, scalar2=s2, op0=op0, op1=op1), reads, writes)

    def stt(out, in0, scalar, in1, op0, op1, reads, writes):
        return V(lambda e: e.scalar_tensor_tensor(out=out, in0=in0, scalar=scalar, in1=in1, op0=op0, op1=op1), reads, writes)

    def cp(out, in_, reads, writes, eng="dve"):
        return S.op(eng, lambda e: e.tensor_copy(out=out, in_=in_), reads, writes)

    def scan(out, d0, d1, init, reads, writes):
        return V(lambda e: e.tensor_tensor_scan(out=out, data0=d0, data1=d1, initial=init, op0=ALU.mult, op1=ALU.add), reads, writes)

    NWS = cfg.get('nws', 4)
    WSL = [S.sb([128, 4096], BF16, "wslot") for _ in range(NWS)]
    wctr = [0]

    def load_w(src_ap, kc, ncols, prt=128):
        key = (src_ap.tensor.name, int(src_ap.offset), kc, ncols, prt)
        if key not in WG:
            gi = len(WG)
            scr = nc.dram_tensor("wg%d" % gi, [prt, kc * ncols], BF16, kind="Internal").ap()
            S.dma("pool", scr.rearrange("p (kc n) -> p kc n", n=ncols), src_ap.rearrange("(kc p) n -> p kc n", p=prt),
                  [], [("wg", gi, gi + 1)], "wcast")
            WG[key] = (gi, scr)
        gi, scr = WG[key]
        i = wctr[0] % NWS
        wctr[0] += 1
        wt = WSL[i]
        view = wt[0:prt, 0:kc * ncols].rearrange("p (kc n) -> p kc n", n=ncols)
        S.dma("sp", wt[0:prt, 0:kc * ncols], scr[:, :], [("wg", gi, gi + 1)], [wt.r(0, kc * ncols * 2)], f"w{i}")
        return wt, view

    ones48 = S.sb([48, 64], F32, "ones48")
    V(lambda e: e.memset(ones48[:, :], 1.0), [], [ones48])
    LT = []
    for i in range(2):
        t = S.sb([128, 128], F32, "LT")
        V(lambda e, t=t: e.memset(t[:, :], 0.0), [], [t])
        V(lambda e, t=t: e.memset(t[0:16, :], 1.0), [], [t])
        cp(t[64:128, 0:64], ident_f[64:128, 64:128], [ident_f], [t])
        LT.append(t)
    RF = []
    for g in range(2):
        t = S.sb([128, 8, 64], F32, "RF")
        V(lambda e, t=t: e.memset(t[:, :, :], 0.0), [], [t])
        for h in range(8):
            if True:
                r = 8 * g + h
                pass
        RF.append(t)
    DEL = []
    for g in range(2):
        d_ = S.sb([48, 8], F32, "DEL")
        io = S.sb([48, 8], F32, "DELi")
        G(lambda e, io=io: e.iota(io[:, :], [[-1, 8]], base=0, channel_multiplier=1, allow_small_or_imprecise_dtypes=True), [], [io])
        ts(d_[0:32, :], io[0:32, :], float(8 * g), None, ALU.is_equal, None, [io], [d_])
        ts(d_[32:48, :], io[32:48, :], float(32 + 8 * g), None, ALU.is_equal, None, [io], [d_])
        DEL.append(d_)
        cp(RF[g][32:48, :, :], d_[32:48, :].unsqueeze(2).broadcast_to([16, 8, 64]), [d_], [RF[g]])
        ts(RF[g][64:128, :, :], iota_pc[64:128, 0:64].unsqueeze(1).broadcast_to([64, 8, 64]), 64.0, NEG, ALU.is_gt, ALU.mult,
           [iota_pc], [RF[g]])
    SHIFT = S.sb([128, 128], BF16, "shift")
    V(lambda e: e.memset(SHIFT[:, :], 0.0), [], [SHIFT])
    cp(SHIFT[0:64, 64:128], ident_b[0:64, 0:64], [ident_b], [SHIFT])
    cp(SHIFT[64:128, 64:128], ident_b[64:128, 64:128], [ident_b], [SHIFT])
    BTOK = S.sb([64, 128], BF16, "btok")
    V(lambda e: e.memset(BTOK[:, :], 0.0), [], [BTOK])
    ST_, XS_, CH_, XR_, XI_, STm, CHm, XRm, XIm = [], [], [], [], [], [], [], [], []
    for l in range(2):
        ST_.append(S.sb([128, 2, 512], F32, "ST"))
        XS_.append(S.sb([128, 2, 4, 256], BF16, "XS"))
        CH_.append(S.sb([128, 10, 3], F32, "CH"))
        XR_.append(S.sb([128, 16], F32, "XR"))
        XI_.append(S.sb([128, 16], F32, "XI"))
        STm.append(S.sb([128, 2, 512], F32, "STm"))
        CHm.append(S.sb([128, 10, 3], F32, "CHm"))
        XRm.append(S.sb([128, 16], F32, "XRm"))
        XIm.append(S.sb([128, 16], F32, "XIm"))
    acol = []
    for l in range(2):
        a_ = S.sb([48, 1], F32, "acol")
        act(a_[:, :], pc(l, "alog", 0, slice(0, 48)), AF.Exp, [P[l]["_t"]], [a_])
        ts(a_[0:32, :], a_[0:32, :], -1.0, None, ALU.mult, None, [a_], [a_])
        acol.append(a_)

    def st_to_xs(l):
        for g in range(2):
            src = ST_[l][64:128, g, :].rearrange("p (pr eo d) -> p pr eo d", pr=4, eo=2)
            dst = XS_[l][64:128, g, :, :].rearrange("p pr (blk d) -> p pr blk d", d=64)[:, :, 1::2, :]
            cp(dst, src, [ST_[l].c(g)], [XS_[l].c(g)])

    xT = S.sb([128, 8, NT], F32, "xT")
    mixT = S.sb([128, 8, NT], F32, "mixT")
    hT = S.sb([128, 8, NT], BF16, "hT")
    ARENA = S.mark()
    seqs = [dict(kind="meta", b=0, T=NMETA, slot=0)]
    for b in range(n_pseq):
        seqs.append(dict(kind="prompt", b=b, T=n_ptiles * NT, slot=b))
    for b in range(n_sseq):
        seqs.append(dict(kind="sample", b=b, T=DSEQ, slot=4 + b))
    if stage < 3:
        seqs = []
    only = cfg.get('only', ['meta', 'prompt', 'sample'])
    seqs = [q for q in seqs if q['kind'] in only]

    def rmsnorm_to(dst_bf, src_f32, ncol_name, l, N):
        mk = S.mark()
        sqb = S.sb([128, 8, N], BF16, "sqb")
        rstd = S.sb([128, N], F32, "rstd")
        for c in range(8):
            act(sqb[:, c, :], src_f32[:, c, 0:N], AF.Square, [src_f32.c(c)], [sqb.c(c)])
        bank = PS[4]
        for c in range(8):
            MM(bank[:, 0:N], ones_b[:, :], sqb[:, c, :], c == 0, c == 7, [ones_b, sqb.c(c)], [bank.r(0, 4 * N)])
        act(rstd[:, :], bank[:, 0:N], AF.Sqrt, [bank.r(0, 4 * N), epsT], [rstd], bias=epsT[:, 0:1], scale=1.0 / D)
        V(lambda e, o_=rstd[:, :]: e.reciprocal(out=o_, in_=o_), [rstd], [rstd])
        for c in range(8):
            stt(dst_bf[:, c, 0:N], src_f32[:, c, 0:N], pc(l, ncol_name, c), rstd[:, :], ALU.mult, ALU.mult,
                [src_f32.c(c), P[l]["_t"], rstd], [dst_bf.c(c)])
        S.reset(mk)

    def proj_fm(l, wsrc, kc, c0, ncols, rhs_fn, rhs_reads, N, evac, prt=128, bankset=(0, 1)):
        mi = 0
        for g0 in range(0, ncols, 512):
            gw = min(512, ncols - g0)
            wt, view = load_w(wsrc[:, c0 + g0:c0 + g0 + gw], kc, gw, prt)
            for m0 in range(0, gw, 128):
                mw = min(128, gw - m0)
                bank = PS[bankset[mi % len(bankset)]]
                for k in range(kc):
                    MM(bank[0:mw, 0:N], view[:, k, m0:m0 + mw], rhs_fn(k), k == 0, k == kc - 1,
                       [wt] + rhs_reads(k), [bank.r(0, 4 * N)])
                evac(mi, bank, mw)
                mi += 1

    for sq in seqs:
        kind, b, slot = sq["kind"], sq["b"], sq["slot"]
        tiles = [(t0, min(NT, sq["T"] - t0)) for t0 in range(0, sq["T"], NT)]
        for l in range(2):
            if kind == "prompt" and upto < 3:
                continue
            if kind == "meta":
                for t in (ST_[l], CH_[l], XR_[l], XI_[l], XS_[l]):
                    V(lambda e, a_=t.h[:]: e.memset(a_, 0.0), [], [t])
            elif kind == "prompt":
                for dst, src in ((ST_[l], STm[l]), (CH_[l], CHm[l]), (XR_[l], XRm[l]), (XI_[l], XIm[l])):
                    cp(dst.h[:], src.h[:], [src], [dst])
                st_to_xs(l)
            else:
                S.dma(OQ, CH_[l][:, :, :].rearrange("p c k -> p (c k)"), I["state_conv"][l, b], [], [CH_[l]], "stin")
                for g in range(2):
                    S.dma(OQ, ST_[l][64:128, g, :], I["state_ssd"][l, b, g], [], [ST_[l].c(g)], "stin")
                S.dma(OQ, XR_[l][:, :], I["state_s5_re"][l, b], [], [XR_[l]], "stin")
                S.dma(OQ, XI_[l][:, :], I["state_s5_im"][l, b], [], [XI_[l]], "stin")
                st_to_xs(l)
                S.reset(ARENA)
                S.dma("pool", vhs[slot, l, 0:PAST, :], I["cache_v"][l, b], [], [("vh%d_%d" % (slot, l), 0, PAST)], "vpre")
                CK = [S.sb([128, 512], F32, "CK") for _ in range(2)]
                KTP = [S.sb([128, 4, 128], BF16, "KTP") for _ in range(2)]
                for tb in range(PAST // 128):
                    ck, kt = CK[tb % 2], KTP[tb % 2]
                    S.dma(OQ, ck[:, :], I["cache_k"][l, b, 128 * tb:128 * tb + 128, :], [], [ck], "ckl%d" % (tb % 2))
                    bank = PS[7]
                    for m in range(4):
                        TR(bank[:, 128 * m:128 * m + 128], ck[:, 128 * m:128 * m + 128], ident_f[:, :], [ck, ident_f],
                           [bank.r(512 * m, 512 * m + 512)])
                    act(kt[:, :, :], bank[:, :].rearrange("p (m t) -> p m t", t=128), AF.Copy, [bank], [kt])
                    S.dma(OQ, kTs[slot, l].rearrange("(m p) t -> p m t", p=128)[:, :, 128 * tb:128 * tb + 128], kt[:, :, :],
                          [kt], [("kT%d_%d" % (slot, l), 128 * tb, 128 * tb + 128)], "kpre%d" % (tb % 2))
        for (t0, N) in tiles:
            S.reset(ARENA)
            pos0 = {"meta": 0, "prompt": NMETA + t0, "sample": PAST}[kind]
            nbk = (N + 127) // 128
            blk = [(tb, min(128, N - 128 * tb)) for tb in range(nbk)]
            if kind == "meta":
                xsrc = I["meta_tokens"]
            elif kind == "prompt":
                xsrc = I["x_prompt"][b, t0:t0 + N, :]
            else:
                xsrc = I["x_sample"][b, t0:t0 + N, :]
            mk0 = S.mark()
            x_tm = S.sb([128, nbk, D], F32, "x_tm")
            for tb, nt in blk:
                S.dma(OQ, x_tm[0:nt, tb, :], xsrc[128 * tb:128 * tb + nt, :], [], [x_tm.c(tb)], "xin")
            for c in range(8):
                bank = PS[c % 4]
                for tb, nt in blk:
                    TR(bank[:, 128 * tb:128 * tb + nt], x_tm[0:nt, tb, 128 * c:128 * c + 128], ident_f[0:nt, 0:nt],
                       [x_tm.c(tb), ident_f], [bank.r(512 * tb, 512 * tb + 4 * nt)])
                act(xT[:, c, 0:N], bank[:, 0:N], AF.Copy, [bank.r(0, 4 * N)], [xT.c(c)])
            S.reset(mk0)
            for l in range(2):
                S.reset(ARENA)
                rmsnorm_to(hT, xT, "nmix", l, N)
                hrd = lambda k: [hT.c(k)]
                hrhs = lambda k, N=N: hT[:, k, 0:N]
                if kind == "meta":
                    kdst = [O["k_prompt"][l, bb, 0:NMETA, :] for bb in range(n_pseq)]
                    vdst = [O["v_prompt"][l, bb, 0:NMETA, :] for bb in range(n_pseq)]
                    slots = list(range(n_pseq))
                elif kind == "prompt":
                    kdst = [O["k_prompt"][l, b, NMETA + t0:NMETA + t0 + N, :]]
                    vdst = [O["v_prompt"][l, b, NMETA + t0:NMETA + t0 + N, :]]
                    slots = [slot]
                else:
                    kdst = [O["k_sample"][l, b, t0:t0 + N, :]]
                    vdst = [O["v_sample"][l, b, t0:t0 + N, :]]
                    slots = [slot]
                mkA = S.mark()
                knT = S.sb([128, 4, N], F32, "knT")
                knb = S.sb([128, 4, N], BF16, "knb")
                qnb = S.sb([128, 4, N], BF16, "qnb")
                ksq = S.sb([128, N], BF16, "ksq")
                rk = S.sb([128, N], F32, "rk")

                def qk_evac(dst32, dstbf, gname):
                    def ev(m, bank, mw):
                        kd = cfg.get("kd", 99)
                        if kd < 1:
                            return
                        act(ksq[:, :], bank[:, 0:N], AF.Square, [bank.r(0, 4 * N)], [ksq])
                        if kd < 2:
                            return
                        b2 = PS[2 + m % 2]
                        MM(b2[:, 0:N], bd64_b[:, :], ksq[:, :], True, True, [bd64_b, ksq], [b2.r(0, 4 * N)])
                        if kd < 3:
                            return
                        act(rk[:, :], b2[:, 0:N], AF.Sqrt, [b2.r(0, 4 * N), epsT], [rk], bias=epsT[:, 0:1], scale=1.0 / 64)
                        if kd < 4:
                            return
                        V(lambda e, o_=rk[:, :]: e.reciprocal(out=o_, in_=o_), [rk], [rk])
                        if kd < 5:
                            return
                        if dst32 is not None:
                            stt(dst32[:, m, :], bank[:, 0:N], pc(l, gname), rk[:, :], ALU.mult, ALU.mult,
                                [bank.r(0, 4 * N), P[l]["_t"], rk], [dst32.c(m)])
                            cp(dstbf[:, m, :], dst32[:, m, :], [dst32.c(m)], [dstbf.c(m)], eng="pool")
                        else:
                            stt(dstbf[:, m, :], bank[:, 0:N], pc(l, gname), rk[:, :], ALU.mult, ALU.mult,
                                [bank.r(0, 4 * N), P[l]["_t"], rk], [dstbf.c(m)])
                    return ev
                proj_fm(l, I["w_in"][l], 8, OFF_K, 512, hrhs, hrd, N, qk_evac(knT, knb, "kn"))
                proj_fm(l, I["w_in"][l], 8, OFF_Q, 512, hrhs, hrd, N, qk_evac(None, qnb, "qn"))
                if cfg.get("kd", 99) < 7:
                    continue
                for sl in slots:
                    S.dma(OQ, kTs[sl, l].rearrange("(m p) t -> p m t", p=128)[:, :, pos0:pos0 + N], knb[:, :, :],
                          [knb], [("kT%d_%d" % (sl, l), pos0, pos0 + N)], "ktw")
                if cfg.get("kd", 99) < 8:
                    continue
                k_tm = S.sb([128, nbk, 512], F32, "k_tm")
                for tb, nt in blk:
                    bank = PS[5]
                    for m in range(4):
                        TR(bank[0:nt, 128 * m:128 * m + 128], knT[:, m, 128 * tb:128 * tb + nt], ident_f[:, :],
                           [knT.c(m), ident_f], [bank.r(512 * m, 512 * m + 512)])
                    act(k_tm[0:nt, tb, :], bank[0:nt, :], AF.Copy, [bank], [k_tm.c(tb)])
                    for dst in kdst:
                        S.dma(OQ, dst[128 * tb:128 * tb + nt, :], k_tm[0:nt, tb, :], [k_tm.c(tb)], [], "kout")
                if cfg.get("kd", 99) < 9:
                    continue
                wt, wv = load_w(I["w_in"][l][:, OFF_V:OFF_V + 512], 8, 512)
                v_tm = S.sb([128, nbk, 512], F32, "v_tm")
                v_bf = S.sb([128, nbk, 512], BF16, "v_bf")
                for tb, nt in blk:
                    bank = PS[6]
                    for kc in range(8):
                        MM(bank[0:nt, :], hT[:, kc, 128 * tb:128 * tb + nt], wv[:, kc, :], kc == 0, kc == 7,
                           [wt, hT.c(kc)], [bank])
                    act(v_tm[0:nt, tb, :], bank[0:nt, :], AF.Copy, [bank], [v_tm.c(tb)])
                    for dst in vdst:
                        S.dma(OQ, dst[128 * tb:128 * tb + nt, :], v_tm[0:nt, tb, :], [v_tm.c(tb)], [], "vout")
                    if cfg.get("kd", 99) < 10:
                        continue
                    cp(v_bf[0:nt, tb, :], v_tm[0:nt, tb, :], [v_tm.c(tb)], [v_bf.c(tb)], eng="pool")
                    if cfg.get("vh", 2) < 2:
                        continue
                    for sl in slots:
                        S.dma(OQ, vhs[sl, l, pos0 + 128 * tb:pos0 + 128 * tb + nt, :], v_bf[0:nt, tb, :],
                              [v_bf.c(tb)], [("vh%d_%d" % (sl, l), pos0 + 128 * tb, pos0 + 128 * tb + nt)], "vhw")
                if upto < 2:
                    continue
                def gates(i):
                    proj_fm(l, I["w_in"][l], 8, OFF_G + 1024 * i, 1024, hrhs, hrd, N,
                            lambda m, bank, mw: act(GT[:, m, :], bank[:, 0:N], AF.Sigmoid, [bank.r(0, 4 * N)], [GT.c(m)]))

                def mix_evac(first):
                    def ev(m, bank, mw):
                        if first:
                            tt(mixT[:, m, 0:N], GT[:, m, :], bank[:, 0:N], ALU.mult, [GT.c(m), bank.r(0, 4 * N)], [mixT.c(m)])
                        else:
                            tt(tmpA[:, :], GT[:, m, :], bank[:, 0:N], ALU.mult, [GT.c(m), bank.r(0, 4 * N)], [tmpA])
                            tt(mixT[:, m, 0:N], mixT[:, m, 0:N], tmpA[:, :], ALU.add, [mixT.c(m), tmpA], [mixT.c(m)])
                    return ev
                nk = pos0 + N
                nb = (nk + 127) // 128
                slot_r = slots[0]
                OC = S.sb([64, 8, N], BF16, "OC")
                mkC = S.mark()
                RS = S.sb([128, N], F32, "RS")
                EXs = [S.sb([128, N], F32, "EX") for _ in range(2)]
                ARGs = [S.sb([128, N], F32, "ARG") for _ in range(2)]
                LPs = [S.sb([128, N], F32, "LP") for _ in range(2)]
                WTs = [S.sb([128, N], BF16, "WT") for _ in range(2)]
                KTb = [S.sb([128, NKMAX], BF16, "KT") for _ in range(2)]
                VBb = [S.sb([128, 17, 128], BF16, "VB") for _ in range(2)]
                nfull, rem = nk // 128, nk % 128
                bctr = 0
                for pr in range(4):
                    KT, VB = KTb[pr % 2], VBb[pr % 2]
                    kname, vname = "kT%d_%d" % (slot_r, l), "vh%d_%d" % (slot_r, l)
                    S.dma(OQ, KT[:, 0:nk], kTs[slot_r, l, 128 * pr:128 * pr + 128, 0:nk], [(kname, 0, nk)], [KT], "ktl%d" % (pr % 2))
                    if nfull:
                        S.dma(OQ, VB[:, 0:nfull, :],
                              vhs[slot_r, l, 0:128 * nfull, 128 * pr:128 * pr + 128].rearrange("(b p) c -> p b c", p=128),
                              [(vname, 0, 128 * nfull)], [VB.r(0, nfull * 256)], "vbl%d" % (pr % 2))
                    if rem:
                        S.dma(OQ, VB[0:rem, nfull, :], vhs[slot_r, l, 128 * nfull:nk, 128 * pr:128 * pr + 128],
                              [(vname, 128 * nfull, nk)], [VB.r(nfull * 256, nfull * 256 + 256)], "vbl%d" % (pr % 2))
                    for hh in range(2):
                        h = 2 * pr + hh
                        R = slice(64 * hh, 64 * hh + 64)
                        V(lambda e, a_=RS[:, :]: e.memset(a_, 0.0), [], [RS])
                        PO = PS[4]
                        blocks = list(reversed(range(nb)))

                        def stage1(bI, i):
                            kb = min(128, nk - 128 * bI)
                            PZ, P2 = PS[i % 2], PS[2 + i % 2]
                            LP, EXb = LPs[i % 2], EXs[i % 2]
                            MM(PZ[0:kb, 0:N], KT[R, 128 * bI:128 * bI + kb], qnb[R, pr, 0:N], True, True,
                               [KT, qnb.c(pr)], [PZ.r(0, 4 * N)])
                            act(EXb[0:kb, :], PZ[0:kb, 0:N], AF.Exp, [PZ.r(0, 4 * N)], [EXb], scale=0.125)
                            act(LP[0:kb, :].bitcast(F32R), EXb[0:kb, :], AF.Ln, [EXb, oneT], [LP], bias=oneT[0:kb, 0:1])
                            if 128 * bI + kb - 1 >= pos0:
                                mt = masks[pos0 - 128 * bI]
                                tt(LP[0:kb, :].bitcast(F32R), LP[0:kb, :], mt[0:kb, 0:N], ALU.mult, [LP, mt], [LP])
                            MM(PZ[0:kb, 0:N], tri_r[0:kb, 0:kb], LP[0:kb, :].bitcast(F32R), False, True,
                               [tri_r, LP], [PZ.r(0, 4 * N)], skip_group_check=True)
                            if bI > 0:
                                MM(P2[:, 0:N], ones_r[0:kb, :], LP[0:kb, :].bitcast(F32R), True, True, [ones_r, LP], [P2.r(0, 4 * N)])

                        def stage2(bI, i):
                            kb = min(128, nk - 128 * bI)
                            PZ, P2 = PS[i % 2], PS[2 + i % 2]
                            WT, AG = WTs[i % 2], ARGs[i % 2]
                            stt(AG[0:kb, :], PZ[0:kb, 0:N], 0.125, RS[0:kb, :], ALU.mult, ALU.subtract,
                                [PZ.r(0, 4 * N), RS], [AG])
                            if bI > 0:
                                tt(RS[:, :], RS[:, :], P2[:, 0:N], ALU.add, [RS, P2.r(0, 4 * N)], [RS])
                            act(WT[0:kb, :], AG[0:kb, :], AF.Exp, [AG], [WT])
                            if 128 * bI + kb - 1 >= pos0:
                                mt = masks[pos0 - 128 * bI]
                                tt(WT[0:kb, :], WT[0:kb, :], mt[0:kb, 0:N], ALU.mult, [WT, mt], [WT])
                            MM(PO[0:64, 0:N], VB[0:kb, bI, 64 * hh:64 * hh + 64], WT[0:kb, :], i == 0, bI == 0,
                               [VB.r(bI * 256, bI * 256 + 256), WT], [PO.r(0, 4 * N)])
                        for i, bI in enumerate(blocks):
                            stage1(bI, i)
                            if i >= 1:
                                stage2(blocks[i - 1], i - 1)
                        stage2(blocks[-1], len(blocks) - 1)
                        act(OC[:, h, :], PO[0:64, 0:N], AF.Copy, [PO.r(0, 4 * N)], [OC.c(h)])
                S.reset(mkC)
                GT = S.sb([128, 8, N], F32, "GT")
                tmpA = S.sb([128, N], F32, "tmpA")
                gates(2)
                proj_fm(l, I["w_lift_c"][l], 8, 0, 1024, lambda h_: OC[:, h_, :], lambda h_: [OC.c(h_)], N, mix_evac(True), prt=64)
                if upto < 3:
                    continue
                S.reset(mkA)
                GT = S.sb([128, 8, N], F32, "GT")
                tmpA = S.sb([128, N], F32, "tmpA")
                Q = min(64, N)
                nch = N // Q
                xp = S.sb([128, 10, N + 3], F32, "xp")
                XBC = S.sb([128, 10, N], BF16, "XBC")
                yT = S.sb([128, 8, N], F32, "yT")
                cp(xp[:, :, 0:3], CH_[l][:, :, :], [CH_[l]], [xp])
                proj_fm(l, I["w_in"][l], 8, OFF_XBC, 1280, hrhs, hrd, N,
                        lambda m, bank, mw: act(xp[:, m, 3:3 + N], bank[:, 0:N], AF.Copy, [bank.r(0, 4 * N)], [xp.c(m)]))
                cp(CH_[l][:, :, :], xp[:, :, N:N + 3], [xp], [CH_[l]])
                for c in range(10):
                    ts(tmpA[:, :], xp[:, c, 0:N], pc(l, "cw", c), pc(l, "cb", c), ALU.mult, ALU.add, [xp.c(c), P[l]["_t"]], [tmpA])
                    for k in range(1, 4):
                        stt(tmpA[:, :], xp[:, c, k:k + N], pc(l, "cw", 10 * k + c), tmpA[:, :], ALU.mult, ALU.add,
                            [xp.c(c), P[l]["_t"], tmpA], [tmpA])
                    act(XBC[:, c, :], tmpA[:, :], AF.Silu, [tmpA], [XBC.c(c)])
                if cfg.get("sd", 99) < 2:
                    continue
                PD = PS[2]
                for k in range(8):
                    MM(PD[0:48, 0:N], WDT[l][:, k, :], hT[:, k, 0:N], k == 0, k == 7, [WDT[l], hT.c(k)], [PD.r(0, 4 * N)])
                dtT = S.sb([48, N], F32, "dtT")
                dA = S.sb([48, N], F32, "dA")
                AC = S.sb([16, N], F32, "AC")
                act(dtT[:, :], PD[0:48, 0:N], AF.Exp, [PD.r(0, 4 * N), P[l]["_t"]], [dtT], bias=pc(l, "dtb", 0, slice(0, 48)))
                act(dtT[:, :], dtT[:, :], AF.Ln, [dtT, oneT], [dtT], bias=oneT[0:48, 0:1])
                ts(dA[:, :], dtT[:, :], acol[l][:, 0:1], None, ALU.mult, None, [dtT, acol[l]], [dA])
                if cfg.get("sd", 99) < 3:
                    continue
                Ets = [S.sb([128, 8, Q], F32, "Et") for _ in range(2)]
                Wts = [S.sb([128, 8, Q], BF16, "Wt") for _ in range(2)]
                if Q < 64:
                    for w__ in Wts:
                        V(lambda e, a_=w__[:, :, :]: e.memset(a_, 0.0), [], [w__])
                dt_tm = S.sb([64, 16], F32, "dt_tm")
                coef = S.sb([64, 8], F32, "coef")
                XW = S.sb([64, 8, 64], BF16, "XW")
                ectr = 0
                for ci in range(nch):
                    cs = slice(ci * Q, ci * Q + Q)
                    LTt = LT[ci % 2]
                    scan(AC[0:16, cs], ones48[0:16, 0:Q], dA[0:16, cs], 0.0, [ones48, dA], [AC])
                    scan(LTt[32:48, 0:Q], ones48[32:48, 0:Q], dA[32:48, cs], 0.0, [ones48, dA], [LTt])
                    PT = PS[7]
                    MM(PT[0:Q, 0:16], dtT[0:16, cs], ident_f[0:16, 0:16], True, True, [dtT, ident_f], [PT.r(0, 64)])
                    act(dt_tm[0:Q, :], PT[0:Q, 0:16], AF.Copy, [PT.r(0, 64)], [dt_tm])
                    if cfg.get("sd", 99) < 4:
                        continue
                    PY = PS[3]
                    for g in range(2):
                        GR = slice(64 * g, 64 * g + 64)
                        Et, Wt = Ets[ectr % 2], Wts[ectr % 2]
                        ectr += 1
                        tt(RF[g][0:16, :, 0:Q], AC[0:16, cs].unsqueeze(1).broadcast_to([16, 8, Q]),
                           DEL[g][0:16, :].unsqueeze(2).broadcast_to([16, 8, Q]), ALU.mult, [AC, DEL[g]], [RF[g]])
                        PE_ = PS[4]
                        pev = PE_[:, 0:8 * Q].rearrange("p (h i) -> p h i", i=Q)
                        MM(pev, LTt[:, :], RF[g][:, :, 0:Q], True, True, [LTt, RF[g]], [PE_])
                        act(Et[:, :, :], pev, AF.Exp, [PE_], [Et])
                        if cfg.get("sd", 99) < 5:
                            continue
                        PG = PS[5]
                        MM(PG[0:Q, 0:Q], XBC[GR, 8, cs], XBC[GR, 9, cs], True, True, [XBC.c(8), XBC.c(9)], [PG.r(0, 256)])
                        tt(Wt[0:Q, :, :], Et[0:Q, :, :], PG[0:Q, 0:Q].unsqueeze(1).broadcast_to([Q, 8, Q]), ALU.mult,
                           [Et, PG.r(0, 256)], [Wt])
                        MM(PG[:, 128:128 + Q], SHIFT[GR, :], XBC[GR, 9, cs], True, True, [SHIFT, XBC.c(9)], [PG.r(512, 768)])
                        tt(Wt[64:128, :, :], Et[64:128, :, :], PG[64:128, 128:128 + Q].unsqueeze(1).broadcast_to([64, 8, Q]), ALU.mult,
                           [Et, PG.r(512, 768)], [Wt])
                        if cfg.get("sd", 99) < 6:
                            continue
                        PX = PS[6]
                        for pr in range(4):
                            MM(PX[0:Q, 128 * pr:128 * pr + 128], XBC[:, 4 * g + pr, cs], ident_b[:, :], True, True,
                               [XBC.c(4 * g + pr), ident_b], [PX.r(512 * pr, 512 * pr + 512)])
                        dstX = XS_[l][0:Q, g, :, :].rearrange("q pr (blk d) -> q pr blk d", d=64)[:, :, 1::2, :]
                        tt(dstX, PX[0:Q, :].rearrange("q (pr eo d) -> q pr eo d", pr=4, eo=2),
                           dt_tm[0:Q, 8 * g:8 * g + 8].rearrange("q (pr eo) -> q pr eo", eo=2).unsqueeze(3).broadcast_to([Q, 4, 2, 64]),
                           ALU.mult, [PX, dt_tm], [XS_[l].c(g)])
                        if cfg.get("sd", 99) < 7:
                            continue
                        for pr in range(4):
                            k = 4 * g + pr
                            for eo in range(2):
                                win = slice(64 + 64 * eo, 192 + 64 * eo)
                                MM(PY[:, k * Q:k * Q + Q], XS_[l][:, g, pr, win], Wt[:, 2 * pr + eo, :], eo == 0, eo == 1,
                                   [XS_[l].c(g), Wt], [PY.r(4 * k * Q, 4 * k * Q + 4 * Q)])
                        if cfg.get("sd", 99) < 8:
                            continue
                        tt(coef[0:Q, :], dt_tm[0:Q, 8 * g:8 * g + 8], Et[0:Q, :, Q - 1], ALU.mult, [dt_tm, Et], [coef])
                        tt(XW[0:Q, :, :], PX[0:Q, :].rearrange("q (h d) -> q h d", d=64),
                           coef[0:Q, :].unsqueeze(2).broadcast_to([Q, 8, 64]), ALU.mult, [PX, coef], [XW])
                        MM(PG[0:Q, 64:128], XBC[GR, 8, cs], ident_b[GR, 64 * g:64 * g + 64], True, True,
                           [XBC.c(8), ident_b], [PG.r(256, 512)])
                        cp(BTOK[0:Q, 64:128], PG[0:Q, 64:128], [PG.r(256, 512)], [BTOK])
                        PSt = PS[2]
                        MM(PSt[:, :], BTOK[0:Q, :], XW[0:Q, :, :], True, True, [BTOK, XW], [PSt])
                        stv = ST_[l][64:128, g, :].rearrange("p (h d) -> p h d", d=64)
                        tt(stv, stv, Et[64:128, :, Q - 1].unsqueeze(2).broadcast_to([64, 8, 64]), ALU.mult, [ST_[l].c(g), Et], [ST_[l].c(g)])
                        tt(ST_[l][64:128, g, :], ST_[l][64:128, g, :], PSt[64:128, :], ALU.add, [ST_[l].c(g), PSt], [ST_[l].c(g)])
                        src = ST_[l][64:128, g, :].rearrange("p (pr eo d) -> p pr eo d", pr=4, eo=2)
                        dst = XS_[l][64:128, g, :, :].rearrange("p pr (blk d) -> p pr blk d", d=64)[:, :, 1::2, :]
                        cp(dst, src, [ST_[l].c(g)], [XS_[l].c(g)])
                    if cfg.get("sd", 99) < 9:
                        continue
                    act(yT[:, :, cs], PY[:, 0:8 * Q].rearrange("p (k i) -> p k i", i=Q), AF.Copy, [PY], [yT])
                if cfg.get("sd", 99) < 10:
                    continue
                for k in range(8):
                    stt(yT[:, k, :], XBC[:, k, :], pc(l, "dsk", k), yT[:, k, :], ALU.mult, ALU.add, [XBC.c(k), P[l]["_t"], yT.c(k)], [yT.c(k)])

                def z_evac(m, bank, mw):
                    act(tmpA[:, :], bank[:, 0:N], AF.Silu, [bank.r(0, 4 * N)], [tmpA])
                    tt(yT[:, m, :], yT[:, m, :], tmpA[:, :], ALU.mult, [yT.c(m), tmpA], [yT.c(m)])
                proj_fm(l, I["w_in"][l], 8, OFF_Z, 1024, hrhs, hrd, N, z_evac)
                ysq = S.sb([128, 8, N], BF16, "ysq")
                YN = S.sb([128, 8, N], BF16, "YN")
                rg = S.sb([128, N], F32, "rg")
                for k in range(8):
                    act(ysq[:, k, :], yT[:, k, :], AF.Square, [yT.c(k)], [ysq.c(k)])
                for g in range(2):
                    bank = PS[4]
                    for k in range(4 * g, 4 * g + 4):
                        MM(bank[:, 0:N], ones_b[:, :], ysq[:, k, :], k == 4 * g, k == 4 * g + 3, [ones_b, ysq.c(k)], [bank.r(0, 4 * N)])
                    act(rg[:, :], bank[:, 0:N], AF.Sqrt, [bank.r(0, 4 * N), epsT], [rg], bias=epsT[:, 0:1], scale=1.0 / 512)
                    V(lambda e, o_=rg[:, :]: e.reciprocal(out=o_, in_=o_), [rg], [rg])
                    for k in range(4 * g, 4 * g + 4):
                        stt(YN[:, k, :], yT[:, k, :], pc(l, "nssd", k), rg[:, :], ALU.mult, ALU.mult, [yT.c(k), P[l]["_t"], rg], [YN.c(k)])
                gates(0)
                proj_fm(l, I["w_lift_a"][l], 8, 0, 1024, lambda k_: YN[:, k_, :], lambda k_: [YN.c(k_)], N, mix_evac(False))
                if upto < 4:
                    continue
                S.reset(mkA)
                GT = S.sb([128, 8, N], F32, "GT")
                tmpA = S.sb([128, N], F32, "tmpA")
                BC = S.sb([128, 4, 16, 128], BF16, "BC")
                if ("bc", l) not in WG:
                    scr_ = nc.dram_tensor("bcs%d" % l, [128, 4 * 2048], BF16, kind="Internal").ap()
                    S.dma("pool", scr_.rearrange("p (i n) -> p i n", i=4), I["s5bc"][l].rearrange("i p n -> p i n"),
                          [], [("bcs", l, l + 1)], "wcast")
                    WG[("bc", l)] = scr_
                S.dma("sp", BC[:, :, :, :].rearrange("p i c n -> p (i c n)"), WG[("bc", l)][:, :], [("bcs", l, l + 1)], [BC], "bcl")
                uT = S.sb([128, 4, N], BF16, "uT")
                u32 = S.sb([128, 4, N], F32, "u32")

                def u_evac(m, bank, mw):
                    act(u32[:, m, :], bank[:, 0:N], AF.Copy, [bank.r(0, 4 * N)], [u32.c(m)])
                    cp(uT[:, m, :], u32[:, m, :], [u32.c(m)], [uT.c(m)], eng="pool")
                proj_fm(l, I["w_in"][l], 8, OFF_U, 512, hrhs, hrd, N, u_evac)
                A_ = S.sb([128, 8, QS], F32, "A_")
                B_ = S.sb([128, 8, QS], F32, "B_")
                C_ = S.sb([128, 8, QS], F32, "C_")
                D_ = S.sb([128, 8, QS], F32, "D_")
                CDs = [(C_, D_), (S.sb([128, 8, QS], F32, "C2"), S.sb([128, 8, QS], F32, "D2"))]
                E_ = S.sb([128, 8, QS], F32, "E_")
                F_ = S.sb([128, 8, QS], F32, "F_")
                xrb = S.sb([128, 8, QS], BF16, "xrb")
                xib = S.sb([128, 8, QS], BF16, "xib")
                WIr = S.sb([128, 8], F32, "WIr")
                WIi = S.sb([128, 8], F32, "WIi")
                t8 = S.sb([128, 8], F32, "t8")
                cosT, sinT, tbr, tbi, magT = S5TAB[l]
                XR, XI = XR_[l], XI_[l]
                nsub = (N + QS - 1) // QS
                for s_ in range(nsub):
                    ncol = min(QS, N - s_ * QS)
                    cs = slice(s_ * QS, s_ * QS + ncol)
                    for hf in range(2):
                        ch = slice(8 * hf, 8 * hf + 8)
                        Pre, Pim = PS[0], PS[1]
                        for cc in range(8):
                            c = 8 * hf + cc
                            MM(Pre[:, cc * QS:cc * QS + ncol], BC[:, 0, c, :], uT[:, c // 4, cs], True, True,
                               [BC.c(0), uT.c(c // 4)], [Pre.r(4 * cc * QS, 4 * cc * QS + 4 * ncol)])
                            MM(Pim[:, cc * QS:cc * QS + ncol], BC[:, 1, c, :], uT[:, c // 4, cs], True, True,
                               [BC.c(1), uT.c(c // 4)], [Pim.r(4 * cc * QS, 4 * cc * QS + 4 * ncol)])
                        PreV = Pre[:, :].rearrange("p (c t) -> p c t", t=QS)[:, :, 0:ncol]
                        PimV = Pim[:, :].rearrange("p (c t) -> p c t", t=QS)[:, :, 0:ncol]
                        C_, D_ = CDs[(2 * s_ + hf) % 2]
                        a_, b_, c_, d_ = A_[:, :, 0:ncol], B_[:, :, 0:ncol], C_[:, :, 0:ncol], D_[:, :, 0:ncol]
                        tr_, ti_ = tbr[:, ch, 0:ncol], tbi[:, ch, 0:ncol]
                        tt(a_, PreV, tr_, ALU.mult, [Pre, tbr], [A_])
                        tt(b_, PimV, ti_, ALU.mult, [Pim, tbi], [B_])
                        tt(a_, a_, b_, ALU.subtract, [A_, B_], [A_])
                        tt(b_, PreV, ti_, ALU.mult, [Pre, tbi], [B_])
                        tt(c_, PimV, tr_, ALU.mult, [Pim, tbr], [C_])
                        tt(b_, b_, c_, ALU.add, [B_, C_], [B_])
                        cos1, sin1 = cosT[:, ch, 1], sinT[:, ch, 1]
                        tt(WIr[:, :], cos1, XR[:, ch], ALU.mult, [cosT, XR], [WIr])
                        tt(t8[:, :], sin1, XI[:, ch], ALU.mult, [sinT, XI], [t8])
                        tt(WIr[:, :], WIr[:, :], t8[:, :], ALU.subtract, [WIr, t8], [WIr])
                        tt(WIi[:, :], sin1, XR[:, ch], ALU.mult, [sinT, XR], [WIi])
                        tt(t8[:, :], cos1, XI[:, ch], ALU.mult, [cosT, XI], [t8])
                        tt(WIi[:, :], WIi[:, :], t8[:, :], ALU.add, [WIi, t8], [WIi])
                        for cc in range(8):
                            c = 8 * hf + cc
                            scan(C_[:, cc, 0:ncol], magT[:, c, 0:ncol], A_[:, cc, 0:ncol], WIr[:, cc:cc + 1], [magT, A_, WIr], [C_.c(cc)])
                            scan(D_[:, cc, 0:ncol], magT[:, c, 0:ncol], B_[:, cc, 0:ncol], WIi[:, cc:cc + 1], [magT, B_, WIi], [D_.c(cc)])
                        cosL, sinL = cosT[:, ch, ncol - 1], sinT[:, ch, ncol - 1]
                        wrl, wil = C_[:, :, ncol - 1], D_[:, :, ncol - 1]
                        tt(XR[:, ch], cosL, wrl, ALU.mult, [cosT, C_], [XR])
                        tt(t8[:, :], sinL, wil, ALU.mult, [sinT, D_], [t8])
                        tt(XR[:, ch], XR[:, ch], t8[:, :], ALU.subtract, [XR, t8], [XR])
                        tt(XI[:, ch], sinL, wrl, ALU.mult, [sinT, C_], [XI])
                        tt(t8[:, :], cosL, wil, ALU.mult, [cosT, D_], [t8])
                        tt(XI[:, ch], XI[:, ch], t8[:, :], ALU.add, [XI, t8], [XI])
                        cos_, sin_ = cosT[:, ch, 0:ncol], sinT[:, ch, 0:ncol]
                        e_, f_ = E_[:, :, 0:ncol], F_[:, :, 0:ncol]
                        tt(e_, c_, cos_, ALU.mult, [C_, cosT], [E_], eng="pool")
                        tt(f_, d_, sin_, ALU.mult, [D_, sinT], [F_], eng="pool")
                        tt(xrb[:, :, 0:ncol], e_, f_, ALU.subtract, [E_, F_], [xrb], eng="pool")
                        tt(e_, c_, sin_, ALU.mult, [C_, sinT], [E_], eng="pool")
                        tt(f_, d_, cos_, ALU.mult, [D_, cosT], [F_], eng="pool")
                        tt(e_, e_, f_, ALU.add, [E_, F_], [E_], eng="pool")
                        ts(xib[:, :, 0:ncol], e_, -1.0, None, ALU.mult, None, [E_], [xib], eng="pool")
                        for cc in range(8):
                            c = 8 * hf + cc
                            j, m_ = c // 4, c % 4
                            bank = PS[6 + j // 2]
                            col0 = (j % 2) * N + s_ * QS
                            MM(bank[:, col0:col0 + ncol], BC[:, 2, c, :], xrb[:, cc, 0:ncol], m_ == 0, False,
                               [BC.c(2), xrb], [bank.r(4 * col0, 4 * col0 + 4 * ncol)])
                            MM(bank[:, col0:col0 + ncol], BC[:, 3, c, :], xib[:, cc, 0:ncol], False, m_ == 3,
                               [BC.c(3), xib], [bank.r(4 * col0, 4 * col0 + 4 * ncol)])
                GB = S.sb([128, 4, N], BF16, "GB")
                t5 = S.sb([128, N], F32, "t5")
                t6 = S.sb([128, N], F32, "t6")
                for j in range(4):
                    bank = PS[6 + j // 2]
                    col0 = (j % 2) * N
                    stt(t5[:, :], u32[:, j, :], pc(l, "d5", j), bank[:, col0:col0 + N], ALU.mult, ALU.add,
                        [u32.c(j), P[l]["_t"], bank.r(4 * col0, 4 * col0 + 4 * N)], [t5])
                    tt(t6[:, :], t5[:, :], t5[:, :], ALU.mult, [t5], [t6])
                    ts(t6[:, :], t6[:, :], 0.044715, 1.0, ALU.mult, ALU.add, [t6], [t6])
                    tt(t6[:, :], t6[:, :], t5[:, :], ALU.mult, [t6, t5], [t6])
                    act(t6[:, :], t6[:, :], AF.Sigmoid, [t6], [t6], scale=1.5957691216057308)
                    tt(GB[:, j, :], t5[:, :], t6[:, :], ALU.mult, [t5, t6], [GB.c(j)])
                gates(1)
                for half in range(2):
                    wtA, vA = load_w(I["w_glu"][l][:, 512 * half:512 * half + 512], 4, 512)
                    wtB, vB = load_w(I["w_glu"][l][:, 1024 + 512 * half:1024 + 512 * half + 512], 4, 512)
                    for mm_ in range(4):
                        m = 4 * half + mm_
                        P1, P2 = PS[0], PS[1]
                        for j in range(4):
                            MM(P1[:, 0:N], vA[:, j, 128 * mm_:128 * mm_ + 128], GB[:, j, :], j == 0, j == 3, [wtA, GB.c(j)], [P1.r(0, 4 * N)])
                        for j in range(4):
                            MM(P2[:, 0:N], vB[:, j, 128 * mm_:128 * mm_ + 128], GB[:, j, :], j == 0, j == 3, [wtB, GB.c(j)], [P2.r(0, 4 * N)])
                        act(t5[:, :], P2[:, 0:N], AF.Sigmoid, [P2.r(0, 4 * N)], [t5])
                        tt(t5[:, :], t5[:, :], P1[:, 0:N], ALU.mult, [t5, P1.r(0, 4 * N)], [t5])
                        tt(t5[:, :], t5[:, :], GT[:, m, :], ALU.mult, [t5, GT.c(m)], [t5])
                        tt(mixT[:, m, 0:N], mixT[:, m, 0:N], t5[:, :], ALU.add, [mixT.c(m), t5], [mixT.c(m)])
                if upto < 5:
                    continue
                S.reset(mkA)
                mixb = S.sb([128, 8, N], BF16, "mixb")
                for c in range(8):
                    cp(mixb[:, c, :], mixT[:, c, 0:N], [mixT.c(c)], [mixb.c(c)], eng="pool")

                def res_evac(m, bank, mw):
                    tt(xT[:, m, 0:N], xT[:, m, 0:N], bank[:, 0:N], ALU.add, [xT.c(m), bank.r(0, 4 * N)], [xT.c(m)])
                proj_fm(l, I["w_out"][l], 8, 0, 1024, lambda k_: mixb[:, k_, :], lambda k_: [mixb.c(k_)], N, res_evac, bankset=(0, 1, 2, 3))
                rmsnorm_to(hT, xT, "nffn", l, N)
                AT = S.sb([128, 32, N], BF16, "AT")
                t5 = S.sb([128, N], F32, "t5")

                def up_evac(m, bank, mw):
                    act(t5[:, :], bank[:, 0:N], AF.Relu, [bank.r(0, 4 * N)], [t5])
                    tt(AT[:, m, :], t5[:, :], t5[:, :], ALU.mult, [t5], [AT.c(m)])
                proj_fm(l, I["w_up"][l], 8, 0, 4096, hrhs, hrd, N, up_evac, bankset=(0, 1, 2, 3))
                for m in range(8):
                    wt, vw = load_w(I["w_down"][l][:, 128 * m:128 * m + 128], 32, 128)
                    bank = PS[m % 4]
                    for k in range(32):
                        MM(bank[:, 0:N], vw[:, k, :], AT[:, k, :], k == 0, k == 31, [wt, AT.c(k)], [bank.r(0, 4 * N)])
                    res_evac(m, bank, 128)
            if upto >= 5 and kind != "meta":
                S.reset(ARENA)
                ydst = O["y_prompt"][b, t0:t0 + N, :] if kind == "prompt" else O["y_sample"][b, t0:t0 + N, :]
                y_tm = S.sb([128, nbk, D], F32, "y_tm")
                for tb, nt in blk:
                    for hc in range(2):
                        bank = PS[5 + hc]
                        for c4 in range(4):
                            c = 4 * hc + c4
                            TR(bank[0:nt, 128 * c4:128 * c4 + 128], xT[:, c, 128 * tb:128 * tb + nt], ident_f[:, :],
                               [xT.c(c), ident_f], [bank.r(512 * c4, 512 * c4 + 512)])
                        act(y_tm[0:nt, tb, 512 * hc:512 * hc + 512], bank[0:nt, :], AF.Copy, [bank], [y_tm.c(tb)])
                    S.dma(OQ, ydst[128 * tb:128 * tb + nt, :], y_tm[0:nt, tb, :], [y_tm.c(tb)], [], "yout")
        if upto >= 3:
            for l in range(2):
                if kind == "meta":
                    for dst, src in ((STm[l], ST_[l]), (CHm[l], CH_[l]), (XRm[l], XR_[l]), (XIm[l], XI_[l])):
                        cp(dst.h[:], src.h[:], [src], [dst])
                else:
                    sfx = "prompt" if kind == "prompt" else "sample"
                    S.dma(OQ, O["conv_" + sfx][l, b], CH_[l][:, :, :].rearrange("p c k -> p (c k)"), [CH_[l]], [], "stout")
                    for g in range(2):
                        S.dma(OQ, O["ssd_" + sfx][l, b, g], ST_[l][64:128, g, :], [ST_[l].c(g)], [], "stout")
                    S.dma(OQ, O["s5re_" + sfx][l, b], XR_[l][:, :], [XR_[l]], [], "stout")
                    S.dma(OQ, O["s5im_" + sfx][l, b], XI_[l][:, :], [XI_[l]], [], "stout")
    OUT_STREAMS[:] = ["kout", "vout", "yout", "stout"]
    S.emit(OUT_STREAMS)
    return nc


OUT_STREAMS = []

_NC_CACHE = {}


def pack_s5bc(inp):
    out = np.zeros((2, 4, 128, 16, 128), np.float32)
    for l in range(2):
        for i, nm in enumerate(("b_re", "b_im")):
            b = np.asarray(inp[nm], np.float32)[l]
            for c in range(16):
                m = c % 4
                for two in range(2):
                    g = 2 * c + two
                    r0 = 32 * m + 16 * two
                    out[l, i, r0:r0 + 16, c, 64 * two:64 * two + 64] = b[g].T
        for i, nm in enumerate(("c_re", "c_im")):
            cc = np.asarray(inp[nm], np.float32)[l]
            for c in range(16):
                m = c % 4
                for two in range(2):
                    g = 2 * c + two
                    c0 = 32 * m + 16 * two
                    out[l, 2 + i, 64 * two:64 * two + 64, c, c0:c0 + 16] = cc[g].T
    return out.reshape(2, 4, 128, 2048)


def kernel(**inp):
    cfg = inp.pop("_cfg", None)
    key = repr(cfg)
    if key not in _NC_CACHE:
        _NC_CACHE[key] = build(cfg)
    nc = _NC_CACHE[key]
    f = lambda a: np.ascontiguousarray(np.asarray(a, dtype=np.float32))
    wnames = ["meta_tokens", "norm_mix", "w_in", "conv_w", "conv_b", "dt_bias", "a_log", "d_ssd", "norm_ssd",
              "lam_re", "lam_im", "log_step", "w_glu", "q_norm", "k_norm",
              "w_lift_a", "w_lift_c", "w_out", "norm_ffn", "w_up", "w_down"]
    shared = {n: f(inp[n]) for n in wnames}
    shared["d_s5"] = f(inp["d_s5"]).reshape(2, 512)
    shared["par"] = pack_params(inp)
    shared["s5bc"] = pack_s5bc(inp)
    in_maps = []
    for c in range(8):
        m = dict(shared)
        m["x_prompt"] = f(inp["x_prompt"][4 * c:4 * c + 4])
        m["x_sample"] = f(inp["x_sample"][2 * c:2 * c + 2])
        m["cache_k"] = f(inp["cache_k"][:, 2 * c:2 * c + 2]).reshape(2, 2, PAST, 512)
        m["cache_v"] = f(inp["cache_v"][:, 2 * c:2 * c + 2]).reshape(2, 2, PAST, 512)
        sc = f(inp["state_conv"][:, 2 * c:2 * c + 2]).reshape(2, 2, 3, 10, 128)
        m["state_conv"] = np.ascontiguousarray(sc.transpose(0, 1, 4, 3, 2)).reshape(2, 2, 128, 30)
        ss = f(inp["state_ssd"][:, 2 * c:2 * c + 2])
        m["state_ssd"] = np.ascontiguousarray(ss.transpose(0, 1, 2, 5, 3, 4)).reshape(2, 2, 2, 64, 512)
        for nm in ("state_s5_re", "state_s5_im"):
            a5 = f(inp[nm][:, 2 * c:2 * c + 2]).reshape(2, 2, 16, 2, 64)
            m[nm] = np.ascontiguousarray(a5.transpose(0, 1, 3, 4, 2)).reshape(2, 2, 128, 16)
        in_maps.append(m)
    ncores = (cfg or {}).get('ncores', 8)
    res = run_bass_kernel_spmd(nc, in_maps[:ncores], core_ids=list(range(ncores)))
    R = list(res.results) + [res.results[0]] * (8 - ncores)
    cat = lambda k, ax: np.concatenate([np.asarray(R[c][k], dtype=np.float32) for c in range(8)], axis=ax)
    y_p = cat("y_prompt", 0)
    y_s = cat("y_sample", 0)
    k_p = cat("k_prompt", 1).reshape(2, 32, TP, 8, 64)
    v_p = cat("v_prompt", 1).reshape(2, 32, TP, 8, 64)
    unconv = lambda a: np.ascontiguousarray(a.reshape(2, -1, 128, 10, 3).transpose(0, 1, 4, 3, 2)).reshape(2, -1, 3, 1280)
    unssd = lambda a: np.ascontiguousarray(a.reshape(2, -1, 2, 64, 8, 64).transpose(0, 1, 2, 4, 5, 3))
    uns5 = lambda a: np.ascontiguousarray(a.reshape(2, -1, 2, 64, 16).transpose(0, 1, 4, 2, 3)).reshape(2, -1, 32, 64)
    conv_p = unconv(cat("conv_prompt", 1))
    ssd_p = unssd(cat("ssd_prompt", 1))
    s5r_p = uns5(cat("s5re_prompt", 1))
    s5i_p = uns5(cat("s5im_prompt", 1))
    k_s = cat("k_sample", 1).reshape(2, 16, DSEQ, 8, 64)
    v_s = cat("v_sample", 1).reshape(2, 16, DSEQ, 8, 64)
    conv_s = unconv(cat("conv_sample", 1))
    ssd_s = unssd(cat("ssd_sample", 1))
    s5r_s = uns5(cat("s5re_sample", 1))
    s5i_s = uns5(cat("s5im_sample", 1))
    return (y_p, y_s, k_p, v_p, conv_p, ssd_p, s5r_p, s5i_p, k_s, v_s, conv_s, ssd_s, s5r_s, s5i_s)
```

```python
import math
import bisect
import numpy as np
import concourse.bass as bass
import concourse.mybir as mybir
from concourse.bass_utils import run_bass_kernel_spmd

F32 = mybir.dt.float32
F32R = mybir.dt.float32r
BF16 = mybir.dt.bfloat16
AF = mybir.ActivationFunctionType
ALU = mybir.AluOpType
AX = mybir.AxisListType

D = 1024
NMETA = 16
SEQ = 2048
TP = NMETA + SEQ
PAST = 2048
DSEQ = 64
NKMAX = 2112
INC = 7440
OFF_Z, OFF_XBC, OFF_DT, OFF_U, OFF_Q, OFF_K, OFF_V, OFF_G = 0, 1024, 2304, 2320, 2832, 3344, 3856, 4368
EPS = 1e-6
NT = 256
QS = 64
NEG = -30000.0
PCOL = {}
_c = 0
for _nm, _w in [("nmix", 8), ("nffn", 8), ("nssd", 8), ("cw", 40), ("cb", 10), ("d5", 4), ("dtb", 1), ("alog", 1),
                ("dsk", 8), ("qn", 1), ("kn", 1), ("lre", 16), ("lim", 16), ("lst", 16)]:
    PCOL[_nm] = (_c, _w)
    _c += _w
NPAR = _c


def pack_params(inp):
    par = np.zeros((2, 128, NPAR), np.float32)
    g = lambda n: np.asarray(inp[n], np.float32)

    def put(l, nm, arr):
        c0, w = PCOL[nm]
        par[l, :, c0:c0 + w] = arr

    for l in range(2):
        put(l, "nmix", g("norm_mix")[l].reshape(8, 128).T)
        put(l, "nffn", g("norm_ffn")[l].reshape(8, 128).T)
        put(l, "nssd", g("norm_ssd")[l].reshape(8, 128).T)
        cw = g("conv_w")[l].reshape(4, 10, 128)
        put(l, "cw", cw.transpose(2, 0, 1).reshape(128, 40))
        put(l, "cb", g("conv_b")[l].reshape(10, 128).T)
        put(l, "d5", g("d_s5")[l].reshape(4, 128).T)
        for nm, src in (("dtb", "dt_bias"), ("alog", "a_log")):
            col = np.zeros((128, 1), np.float32)
            col[0:16, 0] = g(src)[l]
            col[32:48, 0] = g(src)[l]
            put(l, nm, col)
        dsk = np.zeros((128, 8), np.float32)
        d = g("d_ssd")[l]
        for hh in range(2):
            dsk[64 * hh:64 * hh + 64, :] = d[hh::2][None, :]
        put(l, "dsk", dsk)
        put(l, "qn", np.tile(g("q_norm")[l], 2)[:, None])
        put(l, "kn", np.tile(g("k_norm")[l], 2)[:, None])
        for nm, src in (("lre", "lam_re"), ("lim", "lam_im")):
            a = g(src)[l].reshape(16, 2, 64)
            put(l, nm, a.transpose(1, 2, 0).reshape(128, 16))
        ls = g("log_step")[l].reshape(16, 2)
        put(l, "lst", np.repeat(ls.T[:, None, :], 64, axis=1).reshape(128, 16))
    return par
DTSZ = {F32: 4, F32R: 4, BF16: 2}


class Tile:
    def __init__(self, h, space, lo, hi):
        self.h, self.space, self.lo, self.hi = h, space, lo, hi

    def __getitem__(self, k):
        return self.h[k]

    @property
    def all(self):
        return (self.space, self.lo, self.hi)

    def r(self, lo, hi):
        return (self.space, self.lo + lo, self.lo + hi)

    def c(self, i, n=1):
        return (self.space, self.lo + i * self.cb, self.lo + (i + n) * self.cb)


def _reg(x):
    return x.all if isinstance(x, Tile) else x


class Sched:
    ENG = ["pe", "act", "dve", "pool", "sp"]

    def __init__(self, nc):
        self.nc = nc
        self.ops = []
        self.segs = {}
        self.off = 16512
        self.cnt = 0
        self.stream_n = {}
        self.psb = []

    def sb(self, shape, dtype, name="t"):
        nb = int(np.prod(shape[1:])) * DTSZ[dtype]
        off = (self.off + 31) // 32 * 32
        self.cnt += 1
        h = self.nc.alloc_sbuf_tensor_at(f"{name}{self.cnt}", list(shape), dtype, offset=off)
        self.off = off + nb
        self.peak = max(getattr(self, "peak", 0), self.off)
        assert self.off <= 16512 + 208000, ("sbuf overflow", name, self.off)
        t = Tile(h, "sb", off, off + nb)
        t.cb = (int(np.prod(shape[2:])) if len(shape) > 2 else 1) * DTSZ[dtype]
        return t

    def mark(self):
        return self.off

    def reset(self, m):
        self.off = m

    def _access(self, opi, reg, write, deps, norecord=False):
        space, lo, hi = reg
        if space not in self.segs:
            self.segs[space] = ([0], [[None, {}]])
        starts, data = self.segs[space]
        for b in (lo, hi):
            i = bisect.bisect_right(starts, b) - 1
            if starts[i] != b:
                starts.insert(i + 1, b)
                data.insert(i + 1, [data[i][0], dict(data[i][1])])
        i = bisect.bisect_left(starts, lo)
        while i < len(starts) and starts[i] < hi:
            w, rd = data[i]
            if w is not None and w != opi:
                if not write:
                    deps[w] = "raw"
                elif deps.get(w) != "raw":
                    deps[w] = "waw"
            if write:
                for r_ in rd.values():
                    if r_ != opi and r_ not in deps:
                        deps[r_] = "war"
                data[i][0] = opi
                data[i][1] = {}
            elif not norecord:
                key = self.ops[opi]["rk"]
                rd[key] = opi
            i += 1

    def op(self, eng, fn, reads=(), writes=(), stream=None):
        opi = len(self.ops)
        o = {"eng": eng, "fn": fn, "stream": stream, "deps": {}, "sig": False}
        if stream is not None:
            o["rk"] = "dma:" + stream
        else:
            o["rk"] = eng
        self.ops.append(o)
        deps = {}
        wregs = [_reg(w_) for w_ in writes]
        for r_ in reads:
            rr = _reg(r_)
            inplace = any(w[0] == rr[0] and w[1] < rr[2] and rr[1] < w[2] for w in wregs)
            self._access(opi, rr, False, deps, norecord=inplace)
        for w_ in writes:
            rg_ = _reg(w_)
            if eng == "pe" and rg_[0] == "ps":
                rg_ = ("ps", rg_[1] // 2048 * 2048, (rg_[2] + 2047) // 2048 * 2048)
            self._access(opi, rg_, True, deps)
        res = {}
        for d, kind in deps.items():
            od = self.ops[d]
            if od["stream"] is not None:
                res[d] = ("s", od["stream"], 16 * self.stream_n[od["stream"]])
            else:
                if od["eng"] == eng and stream is None:
                    if eng == "pe":
                        continue
                res[d] = ("e", od["eng"], None)
                od["sig"] = True
        lw = self.__dict__.setdefault("last_waiter", {})
        if stream is not None and stream in lw:
            w = lw[stream]
            ow = self.ops[w]
            if ow["eng"] != eng and w not in res:
                if ow["stream"] is not None:
                    res[w] = ("s", ow["stream"], 16 * self.stream_n[ow["stream"]])
                else:
                    res[w] = ("e", ow["eng"], None)
                    ow["sig"] = True
        for d, (k, key, val) in res.items():
            if k == "s":
                lw[key] = opi
        o["deps"] = res
        if stream is not None:
            self.stream_n[stream] = self.stream_n.get(stream, 0) + 1
            o["sval"] = 16 * self.stream_n[stream]
        return opi

    def dma(self, q, out, in_, reads, writes, stream, **kw):
        return self.op(q, lambda e: e.dma_start(out=out, in_=in_, **kw), reads, writes, stream=stream)

    def emit(self, final_streams):
        nc = self.nc
        counts = {e: 0 for e in self.ENG}
        for o in self.ops:
            if o["stream"] is None and o["sig"]:
                counts[o["eng"]] += 1
                o["sval"] = counts[o["eng"]]
        from contextlib import ExitStack
        with ExitStack() as es:
            esem = {e: es.enter_context(nc.semaphore("e_" + e)) for e in ["pe", "act", "dve", "pool"]}
            ssem = {s: es.enter_context(nc.semaphore("s_" + s)) for s in self.stream_n}
            block = es.enter_context(nc.Block())
            per = {e: [o for o in self.ops if o["eng"] == e] for e in self.ENG}

            def run(ename, e):
                known = {}
                for o in per[ename]:
                    for d, (k, key, val) in o["deps"].items():
                        if k == "s":
                            sem, v = ssem[key], val
                        else:
                            sem, v = esem[key], self.ops[d]["sval"]
                        kk = (k, key)
                        if known.get(kk, 0) >= v:
                            continue
                        known[kk] = v
                        e.wait_ge(sem, v)
                    ins = o["fn"](e)
                    if o["stream"] is not None:
                        ins.then_inc(ssem[o["stream"]], 16)
                    elif o["sig"]:
                        ins.then_inc(esem[ename], 1)
                if ename == "sp":
                    for s in final_streams:
                        if s in self.stream_n:
                            e.wait_ge(ssem[s], 16 * self.stream_n[s])

            @block.tensor
            def _(e):
                run("pe", e)

            @block.scalar
            def _(e):
                run("act", e)

            @block.vector
            def _(e):
                run("dve", e)

            @block.gpsimd
            def _(e):
                run("pool", e)

            @block.sync
            def _(e):
                run("sp", e)


def build(cfg=None):
    cfg = cfg or {}
    n_ptiles = cfg.get("n_ptiles", SEQ // NT)
    n_pseq = cfg.get("n_pseq", 4)
    n_sseq = cfg.get("n_sseq", 2)
    nc = bass.Bass("TRN2", target_bir_lowering=False)
    S = Sched(nc)

    def din(name, shape):
        return nc.dram_tensor(name, list(shape), F32, kind="ExternalInput").ap()

    def dout(name, shape):
        return nc.dram_tensor(name, list(shape), F32, kind="ExternalOutput").ap()

    I = {}
    I["x_prompt"] = din("x_prompt", [4, SEQ, D])
    I["x_sample"] = din("x_sample", [2, DSEQ, D])
    I["cache_k"] = din("cache_k", [2, 2, PAST, 512])
    I["cache_v"] = din("cache_v", [2, 2, PAST, 512])
    I["state_conv"] = din("state_conv", [2, 2, 128, 30])
    I["state_ssd"] = din("state_ssd", [2, 2, 2, 64, 512])
    I["state_s5_re"] = din("state_s5_re", [2, 2, 128, 16])
    I["state_s5_im"] = din("state_s5_im", [2, 2, 128, 16])
    I["meta_tokens"] = din("meta_tokens", [NMETA, D])
    for nm, sh in [("norm_mix", [2, D]), ("w_in", [2, D, INC]), ("conv_w", [2, 4, 1280]), ("conv_b", [2, 1280]),
                   ("dt_bias", [2, 16]), ("a_log", [2, 16]), ("d_ssd", [2, 16]), ("norm_ssd", [2, D]),
                   ("lam_re", [2, 32, 64]), ("lam_im", [2, 32, 64]), ("log_step", [2, 32]),
                   ("s5bc", [2, 4, 128, 2048]), ("d_s5", [2, 512]), ("w_glu", [2, 512, 2048]),
                   ("q_norm", [2, 64]), ("k_norm", [2, 64]), ("w_lift_a", [2, D, D]), ("w_lift_c", [2, 512, D]),
                   ("w_out", [2, D, D]), ("norm_ffn", [2, D]), ("w_up", [2, D, 4096]), ("w_down", [2, 4096, D])]:
        I[nm] = din(nm, sh)
    O = {}
    O["y_prompt"] = dout("y_prompt", [4, SEQ, D])
    O["y_sample"] = dout("y_sample", [2, DSEQ, D])
    O["k_prompt"] = dout("k_prompt", [2, 4, TP, 512])
    O["v_prompt"] = dout("v_prompt", [2, 4, TP, 512])
    O["conv_prompt"] = dout("conv_prompt", [2, 4, 128, 30])
    O["ssd_prompt"] = dout("ssd_prompt", [2, 4, 2, 64, 512])
    O["s5re_prompt"] = dout("s5re_prompt", [2, 4, 128, 16])
    O["s5im_prompt"] = dout("s5im_prompt", [2, 4, 128, 16])
    O["k_sample"] = dout("k_sample", [2, 2, DSEQ, 512])
    O["v_sample"] = dout("v_sample", [2, 2, DSEQ, 512])
    O["conv_sample"] = dout("conv_sample", [2, 2, 128, 30])
    O["ssd_sample"] = dout("ssd_sample", [2, 2, 2, 64, 512])
    O["s5re_sample"] = dout("s5re_sample", [2, 2, 128, 16])
    O["s5im_sample"] = dout("s5im_sample", [2, 2, 128, 16])
    kTs = nc.dram_tensor("kT_scr", [6, 2, 512, NKMAX], BF16, kind="Internal").ap()
    vhs = nc.dram_tensor("vh_scr", [6, 2, NKMAX, 512], BF16, kind="Internal").ap()

    PS = []
    for b in range(8):
        h = nc.alloc_psum_tensor(f"psb{b}", [128, 512], F32)
        PS.append(Tile(h, "ps", b * 2048, (b + 1) * 2048))

    def V(fn, reads, writes):
        return S.op("dve", fn, reads, writes)

    def A(fn, reads, writes):
        return S.op("act", fn, reads, writes)

    def G(fn, reads, writes):
        return S.op("pool", fn, reads, writes)

    def MM(out, lhsT, rhs, start, stop, reads, writes, **kw):
        return S.op("pe", lambda e: e.matmul(out, lhsT=lhsT, rhs=rhs, start=start, stop=stop, **kw), reads, writes)

    def TR(out, in_, ident, reads, writes):
        return S.op("pe", lambda e: e.matmul(out, lhsT=in_, rhs=ident, start=True, stop=True), reads, writes)

    def act(out, in_, func, reads, writes, bias=None, scale=None):
        kw = {}
        if bias is not None:
            kw["bias"] = bias
        if scale is not None:
            kw["scale"] = scale
        return A(lambda e: e.activation(out=out, in_=in_, func=func, **kw), reads, writes)

    def ldpar(out, in_, writes):
        return S.dma("sp", out, in_, [], writes, "par", allow_slow_non_contiguous=True)

    iota_pc = S.sb([128, 256], F32, "iota")
    ident_f = S.sb([128, 128], F32, "identf")
    ident_b = S.sb([128, 128], BF16, "identb")
    ones_b = S.sb([128, 128], BF16, "onesb")
    bd64_b = S.sb([128, 128], BF16, "bd64")
    tri_r = S.sb([128, 128], F32R, "tri")
    ones_r = S.sb([128, 128], F32R, "onesr")
    iota_t = S.sb([128, QS + 1], F32, "iotat")
    ones_f = S.sb([128, 128], F32, "onesf")
    masks = {}
    G(lambda e: e.iota(iota_pc[:, :], [[-1, 256]], base=0, channel_multiplier=1,
                       allow_small_or_imprecise_dtypes=True), [], [iota_pc])
    G(lambda e: e.iota(iota_t[:, :], [[1, QS + 1]], base=0, channel_multiplier=0,
                       allow_small_or_imprecise_dtypes=True), [], [iota_t])
    V(lambda e: e.tensor_single_scalar(out=ident_f[:, :], in_=iota_pc[:, 0:128], scalar=0.0, op=ALU.is_equal),
      [iota_pc], [ident_f])
    V(lambda e: e.tensor_copy(out=ident_b[:, :], in_=ident_f[:, :]), [ident_f], [ident_b])
    V(lambda e: e.memset(ones_b[:, :], 1.0), [], [ones_b])
    V(lambda e: e.memset(bd64_b[:, :], 0.0), [], [bd64_b])
    V(lambda e: e.memset(bd64_b[0:64, 0:64], 1.0), [], [bd64_b])
    V(lambda e: e.memset(bd64_b[64:128, 64:128], 1.0), [], [bd64_b])
    V(lambda e: e.tensor_scalar(out=tri_r[:, :], in0=iota_pc[:, 0:128], scalar1=0.0, scalar2=-8.0,
                                op0=ALU.is_ge, op1=ALU.mult), [iota_pc], [tri_r])
    V(lambda e: e.memset(ones_f[:, :], 1.0), [], [ones_f])
    V(lambda e: e.tensor_copy(out=ones_r[:, :], in_=ones_f[:, :]), [ones_f], [ones_r])
    for off in (16, -112, -240, 0):
        m = S.sb([128, 256], F32, "mask")
        V(lambda e, m=m, off=off: e.tensor_single_scalar(out=m[:, :], in_=iota_pc[:, :], scalar=float(off),
                                                         op=ALU.is_lt), [iota_pc], [m])
        masks[off] = m

    P = []
    stage = cfg.get('stage', 99)
    I["par"] = din("par", [2, 128, NPAR])
    for l in range(2):
        pt = S.sb([128, NPAR], F32, "par")
        S.dma("sp", pt[:, :], I["par"][l], [], [pt], "par")
        p = {"_t": pt}
        for nm, (c0, w) in PCOL.items():
            p[nm] = (pt, c0, w)
        P.append(p)

    def pc(l, nm, j=0, rows=slice(0, 128)):
        pt, c0, w = P[l][nm]
        return pt[rows, c0 + j:c0 + j + 1]

    def pv(l, nm):
        pt, c0, w = P[l][nm]
        return pt[:, c0:c0 + w]

    TWO_PI = 2.0 * math.pi
    MAGIC = 12582912.0
    S5TAB = []
    for l in range(2):
        S5TAB.append((S.sb([128, 16, QS + 1], F32, "cosT"), S.sb([128, 16, QS + 1], F32, "sinT"),
                      S.sb([128, 16, QS], F32, "tbr"), S.sb([128, 16, QS], F32, "tbi"),
                      S.sb([128, 16, QS], F32, "magT")))
    S5L = [(S.sb([128, 16], F32, "lre"), S.sb([128, 16], F32, "lim"), S.sb([128, 16], F32, "lst")) for l in range(2)]
    ARENA0 = S.mark()
    S5SCR = [S.sb([128, 16], F32, "s5s") for _ in range(8)] + [S.sb([128, 16, QS + 1], F32, "s5w") for _ in range(3)]
    for l in range(2 if stage >= 2 else 0):
        p = P[l]
        lre, lim, lst = S5L[l]
        for dst, nm in ((lre, "lre"), (lim, "lim"), (lst, "lst")):
            V(lambda e, dst=dst, nm=nm, l=l: e.tensor_copy(out=dst[:, :], in_=pv(l, nm)), [P[l]["_t"]], [dst])
        step, th, mag, t0_, t1_, t2_, fre, fim, ang, w1, w2 = S5SCR
        cosT, sinT, tbr, tbi, magT = S5TAB[l]
        p.update(cosT=cosT, sinT=sinT, tbr=tbr, tbi=tbi, magT=magT, mag=mag)
        act(step[:, :], lst[:, :], AF.Exp, [lst], [step])
        V(lambda e, th=th, lim=lim, step=step: e.tensor_tensor(out=th[:, :], in0=lim[:, :], in1=step[:, :], op=ALU.mult),
          [lim, step], [th])
        V(lambda e, t0_=t0_, lre=lre, step=step: e.tensor_tensor(out=t0_[:, :], in0=lre[:, :], in1=step[:, :], op=ALU.mult),
          [lre, step], [t0_])
        act(mag[:, :], t0_[:, :], AF.Exp, [t0_], [mag])
        V(lambda e, ang=ang, th=th: e.tensor_tensor(
            out=ang[:, :, :], in0=iota_t[:, :].unsqueeze(1).broadcast_to([128, 16, QS + 1]),
            in1=th[:, :].unsqueeze(2).broadcast_to([128, 16, QS + 1]), op=ALU.mult), [iota_t, th], [ang])
        for which, outT in (("sin", sinT), ("cos", cosT)):
            addc = 0.0 if which == "sin" else 0.25
            V(lambda e, ang=ang, w1=w1, addc=addc: e.tensor_scalar(
                out=w1[:, :, :], in0=ang[:, :, :], scalar1=1.0 / TWO_PI, scalar2=addc, op0=ALU.mult, op1=ALU.add),
              [ang], [w1])
            V(lambda e, w1=w1, w2=w2: e.tensor_scalar(out=w2[:, :, :], in0=w1[:, :, :], scalar1=MAGIC, scalar2=None,
                                                     op0=ALU.add), [w1], [w2])
            V(lambda e, w2=w2: e.tensor_scalar(out=w2[:, :, :], in0=w2[:, :, :], scalar1=-MAGIC, scalar2=None,
                                               op0=ALU.add), [w2], [w2])
            V(lambda e, w1=w1, w2=w2: e.tensor_tensor(out=w1[:, :, :], in0=w1[:, :, :], in1=w2[:, :, :],
                                                     op=ALU.subtract), [w1, w2], [w1])
            V(lambda e, w1=w1: e.tensor_scalar(out=w1[:, :, :], in0=w1[:, :, :], scalar1=-0.4999, scalar2=0.4999,
                                               op0=ALU.max, op1=ALU.min), [w1], [w1])
            act(outT[:, :, :], w1[:, :, :], AF.Sin, [w1], [outT], scale=TWO_PI)
        abr, abi = t1_, t2_
        V(lambda e, abr=abr, cosT=cosT, mag=mag: e.tensor_tensor(out=abr[:, :], in0=cosT[:, :, 1], in1=mag[:, :], op=ALU.mult),
          [cosT, mag], [abr])
        V(lambda e, abi=abi, sinT=sinT, mag=mag: e.tensor_tensor(out=abi[:, :], in0=sinT[:, :, 1], in1=mag[:, :], op=ALU.mult),
          [sinT, mag], [abi])
        V(lambda e, abr=abr: e.tensor_scalar(out=abr[:, :], in0=abr[:, :], scalar1=-1.0, scalar2=None, op0=ALU.add),
          [abr], [abr])
        den = step
        V(lambda e, den=den, lre=lre: e.tensor_tensor(out=den[:, :], in0=lre[:, :], in1=lre[:, :], op=ALU.mult), [lre], [den])
        V(lambda e, t0_=t0_, lim=lim: e.tensor_tensor(out=t0_[:, :], in0=lim[:, :], in1=lim[:, :], op=ALU.mult), [lim], [t0_])
        V(lambda e, den=den, t0_=t0_: e.tensor_tensor(out=den[:, :], in0=den[:, :], in1=t0_[:, :], op=ALU.add), [den, t0_], [den])
        V(lambda e, den=den: e.reciprocal(out=den[:, :], in_=den[:, :]), [den], [den])
        V(lambda e, fre=fre, abr=abr, lre=lre: e.tensor_tensor(out=fre[:, :], in0=abr[:, :], in1=lre[:, :], op=ALU.mult), [abr, lre], [fre])
        V(lambda e, t0_=t0_, abi=abi, lim=lim: e.tensor_tensor(out=t0_[:, :], in0=abi[:, :], in1=lim[:, :], op=ALU.mult), [abi, lim], [t0_])
        V(lambda e, fre=fre, t0_=t0_: e.tensor_tensor(out=fre[:, :], in0=fre[:, :], in1=t0_[:, :], op=ALU.add), [fre, t0_], [fre])
        V(lambda e, fre=fre, den=den: e.tensor_tensor(out=fre[:, :], in0=fre[:, :], in1=den[:, :], op=ALU.mult), [fre, den], [fre])
        V(lambda e, fim=fim, abi=abi, lre=lre: e.tensor_tensor(out=fim[:, :], in0=abi[:, :], in1=lre[:, :], op=ALU.mult), [abi, lre], [fim])
        V(lambda e, t0_=t0_, abr=abr, lim=lim: e.tensor_tensor(out=t0_[:, :], in0=abr[:, :], in1=lim[:, :], op=ALU.mult), [abr, lim], [t0_])
        V(lambda e, fim=fim, t0_=t0_: e.tensor_tensor(out=fim[:, :], in0=fim[:, :], in1=t0_[:, :], op=ALU.subtract), [fim, t0_], [fim])
        V(lambda e, fim=fim, den=den: e.tensor_tensor(out=fim[:, :], in0=fim[:, :], in1=den[:, :], op=ALU.mult), [fim, den], [fim])
        frb = lambda f: f[:, :].unsqueeze(2).broadcast_to([128, 16, QS])
        V(lambda e, w1=w1, cosT=cosT, fre=fre: e.tensor_tensor(out=w1[:, :, 0:QS], in0=cosT[:, :, 0:QS], in1=frb(fre), op=ALU.mult), [cosT, fre], [w1])
        V(lambda e, w2=w2, sinT=sinT, fim=fim: e.tensor_tensor(out=w2[:, :, 0:QS], in0=sinT[:, :, 0:QS], in1=frb(fim), op=ALU.mult), [sinT, fim], [w2])
        V(lambda e, tbr=tbr, w1=w1, w2=w2: e.tensor_tensor(out=tbr[:, :, :], in0=w1[:, :, 0:QS], in1=w2[:, :, 0:QS], op=ALU.add), [w1, w2], [tbr])
        V(lambda e, w1=w1, cosT=cosT, fim=fim: e.tensor_tensor(out=w1[:, :, 0:QS], in0=cosT[:, :, 0:QS], in1=frb(fim), op=ALU.mult), [cosT, fim], [w1])
        V(lambda e, w2=w2, sinT=sinT, fre=fre: e.tensor_tensor(out=w2[:, :, 0:QS], in0=sinT[:, :, 0:QS], in1=frb(fre), op=ALU.mult), [sinT, fre], [w2])
        V(lambda e, tbi=tbi, w1=w1, w2=w2: e.tensor_tensor(out=tbi[:, :, :], in0=w1[:, :, 0:QS], in1=w2[:, :, 0:QS], op=ALU.subtract), [w1, w2], [tbi])
        V(lambda e, magT=magT, mag=mag: e.tensor_tensor(
            out=magT[:, :, :], in0=ones_f[:, 0:QS].unsqueeze(1).broadcast_to([128, 16, QS]),
            in1=mag[:, :].unsqueeze(2).broadcast_to([128, 16, QS]), op=ALU.mult), [ones_f, mag], [magT])
    OQ = cfg.get("oq", "pool")
    WG = {}
    upto = cfg.get("upto", 99)
    epsT = S.sb([128, 1], F32, "eps")
    V(lambda e: e.memset(epsT[:, :], EPS), [], [epsT])
    oneT = S.sb([128, 1], F32, "one")
    V(lambda e: e.memset(oneT[:, :], 1.0), [], [oneT])
    WDT = []
    for l in range(2):
        w_ = S.sb([128, 8, 48], BF16, "WDT")
        V(lambda e, w_=w_: e.memset(w_[:, :, :], 0.0), [], [w_])
        for c0 in (0, 32):
            S.dma("pool", w_[:, :, c0:c0 + 16], I["w_in"][l][:, OFF_DT:OFF_DT + 16].rearrange("(kc p) n -> p kc n", p=128),
                  [], [w_], "wdt")
        WDT.append(w_)

    def tt(out, in0, in1, op, reads, writes, eng="dve"):
        return S.op(eng, lambda e: e.tensor_tensor(out=out, in0=in0, in1=in1, op=op), reads, writes)

    def ts(out, in0, s1, s2, op0, op1, reads, writes, eng="dve"):
        if op1 is None:
            return S.op(eng, lambda e: e.tensor_scalar(out=out, in0=in0, scalar1=s1, scalar2=None, op0=op0), reads, writes)
        return S.op(eng, lambda e: e.tensor_scalar(out=out, in0=in0, scalar1=s1, scalar2=s2, op0=op0, op1=op1), reads, writes)

    def stt(out, in0, scalar, in1, op0, op1, reads, writes):
        return V(lambda e: e.scalar_tensor_tensor(out=out, in0=in0, scalar=scalar, in1=in1, op0=op0, op1=op1), reads, writes)

    def cp(out, in_, reads, writes, eng="dve"):
        return S.op(eng, lambda e: e.tensor_copy(out=out, in_=in_), reads, writes)

    def scan(out, d0, d1, init, reads, writes):
        return V(lambda e: e.tensor_tensor_scan(out=out, data0=d0, data1=d1, initial=init, op0=ALU.mult, op1=ALU.add), reads, writes)

    NWS = cfg.get('nws', 4)
    WSL = [S.sb([128, 4096], BF16, "wslot") for _ in range(NWS)]
    wctr = [0]

    def load_w(src_ap, kc, ncols, prt=128):
        key = (src_ap.tensor.name, int(src_ap.offset), kc, ncols, prt)
        if key not in WG:
            gi = len(WG)
            scr = nc.dram_tensor("wg%d" % gi, [prt, kc * ncols], BF16, kind="Internal").ap()
            S.dma("pool", scr.rearrange("p (kc n) -> p kc n", n=ncols), src_ap.rearrange("(kc p) n -> p kc n", p=prt),
                  [], [("wg", gi, gi + 1)], "wcast")
            WG[key] = (gi, scr)
        gi, scr = WG[key]
        i = wctr[0] % NWS
        wctr[0] += 1
        wt = WSL[i]
        view = wt[0:prt, 0:kc * ncols].rearrange("p (kc n) -> p kc n", n=ncols)
        S.dma("sp", wt[0:prt, 0:kc * ncols], scr[:, :], [("wg", gi, gi + 1)], [wt.r(0, kc * ncols * 2)], f"w{i}")
        return wt, view

    ones48 = S.sb([48, 64], F32, "ones48")
    V(lambda e: e.memset(ones48[:, :], 1.0), [], [ones48])
    LT = []
    for i in range(2):
        t = S.sb([128, 128], F32, "LT")
        V(lambda e, t=t: e.memset(t[:, :], 0.0), [], [t])
        V(lambda e, t=t: e.memset(t[0:16, :], 1.0), [], [t])
        cp(t[64:128, 0:64], ident_f[64:128, 64:128], [ident_f], [t])
        LT.append(t)
    RF = []
    for g in range(2):
        t = S.sb([128, 8, 64], F32, "RF")
        V(lambda e, t=t: e.memset(t[:, :, :], 0.0), [], [t])
        for h in range(8):
            if True:
                r = 8 * g + h
                pass
        RF.append(t)
    DEL = []
    for g in range(2):
        d_ = S.sb([48, 8], F32, "DEL")
        io = S.sb([48, 8], F32, "DELi")
        G(lambda e, io=io: e.iota(io[:, :], [[-1, 8]], base=0, channel_multiplier=1, allow_small_or_imprecise_dtypes=True), [], [io])
        ts(d_[0:32, :], io[0:32, :], float(8 * g), None, ALU.is_equal, None, [io], [d_])
        ts(d_[32:48, :], io[32:48, :], float(32 + 8 * g), None, ALU.is_equal, None, [io], [d_])
        DEL.append(d_)
        cp(RF[g][32:48, :, :], d_[32:48, :].unsqueeze(2).broadcast_to([16, 8, 64]), [d_], [RF[g]])
        ts(RF[g][64:128, :, :], iota_pc[64:128, 0:64].unsqueeze(1).broadcast_to([64, 8, 64]), 64.0, NEG, ALU.is_gt, ALU.mult,
           [iota_pc], [RF[g]])
    SHIFT = S.sb([128, 128], BF16, "shift")
    V(lambda e: e.memset(SHIFT[:, :], 0.0), [], [SHIFT])
    cp(SHIFT[0:64, 64:128], ident_b[0:64, 0:64], [ident_b], [SHIFT])
    cp(SHIFT[64:128, 64:128], ident_b[64:128, 64:128], [ident_b], [SHIFT])
    BTOK = S.sb([64, 128], BF16, "btok")
    V(lambda e: e.memset(BTOK[:, :], 0.0), [], [BTOK])
    ST_, XS_, CH_, XR_, XI_, STm, CHm, XRm, XIm = [], [], [], [], [], [], [], [], []
    for l in range(2):
        ST_.append(S.sb([128, 2, 512], F32, "ST"))
        XS_.append(S.sb([128, 2, 4, 256], BF16, "XS"))
        CH_.append(S.sb([128, 10, 3], F32, "CH"))
        XR_.append(S.sb([128, 16], F32, "XR"))
        XI_.append(S.sb([128, 16], F32, "XI"))
        STm.append(S.sb([128, 2, 512], F32, "STm"))
        CHm.append(S.sb([128, 10, 3], F32, "CHm"))
        XRm.append(S.sb([128, 16], F32, "XRm"))
        XIm.append(S.sb([128, 16], F32, "XIm"))
    acol = []
    for l in range(2):
        a_ = S.sb([48, 1], F32, "acol")
        act(a_[:, :], pc(l, "alog", 0, slice(0, 48)), AF.Exp, [P[l]["_t"]], [a_])
        ts(a_[0:32, :], a_[0:32, :], -1.0, None, ALU.mult, None, [a_], [a_])
        acol.append(a_)

    def st_to_xs(l):
        for g in range(2):
            src = ST_[l][64:128, g, :].rearrange("p (pr eo d) -> p pr eo d", pr=4, eo=2)
            dst = XS_[l][64:128, g, :, :].rearrange("p pr (blk d) -> p pr blk d", d=64)[:, :, 1::2, :]
            cp(dst, src, [ST_[l].c(g)], [XS_[l].c(g)])

    xT = S.sb([128, 8, NT], F32, "xT")
    mixT = S.sb([128, 8, NT], F32, "mixT")
    hT = S.sb([128, 8, NT], BF16, "hT")
    ARENA = S.mark()
    seqs = [dict(kind="meta", b=0, T=NMETA, slot=0)]
    for b in range(n_pseq):
        seqs.append(dict(kind="prompt", b=b, T=n_ptiles * NT, slot=b))
    for b in range(n_sseq):
        seqs.append(dict(kind="sample", b=b, T=DSEQ, slot=4 + b))
    if stage < 3:
        seqs = []
    only = cfg.get('only', ['meta', 'prompt', 'sample'])
    seqs = [q for q in seqs if q['kind'] in only]

    def rmsnorm_to(dst_bf, src_f32, ncol_name, l, N):
        mk = S.mark()
        sqb = S.sb([128, 8, N], BF16, "sqb")
        rstd = S.sb([128, N], F32, "rstd")
        for c in range(8):
            act(sqb[:, c, :], src_f32[:, c, 0:N], AF.Square, [src_f32.c(c)], [sqb.c(c)])
        bank = PS[4]
        for c in range(8):
            MM(bank[:, 0:N], ones_b[:, :], sqb[:, c, :], c == 0, c == 7, [ones_b, sqb.c(c)], [bank.r(0, 4 * N)])
        act(rstd[:, :], bank[:, 0:N], AF.Sqrt, [bank.r(0, 4 * N), epsT], [rstd], bias=epsT[:, 0:1], scale=1.0 / D)
        V(lambda e, o_=rstd[:, :]: e.reciprocal(out=o_, in_=o_), [rstd], [rstd])
        for c in range(8):
            stt(dst_bf[:, c, 0:N], src_f32[:, c, 0:N], pc(l, ncol_name, c), rstd[:, :], ALU.mult, ALU.mult,
                [src_f32.c(c), P[l]["_t"], rstd], [dst_bf.c(c)])
        S.reset(mk)

    def proj_fm(l, wsrc, kc, c0, ncols, rhs_fn, rhs_reads, N, evac, prt=128, bankset=(0, 1)):
        mi = 0
        for g0 in range(0, ncols, 512):
            gw = min(512, ncols - g0)
            wt, view = load_w(wsrc[:, c0 + g0:c0 + g0 + gw], kc, gw, prt)
            for m0 in range(0, gw, 128):
                mw = min(128, gw - m0)
                bank = PS[bankset[mi % len(bankset)]]
                for k in range(kc):
                    MM(bank[0:mw, 0:N], view[:, k, m0:m0 + mw], rhs_fn(k), k == 0, k == kc - 1,
                       [wt] + rhs_reads(k), [bank.r(0, 4 * N)])
                evac(mi, bank, mw)
                mi += 1

    for sq in seqs:
        kind, b, slot = sq["kind"], sq["b"], sq["slot"]
        tiles = [(t0, min(NT, sq["T"] - t0)) for t0 in range(0, sq["T"], NT)]
        for l in range(2):
            if kind == "prompt" and upto < 3:
                continue
            if kind == "meta":
                for t in (ST_[l], CH_[l], XR_[l], XI_[l], XS_[l]):
                    V(lambda e, a_=t.h[:]: e.memset(a_, 0.0), [], [t])
            elif kind == "prompt":
                for dst, src in ((ST_[l], STm[l]), (CH_[l], CHm[l]), (XR_[l], XRm[l]), (XI_[l], XIm[l])):
                    cp(dst.h[:], src.h[:], [src], [dst])
                st_to_xs(l)
            else:
                S.dma(OQ, CH_[l][:, :, :].rearrange("p c k -> p (c k)"), I["state_conv"][l, b], [], [CH_[l]], "stin")
                for g in range(2):
                    S.dma(OQ, ST_[l][64:128, g, :], I["state_ssd"][l, b, g], [], [ST_[l].c(g)], "stin")
                S.dma(OQ, XR_[l][:, :], I["state_s5_re"][l, b], [], [XR_[l]], "stin")
                S.dma(OQ, XI_[l][:, :], I["state_s5_im"][l, b], [], [XI_[l]], "stin")
                st_to_xs(l)
                S.reset(ARENA)
                S.dma("pool", vhs[slot, l, 0:PAST, :], I["cache_v"][l, b], [], [("vh%d_%d" % (slot, l), 0, PAST)], "vpre")
                CK = [S.sb([128, 512], F32, "CK") for _ in range(2)]
                KTP = [S.sb([128, 4, 128], BF16, "KTP") for _ in range(2)]
                for tb in range(PAST // 128):
                    ck, kt = CK[tb % 2], KTP[tb % 2]
                    S.dma(OQ, ck[:, :], I["cache_k"][l, b, 128 * tb:128 * tb + 128, :], [], [ck], "ckl%d" % (tb % 2))
                    bank = PS[7]
                    for m in range(4):
                        TR(bank[:, 128 * m:128 * m + 128], ck[:, 128 * m:128 * m + 128], ident_f[:, :], [ck, ident_f],
                           [bank.r(512 * m, 512 * m + 512)])
                    act(kt[:, :, :], bank[:, :].rearrange("p (m t) -> p m t", t=128), AF.Copy, [bank], [kt])
                    S.dma(OQ, kTs[slot, l].rearrange("(m p) t -> p m t", p=128)[:, :, 128 * tb:128 * tb + 128], kt[:, :, :],
                          [kt], [("kT%d_%d" % (slot, l), 128 * tb, 128 * tb + 128)], "kpre%d" % (tb % 2))
        for (t0, N) in tiles:
            S.reset(ARENA)
            pos0 = {"meta": 0, "prompt": NMETA + t0, "sample": PAST}[kind]
            nbk = (N + 127) // 128
            blk = [(tb, min(128, N - 128 * tb)) for tb in range(nbk)]
            if kind == "meta":
                xsrc = I["meta_tokens"]
            elif kind == "prompt":
                xsrc = I["x_prompt"][b, t0:t0 + N, :]
            else:
                xsrc = I["x_sample"][b, t0:t0 + N, :]
            mk0 = S.mark()
            x_tm = S.sb([128, nbk, D], F32, "x_tm")
            for tb, nt in blk:
                S.dma(OQ, x_tm[0:nt, tb, :], xsrc[128 * tb:128 * tb + nt, :], [], [x_tm.c(tb)], "xin")
            for c in range(8):
                bank = PS[c % 4]
                for tb, nt in blk:
                    TR(bank[:, 128 * tb:128 * tb + nt], x_tm[0:nt, tb, 128 * c:128 * c + 128], ident_f[0:nt, 0:nt],
                       [x_tm.c(tb), ident_f], [bank.r(512 * tb, 512 * tb + 4 * nt)])
                act(xT[:, c, 0:N], bank[:, 0:N], AF.Copy, [bank.r(0, 4 * N)], [xT.c(c)])
            S.reset(mk0)
            for l in range(2):
                S.reset(ARENA)
                rmsnorm_to(hT, xT, "nmix", l, N)
                hrd = lambda k: [hT.c(k)]
                hrhs = lambda k, N=N: hT[:, k, 0:N]
                if kind == "meta":
                    kdst = [O["k_prompt"][l, bb, 0:NMETA, :] for bb in range(n_pseq)]
                    vdst = [O["v_prompt"][l, bb, 0:NMETA, :] for bb in range(n_pseq)]
                    slots = list(range(n_pseq))
                elif kind == "prompt":
                    kdst = [O["k_prompt"][l, b, NMETA + t0:NMETA + t0 + N, :]]
                    vdst = [O["v_prompt"][l, b, NMETA + t0:NMETA + t0 + N, :]]
                    slots = [slot]
                else:
                    kdst = [O["k_sample"][l, b, t0:t0 + N, :]]
                    vdst = [O["v_sample"][l, b, t0:t0 + N, :]]
                    slots = [slot]
                mkA = S.mark()
                knT = S.sb([128, 4, N], F32, "knT")
                knb = S.sb([128, 4, N], BF16, "knb")
                qnb = S.sb([128, 4, N], BF16, "qnb")
                ksq = S.sb([128, N], BF16, "ksq")
                rk = S.sb([128, N], F32, "rk")

                def qk_evac(dst32, dstbf, gname):
                    def ev(m, bank, mw):
                        kd = cfg.get("kd", 99)
                        if kd < 1:
                            return
                        act(ksq[:, :], bank[:, 0:N], AF.Square, [bank.r(0, 4 * N)], [ksq])
                        if kd < 2:
                            return
                        b2 = PS[2 + m % 2]
                        MM(b2[:, 0:N], bd64_b[:, :], ksq[:, :], True, True, [bd64_b, ksq], [b2.r(0, 4 * N)])
                        if kd < 3:
                            return
                        act(rk[:, :], b2[:, 0:N], AF.Sqrt, [b2.r(0, 4 * N), epsT], [rk], bias=epsT[:, 0:1], scale=1.0 / 64)
                        if kd < 4:
                            return
                        V(lambda e, o_=rk[:, :]: e.reciprocal(out=o_, in_=o_), [rk], [rk])
                        if kd < 5:
                            return
                        if dst32 is not None:
                            stt(dst32[:, m, :], bank[:, 0:N], pc(l, gname), rk[:, :], ALU.mult, ALU.mult,
                                [bank.r(0, 4 * N), P[l]["_t"], rk], [dst32.c(m)])
                            cp(dstbf[:, m, :], dst32[:, m, :], [dst32.c(m)], [dstbf.c(m)], eng="pool")
                        else:
                            stt(dstbf[:, m, :], bank[:, 0:N], pc(l, gname), rk[:, :], ALU.mult, ALU.mult,
                                [bank.r(0, 4 * N), P[l]["_t"], rk], [dstbf.c(m)])
                    return ev
                proj_fm(l, I["w_in"][l], 8, OFF_K, 512, hrhs, hrd, N, qk_evac(knT, knb, "kn"))
                proj_fm(l, I["w_in"][l], 8, OFF_Q, 512, hrhs, hrd, N, qk_evac(None, qnb, "qn"))
                if cfg.get("kd", 99) < 7:
                    continue
                for sl in slots:
                    S.dma(OQ, kTs[sl, l].rearrange("(m p) t -> p m t", p=128)[:, :, pos0:pos0 + N], knb[:, :, :],
                          [knb], [("kT%d_%d" % (sl, l), pos0, pos0 + N)], "ktw")
                if cfg.get("kd", 99) < 8:
                    continue
                k_tm = S.sb([128, nbk, 512], F32, "k_tm")
                for tb, nt in blk:
                    bank = PS[5]
                    for m in range(4):
                        TR(bank[0:nt, 128 * m:128 * m + 128], knT[:, m, 128 * tb:128 * tb + nt], ident_f[:, :],
                           [knT.c(m), ident_f], [bank.r(512 * m, 512 * m + 512)])
                    act(k_tm[0:nt, tb, :], bank[0:nt, :], AF.Copy, [bank], [k_tm.c(tb)])
                    for dst in kdst:
                        S.dma(OQ, dst[128 * tb:128 * tb + nt, :], k_tm[0:nt, tb, :], [k_tm.c(tb)], [], "kout")
                if cfg.get("kd", 99) < 9:
                    continue
                wt, wv = load_w(I["w_in"][l][:, OFF_V:OFF_V + 512], 8, 512)
                v_tm = S.sb([128, nbk, 512], F32, "v_tm")
                v_bf = S.sb([128, nbk, 512], BF16, "v_bf")
                for tb, nt in blk:
                    bank = PS[6]
                    for kc in range(8):
                        MM(bank[0:nt, :], hT[:, kc, 128 * tb:128 * tb + nt], wv[:, kc, :], kc == 0, kc == 7,
                           [wt, hT.c(kc)], [bank])
                    act(v_tm[0:nt, tb, :], bank[0:nt, :], AF.Copy, [bank], [v_tm.c(tb)])
                    for dst in vdst:
                        S.dma(OQ, dst[128 * tb:128 * tb + nt, :], v_tm[0:nt, tb, :], [v_tm.c(tb)], [], "vout")
                    if cfg.get("kd", 99) < 10:
                        continue
                    cp(v_bf[0:nt, tb, :], v_tm[0:nt, tb, :], [v_tm.c(tb)], [v_bf.c(tb)], eng="pool")
                    if cfg.get("vh", 2) < 2:
                        continue
                    for sl in slots:
                        S.dma(OQ, vhs[sl, l, pos0 + 128 * tb:pos0 + 128 * tb + nt, :], v_bf[0:nt, tb, :],
                              [v_bf.c(tb)], [("vh%d_%d" % (sl, l), pos0 + 128 * tb, pos0 + 128 * tb + nt)], "vhw")
                if upto < 2:
                    continue
                def gates(i):
                    proj_fm(l, I["w_in"][l], 8, OFF_G + 1024 * i, 1024, hrhs, hrd, N,
                            lambda m, bank, mw: act(GT[:, m, :], bank[:, 0:N], AF.Sigmoid, [bank.r(0, 4 * N)], [GT.c(m)]))

                def mix_evac(first):
                    def ev(m, bank, mw):
                        if first:
                            tt(mixT[:, m, 0:N], GT[:, m, :], bank[:, 0:N], ALU.mult, [GT.c(m), bank.r(0, 4 * N)], [mixT.c(m)])
                        else:
                            tt(tmpA[:, :], GT[:, m, :], bank[:, 0:N], ALU.mult, [GT.c(m), bank.r(0, 4 * N)], [tmpA])
                            tt(mixT[:, m, 0:N], mixT[:, m, 0:N], tmpA[:, :], ALU.add, [mixT.c(m), tmpA], [mixT.c(m)])
                    return ev
                nk = pos0 + N
                nb = (nk + 127) // 128
                slot_r = slots[0]
                OC = S.sb([64, 8, N], BF16, "OC")
                mkC = S.mark()
                RS = S.sb([128, N], F32, "RS")
                EXs = [S.sb([128, N], F32, "EX") for _ in range(2)]
                ARGs = [S.sb([128, N], F32, "ARG") for _ in range(2)]
                LPs = [S.sb([128, N], F32, "LP") for _ in range(2)]
                WTs = [S.sb([128, N], BF16, "WT") for _ in range(2)]
                KTb = [S.sb([128, NKMAX], BF16, "KT") for _ in range(2)]
                VBb = [S.sb([128, 17, 128], BF16, "VB") for _ in range(2)]
                nfull, rem = nk // 128, nk % 128
                bctr = 0
                for pr in range(4):
                    KT, VB = KTb[pr % 2], VBb[pr % 2]
                    kname, vname = "kT%d_%d" % (slot_r, l), "vh%d_%d" % (slot_r, l)
                    S.dma(OQ, KT[:, 0:nk], kTs[slot_r, l, 128 * pr:128 * pr + 128, 0:nk], [(kname, 0, nk)], [KT], "ktl%d" % (pr % 2))
                    if nfull:
                        S.dma(OQ, VB[:, 0:nfull, :],
                              vhs[slot_r, l, 0:128 * nfull, 128 * pr:128 * pr + 128].rearrange("(b p) c -> p b c", p=128),
                              [(vname, 0, 128 * nfull)], [VB.r(0, nfull * 256)], "vbl%d" % (pr % 2))
                    if rem:
                        S.dma(OQ, VB[0:rem, nfull, :], vhs[slot_r, l, 128 * nfull:nk, 128 * pr:128 * pr + 128],
                              [(vname, 128 * nfull, nk)], [VB.r(nfull * 256, nfull * 256 + 256)], "vbl%d" % (pr % 2))
                    for hh in range(2):
                        h = 2 * pr + hh
                        R = slice(64 * hh, 64 * hh + 64)
                        V(lambda e, a_=RS[:, :]: e.memset(a_, 0.0), [], [RS])
                        PO = PS[4]
                        blocks = list(reversed(range(nb)))

                        def stage1(bI, i):
                            kb = min(128, nk - 128 * bI)
                            PZ, P2 = PS[i % 2], PS[2 + i % 2]
                            LP, EXb = LPs[i % 2], EXs[i % 2]
                            MM(PZ[0:kb, 0:N], KT[R, 128 * bI:128 * bI + kb], qnb[R, pr, 0:N], True, True,
                               [KT, qnb.c(pr)], [PZ.r(0, 4 * N)])
                            act(EXb[0:kb, :], PZ[0:kb, 0:N], AF.Exp, [PZ.r(0, 4 * N)], [EXb], scale=0.125)
                            act(LP[0:kb, :].bitcast(F32R), EXb[0:kb, :], AF.Ln, [EXb, oneT], [LP], bias=oneT[0:kb, 0:1])
                            if 128 * bI + kb - 1 >= pos0:
                                mt = masks[pos0 - 128 * bI]
                                tt(LP[0:kb, :].bitcast(F32R), LP[0:kb, :], mt[0:kb, 0:N], ALU.mult, [LP, mt], [LP])

                        def stage1b(bI, i):
                            kb = min(128, nk - 128 * bI)
                            PZ, P2 = PS[i % 2], PS[2 + i % 2]
                            LP = LPs[i % 2]
                            MM(PZ[0:kb, 0:N], tri_r[0:kb, 0:kb], LP[0:kb, :].bitcast(F32R), False, True,
                               [tri_r, LP], [PZ.r(0, 4 * N)], skip_group_check=True)
                            if bI > 0:
                                MM(P2[:, 0:N], ones_r[0:kb, :], LP[0:kb, :].bitcast(F32R), True, True, [ones_r, LP], [P2.r(0, 4 * N)])

                        def stage2(bI, i):
                            kb = min(128, nk - 128 * bI)
                            PZ, P2 = PS[i % 2], PS[2 + i % 2]
                            WT, AG = WTs[i % 2], ARGs[i % 2]
                            stt(AG[0:kb, :], PZ[0:kb, 0:N], 0.125, RS[0:kb, :], ALU.mult, ALU.subtract,
                                [PZ.r(0, 4 * N), RS], [AG])
                            if bI > 0:
                                tt(RS[:, :], RS[:, :], P2[:, 0:N], ALU.add, [RS, P2.r(0, 4 * N)], [RS])
                            act(WT[0:kb, :], AG[0:kb, :], AF.Exp, [AG], [WT])
                            if 128 * bI + kb - 1 >= pos0:
                                mt = masks[pos0 - 128 * bI]
                                tt(WT[0:kb, :], WT[0:kb, :], mt[0:kb, 0:N], ALU.mult, [WT, mt], [WT])
                            MM(PO[0:64, 0:N], VB[0:kb, bI, 64 * hh:64 * hh + 64], WT[0:kb, :], i == 0, bI == 0,
                               [VB.r(bI * 256, bI * 256 + 256), WT], [PO.r(0, 4 * N)])
                        for i, bI in enumerate(blocks):
                            stage1(bI, i)
                            if i >= 1:
                                stage2(blocks[i - 1], i - 1)
                            stage1b(bI, i)
                        stage2(blocks[-1], len(blocks) - 1)
                        act(OC[:, h, :], PO[0:64, 0:N], AF.Copy, [PO.r(0, 4 * N)], [OC.c(h)])
                S.reset(mkC)
                GT = S.sb([128, 8, N], F32, "GT")
                tmpA = S.sb([128, N], F32, "tmpA")
                gates(2)
                proj_fm(l, I["w_lift_c"][l], 8, 0, 1024, lambda h_: OC[:, h_, :], lambda h_: [OC.c(h_)], N, mix_evac(True), prt=64)
                if upto < 3:
                    continue
                S.reset(mkA)
                GT = S.sb([128, 8, N], F32, "GT")
                tmpA = S.sb([128, N], F32, "tmpA")
                Q = min(64, N)
                nch = N // Q
                xp = S.sb([128, 10, N + 3], F32, "xp")
                XBC = S.sb([128, 10, N], BF16, "XBC")
                yT = S.sb([128, 8, N], F32, "yT")
                cp(xp[:, :, 0:3], CH_[l][:, :, :], [CH_[l]], [xp])
                proj_fm(l, I["w_in"][l], 8, OFF_XBC, 1280, hrhs, hrd, N,
                        lambda m, bank, mw: act(xp[:, m, 3:3 + N], bank[:, 0:N], AF.Copy, [bank.r(0, 4 * N)], [xp.c(m)]))
                cp(CH_[l][:, :, :], xp[:, :, N:N + 3], [xp], [CH_[l]])
                for c in range(10):
                    ts(tmpA[:, :], xp[:, c, 0:N], pc(l, "cw", c), pc(l, "cb", c), ALU.mult, ALU.add, [xp.c(c), P[l]["_t"]], [tmpA])
                    for k in range(1, 4):
                        stt(tmpA[:, :], xp[:, c, k:k + N], pc(l, "cw", 10 * k + c), tmpA[:, :], ALU.mult, ALU.add,
                            [xp.c(c), P[l]["_t"], tmpA], [tmpA])
                    act(XBC[:, c, :], tmpA[:, :], AF.Silu, [tmpA], [XBC.c(c)])
                if cfg.get("sd", 99) < 2:
                    continue
                PD = PS[2]
                for k in range(8):
                    MM(PD[0:48, 0:N], WDT[l][:, k, :], hT[:, k, 0:N], k == 0, k == 7, [WDT[l], hT.c(k)], [PD.r(0, 4 * N)])
                dtT = S.sb([48, N], F32, "dtT")
                dA = S.sb([48, N], F32, "dA")
                AC = S.sb([16, N], F32, "AC")
                act(dtT[:, :], PD[0:48, 0:N], AF.Exp, [PD.r(0, 4 * N), P[l]["_t"]], [dtT], bias=pc(l, "dtb", 0, slice(0, 48)))
                act(dtT[:, :], dtT[:, :], AF.Ln, [dtT, oneT], [dtT], bias=oneT[0:48, 0:1])
                ts(dA[:, :], dtT[:, :], acol[l][:, 0:1], None, ALU.mult, None, [dtT, acol[l]], [dA])
                if cfg.get("sd", 99) < 3:
                    continue
                Ets = [S.sb([128, 8, Q], F32, "Et") for _ in range(2)]
                Wts = [S.sb([128, 8, Q], BF16, "Wt") for _ in range(2)]
                if Q < 64:
                    for w__ in Wts:
                        V(lambda e, a_=w__[:, :, :]: e.memset(a_, 0.0), [], [w__])
                dt_tm = S.sb([64, 16], F32, "dt_tm")
                coef = S.sb([64, 8], F32, "coef")
                XW = S.sb([64, 8, 64], BF16, "XW")
                ectr = 0
                for ci in range(nch):
                    cs = slice(ci * Q, ci * Q + Q)
                    LTt = LT[ci % 2]
                    scan(AC[0:16, cs], ones48[0:16, 0:Q], dA[0:16, cs], 0.0, [ones48, dA], [AC])
                    scan(LTt[32:48, 0:Q], ones48[32:48, 0:Q], dA[32:48, cs], 0.0, [ones48, dA], [LTt])
                    PT = PS[7]
                    MM(PT[0:Q, 0:16], dtT[0:16, cs], ident_f[0:16, 0:16], True, True, [dtT, ident_f], [PT.r(0, 64)])
                    act(dt_tm[0:Q, :], PT[0:Q, 0:16], AF.Copy, [PT.r(0, 64)], [dt_tm])
                    if cfg.get("sd", 99) < 4:
                        continue
                    PY = PS[3]
                    for g in range(2):
                        GR = slice(64 * g, 64 * g + 64)
                        Et, Wt = Ets[ectr % 2], Wts[ectr % 2]
                        ectr += 1
                        tt(RF[g][0:16, :, 0:Q], AC[0:16, cs].unsqueeze(1).broadcast_to([16, 8, Q]),
                           DEL[g][0:16, :].unsqueeze(2).broadcast_to([16, 8, Q]), ALU.mult, [AC, DEL[g]], [RF[g]])
                        PE_ = PS[4]
                        pev = PE_[:, 0:8 * Q].rearrange("p (h i) -> p h i", i=Q)
                        MM(pev, LTt[:, :], RF[g][:, :, 0:Q], True, True, [LTt, RF[g]], [PE_])
                        act(Et[:, :, :], pev, AF.Exp, [PE_], [Et])
                        if cfg.get("sd", 99) < 5:
                            continue
                        PG = PS[5]
                        MM(PG[0:Q, 0:Q], XBC[GR, 8, cs], XBC[GR, 9, cs], True, True, [XBC.c(8), XBC.c(9)], [PG.r(0, 256)])
                        tt(Wt[0:Q, :, :], Et[0:Q, :, :], PG[0:Q, 0:Q].unsqueeze(1).broadcast_to([Q, 8, Q]), ALU.mult,
                           [Et, PG.r(0, 256)], [Wt])
                        MM(PG[:, 128:128 + Q], SHIFT[GR, :], XBC[GR, 9, cs], True, True, [SHIFT, XBC.c(9)], [PG.r(512, 768)])
                        tt(Wt[64:128, :, :], Et[64:128, :, :], PG[64:128, 128:128 + Q].unsqueeze(1).broadcast_to([64, 8, Q]), ALU.mult,
                           [Et, PG.r(512, 768)], [Wt])
                        if cfg.get("sd", 99) < 6:
                            continue
                        PX = PS[6]
                        for pr in range(4):
                            MM(PX[0:Q, 128 * pr:128 * pr + 128], XBC[:, 4 * g + pr, cs], ident_b[:, :], True, True,
                               [XBC.c(4 * g + pr), ident_b], [PX.r(512 * pr, 512 * pr + 512)])
                        dstX = XS_[l][0:Q, g, :, :].rearrange("q pr (blk d) -> q pr blk d", d=64)[:, :, 1::2, :]
                        tt(dstX, PX[0:Q, :].rearrange("q (pr eo d) -> q pr eo d", pr=4, eo=2),
                           dt_tm[0:Q, 8 * g:8 * g + 8].rearrange("q (pr eo) -> q pr eo", eo=2).unsqueeze(3).broadcast_to([Q, 4, 2, 64]),
                           ALU.mult, [PX, dt_tm], [XS_[l].c(g)])
                        if cfg.get("sd", 99) < 7:
                            continue
                        for pr in range(4):
                            k = 4 * g + pr
                            for eo in range(2):
                                win = slice(64 + 64 * eo, 192 + 64 * eo)
                                MM(PY[:, k * Q:k * Q + Q], XS_[l][:, g, pr, win], Wt[:, 2 * pr + eo, :], eo == 0, eo == 1,
                                   [XS_[l].c(g), Wt], [PY.r(4 * k * Q, 4 * k * Q + 4 * Q)])
                        if cfg.get("sd", 99) < 8:
                            continue
                        tt(coef[0:Q, :], dt_tm[0:Q, 8 * g:8 * g + 8], Et[0:Q, :, Q - 1], ALU.mult, [dt_tm, Et], [coef])
                        tt(XW[0:Q, :, :], PX[0:Q, :].rearrange("q (h d) -> q h d", d=64),
                           coef[0:Q, :].unsqueeze(2).broadcast_to([Q, 8, 64]), ALU.mult, [PX, coef], [XW])
                        MM(PG[0:Q, 64:128], XBC[GR, 8, cs], ident_b[GR, 64 * g:64 * g + 64], True, True,
                           [XBC.c(8), ident_b], [PG.r(256, 512)])
                        cp(BTOK[0:Q, 64:128], PG[0:Q, 64:128], [PG.r(256, 512)], [BTOK])
                        PSt = PS[2]
                        MM(PSt[:, :], BTOK[0:Q, :], XW[0:Q, :, :], True, True, [BTOK, XW], [PSt])
                        stv = ST_[l][64:128, g, :].rearrange("p (h d) -> p h d", d=64)
                        tt(stv, stv, Et[64:128, :, Q - 1].unsqueeze(2).broadcast_to([64, 8, 64]), ALU.mult, [ST_[l].c(g), Et], [ST_[l].c(g)])
                        tt(ST_[l][64:128, g, :], ST_[l][64:128, g, :], PSt[64:128, :], ALU.add, [ST_[l].c(g), PSt], [ST_[l].c(g)])
                        src = ST_[l][64:128, g, :].rearrange("p (pr eo d) -> p pr eo d", pr=4, eo=2)
                        dst = XS_[l][64:128, g, :, :].rearrange("p pr (blk d) -> p pr blk d", d=64)[:, :, 1::2, :]
                        cp(dst, src, [ST_[l].c(g)], [XS_[l].c(g)])
                    if cfg.get("sd", 99) < 9:
                        continue
                    act(yT[:, :, cs], PY[:, 0:8 * Q].rearrange("p (k i) -> p k i", i=Q), AF.Copy, [PY], [yT])
                if cfg.get("sd", 99) < 10:
                    continue
                for k in range(8):
                    stt(yT[:, k, :], XBC[:, k, :], pc(l, "dsk", k), yT[:, k, :], ALU.mult, ALU.add, [XBC.c(k), P[l]["_t"], yT.c(k)], [yT.c(k)])

                def z_evac(m, bank, mw):
                    act(tmpA[:, :], bank[:, 0:N], AF.Silu, [bank.r(0, 4 * N)], [tmpA])
                    tt(yT[:, m, :], yT[:, m, :], tmpA[:, :], ALU.mult, [yT.c(m), tmpA], [yT.c(m)])
                proj_fm(l, I["w_in"][l], 8, OFF_Z, 1024, hrhs, hrd, N, z_evac)
                ysq = S.sb([128, 8, N], BF16, "ysq")
                YN = S.sb([128, 8, N], BF16, "YN")
                rg = S.sb([128, N], F32, "rg")
                for k in range(8):
                    act(ysq[:, k, :], yT[:, k, :], AF.Square, [yT.c(k)], [ysq.c(k)])
                for g in range(2):
                    bank = PS[4]
                    for k in range(4 * g, 4 * g + 4):
                        MM(bank[:, 0:N], ones_b[:, :], ysq[:, k, :], k == 4 * g, k == 4 * g + 3, [ones_b, ysq.c(k)], [bank.r(0, 4 * N)])
                    act(rg[:, :], bank[:, 0:N], AF.Sqrt, [bank.r(0, 4 * N), epsT], [rg], bias=epsT[:, 0:1], scale=1.0 / 512)
                    V(lambda e, o_=rg[:, :]: e.reciprocal(out=o_, in_=o_), [rg], [rg])
                    for k in range(4 * g, 4 * g + 4):
                        stt(YN[:, k, :], yT[:, k, :], pc(l, "nssd", k), rg[:, :], ALU.mult, ALU.mult, [yT.c(k), P[l]["_t"], rg], [YN.c(k)])
                gates(0)
                proj_fm(l, I["w_lift_a"][l], 8, 0, 1024, lambda k_: YN[:, k_, :], lambda k_: [YN.c(k_)], N, mix_evac(False))
                if upto < 4:
                    continue
                S.reset(mkA)
                GT = S.sb([128, 8, N], F32, "GT")
                tmpA = S.sb([128, N], F32, "tmpA")
                BC = S.sb([128, 4, 16, 128], BF16, "BC")
                if ("bc", l) not in WG:
                    scr_ = nc.dram_tensor("bcs%d" % l, [128, 4 * 2048], BF16, kind="Internal").ap()
                    S.dma("pool", scr_.rearrange("p (i n) -> p i n", i=4), I["s5bc"][l].rearrange("i p n -> p i n"),
                          [], [("bcs", l, l + 1)], "wcast")
                    WG[("bc", l)] = scr_
                S.dma(OQ, BC[:, :, :, :].rearrange("p i c n -> p (i c n)"), WG[("bc", l)][:, :], [("bcs", l, l + 1)], [BC], "bcl")
                uT = S.sb([128, 4, N], BF16, "uT")
                u32 = S.sb([128, 4, N], F32, "u32")

                def u_evac(m, bank, mw):
                    act(u32[:, m, :], bank[:, 0:N], AF.Copy, [bank.r(0, 4 * N)], [u32.c(m)])
                    cp(uT[:, m, :], u32[:, m, :], [u32.c(m)], [uT.c(m)], eng="pool")
                proj_fm(l, I["w_in"][l], 8, OFF_U, 512, hrhs, hrd, N, u_evac)
                A_ = S.sb([128, 8, QS], F32, "A_")
                B_ = S.sb([128, 8, QS], F32, "B_")
                C_ = S.sb([128, 8, QS], F32, "C_")
                D_ = S.sb([128, 8, QS], F32, "D_")
                CDs = [(C_, D_), (S.sb([128, 8, QS], F32, "C2"), S.sb([128, 8, QS], F32, "D2"))]
                E_ = S.sb([128, 8, QS], F32, "E_")
                F_ = S.sb([128, 8, QS], F32, "F_")
                xrb = S.sb([128, 8, QS], BF16, "xrb")
                xib = S.sb([128, 8, QS], BF16, "xib")
                WIr = S.sb([128, 8], F32, "WIr")
                WIi = S.sb([128, 8], F32, "WIi")
                t8 = S.sb([128, 8], F32, "t8")
                cosT, sinT, tbr, tbi, magT = S5TAB[l]
                XR, XI = XR_[l], XI_[l]
                nsub = (N + QS - 1) // QS
                for s_ in range(nsub):
                    ncol = min(QS, N - s_ * QS)
                    cs = slice(s_ * QS, s_ * QS + ncol)
                    for hf in range(2):
                        ch = slice(8 * hf, 8 * hf + 8)
                        Pre, Pim = PS[0], PS[1]
                        for cc in range(8):
                            c = 8 * hf + cc
                            MM(Pre[:, cc * QS:cc * QS + ncol], BC[:, 0, c, :], uT[:, c // 4, cs], True, True,
                               [BC.c(0), uT.c(c // 4)], [Pre.r(4 * cc * QS, 4 * cc * QS + 4 * ncol)])
                            MM(Pim[:, cc * QS:cc * QS + ncol], BC[:, 1, c, :], uT[:, c // 4, cs], True, True,
                               [BC.c(1), uT.c(c // 4)], [Pim.r(4 * cc * QS, 4 * cc * QS + 4 * ncol)])
                        PreV = Pre[:, :].rearrange("p (c t) -> p c t", t=QS)[:, :, 0:ncol]
                        PimV = Pim[:, :].rearrange("p (c t) -> p c t", t=QS)[:, :, 0:ncol]
                        C_, D_ = CDs[(2 * s_ + hf) % 2]
                        a_, b_, c_, d_ = A_[:, :, 0:ncol], B_[:, :, 0:ncol], C_[:, :, 0:ncol], D_[:, :, 0:ncol]
                        tr_, ti_ = tbr[:, ch, 0:ncol], tbi[:, ch, 0:ncol]
                        tt(a_, PreV, tr_, ALU.mult, [Pre, tbr], [A_])
                        tt(b_, PimV, ti_, ALU.mult, [Pim, tbi], [B_])
                        tt(a_, a_, b_, ALU.subtract, [A_, B_], [A_])
                        tt(b_, PreV, ti_, ALU.mult, [Pre, tbi], [B_])
                        tt(c_, PimV, tr_, ALU.mult, [Pim, tbr], [C_])
                        tt(b_, b_, c_, ALU.add, [B_, C_], [B_])
                        cos1, sin1 = cosT[:, ch, 1], sinT[:, ch, 1]
                        tt(WIr[:, :], cos1, XR[:, ch], ALU.mult, [cosT, XR], [WIr])
                        tt(t8[:, :], sin1, XI[:, ch], ALU.mult, [sinT, XI], [t8])
                        tt(WIr[:, :], WIr[:, :], t8[:, :], ALU.subtract, [WIr, t8], [WIr])
                        tt(WIi[:, :], sin1, XR[:, ch], ALU.mult, [sinT, XR], [WIi])
                        tt(t8[:, :], cos1, XI[:, ch], ALU.mult, [cosT, XI], [t8])
                        tt(WIi[:, :], WIi[:, :], t8[:, :], ALU.add, [WIi, t8], [WIi])
                        for cc in range(8):
                            c = 8 * hf + cc
                            scan(C_[:, cc, 0:ncol], magT[:, c, 0:ncol], A_[:, cc, 0:ncol], WIr[:, cc:cc + 1], [magT, A_, WIr], [C_.c(cc)])
                            scan(D_[:, cc, 0:ncol], magT[:, c, 0:ncol], B_[:, cc, 0:ncol], WIi[:, cc:cc + 1], [magT, B_, WIi], [D_.c(cc)])
                        cosL, sinL = cosT[:, ch, ncol - 1], sinT[:, ch, ncol - 1]
                        wrl, wil = C_[:, :, ncol - 1], D_[:, :, ncol - 1]
                        tt(XR[:, ch], cosL, wrl, ALU.mult, [cosT, C_], [XR])
                        tt(t8[:, :], sinL, wil, ALU.mult, [sinT, D_], [t8])
                        tt(XR[:, ch], XR[:, ch], t8[:, :], ALU.subtract, [XR, t8], [XR])
                        tt(XI[:, ch], sinL, wrl, ALU.mult, [sinT, C_], [XI])
                        tt(t8[:, :], cosL, wil, ALU.mult, [cosT, D_], [t8])
                        tt(XI[:, ch], XI[:, ch], t8[:, :], ALU.add, [XI, t8], [XI])
                        cos_, sin_ = cosT[:, ch, 0:ncol], sinT[:, ch, 0:ncol]
                        e_, f_ = E_[:, :, 0:ncol], F_[:, :, 0:ncol]
                        tt(e_, c_, cos_, ALU.mult, [C_, cosT], [E_])
                        tt(f_, d_, sin_, ALU.mult, [D_, sinT], [F_])
                        tt(xrb[:, :, 0:ncol], e_, f_, ALU.subtract, [E_, F_], [xrb])
                        tt(e_, c_, sin_, ALU.mult, [C_, sinT], [E_])
                        tt(f_, d_, cos_, ALU.mult, [D_, cosT], [F_])
                        stt(xib[:, :, 0:ncol], e_, -1.0, f_, ALU.mult, ALU.subtract, [E_, F_], [xib])
                        for cc in range(8):
                            c = 8 * hf + cc
                            j, m_ = c // 4, c % 4
                            bank = PS[6 + j // 2]
                            col0 = (j % 2) * N + s_ * QS
                            MM(bank[:, col0:col0 + ncol], BC[:, 2, c, :], xrb[:, cc, 0:ncol], m_ == 0, False,
                               [BC.c(2), xrb], [bank.r(4 * col0, 4 * col0 + 4 * ncol)])
                            MM(bank[:, col0:col0 + ncol], BC[:, 3, c, :], xib[:, cc, 0:ncol], False, m_ == 3,
                               [BC.c(3), xib], [bank.r(4 * col0, 4 * col0 + 4 * ncol)])
                GB = S.sb([128, 4, N], BF16, "GB")
                t5 = S.sb([128, N], F32, "t5")
                t6 = S.sb([128, N], F32, "t6")
                for j in range(4):
                    bank = PS[6 + j // 2]
                    col0 = (j % 2) * N
                    stt(t5[:, :], u32[:, j, :], pc(l, "d5", j), bank[:, col0:col0 + N], ALU.mult, ALU.add,
                        [u32.c(j), P[l]["_t"], bank.r(4 * col0, 4 * col0 + 4 * N)], [t5])
                    tt(t6[:, :], t5[:, :], t5[:, :], ALU.mult, [t5], [t6])
                    ts(t6[:, :], t6[:, :], 0.044715, 1.0, ALU.mult, ALU.add, [t6], [t6])
                    tt(t6[:, :], t6[:, :], t5[:, :], ALU.mult, [t6, t5], [t6])
                    act(t6[:, :], t6[:, :], AF.Sigmoid, [t6], [t6], scale=1.5957691216057308)
                    tt(GB[:, j, :], t5[:, :], t6[:, :], ALU.mult, [t5, t6], [GB.c(j)])
                gates(1)
                for half in range(2):
                    wtA, vA = load_w(I["w_glu"][l][:, 512 * half:512 * half + 512], 4, 512)
                    wtB, vB = load_w(I["w_glu"][l][:, 1024 + 512 * half:1024 + 512 * half + 512], 4, 512)
                    for mm_ in range(4):
                        m = 4 * half + mm_
                        P1, P2 = PS[0], PS[1]
                        for j in range(4):
                            MM(P1[:, 0:N], vA[:, j, 128 * mm_:128 * mm_ + 128], GB[:, j, :], j == 0, j == 3, [wtA, GB.c(j)], [P1.r(0, 4 * N)])
                        for j in range(4):
                            MM(P2[:, 0:N], vB[:, j, 128 * mm_:128 * mm_ + 128], GB[:, j, :], j == 0, j == 3, [wtB, GB.c(j)], [P2.r(0, 4 * N)])
                        act(t5[:, :], P2[:, 0:N], AF.Sigmoid, [P2.r(0, 4 * N)], [t5])
                        tt(t5[:, :], t5[:, :], P1[:, 0:N], ALU.mult, [t5, P1.r(0, 4 * N)], [t5])
                        tt(t5[:, :], t5[:, :], GT[:, m, :], ALU.mult, [t5, GT.c(m)], [t5])
                        tt(mixT[:, m, 0:N], mixT[:, m, 0:N], t5[:, :], ALU.add, [mixT.c(m), t5], [mixT.c(m)])
                if upto < 5:
                    continue
                S.reset(mkA)
                mixb = S.sb([128, 8, N], BF16, "mixb")
                for c in range(8):
                    cp(mixb[:, c, :], mixT[:, c, 0:N], [mixT.c(c)], [mixb.c(c)], eng="pool")

                def res_evac(m, bank, mw):
                    tt(xT[:, m, 0:N], xT[:, m, 0:N], bank[:, 0:N], ALU.add, [xT.c(m), bank.r(0, 4 * N)], [xT.c(m)])
                proj_fm(l, I["w_out"][l], 8, 0, 1024, lambda k_: mixb[:, k_, :], lambda k_: [mixb.c(k_)], N, res_evac, bankset=(0, 1, 2, 3))
                rmsnorm_to(hT, xT, "nffn", l, N)
                AT = S.sb([128, 32, N], BF16, "AT")
                t5 = S.sb([128, N], F32, "t5")

                def up_evac(m, bank, mw):
                    act(t5[:, :], bank[:, 0:N], AF.Relu, [bank.r(0, 4 * N)], [t5])
                    tt(AT[:, m, :], t5[:, :], t5[:, :], ALU.mult, [t5], [AT.c(m)])
                proj_fm(l, I["w_up"][l], 8, 0, 4096, hrhs, hrd, N, up_evac, bankset=(0, 1, 2, 3))
                for m in range(8):
                    wt, vw = load_w(I["w_down"][l][:, 128 * m:128 * m + 128], 32, 128)
                    bank = PS[m % 4]
                    for k in range(32):
                        MM(bank[:, 0:N], vw[:, k, :], AT[:, k, :], k == 0, k == 31, [wt, AT.c(k)], [bank.r(0, 4 * N)])
                    res_evac(m, bank, 128)
            if upto >= 5 and kind != "meta":
                S.reset(ARENA)
                ydst = O["y_prompt"][b, t0:t0 + N, :] if kind == "prompt" else O["y_sample"][b, t0:t0 + N, :]
                y_tm = S.sb([128, nbk, D], F32, "y_tm")
                for tb, nt in blk:
                    for hc in range(2):
                        bank = PS[5 + hc]
                        for c4 in range(4):
                            c = 4 * hc + c4
                            TR(bank[0:nt, 128 * c4:128 * c4 + 128], xT[:, c, 128 * tb:128 * tb + nt], ident_f[:, :],
                               [xT.c(c), ident_f], [bank.r(512 * c4, 512 * c4 + 512)])
                        act(y_tm[0:nt, tb, 512 * hc:512 * hc + 512], bank[0:nt, :], AF.Copy, [bank], [y_tm.c(tb)])
                    S.dma(OQ, ydst[128 * tb:128 * tb + nt, :], y_tm[0:nt, tb, :], [y_tm.c(tb)], [], "yout")
        if upto >= 3:
            for l in range(2):
                if kind == "meta":
                    for dst, src in ((STm[l], ST_[l]), (CHm[l], CH_[l]), (XRm[l], XR_[l]), (XIm[l], XI_[l])):
                        cp(dst.h[:], src.h[:], [src], [dst])
                else:
                    sfx = "prompt" if kind == "prompt" else "sample"
                    S.dma(OQ, O["conv_" + sfx][l, b], CH_[l][:, :, :].rearrange("p c k -> p (c k)"), [CH_[l]], [], "stout")
                    for g in range(2):
                        S.dma(OQ, O["ssd_" + sfx][l, b, g], ST_[l][64:128, g, :], [ST_[l].c(g)], [], "stout")
                    S.dma(OQ, O["s5re_" + sfx][l, b], XR_[l][:, :], [XR_[l]], [], "stout")
                    S.dma(OQ, O["s5im_" + sfx][l, b], XI_[l][:, :], [XI_[l]], [], "stout")
    OUT_STREAMS[:] = ["kout", "vout", "yout", "stout"]
    S.emit(OUT_STREAMS)
    return nc


OUT_STREAMS = []

_NC_CACHE = {}


def pack_s5bc(inp):
    out = np.zeros((2, 4, 128, 16, 128), np.float32)
    for l in range(2):
        for i, nm in enumerate(("b_re", "b_im")):
            b = np.asarray(inp[nm], np.float32)[l]
            for c in range(16):
                m = c % 4
                for two in range(2):
                    g = 2 * c + two
                    r0 = 32 * m + 16 * two
                    out[l, i, r0:r0 + 16, c, 64 * two:64 * two + 64] = b[g].T
        for i, nm in enumerate(("c_re", "c_im")):
            cc = np.asarray(inp[nm], np.float32)[l]
            for c in range(16):
                m = c % 4
                for two in range(2):
                    g = 2 * c + two
                    c0 = 32 * m + 16 * two
                    out[l, 2 + i, 64 * two:64 * two + 64, c, c0:c0 + 16] = cc[g].T
    return out.reshape(2, 4, 128, 2048)


def kernel(**inp):
    cfg = inp.pop("_cfg", None)
    key = repr(cfg)
    if key not in _NC_CACHE:
        _NC_CACHE[key] = build(cfg)
    nc = _NC_CACHE[key]
    f = lambda a: np.ascontiguousarray(np.asarray(a, dtype=np.float32))
    wnames = ["meta_tokens", "norm_mix", "w_in", "conv_w", "conv_b", "dt_bias", "a_log", "d_ssd", "norm_ssd",
              "lam_re", "lam_im", "log_step", "w_glu", "q_norm", "k_norm",
              "w_lift_a", "w_lift_c", "w_out", "norm_ffn", "w_up", "w_down"]
    shared = {n: f(inp[n]) for n in wnames}
    shared["d_s5"] = f(inp["d_s5"]).reshape(2, 512)
    shared["par"] = pack_params(inp)
    shared["s5bc"] = pack_s5bc(inp)
    in_maps = []
    for c in range(8):
        m = dict(shared)
        m["x_prompt"] = f(inp["x_prompt"][4 * c:4 * c + 4])
        m["x_sample"] = f(inp["x_sample"][2 * c:2 * c + 2])
        m["cache_k"] = f(inp["cache_k"][:, 2 * c:2 * c + 2]).reshape(2, 2, PAST, 512)
        m["cache_v"] = f(inp["cache_v"][:, 2 * c:2 * c + 2]).reshape(2, 2, PAST, 512)
        sc = f(inp["state_conv"][:, 2 * c:2 * c + 2]).reshape(2, 2, 3, 10, 128)
        m["state_conv"] = np.ascontiguousarray(sc.transpose(0, 1, 4, 3, 2)).reshape(2, 2, 128, 30)
        ss = f(inp["state_ssd"][:, 2 * c:2 * c + 2])
        m["state_ssd"] = np.ascontiguousarray(ss.transpose(0, 1, 2, 5, 3, 4)).reshape(2, 2, 2, 64, 512)
        for nm in ("state_s5_re", "state_s5_im"):
            a5 = f(inp[nm][:, 2 * c:2 * c + 2]).reshape(2, 2, 16, 2, 64)
            m[nm] = np.ascontiguousarray(a5.transpose(0, 1, 3, 4, 2)).reshape(2, 2, 128, 16)
        in_maps.append(m)
    ncores = (cfg or {}).get('ncores', 8)
    res = run_bass_kernel_spmd(nc, in_maps[:ncores], core_ids=list(range(ncores)))
    R = list(res.results) + [res.results[0]] * (8 - ncores)
    cat = lambda k, ax: np.concatenate([np.asarray(R[c][k], dtype=np.float32) for c in range(8)], axis=ax)
    y_p = cat("y_prompt", 0)
    y_s = cat("y_sample", 0)
    k_p = cat("k_prompt", 1).reshape(2, 32, TP, 8, 64)
    v_p = cat("v_prompt", 1).reshape(2, 32, TP, 8, 64)
    unconv = lambda a: np.ascontiguousarray(a.reshape(2, -1, 128, 10, 3).transpose(0, 1, 4, 3, 2)).reshape(2, -1, 3, 1280)
    unssd = lambda a: np.ascontiguousarray(a.reshape(2, -1, 2, 64, 8, 64).transpose(0, 1, 2, 4, 5, 3))
    uns5 = lambda a: np.ascontiguousarray(a.reshape(2, -1, 2, 64, 16).transpose(0, 1, 4, 2, 3)).reshape(2, -1, 32, 64)
    conv_p = unconv(cat("conv_prompt", 1))
    ssd_p = unssd(cat("ssd_prompt", 1))
    s5r_p = uns5(cat("s5re_prompt", 1))
    s5i_p = uns5(cat("s5im_prompt", 1))
    k_s = cat("k_sample", 1).reshape(2, 16, DSEQ, 8, 64)
    v_s = cat("v_sample", 1).reshape(2, 16, DSEQ, 8, 64)
    conv_s = unconv(cat("conv_sample", 1))
    ssd_s = unssd(cat("ssd_sample", 1))
    s5r_s = uns5(cat("s5re_sample", 1))
    s5i_s = uns5(cat("s5im_sample", 1))
    return (y_p, y_s, k_p, v_p, conv_p, ssd_p, s5r_p, s5i_p, k_s, v_s, conv_s, ssd_s, s5r_s, s5i_s)
```

```python
import math
import bisect
import numpy as np
import concourse.bass as bass
import concourse.mybir as mybir
from concourse.bass_utils import run_bass_kernel_spmd

F32 = mybir.dt.float32
F32R = mybir.dt.float32r
BF16 = mybir.dt.bfloat16
AF = mybir.ActivationFunctionType
ALU = mybir.AluOpType
AX = mybir.AxisListType

D = 1024
NMETA = 16
SEQ = 2048
TP = NMETA + SEQ
PAST = 2048
DSEQ = 64
NKMAX = 2112
INC = 7440
OFF_Z, OFF_XBC, OFF_DT, OFF_U, OFF_Q, OFF_K, OFF_V, OFF_G = 0, 1024, 2304, 2320, 2832, 3344, 3856, 4368
EPS = 1e-6
NT = 256
QS = 64
NEG = -30000.0
PCOL = {}
_c = 0
for _nm, _w in [("nmix", 8), ("nffn", 8), ("nssd", 8), ("cw", 40), ("cb", 10), ("d5", 4), ("dtb", 1), ("alog", 1),
                ("dsk", 8), ("qn", 1), ("kn", 1), ("lre", 16), ("lim", 16), ("lst", 16)]:
    PCOL[_nm] = (_c, _w)
    _c += _w
NPAR = _c


def pack_params(inp):
    par = np.zeros((2, 128, NPAR), np.float32)
    g = lambda n: np.asarray(inp[n], np.float32)

    def put(l, nm, arr):
        c0, w = PCOL[nm]
        par[l, :, c0:c0 + w] = arr

    for l in range(2):
        put(l, "nmix", g("norm_mix")[l].reshape(8, 128).T)
        put(l, "nffn", g("norm_ffn")[l].reshape(8, 128).T)
        put(l, "nssd", g("norm_ssd")[l].reshape(8, 128).T)
        cw = g("conv_w")[l].reshape(4, 10, 128)
        put(l, "cw", cw.transpose(2, 0, 1).reshape(128, 40))
        put(l, "cb", g("conv_b")[l].reshape(10, 128).T)
        put(l, "d5", g("d_s5")[l].reshape(4, 128).T)
        for nm, src in (("dtb", "dt_bias"), ("alog", "a_log")):
            col = np.zeros((128, 1), np.float32)
            col[0:16, 0] = g(src)[l]
            col[32:48, 0] = g(src)[l]
            put(l, nm, col)
        dsk = np.zeros((128, 8), np.float32)
        d = g("d_ssd")[l]
        for hh in range(2):
            dsk[64 * hh:64 * hh + 64, :] = d[hh::2][None, :]
        put(l, "dsk", dsk)
        put(l, "qn", np.tile(g("q_norm")[l], 2)[:, None])
        put(l, "kn", np.tile(g("k_norm")[l], 2)[:, None])
        for nm, src in (("lre", "lam_re"), ("lim", "lam_im")):
            a = g(src)[l].reshape(16, 2, 64)
            put(l, nm, a.transpose(1, 2, 0).reshape(128, 16))
        ls = g("log_step")[l].reshape(16, 2)
        put(l, "lst", np.repeat(ls.T[:, None, :], 64, axis=1).reshape(128, 16))
    return par
DTSZ = {F32: 4, F32R: 4, BF16: 2}


class Tile:
    def __init__(self, h, space, lo, hi):
        self.h, self.space, self.lo, self.hi = h, space, lo, hi

    def __getitem__(self, k):
        return self.h[k]

    @property
    def all(self):
        return (self.space, self.lo, self.hi)

    def r(self, lo, hi):
        return (self.space, self.lo + lo, self.lo + hi)

    def c(self, i, n=1):
        return (self.space, self.lo + i * self.cb, self.lo + (i + n) * self.cb)


def _reg(x):
    return x.all if isinstance(x, Tile) else x


class Sched:
    ENG = ["pe", "act", "dve", "pool", "sp"]

    def __init__(self, nc):
        self.nc = nc
        self.ops = []
        self.segs = {}
        self.off = 16512
        self.cnt = 0
        self.stream_n = {}
        self.psb = []

    def sb(self, shape, dtype, name="t"):
        nb = int(np.prod(shape[1:])) * DTSZ[dtype]
        off = (self.off + 31) // 32 * 32
        self.cnt += 1
        h = self.nc.alloc_sbuf_tensor_at(f"{name}{self.cnt}", list(shape), dtype, offset=off)
        self.off = off + nb
        self.peak = max(getattr(self, "peak", 0), self.off)
        assert self.off <= 16512 + 208000, ("sbuf overflow", name, self.off)
        t = Tile(h, "sb", off, off + nb)
        t.cb = (int(np.prod(shape[2:])) if len(shape) > 2 else 1) * DTSZ[dtype]
        return t

    def mark(self):
        return self.off

    def reset(self, m):
        self.off = m

    def _access(self, opi, reg, write, deps, norecord=False):
        space, lo, hi = reg
        if space not in self.segs:
            self.segs[space] = ([0], [[None, {}]])
        starts, data = self.segs[space]
        for b in (lo, hi):
            i = bisect.bisect_right(starts, b) - 1
            if starts[i] != b:
                starts.insert(i + 1, b)
                data.insert(i + 1, [data[i][0], dict(data[i][1])])
        i = bisect.bisect_left(starts, lo)
        while i < len(starts) and starts[i] < hi:
            w, rd = data[i]
            if w is not None and w != opi:
                if not write:
                    deps[w] = "raw"
                elif deps.get(w) != "raw":
                    deps[w] = "waw"
            if write:
                for r_ in rd.values():
                    if r_ != opi and r_ not in deps:
                        deps[r_] = "war"
                data[i][0] = opi
                data[i][1] = {}
            elif not norecord:
                key = self.ops[opi]["rk"]
                rd[key] = opi
            i += 1

    def op(self, eng, fn, reads=(), writes=(), stream=None):
        opi = len(self.ops)
        o = {"eng": eng, "fn": fn, "stream": stream, "deps": {}, "sig": False}
        if stream is not None:
            o["rk"] = "dma:" + stream
        else:
            o["rk"] = eng
        self.ops.append(o)
        deps = {}
        wregs = [_reg(w_) for w_ in writes]
        for r_ in reads:
            rr = _reg(r_)
            inplace = any(w[0] == rr[0] and w[1] < rr[2] and rr[1] < w[2] for w in wregs)
            self._access(opi, rr, False, deps, norecord=inplace)
        for w_ in writes:
            rg_ = _reg(w_)
            if eng == "pe" and rg_[0] == "ps":
                rg_ = ("ps", rg_[1] // 2048 * 2048, (rg_[2] + 2047) // 2048 * 2048)
            self._access(opi, rg_, True, deps)
        res = {}
        for d, kind in deps.items():
            od = self.ops[d]
            if od["stream"] is not None:
                res[d] = ("s", od["stream"], 16 * self.stream_n[od["stream"]])
            else:
                if od["eng"] == eng and stream is None:
                    if eng == "pe":
                        continue
                res[d] = ("e", od["eng"], None)
                od["sig"] = True
        lw = self.__dict__.setdefault("last_waiter", {})
        if stream is not None and stream in lw:
            w = lw[stream]
            ow = self.ops[w]
            if ow["eng"] != eng and w not in res:
                if ow["stream"] is not None:
                    res[w] = ("s", ow["stream"], 16 * self.stream_n[ow["stream"]])
                else:
                    res[w] = ("e", ow["eng"], None)
                    ow["sig"] = True
        for d, (k, key, val) in res.items():
            if k == "s":
                lw[key] = opi
        o["deps"] = res
        if stream is not None:
            self.stream_n[stream] = self.stream_n.get(stream, 0) + 1
            o["sval"] = 16 * self.stream_n[stream]
        return opi

    def dma(self, q, out, in_, reads, writes, stream, **kw):
        return self.op(q, lambda e: e.dma_start(out=out, in_=in_, **kw), reads, writes, stream=stream)

    def emit(self, final_streams):
        nc = self.nc
        counts = {e: 0 for e in self.ENG}
        for o in self.ops:
            if o["stream"] is None and o["sig"]:
                counts[o["eng"]] += 1
                o["sval"] = counts[o["eng"]]
        from contextlib import ExitStack
        with ExitStack() as es:
            esem = {e: es.enter_context(nc.semaphore("e_" + e)) for e in ["pe", "act", "dve", "pool"]}
            ssem = {s: es.enter_context(nc.semaphore("s_" + s)) for s in self.stream_n}
            block = es.enter_context(nc.Block())
            per = {e: [o for o in self.ops if o["eng"] == e] for e in self.ENG}

            def run(ename, e):
                known = {}
                for o in per[ename]:
                    for d, (k, key, val) in o["deps"].items():
                        if k == "s":
                            sem, v = ssem[key], val
                        else:
                            sem, v = esem[key], self.ops[d]["sval"]
                        kk = (k, key)
                        if known.get(kk, 0) >= v:
                            continue
                        known[kk] = v
                        e.wait_ge(sem, v)
                    ins = o["fn"](e)
                    if o["stream"] is not None:
                        ins.then_inc(ssem[o["stream"]], 16)
                    elif o["sig"]:
                        ins.then_inc(esem[ename], 1)
                if ename == "sp":
                    for s in final_streams:
                        if s in self.stream_n:
                            e.wait_ge(ssem[s], 16 * self.stream_n[s])

            @block.tensor
            def _(e):
                run("pe", e)

            @block.scalar
            def _(e):
                run("act", e)

            @block.vector
            def _(e):
                run("dve", e)

            @block.gpsimd
            def _(e):
                run("pool", e)

            @block.sync
            def _(e):
                run("sp", e)


def build(cfg=None):
    cfg = cfg or {}
    n_ptiles = cfg.get("n_ptiles", SEQ // NT)
    n_pseq = cfg.get("n_pseq", 4)
    n_sseq = cfg.get("n_sseq", 2)
    nc = bass.Bass("TRN2", target_bir_lowering=False)
    S = Sched(nc)

    def din(name, shape):
        return nc.dram_tensor(name, list(shape), F32, kind="ExternalInput").ap()

    def dout(name, shape):
        return nc.dram_tensor(name, list(shape), F32, kind="ExternalOutput").ap()

    I = {}
    I["x_prompt"] = din("x_prompt", [4, SEQ, D])
    I["x_sample"] = din("x_sample", [2, DSEQ, D])
    I["cache_k"] = din("cache_k", [2, 2, PAST, 512])
    I["cache_v"] = din("cache_v", [2, 2, PAST, 512])
    I["state_conv"] = din("state_conv", [2, 2, 128, 30])
    I["state_ssd"] = din("state_ssd", [2, 2, 2, 64, 512])
    I["state_s5_re"] = din("state_s5_re", [2, 2, 128, 16])
    I["state_s5_im"] = din("state_s5_im", [2, 2, 128, 16])
    I["meta_tokens"] = din("meta_tokens", [NMETA, D])
    for nm, sh in [("norm_mix", [2, D]), ("w_in", [2, D, INC]), ("conv_w", [2, 4, 1280]), ("conv_b", [2, 1280]),
                   ("dt_bias", [2, 16]), ("a_log", [2, 16]), ("d_ssd", [2, 16]), ("norm_ssd", [2, D]),
                   ("lam_re", [2, 32, 64]), ("lam_im", [2, 32, 64]), ("log_step", [2, 32]),
                   ("s5bc", [2, 4, 128, 2048]), ("d_s5", [2, 512]), ("w_glu", [2, 512, 2048]),
                   ("q_norm", [2, 64]), ("k_norm", [2, 64]), ("w_lift_a", [2, D, D]), ("w_lift_c", [2, 512, D]),
                   ("w_out", [2, D, D]), ("norm_ffn", [2, D]), ("w_up", [2, D, 4096]), ("w_down", [2, 4096, D])]:
        I[nm] = din(nm, sh)
    O = {}
    O["y_prompt"] = dout("y_prompt", [4, SEQ, D])
    O["y_sample"] = dout("y_sample", [2, DSEQ, D])
    O["k_prompt"] = dout("k_prompt", [2, 4, TP, 512])
    O["v_prompt"] = dout("v_prompt", [2, 4, TP, 512])
    O["conv_prompt"] = dout("conv_prompt", [2, 4, 128, 30])
    O["ssd_prompt"] = dout("ssd_prompt", [2, 4, 2, 64, 512])
    O["s5re_prompt"] = dout("s5re_prompt", [2, 4, 128, 16])
    O["s5im_prompt"] = dout("s5im_prompt", [2, 4, 128, 16])
    O["k_sample"] = dout("k_sample", [2, 2, DSEQ, 512])
    O["v_sample"] = dout("v_sample", [2, 2, DSEQ, 512])
    O["conv_sample"] = dout("conv_sample", [2, 2, 128, 30])
    O["ssd_sample"] = dout("ssd_sample", [2, 2, 2, 64, 512])
    O["s5re_sample"] = dout("s5re_sample", [2, 2, 128, 16])
    O["s5im_sample"] = dout("s5im_sample", [2, 2, 128, 16])
    kTs = nc.dram_tensor("kT_scr", [6, 2, 512, NKMAX], BF16, kind="Internal").ap()
    vhs = nc.dram_tensor("vh_scr", [6, 2, NKMAX, 512], BF16, kind="Internal").ap()

    PS = []
    for b in range(8):
        h = nc.alloc_psum_tensor(f"psb{b}", [128, 512], F32)
        PS.append(Tile(h, "ps", b * 2048, (b + 1) * 2048))

    def V(fn, reads, writes):
        return S.op("dve", fn, reads, writes)

    def A(fn, reads, writes):
        return S.op("act", fn, reads, writes)

    def G(fn, reads, writes):
        return S.op("pool", fn, reads, writes)

    def MM(out, lhsT, rhs, start, stop, reads, writes, **kw):
        return S.op("pe", lambda e: e.matmul(out, lhsT=lhsT, rhs=rhs, start=start, stop=stop, **kw), reads, writes)

    def TR(out, in_, ident, reads, writes):
        return S.op("pe", lambda e: e.matmul(out, lhsT=in_, rhs=ident, start=True, stop=True), reads, writes)

    def act(out, in_, func, reads, writes, bias=None, scale=None):
        kw = {}
        if bias is not None:
            kw["bias"] = bias
        if scale is not None:
            kw["scale"] = scale
        return A(lambda e: e.activation(out=out, in_=in_, func=func, **kw), reads, writes)

    def ldpar(out, in_, writes):
        return S.dma("sp", out, in_, [], writes, "par", allow_slow_non_contiguous=True)

    iota_pc = S.sb([128, 256], F32, "iota")
    ident_f = S.sb([128, 128], F32, "identf")
    ident_b = S.sb([128, 128], BF16, "identb")
    ones_b = S.sb([128, 128], BF16, "onesb")
    bd64_b = S.sb([128, 128], BF16, "bd64")
    tri_r = S.sb([128, 128], F32R, "tri")
    ones_r = S.sb([128, 128], F32R, "onesr")
    iota_t = S.sb([128, QS + 1], F32, "iotat")
    ones_f = S.sb([128, 128], F32, "onesf")
    masks = {}
    G(lambda e: e.iota(iota_pc[:, :], [[-1, 256]], base=0, channel_multiplier=1,
                       allow_small_or_imprecise_dtypes=True), [], [iota_pc])
    G(lambda e: e.iota(iota_t[:, :], [[1, QS + 1]], base=0, channel_multiplier=0,
                       allow_small_or_imprecise_dtypes=True), [], [iota_t])
    V(lambda e: e.tensor_single_scalar(out=ident_f[:, :], in_=iota_pc[:, 0:128], scalar=0.0, op=ALU.is_equal),
      [iota_pc], [ident_f])
    V(lambda e: e.tensor_copy(out=ident_b[:, :], in_=ident_f[:, :]), [ident_f], [ident_b])
    V(lambda e: e.memset(ones_b[:, :], 1.0), [], [ones_b])
    V(lambda e: e.memset(bd64_b[:, :], 0.0), [], [bd64_b])
    V(lambda e: e.memset(bd64_b[0:64, 0:64], 1.0), [], [bd64_b])
    V(lambda e: e.memset(bd64_b[64:128, 64:128], 1.0), [], [bd64_b])
    V(lambda e: e.tensor_scalar(out=tri_r[:, :], in0=iota_pc[:, 0:128], scalar1=0.0, scalar2=-8.0,
                                op0=ALU.is_ge, op1=ALU.mult), [iota_pc], [tri_r])
    V(lambda e: e.memset(ones_f[:, :], 1.0), [], [ones_f])
    V(lambda e: e.tensor_copy(out=ones_r[:, :], in_=ones_f[:, :]), [ones_f], [ones_r])
    for off in (16, -112, -240, 0):
        m = S.sb([128, 256], F32, "mask")
        V(lambda e, m=m, off=off: e.tensor_single_scalar(out=m[:, :], in_=iota_pc[:, :], scalar=float(off),
                                                         op=ALU.is_lt), [iota_pc], [m])
        masks[off] = m

    P = []
    stage = cfg.get('stage', 99)
    I["par"] = din("par", [2, 128, NPAR])
    for l in range(2):
        pt = S.sb([128, NPAR], F32, "par")
        S.dma("sp", pt[:, :], I["par"][l], [], [pt], "par")
        p = {"_t": pt}
        for nm, (c0, w) in PCOL.items():
            p[nm] = (pt, c0, w)
        P.append(p)

    def pc(l, nm, j=0, rows=slice(0, 128)):
        pt, c0, w = P[l][nm]
        return pt[rows, c0 + j:c0 + j + 1]

    def pv(l, nm):
        pt, c0, w = P[l][nm]
        return pt[:, c0:c0 + w]

    TWO_PI = 2.0 * math.pi
    MAGIC = 12582912.0
    S5TAB = []
    for l in range(2):
        S5TAB.append((S.sb([128, 16, QS + 1], F32, "cosT"), S.sb([128, 16, QS + 1], F32, "sinT"),
                      S.sb([128, 16, QS], F32, "tbr"), S.sb([128, 16, QS], F32, "tbi"),
                      S.sb([128, 16, QS], F32, "magT")))
    S5L = [(S.sb([128, 16], F32, "lre"), S.sb([128, 16], F32, "lim"), S.sb([128, 16], F32, "lst")) for l in range(2)]
    ARENA0 = S.mark()
    S5SCR = [S.sb([128, 16], F32, "s5s") for _ in range(8)] + [S.sb([128, 16, QS + 1], F32, "s5w") for _ in range(3)]
    for l in range(2 if stage >= 2 else 0):
        p = P[l]
        lre, lim, lst = S5L[l]
        for dst, nm in ((lre, "lre"), (lim, "lim"), (lst, "lst")):
            V(lambda e, dst=dst, nm=nm, l=l: e.tensor_copy(out=dst[:, :], in_=pv(l, nm)), [P[l]["_t"]], [dst])
        step, th, mag, t0_, t1_, t2_, fre, fim, ang, w1, w2 = S5SCR
        cosT, sinT, tbr, tbi, magT = S5TAB[l]
        p.update(cosT=cosT, sinT=sinT, tbr=tbr, tbi=tbi, magT=magT, mag=mag)
        act(step[:, :], lst[:, :], AF.Exp, [lst], [step])
        V(lambda e, th=th, lim=lim, step=step: e.tensor_tensor(out=th[:, :], in0=lim[:, :], in1=step[:, :], op=ALU.mult),
          [lim, step], [th])
        V(lambda e, t0_=t0_, lre=lre, step=step: e.tensor_tensor(out=t0_[:, :], in0=lre[:, :], in1=step[:, :], op=ALU.mult),
          [lre, step], [t0_])
        act(mag[:, :], t0_[:, :], AF.Exp, [t0_], [mag])
        V(lambda e, ang=ang, th=th: e.tensor_tensor(
            out=ang[:, :, :], in0=iota_t[:, :].unsqueeze(1).broadcast_to([128, 16, QS + 1]),
            in1=th[:, :].unsqueeze(2).broadcast_to([128, 16, QS + 1]), op=ALU.mult), [iota_t, th], [ang])
        for which, outT in (("sin", sinT), ("cos", cosT)):
            addc = 0.0 if which == "sin" else 0.25
            V(lambda e, ang=ang, w1=w1, addc=addc: e.tensor_scalar(
                out=w1[:, :, :], in0=ang[:, :, :], scalar1=1.0 / TWO_PI, scalar2=addc, op0=ALU.mult, op1=ALU.add),
              [ang], [w1])
            V(lambda e, w1=w1, w2=w2: e.tensor_scalar(out=w2[:, :, :], in0=w1[:, :, :], scalar1=MAGIC, scalar2=None,
                                                     op0=ALU.add), [w1], [w2])
            V(lambda e, w2=w2: e.tensor_scalar(out=w2[:, :, :], in0=w2[:, :, :], scalar1=-MAGIC, scalar2=None,
                                               op0=ALU.add), [w2], [w2])
            V(lambda e, w1=w1, w2=w2: e.tensor_tensor(out=w1[:, :, :], in0=w1[:, :, :], in1=w2[:, :, :],
                                                     op=ALU.subtract), [w1, w2], [w1])
            V(lambda e, w1=w1: e.tensor_scalar(out=w1[:, :, :], in0=w1[:, :, :], scalar1=-0.4999, scalar2=0.4999,
                                               op0=ALU.max, op1=ALU.min), [w1], [w1])
            act(outT[:, :, :], w1[:, :, :], AF.Sin, [w1], [outT], scale=TWO_PI)
        abr, abi = t1_, t2_
        V(lambda e, abr=abr, cosT=cosT, mag=mag: e.tensor_tensor(out=abr[:, :], in0=cosT[:, :, 1], in1=mag[:, :], op=ALU.mult),
          [cosT, mag], [abr])
        V(lambda e, abi=abi, sinT=sinT, mag=mag: e.tensor_tensor(out=abi[:, :], in0=sinT[:, :, 1], in1=mag[:, :], op=ALU.mult),
          [sinT, mag], [abi])
        V(lambda e, abr=abr: e.tensor_scalar(out=abr[:, :], in0=abr[:, :], scalar1=-1.0, scalar2=None, op0=ALU.add),
          [abr], [abr])
        den = step
        V(lambda e, den=den, lre=lre: e.tensor_tensor(out=den[:, :], in0=lre[:, :], in1=lre[:, :], op=ALU.mult), [lre], [den])
        V(lambda e, t0_=t0_, lim=lim: e.tensor_tensor(out=t0_[:, :], in0=lim[:, :], in1=lim[:, :], op=ALU.mult), [lim], [t0_])
        V(lambda e, den=den, t0_=t0_: e.tensor_tensor(out=den[:, :], in0=den[:, :], in1=t0_[:, :], op=ALU.add), [den, t0_], [den])
        V(lambda e, den=den: e.reciprocal(out=den[:, :], in_=den[:, :]), [den], [den])
        V(lambda e, fre=fre, abr=abr, lre=lre: e.tensor_tensor(out=fre[:, :], in0=abr[:, :], in1=lre[:, :], op=ALU.mult), [abr, lre], [fre])
        V(lambda e, t0_=t0_, abi=abi, lim=lim: e.tensor_tensor(out=t0_[:, :], in0=abi[:, :], in1=lim[:, :], op=ALU.mult), [abi, lim], [t0_])
        V(lambda e, fre=fre, t0_=t0_: e.tensor_tensor(out=fre[:, :], in0=fre[:, :], in1=t0_[:, :], op=ALU.add), [fre, t0_], [fre])
        V(lambda e, fre=fre, den=den: e.tensor_tensor(out=fre[:, :], in0=fre[:, :], in1=den[:, :], op=ALU.mult), [fre, den], [fre])
        V(lambda e, fim=fim, abi=abi, lre=lre: e.tensor_tensor(out=fim[:, :], in0=abi[:, :], in1=lre[:, :], op=ALU.mult), [abi, lre], [fim])
        V(lambda e, t0_=t0_, abr=abr, lim=lim: e.tensor_tensor(out=t0_[:, :], in0=abr[:, :], in1=lim[:, :], op=ALU.mult), [abr, lim], [t0_])
        V(lambda e, fim=fim, t0_=t0_: e.tensor_tensor(out=fim[:, :], in0=fim[:, :], in1=t0_[:, :], op=ALU.subtract), [fim, t0_], [fim])
        V(lambda e, fim=fim, den=den: e.tensor_tensor(out=fim[:, :], in0=fim[:, :], in1=den[:, :], op=ALU.mult), [fim, den], [fim])
        frb = lambda f: f[:, :].unsqueeze(2).broadcast_to([128, 16, QS])
        V(lambda e, w1=w1, cosT=cosT, fre=fre: e.tensor_tensor(out=w1[:, :, 0:QS], in0=cosT[:, :, 0:QS], in1=frb(fre), op=ALU.mult), [cosT, fre], [w1])
        V(lambda e, w2=w2, sinT=sinT, fim=fim: e.tensor_tensor(out=w2[:, :, 0:QS], in0=sinT[:, :, 0:QS], in1=frb(fim), op=ALU.mult), [sinT, fim], [w2])
        V(lambda e, tbr=tbr, w1=w1, w2=w2: e.tensor_tensor(out=tbr[:, :, :], in0=w1[:, :, 0:QS], in1=w2[:, :, 0:QS], op=ALU.add), [w1, w2], [tbr])
        V(lambda e, w1=w1, cosT=cosT, fim=fim: e.tensor_tensor(out=w1[:, :, 0:QS], in0=cosT[:, :, 0:QS], in1=frb(fim), op=ALU.mult), [cosT, fim], [w1])
        V(lambda e, w2=w2, sinT=sinT, fre=fre: e.tensor_tensor(out=w2[:, :, 0:QS], in0=sinT[:, :, 0:QS], in1=frb(fre), op=ALU.mult), [sinT, fre], [w2])
        V(lambda e, tbi=tbi, w1=w1, w2=w2: e.tensor_tensor(out=tbi[:, :, :], in0=w1[:, :, 0:QS], in1=w2[:, :, 0:QS], op=ALU.subtract), [w1, w2], [tbi])
        V(lambda e, magT=magT, mag=mag: e.tensor_tensor(
            out=magT[:, :, :], in0=ones_f[:, 0:QS].unsqueeze(1).broadcast_to([128, 16, QS]),
            in1=mag[:, :].unsqueeze(2).broadcast_to([128, 16, QS]), op=ALU.mult), [ones_f, mag], [magT])
    OQ = cfg.get("oq", "pool")
    WG = {}
    upto = cfg.get("upto", 99)
    epsT = S.sb([128, 1], F32, "eps")
    V(lambda e: e.memset(epsT[:, :], EPS), [], [epsT])
    oneT = S.sb([128, 1], F32, "one")
    V(lambda e: e.memset(oneT[:, :], 1.0), [], [oneT])
    WDT = []
    for l in range(2):
        w_ = S.sb([128, 8, 48], BF16, "WDT")
        V(lambda e, w_=w_: e.memset(w_[:, :, :], 0.0), [], [w_])
        for c0 in (0, 32):
            S.dma("pool", w_[:, :, c0:c0 + 16], I["w_in"][l][:, OFF_DT:OFF_DT + 16].rearrange("(kc p) n -> p kc n", p=128),
                  [], [w_], "wdt")
        WDT.append(w_)

    def tt(out, in0, in1, op, reads, writes, eng="dve"):
        return S.op(eng, lambda e: e.tensor_tensor(out=out, in0=in0, in1=in1, op=op), reads, writes)

    def ts(out, in0, s1, s2, op0, op1, reads, writes, eng="dve"):
        if op1 is None:
            return S.op(eng, lambda e: e.tensor_scalar(out=out, in0=in0, scalar1=s1, scalar2=None, op0=op0), reads, writes)
        return S.op(eng, lambda e: e.tensor_scalar(out=out, in0=in0, scalar1=s1, scalar2=s2, op0=op0, op1=op1), reads, writes)

    def stt(out, in0, scalar, in1, op0, op1, reads, writes):
        return V(lambda e: e.scalar_tensor_tensor(out=out, in0=in0, scalar=scalar, in1=in1, op0=op0, op1=op1), reads, writes)

    def cp(out, in_, reads, writes, eng="dve"):
        return S.op(eng, lambda e: e.tensor_copy(out=out, in_=in_), reads, writes)

    def scan(out, d0, d1, init, reads, writes):
        return V(lambda e: e.tensor_tensor_scan(out=out, data0=d0, data1=d1, initial=init, op0=ALU.mult, op1=ALU.add), reads, writes)

    NWS = cfg.get('nws', 4)
    WSL = [S.sb([128, 4096], BF16, "wslot") for _ in range(NWS)]
    wctr = [0]

    def load_w(src_ap, kc, ncols, prt=128):
        key = (src_ap.tensor.name, int(src_ap.offset), kc, ncols, prt)
        if key not in WG:
            gi = len(WG)
            scr = nc.dram_tensor("wg%d" % gi, [prt, kc * ncols], BF16, kind="Internal").ap()
            S.dma("pool", scr.rearrange("p (kc n) -> p kc n", n=ncols), src_ap.rearrange("(kc p) n -> p kc n", p=prt),
                  [], [("wg", gi, gi + 1)], "wcast")
            WG[key] = (gi, scr)
        gi, scr = WG[key]
        i = wctr[0] % NWS
        wctr[0] += 1
        wt = WSL[i]
        view = wt[0:prt, 0:kc * ncols].rearrange("p (kc n) -> p kc n", n=ncols)
        S.dma("sp", wt[0:prt, 0:kc * ncols], scr[:, :], [("wg", gi, gi + 1)], [wt.r(0, kc * ncols * 2)], f"w{i}")
        return wt, view

    ones48 = S.sb([48, 64], F32, "ones48")
    V(lambda e: e.memset(ones48[:, :], 1.0), [], [ones48])
    LT = []
    for i in range(2):
        t = S.sb([128, 128], F32, "LT")
        V(lambda e, t=t: e.memset(t[:, :], 0.0), [], [t])
        V(lambda e, t=t: e.memset(t[0:16, :], 1.0), [], [t])
        cp(t[64:128, 0:64], ident_f[64:128, 64:128], [ident_f], [t])
        LT.append(t)
    RF = []
    for g in range(2):
        t = S.sb([128, 8, 64], F32, "RF")
        V(lambda e, t=t: e.memset(t[:, :, :], 0.0), [], [t])
        for h in range(8):
            if True:
                r = 8 * g + h
                pass
        RF.append(t)
    DEL = []
    for g in range(2):
        d_ = S.sb([48, 8], F32, "DEL")
        io = S.sb([48, 8], F32, "DELi")
        G(lambda e, io=io: e.iota(io[:, :], [[-1, 8]], base=0, channel_multiplier=1, allow_small_or_imprecise_dtypes=True), [], [io])
        ts(d_[0:32, :], io[0:32, :], float(8 * g), None, ALU.is_equal, None, [io], [d_])
        ts(d_[32:48, :], io[32:48, :], float(32 + 8 * g), None, ALU.is_equal, None, [io], [d_])
        DEL.append(d_)
        cp(RF[g][32:48, :, :], d_[32:48, :].unsqueeze(2).broadcast_to([16, 8, 64]), [d_], [RF[g]])
        ts(RF[g][64:128, :, :], iota_pc[64:128, 0:64].unsqueeze(1).broadcast_to([64, 8, 64]), 64.0, NEG, ALU.is_gt, ALU.mult,
           [iota_pc], [RF[g]])
    SHIFT = S.sb([128, 128], BF16, "shift")
    V(lambda e: e.memset(SHIFT[:, :], 0.0), [], [SHIFT])
    cp(SHIFT[0:64, 64:128], ident_b[0:64, 0:64], [ident_b], [SHIFT])
    cp(SHIFT[64:128, 64:128], ident_b[64:128, 64:128], [ident_b], [SHIFT])
    BTOK = S.sb([64, 128], BF16, "btok")
    V(lambda e: e.memset(BTOK[:, :], 0.0), [], [BTOK])
    ST_, XS_, CH_, XR_, XI_, STm, CHm, XRm, XIm = [], [], [], [], [], [], [], [], []
    for l in range(2):
        ST_.append(S.sb([128, 2, 512], F32, "ST"))
        XS_.append(S.sb([128, 2, 4, 256], BF16, "XS"))
        CH_.append(S.sb([128, 10, 3], F32, "CH"))
        XR_.append(S.sb([128, 16], F32, "XR"))
        XI_.append(S.sb([128, 16], F32, "XI"))
        STm.append(S.sb([128, 2, 512], F32, "STm"))
        CHm.append(S.sb([128, 10, 3], F32, "CHm"))
        XRm.append(S.sb([128, 16], F32, "XRm"))
        XIm.append(S.sb([128, 16], F32, "XIm"))
    acol = []
    for l in range(2):
        a_ = S.sb([48, 1], F32, "acol")
        act(a_[:, :], pc(l, "alog", 0, slice(0, 48)), AF.Exp, [P[l]["_t"]], [a_])
        ts(a_[0:32, :], a_[0:32, :], -1.0, None, ALU.mult, None, [a_], [a_])
        acol.append(a_)

    def st_to_xs(l):
        for g in range(2):
            src = ST_[l][64:128, g, :].rearrange("p (pr eo d) -> p pr eo d", pr=4, eo=2)
            dst = XS_[l][64:128, g, :, :].rearrange("p pr (blk d) -> p pr blk d", d=64)[:, :, 1::2, :]
            cp(dst, src, [ST_[l].c(g)], [XS_[l].c(g)])

    xT = S.sb([128, 8, NT], F32, "xT")
    mixT = S.sb([128, 8, NT], F32, "mixT")
    hT = S.sb([128, 8, NT], BF16, "hT")
    ARENA = S.mark()
    seqs = [dict(kind="meta", b=0, T=NMETA, slot=0)]
    for b in range(n_pseq):
        seqs.append(dict(kind="prompt", b=b, T=n_ptiles * NT, slot=b))
    for b in range(n_sseq):
        seqs.append(dict(kind="sample", b=b, T=DSEQ, slot=4 + b))
    if stage < 3:
        seqs = []
    only = cfg.get('only', ['meta', 'prompt', 'sample'])
    seqs = [q for q in seqs if q['kind'] in only]

    def rmsnorm_to(dst_bf, src_f32, ncol_name, l, N):
        mk = S.mark()
        sqb = S.sb([128, 8, N], BF16, "sqb")
        rstd = S.sb([128, N], F32, "rstd")
        for c in range(8):
            act(sqb[:, c, :], src_f32[:, c, 0:N], AF.Square, [src_f32.c(c)], [sqb.c(c)])
        bank = PS[4]
        for c in range(8):
            MM(bank[:, 0:N], ones_b[:, :], sqb[:, c, :], c == 0, c == 7, [ones_b, sqb.c(c)], [bank.r(0, 4 * N)])
        act(rstd[:, :], bank[:, 0:N], AF.Sqrt, [bank.r(0, 4 * N), epsT], [rstd], bias=epsT[:, 0:1], scale=1.0 / D)
        V(lambda e, o_=rstd[:, :]: e.reciprocal(out=o_, in_=o_), [rstd], [rstd])
        for c in range(8):
            stt(dst_bf[:, c, 0:N], src_f32[:, c, 0:N], pc(l, ncol_name, c), rstd[:, :], ALU.mult, ALU.mult,
                [src_f32.c(c), P[l]["_t"], rstd], [dst_bf.c(c)])
        S.reset(mk)

    def proj_fm(l, wsrc, kc, c0, ncols, rhs_fn, rhs_reads, N, evac, prt=128, bankset=(0, 1)):
        mi = 0
        for g0 in range(0, ncols, 512):
            gw = min(512, ncols - g0)
            wt, view = load_w(wsrc[:, c0 + g0:c0 + g0 + gw], kc, gw, prt)
            for m0 in range(0, gw, 128):
                mw = min(128, gw - m0)
                bank = PS[bankset[mi % len(bankset)]]
                for k in range(kc):
                    MM(bank[0:mw, 0:N], view[:, k, m0:m0 + mw], rhs_fn(k), k == 0, k == kc - 1,
                       [wt] + rhs_reads(k), [bank.r(0, 4 * N)])
                evac(mi, bank, mw)
                mi += 1

    for sq in seqs:
        kind, b, slot = sq["kind"], sq["b"], sq["slot"]
        tiles = [(t0, min(NT, sq["T"] - t0)) for t0 in range(0, sq["T"], NT)]
        for l in range(2):
            if kind == "prompt" and upto < 3:
                continue
            if kind == "meta":
                for t in (ST_[l], CH_[l], XR_[l], XI_[l], XS_[l]):
                    V(lambda e, a_=t.h[:]: e.memset(a_, 0.0), [], [t])
            elif kind == "prompt":
                for dst, src in ((ST_[l], STm[l]), (CH_[l], CHm[l]), (XR_[l], XRm[l]), (XI_[l], XIm[l])):
                    cp(dst.h[:], src.h[:], [src], [dst])
                st_to_xs(l)
            else:
                S.dma(OQ, CH_[l][:, :, :].rearrange("p c k -> p (c k)"), I["state_conv"][l, b], [], [CH_[l]], "stin")
                for g in range(2):
                    S.dma(OQ, ST_[l][64:128, g, :], I["state_ssd"][l, b, g], [], [ST_[l].c(g)], "stin")
                S.dma(OQ, XR_[l][:, :], I["state_s5_re"][l, b], [], [XR_[l]], "stin")
                S.dma(OQ, XI_[l][:, :], I["state_s5_im"][l, b], [], [XI_[l]], "stin")
                st_to_xs(l)
                S.reset(ARENA)
                S.dma("pool", vhs[slot, l, 0:PAST, :], I["cache_v"][l, b], [], [("vh%d_%d" % (slot, l), 0, PAST)], "vpre")
                CK = [S.sb([128, 512], F32, "CK") for _ in range(2)]
                KTP = [S.sb([128, 4, 128], BF16, "KTP") for _ in range(2)]
                for tb in range(PAST // 128):
                    ck, kt = CK[tb % 2], KTP[tb % 2]
                    S.dma(OQ, ck[:, :], I["cache_k"][l, b, 128 * tb:128 * tb + 128, :], [], [ck], "ckl%d" % (tb % 2))
                    bank = PS[7]
                    for m in range(4):
                        TR(bank[:, 128 * m:128 * m + 128], ck[:, 128 * m:128 * m + 128], ident_f[:, :], [ck, ident_f],
                           [bank.r(512 * m, 512 * m + 512)])
                    act(kt[:, :, :], bank[:, :].rearrange("p (m t) -> p m t", t=128), AF.Copy, [bank], [kt])
                    S.dma(OQ, kTs[slot, l].rearrange("(m p) t -> p m t", p=128)[:, :, 128 * tb:128 * tb + 128], kt[:, :, :],
                          [kt], [("kT%d_%d" % (slot, l), 128 * tb, 128 * tb + 128)], "kpre%d" % (tb % 2))
        for (t0, N) in tiles:
            S.reset(ARENA)
            pos0 = {"meta": 0, "prompt": NMETA + t0, "sample": PAST}[kind]
            nbk = (N + 127) // 128
            blk = [(tb, min(128, N - 128 * tb)) for tb in range(nbk)]
            if kind == "meta":
                xsrc = I["meta_tokens"]
            elif kind == "prompt":
                xsrc = I["x_prompt"][b, t0:t0 + N, :]
            else:
                xsrc = I["x_sample"][b, t0:t0 + N, :]
            mk0 = S.mark()
            x_tm = S.sb([128, nbk, D], F32, "x_tm")
            for tb, nt in blk:
                S.dma(OQ, x_tm[0:nt, tb, :], xsrc[128 * tb:128 * tb + nt, :], [], [x_tm.c(tb)], "xin")
            for c in range(8):
                bank = PS[c % 4]
                for tb, nt in blk:
                    TR(bank[:, 128 * tb:128 * tb + nt], x_tm[0:nt, tb, 128 * c:128 * c + 128], ident_f[0:nt, 0:nt],
                       [x_tm.c(tb), ident_f], [bank.r(512 * tb, 512 * tb + 4 * nt)])
                act(xT[:, c, 0:N], bank[:, 0:N], AF.Copy, [bank.r(0, 4 * N)], [xT.c(c)])
            S.reset(mk0)
            for l in range(2):
                S.reset(ARENA)
                rmsnorm_to(hT, xT, "nmix", l, N)
                hrd = lambda k: [hT.c(k)]
                hrhs = lambda k, N=N: hT[:, k, 0:N]
                if kind == "meta":
                    kdst = [O["k_prompt"][l, bb, 0:NMETA, :] for bb in range(n_pseq)]
                    vdst = [O["v_prompt"][l, bb, 0:NMETA, :] for bb in range(n_pseq)]
                    slots = list(range(n_pseq))
                elif kind == "prompt":
                    kdst = [O["k_prompt"][l, b, NMETA + t0:NMETA + t0 + N, :]]
                    vdst = [O["v_prompt"][l, b, NMETA + t0:NMETA + t0 + N, :]]
                    slots = [slot]
                else:
                    kdst = [O["k_sample"][l, b, t0:t0 + N, :]]
                    vdst = [O["v_sample"][l, b, t0:t0 + N, :]]
                    slots = [slot]
                mkA = S.mark()
                knT = S.sb([128, 4, N], F32, "knT")
                knb = S.sb([128, 4, N], BF16, "knb")
                qnb = S.sb([128, 4, N], BF16, "qnb")
                ksq = S.sb([128, N], BF16, "ksq")
                rk = S.sb([128, N], F32, "rk")

                def qk_evac(dst32, dstbf, gname):
                    def ev(m, bank, mw):
                        kd = cfg.get("kd", 99)
                        if kd < 1:
                            return
                        act(ksq[:, :], bank[:, 0:N], AF.Square, [bank.r(0, 4 * N)], [ksq])
                        if kd < 2:
                            return
                        b2 = PS[2 + m % 2]
                        MM(b2[:, 0:N], bd64_b[:, :], ksq[:, :], True, True, [bd64_b, ksq], [b2.r(0, 4 * N)])
                        if kd < 3:
                            return
                        act(rk[:, :], b2[:, 0:N], AF.Sqrt, [b2.r(0, 4 * N), epsT], [rk], bias=epsT[:, 0:1], scale=1.0 / 64)
                        if kd < 4:
                            return
                        V(lambda e, o_=rk[:, :]: e.reciprocal(out=o_, in_=o_), [rk], [rk])
                        if kd < 5:
                            return
                        if dst32 is not None:
                            stt(dst32[:, m, :], bank[:, 0:N], pc(l, gname), rk[:, :], ALU.mult, ALU.mult,
                                [bank.r(0, 4 * N), P[l]["_t"], rk], [dst32.c(m)])
                            cp(dstbf[:, m, :], dst32[:, m, :], [dst32.c(m)], [dstbf.c(m)], eng="pool")
                        else:
                            stt(dstbf[:, m, :], bank[:, 0:N], pc(l, gname), rk[:, :], ALU.mult, ALU.mult,
                                [bank.r(0, 4 * N), P[l]["_t"], rk], [dstbf.c(m)])
                    return ev
                proj_fm(l, I["w_in"][l], 8, OFF_K, 512, hrhs, hrd, N, qk_evac(knT, knb, "kn"))
                proj_fm(l, I["w_in"][l], 8, OFF_Q, 512, hrhs, hrd, N, qk_evac(None, qnb, "qn"))
                if cfg.get("kd", 99) < 7:
                    continue
                for sl in slots:
                    S.dma(OQ, kTs[sl, l].rearrange("(m p) t -> p m t", p=128)[:, :, pos0:pos0 + N], knb[:, :, :],
                          [knb], [("kT%d_%d" % (sl, l), pos0, pos0 + N)], "ktw")
                if cfg.get("kd", 99) < 8:
                    continue
                k_tm = S.sb([128, nbk, 512], F32, "k_tm")
                for tb, nt in blk:
                    bank = PS[5]
                    for m in range(4):
                        TR(bank[0:nt, 128 * m:128 * m + 128], knT[:, m, 128 * tb:128 * tb + nt], ident_f[:, :],
                           [knT.c(m), ident_f], [bank.r(512 * m, 512 * m + 512)])
                    act(k_tm[0:nt, tb, :], bank[0:nt, :], AF.Copy, [bank], [k_tm.c(tb)])
                    for dst in kdst:
                        S.dma(OQ, dst[128 * tb:128 * tb + nt, :], k_tm[0:nt, tb, :], [k_tm.c(tb)], [], "kout")
                if cfg.get("kd", 99) < 9:
                    continue
                wt, wv = load_w(I["w_in"][l][:, OFF_V:OFF_V + 512], 8, 512)
                v_tm = S.sb([128, nbk, 512], F32, "v_tm")
                v_bf = S.sb([128, nbk, 512], BF16, "v_bf")
                for tb, nt in blk:
                    bank = PS[6]
                    for kc in range(8):
                        MM(bank[0:nt, :], hT[:, kc, 128 * tb:128 * tb + nt], wv[:, kc, :], kc == 0, kc == 7,
                           [wt, hT.c(kc)], [bank])
                    act(v_tm[0:nt, tb, :], bank[0:nt, :], AF.Copy, [bank], [v_tm.c(tb)])
                    for dst in vdst:
                        S.dma(OQ, dst[128 * tb:128 * tb + nt, :], v_tm[0:nt, tb, :], [v_tm.c(tb)], [], "vout")
                    if cfg.get("kd", 99) < 10:
                        continue
                    cp(v_bf[0:nt, tb, :], v_tm[0:nt, tb, :], [v_tm.c(tb)], [v_bf.c(tb)], eng="pool")
                    if cfg.get("vh", 2) < 2:
                        continue
                    for sl in slots:
                        S.dma(OQ, vhs[sl, l, pos0 + 128 * tb:pos0 + 128 * tb + nt, :], v_bf[0:nt, tb, :],
                              [v_bf.c(tb)], [("vh%d_%d" % (sl, l), pos0 + 128 * tb, pos0 + 128 * tb + nt)], "vhw")
                if upto < 2:
                    continue
                def gates(i):
                    proj_fm(l, I["w_in"][l], 8, OFF_G + 1024 * i, 1024, hrhs, hrd, N,
                            lambda m, bank, mw: act(GT[:, m, :], bank[:, 0:N], AF.Sigmoid, [bank.r(0, 4 * N)], [GT.c(m)]))

                def mix_evac(first):
                    def ev(m, bank, mw):
                        if first:
                            tt(mixT[:, m, 0:N], GT[:, m, :], bank[:, 0:N], ALU.mult, [GT.c(m), bank.r(0, 4 * N)], [mixT.c(m)])
                        else:
                            tt(tmpA[:, :], GT[:, m, :], bank[:, 0:N], ALU.mult, [GT.c(m), bank.r(0, 4 * N)], [tmpA])
                            tt(mixT[:, m, 0:N], mixT[:, m, 0:N], tmpA[:, :], ALU.add, [mixT.c(m), tmpA], [mixT.c(m)])
                    return ev
                nk = pos0 + N
                nb = (nk + 127) // 128
                slot_r = slots[0]
                OC = S.sb([64, 8, N], BF16, "OC")
                mkC = S.mark()
                RS = S.sb([128, N], F32, "RS")
                EXs = [S.sb([128, N], F32, "EX") for _ in range(2)]
                ARGs = [S.sb([128, N], F32, "ARG") for _ in range(2)]
                LPs = [S.sb([128, N], F32, "LP") for _ in range(2)]
                WTs = [S.sb([128, N], BF16, "WT") for _ in range(2)]
                KTb = [S.sb([128, NKMAX], BF16, "KT") for _ in range(2)]
                VBb = [S.sb([128, 17, 128], BF16, "VB") for _ in range(2)]
                nfull, rem = nk // 128, nk % 128
                bctr = 0
                for pr in range(4):
                    KT, VB = KTb[pr % 2], VBb[pr % 2]
                    kname, vname = "kT%d_%d" % (slot_r, l), "vh%d_%d" % (slot_r, l)
                    S.dma(OQ, KT[:, 0:nk], kTs[slot_r, l, 128 * pr:128 * pr + 128, 0:nk], [(kname, 0, nk)], [KT], "ktl%d" % (pr % 2))
                    if nfull:
                        S.dma(OQ, VB[:, 0:nfull, :],
                              vhs[slot_r, l, 0:128 * nfull, 128 * pr:128 * pr + 128].rearrange("(b p) c -> p b c", p=128),
                              [(vname, 0, 128 * nfull)], [VB.r(0, nfull * 256)], "vbl%d" % (pr % 2))
                    if rem:
                        S.dma(OQ, VB[0:rem, nfull, :], vhs[slot_r, l, 128 * nfull:nk, 128 * pr:128 * pr + 128],
                              [(vname, 128 * nfull, nk)], [VB.r(nfull * 256, nfull * 256 + 256)], "vbl%d" % (pr % 2))
                    for hh in range(2):
                        h = 2 * pr + hh
                        R = slice(64 * hh, 64 * hh + 64)
                        V(lambda e, a_=RS[:, :]: e.memset(a_, 0.0), [], [RS])
                        PO = PS[4]
                        blocks = list(reversed(range(nb)))

                        def stage1(bI, i):
                            kb = min(128, nk - 128 * bI)
                            PZ, P2 = PS[i % 2], PS[2 + i % 2]
                            LP, EXb = LPs[i % 2], EXs[i % 2]
                            MM(PZ[0:kb, 0:N], KT[R, 128 * bI:128 * bI + kb], qnb[R, pr, 0:N], True, True,
                               [KT, qnb.c(pr)], [PZ.r(0, 4 * N)])
                            act(EXb[0:kb, :], PZ[0:kb, 0:N], AF.Exp, [PZ.r(0, 4 * N)], [EXb], scale=0.125)
                            act(LP[0:kb, :].bitcast(F32R), EXb[0:kb, :], AF.Ln, [EXb, oneT], [LP], bias=oneT[0:kb, 0:1])
                            if 128 * bI + kb - 1 >= pos0:
                                mt = masks[pos0 - 128 * bI]
                                tt(LP[0:kb, :].bitcast(F32R), LP[0:kb, :], mt[0:kb, 0:N], ALU.mult, [LP, mt], [LP])

                        def stage1b(bI, i):
                            kb = min(128, nk - 128 * bI)
                            PZ, P2 = PS[i % 2], PS[2 + i % 2]
                            LP = LPs[i % 2]
                            MM(PZ[0:kb, 0:N], tri_r[0:kb, 0:kb], LP[0:kb, :].bitcast(F32R), False, True,
                               [tri_r, LP], [PZ.r(0, 4 * N)], skip_group_check=True)
                            if bI > 0:
                                MM(P2[:, 0:N], ones_r[0:kb, :], LP[0:kb, :].bitcast(F32R), True, True, [ones_r, LP], [P2.r(0, 4 * N)])

                        def stage2(bI, i):
                            kb = min(128, nk - 128 * bI)
                            PZ, P2 = PS[i % 2], PS[2 + i % 2]
                            WT, AG = WTs[i % 2], ARGs[i % 2]
                            stt(AG[0:kb, :], PZ[0:kb, 0:N], 0.125, RS[0:kb, :], ALU.mult, ALU.subtract,
                                [PZ.r(0, 4 * N), RS], [AG])
                            if bI > 0:
                                tt(RS[:, :], RS[:, :], P2[:, 0:N], ALU.add, [RS, P2.r(0, 4 * N)], [RS])
                            act(WT[0:kb, :], AG[0:kb, :], AF.Exp, [AG], [WT])
                            if 128 * bI + kb - 1 >= pos0:
                                mt = masks[pos0 - 128 * bI]
                                tt(WT[0:kb, :], WT[0:kb, :], mt[0:kb, 0:N], ALU.mult, [WT, mt], [WT])
                            MM(PO[0:64, 0:N], VB[0:kb, bI, 64 * hh:64 * hh + 64], WT[0:kb, :], i == 0, bI == 0,
                               [VB.r(bI * 256, bI * 256 + 256), WT], [PO.r(0, 4 * N)])
                        for i, bI in enumerate(blocks):
                            stage1(bI, i)
                            stage1b(bI, i)
                            if i >= 1:
                                stage2(blocks[i - 1], i - 1)
                        stage2(blocks[-1], len(blocks) - 1)
                        act(OC[:, h, :], PO[0:64, 0:N], AF.Copy, [PO.r(0, 4 * N)], [OC.c(h)])
                S.reset(mkC)
                GT = S.sb([128, 8, N], F32, "GT")
                tmpA = S.sb([128, N], F32, "tmpA")
                gates(2)
                proj_fm(l, I["w_lift_c"][l], 8, 0, 1024, lambda h_: OC[:, h_, :], lambda h_: [OC.c(h_)], N, mix_evac(True), prt=64)
                if upto < 3:
                    continue
                S.reset(mkA)
                GT = S.sb([128, 8, N], F32, "GT")
                tmpA = S.sb([128, N], F32, "tmpA")
                Q = min(64, N)
                nch = N // Q
                xp = S.sb([128, 10, N + 3], F32, "xp")
                XBC = S.sb([128, 10, N], BF16, "XBC")
                yT = S.sb([128, 8, N], F32, "yT")
                cp(xp[:, :, 0:3], CH_[l][:, :, :], [CH_[l]], [xp])
                proj_fm(l, I["w_in"][l], 8, OFF_XBC, 1280, hrhs, hrd, N,
                        lambda m, bank, mw: act(xp[:, m, 3:3 + N], bank[:, 0:N], AF.Copy, [bank.r(0, 4 * N)], [xp.c(m)]))
                cp(CH_[l][:, :, :], xp[:, :, N:N + 3], [xp], [CH_[l]])
                for c in range(10):
                    ts(tmpA[:, :], xp[:, c, 0:N], pc(l, "cw", c), pc(l, "cb", c), ALU.mult, ALU.add, [xp.c(c), P[l]["_t"]], [tmpA])
                    for k in range(1, 4):
                        stt(tmpA[:, :], xp[:, c, k:k + N], pc(l, "cw", 10 * k + c), tmpA[:, :], ALU.mult, ALU.add,
                            [xp.c(c), P[l]["_t"], tmpA], [tmpA])
                    act(XBC[:, c, :], tmpA[:, :], AF.Silu, [tmpA], [XBC.c(c)])
                if cfg.get("sd", 99) < 2:
                    continue
                PD = PS[2]
                for k in range(8):
                    MM(PD[0:48, 0:N], WDT[l][:, k, :], hT[:, k, 0:N], k == 0, k == 7, [WDT[l], hT.c(k)], [PD.r(0, 4 * N)])
                dtT = S.sb([48, N], F32, "dtT")
                dA = S.sb([48, N], F32, "dA")
                AC = S.sb([16, N], F32, "AC")
                act(dtT[:, :], PD[0:48, 0:N], AF.Exp, [PD.r(0, 4 * N), P[l]["_t"]], [dtT], bias=pc(l, "dtb", 0, slice(0, 48)))
                act(dtT[:, :], dtT[:, :], AF.Ln, [dtT, oneT], [dtT], bias=oneT[0:48, 0:1])
                ts(dA[:, :], dtT[:, :], acol[l][:, 0:1], None, ALU.mult, None, [dtT, acol[l]], [dA])
                if cfg.get("sd", 99) < 3:
                    continue
                Ets = [S.sb([128, 8, Q], F32, "Et") for _ in range(2)]
                Wts = [S.sb([128, 8, Q], BF16, "Wt") for _ in range(2)]
                if Q < 64:
                    for w__ in Wts:
                        V(lambda e, a_=w__[:, :, :]: e.memset(a_, 0.0), [], [w__])
                dt_tm = S.sb([64, 16], F32, "dt_tm")
                coef = S.sb([64, 8], F32, "coef")
                XW = S.sb([64, 8, 64], BF16, "XW")
                ectr = 0
                for ci in range(nch):
                    cs = slice(ci * Q, ci * Q + Q)
                    LTt = LT[ci % 2]
                    scan(AC[0:16, cs], ones48[0:16, 0:Q], dA[0:16, cs], 0.0, [ones48, dA], [AC])
                    scan(LTt[32:48, 0:Q], ones48[32:48, 0:Q], dA[32:48, cs], 0.0, [ones48, dA], [LTt])
                    PT = PS[7]
                    MM(PT[0:Q, 0:16], dtT[0:16, cs], ident_f[0:16, 0:16], True, True, [dtT, ident_f], [PT.r(0, 64)])
                    act(dt_tm[0:Q, :], PT[0:Q, 0:16], AF.Copy, [PT.r(0, 64)], [dt_tm])
                    if cfg.get("sd", 99) < 4:
                        continue
                    PY = PS[3]
                    for g in range(2):
                        GR = slice(64 * g, 64 * g + 64)
                        Et, Wt = Ets[ectr % 2], Wts[ectr % 2]
                        ectr += 1
                        tt(RF[g][0:16, :, 0:Q], AC[0:16, cs].unsqueeze(1).broadcast_to([16, 8, Q]),
                           DEL[g][0:16, :].unsqueeze(2).broadcast_to([16, 8, Q]), ALU.mult, [AC, DEL[g]], [RF[g]])
                        PE_ = PS[4]
                        pev = PE_[:, 0:8 * Q].rearrange("p (h i) -> p h i", i=Q)
                        MM(pev, LTt[:, :], RF[g][:, :, 0:Q], True, True, [LTt, RF[g]], [PE_])
                        act(Et[:, :, :], pev, AF.Exp, [PE_], [Et])
                        if cfg.get("sd", 99) < 5:
                            continue
                        PG = PS[5]
                        MM(PG[0:Q, 0:Q], XBC[GR, 8, cs], XBC[GR, 9, cs], True, True, [XBC.c(8), XBC.c(9)], [PG.r(0, 256)])
                        tt(Wt[0:Q, :, :], Et[0:Q, :, :], PG[0:Q, 0:Q].unsqueeze(1).broadcast_to([Q, 8, Q]), ALU.mult,
                           [Et, PG.r(0, 256)], [Wt])
                        MM(PG[:, 128:128 + Q], SHIFT[GR, :], XBC[GR, 9, cs], True, True, [SHIFT, XBC.c(9)], [PG.r(512, 768)])
                        tt(Wt[64:128, :, :], Et[64:128, :, :], PG[64:128, 128:128 + Q].unsqueeze(1).broadcast_to([64, 8, Q]), ALU.mult,
                           [Et, PG.r(512, 768)], [Wt])
                        if cfg.get("sd", 99) < 6:
                            continue
                        PX = PS[6]
                        for pr in range(4):
                            MM(PX[0:Q, 128 * pr:128 * pr + 128], XBC[:, 4 * g + pr, cs], ident_b[:, :], True, True,
                               [XBC.c(4 * g + pr), ident_b], [PX.r(512 * pr, 512 * pr + 512)])
                        dstX = XS_[l][0:Q, g, :, :].rearrange("q pr (blk d) -> q pr blk d", d=64)[:, :, 1::2, :]
                        tt(dstX, PX[0:Q, :].rearrange("q (pr eo d) -> q pr eo d", pr=4, eo=2),
                           dt_tm[0:Q, 8 * g:8 * g + 8].rearrange("q (pr eo) -> q pr eo", eo=2).unsqueeze(3).broadcast_to([Q, 4, 2, 64]),
                           ALU.mult, [PX, dt_tm], [XS_[l].c(g)])
                        if cfg.get("sd", 99) < 7:
                            continue
                        for pr in range(4):
                            k = 4 * g + pr
                            for eo in range(2):
                                win = slice(64 + 64 * eo, 192 + 64 * eo)
                                MM(PY[:, k * Q:k * Q + Q], XS_[l][:, g, pr, win], Wt[:, 2 * pr + eo, :], eo == 0, eo == 1,
                                   [XS_[l].c(g), Wt], [PY.r(4 * k * Q, 4 * k * Q + 4 * Q)])
                        if cfg.get("sd", 99) < 8:
                            continue
                        tt(coef[0:Q, :], dt_tm[0:Q, 8 * g:8 * g + 8], Et[0:Q, :, Q - 1], ALU.mult, [dt_tm, Et], [coef])
                        tt(XW[0:Q, :, :], PX[0:Q, :].rearrange("q (h d) -> q h d", d=64),
                           coef[0:Q, :].unsqueeze(2).broadcast_to([Q, 8, 64]), ALU.mult, [PX, coef], [XW])
                        MM(PG[0:Q, 64:128], XBC[GR, 8, cs], ident_b[GR, 64 * g:64 * g + 64], True, True,
                           [XBC.c(8), ident_b], [PG.r(256, 512)])
                        cp(BTOK[0:Q, 64:128], PG[0:Q, 64:128], [PG.r(256, 512)], [BTOK])
                        PSt = PS[2]
                        MM(PSt[:, :], BTOK[0:Q, :], XW[0:Q, :, :], True, True, [BTOK, XW], [PSt])
                        stv = ST_[l][64:128, g, :].rearrange("p (h d) -> p h d", d=64)
                        tt(stv, stv, Et[64:128, :, Q - 1].unsqueeze(2).broadcast_to([64, 8, 64]), ALU.mult, [ST_[l].c(g), Et], [ST_[l].c(g)])
                        tt(ST_[l][64:128, g, :], ST_[l][64:128, g, :], PSt[64:128, :], ALU.add, [ST_[l].c(g), PSt], [ST_[l].c(g)])
                        src = ST_[l][64:128, g, :].rearrange("p (pr eo d) -> p pr eo d", pr=4, eo=2)
                        dst = XS_[l][64:128, g, :, :].rearrange("p pr (blk d) -> p pr blk d", d=64)[:, :, 1::2, :]
                        cp(dst, src, [ST_[l].c(g)], [XS_[l].c(g)])
                    if cfg.get("sd", 99) < 9:
                        continue
                    act(yT[:, :, cs], PY[:, 0:8 * Q].rearrange("p (k i) -> p k i", i=Q), AF.Copy, [PY], [yT])
                if cfg.get("sd", 99) < 10:
                    continue
                for k in range(8):
                    stt(yT[:, k, :], XBC[:, k, :], pc(l, "dsk", k), yT[:, k, :], ALU.mult, ALU.add, [XBC.c(k), P[l]["_t"], yT.c(k)], [yT.c(k)])

                def z_evac(m, bank, mw):
                    act(tmpA[:, :], bank[:, 0:N], AF.Silu, [bank.r(0, 4 * N)], [tmpA])
                    tt(yT[:, m, :], yT[:, m, :], tmpA[:, :], ALU.mult, [yT.c(m), tmpA], [yT.c(m)])
                proj_fm(l, I["w_in"][l], 8, OFF_Z, 1024, hrhs, hrd, N, z_evac)
                ysq = S.sb([128, 8, N], BF16, "ysq")
                YN = S.sb([128, 8, N], BF16, "YN")
                rg = S.sb([128, N], F32, "rg")
                for k in range(8):
                    act(ysq[:, k, :], yT[:, k, :], AF.Square, [yT.c(k)], [ysq.c(k)])
                for g in range(2):
                    bank = PS[4]
                    for k in range(4 * g, 4 * g + 4):
                        MM(bank[:, 0:N], ones_b[:, :], ysq[:, k, :], k == 4 * g, k == 4 * g + 3, [ones_b, ysq.c(k)], [bank.r(0, 4 * N)])
                    act(rg[:, :], bank[:, 0:N], AF.Sqrt, [bank.r(0, 4 * N), epsT], [rg], bias=epsT[:, 0:1], scale=1.0 / 512)
                    V(lambda e, o_=rg[:, :]: e.reciprocal(out=o_, in_=o_), [rg], [rg])
                    for k in range(4 * g, 4 * g + 4):
                        stt(YN[:, k, :], yT[:, k, :], pc(l, "nssd", k), rg[:, :], ALU.mult, ALU.mult, [yT.c(k), P[l]["_t"], rg], [YN.c(k)])
                gates(0)
                proj_fm(l, I["w_lift_a"][l], 8, 0, 1024, lambda k_: YN[:, k_, :], lambda k_: [YN.c(k_)], N, mix_evac(False))
                if upto < 4:
                    continue
                S.reset(mkA)
                GT = S.sb([128, 8, N], F32, "GT")
                tmpA = S.sb([128, N], F32, "tmpA")
                BC = S.sb([128, 4, 16, 128], BF16, "BC")
                if ("bc", l) not in WG:
                    scr_ = nc.dram_tensor("bcs%d" % l, [128, 4 * 2048], BF16, kind="Internal").ap()
                    S.dma("pool", scr_.rearrange("p (i n) -> p i n", i=4), I["s5bc"][l].rearrange("i p n -> p i n"),
                          [], [("bcs", l, l + 1)], "wcast")
                    WG[("bc", l)] = scr_
                S.dma(OQ, BC[:, :, :, :].rearrange("p i c n -> p (i c n)"), WG[("bc", l)][:, :], [("bcs", l, l + 1)], [BC], "bcl")
                uT = S.sb([128, 4, N], BF16, "uT")
                u32 = S.sb([128, 4, N], F32, "u32")

                def u_evac(m, bank, mw):
                    act(u32[:, m, :], bank[:, 0:N], AF.Copy, [bank.r(0, 4 * N)], [u32.c(m)])
                    cp(uT[:, m, :], u32[:, m, :], [u32.c(m)], [uT.c(m)], eng="pool")
                proj_fm(l, I["w_in"][l], 8, OFF_U, 512, hrhs, hrd, N, u_evac)
                A_ = S.sb([128, 8, QS], F32, "A_")
                B_ = S.sb([128, 8, QS], F32, "B_")
                C_ = S.sb([128, 8, QS], F32, "C_")
                D_ = S.sb([128, 8, QS], F32, "D_")
                CDs = [(C_, D_), (S.sb([128, 8, QS], F32, "C2"), S.sb([128, 8, QS], F32, "D2"))]
                E_ = S.sb([128, 8, QS], F32, "E_")
                F_ = S.sb([128, 8, QS], F32, "F_")
                xrb = S.sb([128, 8, QS], BF16, "xrb")
                xib = S.sb([128, 8, QS], BF16, "xib")
                WIr = S.sb([128, 8], F32, "WIr")
                WIi = S.sb([128, 8], F32, "WIi")
                t8 = S.sb([128, 8], F32, "t8")
                cosT, sinT, tbr, tbi, magT = S5TAB[l]
                XR, XI = XR_[l], XI_[l]
                nsub = (N + QS - 1) // QS
                for s_ in range(nsub):
                    ncol = min(QS, N - s_ * QS)
                    cs = slice(s_ * QS, s_ * QS + ncol)
                    for hf in range(2):
                        ch = slice(8 * hf, 8 * hf + 8)
                        Pre, Pim = PS[0], PS[1]
                        for cc in range(8):
                            c = 8 * hf + cc
                            MM(Pre[:, cc * QS:cc * QS + ncol], BC[:, 0, c, :], uT[:, c // 4, cs], True, True,
                               [BC.c(0), uT.c(c // 4)], [Pre.r(4 * cc * QS, 4 * cc * QS + 4 * ncol)])
                            MM(Pim[:, cc * QS:cc * QS + ncol], BC[:, 1, c, :], uT[:, c // 4, cs], True, True,
                               [BC.c(1), uT.c(c // 4)], [Pim.r(4 * cc * QS, 4 * cc * QS + 4 * ncol)])
                        PreV = Pre[:, :].rearrange("p (c t) -> p c t", t=QS)[:, :, 0:ncol]
                        PimV = Pim[:, :].rearrange("p (c t) -> p c t", t=QS)[:, :, 0:ncol]
                        C_, D_ = CDs[(2 * s_ + hf) % 2]
                        a_, b_, c_, d_ = A_[:, :, 0:ncol], B_[:, :, 0:ncol], C_[:, :, 0:ncol], D_[:, :, 0:ncol]
                        tr_, ti_ = tbr[:, ch, 0:ncol], tbi[:, ch, 0:ncol]
                        tt(a_, PreV, tr_, ALU.mult, [Pre, tbr], [A_])
                        tt(b_, PimV, ti_, ALU.mult, [Pim, tbi], [B_])
                        tt(a_, a_, b_, ALU.subtract, [A_, B_], [A_])
                        tt(b_, PreV, ti_, ALU.mult, [Pre, tbi], [B_])
                        tt(c_, PimV, tr_, ALU.mult, [Pim, tbr], [C_])
                        tt(b_, b_, c_, ALU.add, [B_, C_], [B_])
                        cos1, sin1 = cosT[:, ch, 1], sinT[:, ch, 1]
                        tt(WIr[:, :], cos1, XR[:, ch], ALU.mult, [cosT, XR], [WIr])
                        tt(t8[:, :], sin1, XI[:, ch], ALU.mult, [sinT, XI], [t8])
                        tt(WIr[:, :], WIr[:, :], t8[:, :], ALU.subtract, [WIr, t8], [WIr])
                        tt(WIi[:, :], sin1, XR[:, ch], ALU.mult, [sinT, XR], [WIi])
                        tt(t8[:, :], cos1, XI[:, ch], ALU.mult, [cosT, XI], [t8])
                        tt(WIi[:, :], WIi[:, :], t8[:, :], ALU.add, [WIi, t8], [WIi])
                        for cc in range(8):
                            c = 8 * hf + cc
                            scan(C_[:, cc, 0:ncol], magT[:, c, 0:ncol], A_[:, cc, 0:ncol], WIr[:, cc:cc + 1], [magT, A_, WIr], [C_.c(cc)])
                            scan(D_[:, cc, 0:ncol], magT[:, c, 0:ncol], B_[:, cc, 0:ncol], WIi[:, cc:cc + 1], [magT, B_, WIi], [D_.c(cc)])
                        cosL, sinL = cosT[:, ch, ncol - 1], sinT[:, ch, ncol - 1]
                        wrl, wil = C_[:, :, ncol - 1], D_[:, :, ncol - 1]
                        tt(XR[:, ch], cosL, wrl, ALU.mult, [cosT, C_], [XR])
                        tt(t8[:, :], sinL, wil, ALU.mult, [sinT, D_], [t8])
                        tt(XR[:, ch], XR[:, ch], t8[:, :], ALU.subtract, [XR, t8], [XR])
                        tt(XI[:, ch], sinL, wrl, ALU.mult, [sinT, C_], [XI])
                        tt(t8[:, :], cosL, wil, ALU.mult, [cosT, D_], [t8])
                        tt(XI[:, ch], XI[:, ch], t8[:, :], ALU.add, [XI, t8], [XI])
                        cos_, sin_ = cosT[:, ch, 0:ncol], sinT[:, ch, 0:ncol]
                        e_, f_ = E_[:, :, 0:ncol], F_[:, :, 0:ncol]
                        tt(e_, c_, cos_, ALU.mult, [C_, cosT], [E_])
                        tt(f_, d_, sin_, ALU.mult, [D_, sinT], [F_])
                        tt(xrb[:, :, 0:ncol], e_, f_, ALU.subtract, [E_, F_], [xrb])
                        tt(e_, c_, sin_, ALU.mult, [C_, sinT], [E_])
                        tt(f_, d_, cos_, ALU.mult, [D_, cosT], [F_])
                        stt(xib[:, :, 0:ncol], e_, -1.0, f_, ALU.mult, ALU.subtract, [E_, F_], [xib])
                        for cc in range(8):
                            c = 8 * hf + cc
                            j, m_ = c // 4, c % 4
                            bank = PS[6 + j // 2]
                            col0 = (j % 2) * N + s_ * QS
                            MM(bank[:, col0:col0 + ncol], BC[:, 2, c, :], xrb[:, cc, 0:ncol], m_ == 0, False,
                               [BC.c(2), xrb], [bank.r(4 * col0, 4 * col0 + 4 * ncol)])
                            MM(bank[:, col0:col0 + ncol], BC[:, 3, c, :], xib[:, cc, 0:ncol], False, m_ == 3,
                               [BC.c(3), xib], [bank.r(4 * col0, 4 * col0 + 4 * ncol)])
                GB = S.sb([128, 4, N], BF16, "GB")
                t5 = S.sb([128, N], F32, "t5")
                t6 = S.sb([128, N], F32, "t6")
                for j in range(4):
                    bank = PS[6 + j // 2]
                    col0 = (j % 2) * N
                    stt(t5[:, :], u32[:, j, :], pc(l, "d5", j), bank[:, col0:col0 + N], ALU.mult, ALU.add,
                        [u32.c(j), P[l]["_t"], bank.r(4 * col0, 4 * col0 + 4 * N)], [t5])
                    tt(t6[:, :], t5[:, :], t5[:, :], ALU.mult, [t5], [t6])
                    ts(t6[:, :], t6[:, :], 0.044715, 1.0, ALU.mult, ALU.add, [t6], [t6])
                    tt(t6[:, :], t6[:, :], t5[:, :], ALU.mult, [t6, t5], [t6])
                    act(t6[:, :], t6[:, :], AF.Sigmoid, [t6], [t6], scale=1.5957691216057308)
                    tt(GB[:, j, :], t5[:, :], t6[:, :], ALU.mult, [t5, t6], [GB.c(j)])
                gates(1)
                for half in range(2):
                    wtA, vA = load_w(I["w_glu"][l][:, 512 * half:512 * half + 512], 4, 512)
                    wtB, vB = load_w(I["w_glu"][l][:, 1024 + 512 * half:1024 + 512 * half + 512], 4, 512)
                    for mm_ in range(4):
                        m = 4 * half + mm_
                        P1, P2 = PS[0], PS[1]
                        for j in range(4):
                            MM(P1[:, 0:N], vA[:, j, 128 * mm_:128 * mm_ + 128], GB[:, j, :], j == 0, j == 3, [wtA, GB.c(j)], [P1.r(0, 4 * N)])
                        for j in range(4):
                            MM(P2[:, 0:N], vB[:, j, 128 * mm_:128 * mm_ + 128], GB[:, j, :], j == 0, j == 3, [wtB, GB.c(j)], [P2.r(0, 4 * N)])
                        act(t5[:, :], P2[:, 0:N], AF.Sigmoid, [P2.r(0, 4 * N)], [t5])
                        tt(t5[:, :], t5[:, :], P1[:, 0:N], ALU.mult, [t5, P1.r(0, 4 * N)], [t5])
                        tt(t5[:, :], t5[:, :], GT[:, m, :], ALU.mult, [t5, GT.c(m)], [t5])
                        tt(mixT[:, m, 0:N], mixT[:, m, 0:N], t5[:, :], ALU.add, [mixT.c(m), t5], [mixT.c(m)])
                if upto < 5:
                    continue
                S.reset(mkA)
                mixb = S.sb([128, 8, N], BF16, "mixb")
                for c in range(8):
                    cp(mixb[:, c, :], mixT[:, c, 0:N], [mixT.c(c)], [mixb.c(c)], eng="pool")

                def res_evac(m, bank, mw):
                    tt(xT[:, m, 0:N], xT[:, m, 0:N], bank[:, 0:N], ALU.add, [xT.c(m), bank.r(0, 4 * N)], [xT.c(m)])
                proj_fm(l, I["w_out"][l], 8, 0, 1024, lambda k_: mixb[:, k_, :], lambda k_: [mixb.c(k_)], N, res_evac, bankset=(0, 1, 2, 3))
                rmsnorm_to(hT, xT, "nffn", l, N)
                AT = S.sb([128, 32, N], BF16, "AT")
                t5 = S.sb([128, N], F32, "t5")

                def up_evac(m, bank, mw):
                    act(t5[:, :], bank[:, 0:N], AF.Relu, [bank.r(0, 4 * N)], [t5])
                    tt(AT[:, m, :], t5[:, :], t5[:, :], ALU.mult, [t5], [AT.c(m)])
                proj_fm(l, I["w_up"][l], 8, 0, 4096, hrhs, hrd, N, up_evac, bankset=(0, 1, 2, 3))
                for m in range(8):
                    wt, vw = load_w(I["w_down"][l][:, 128 * m:128 * m + 128], 32, 128)
                    bank = PS[m % 4]
                    for k in range(32):
                        MM(bank[:, 0:N], vw[:, k, :], AT[:, k, :], k == 0, k == 31, [wt, AT.c(k)], [bank.r(0, 4 * N)])
                    res_evac(m, bank, 128)
            if upto >= 5 and kind != "meta":
                S.reset(ARENA)
                ydst = O["y_prompt"][b, t0:t0 + N, :] if kind == "prompt" else O["y_sample"][b, t0:t0 + N, :]
                y_tm = S.sb([128, nbk, D], F32, "y_tm")
                for tb, nt in blk:
                    for hc in range(2):
                        bank = PS[5 + hc]
                        for c4 in range(4):
                            c = 4 * hc + c4
                            TR(bank[0:nt, 128 * c4:128 * c4 + 128], xT[:, c, 128 * tb:128 * tb + nt], ident_f[:, :],
                               [xT.c(c), ident_f], [bank.r(512 * c4, 512 * c4 + 512)])
                        act(y_tm[0:nt, tb, 512 * hc:512 * hc + 512], bank[0:nt, :], AF.Copy, [bank], [y_tm.c(tb)])
                    S.dma(OQ, ydst[128 * tb:128 * tb + nt, :], y_tm[0:nt, tb, :], [y_tm.c(tb)], [], "yout")
        if upto >= 3:
            for l in range(2):
                if kind == "meta":
                    for dst, src in ((STm[l], ST_[l]), (CHm[l], CH_[l]), (XRm[l], XR_[l]), (XIm[l], XI_[l])):
                        cp(dst.h[:], src.h[:], [src], [dst])
                else:
                    sfx = "prompt" if kind == "prompt" else "sample"
                    S.dma(OQ, O["conv_" + sfx][l, b], CH_[l][:, :, :].rearrange("p c k -> p (c k)"), [CH_[l]], [], "stout")
                    for g in range(2):
                        S.dma(OQ, O["ssd_" + sfx][l, b, g], ST_[l][64:128, g, :], [ST_[l].c(g)], [], "stout")
                    S.dma(OQ, O["s5re_" + sfx][l, b], XR_[l][:, :], [XR_[l]], [], "stout")
                    S.dma(OQ, O["s5im_" + sfx][l, b], XI_[l][:, :], [XI_[l]], [], "stout")
    OUT_STREAMS[:] = ["kout", "vout", "yout", "stout"]
    S.emit(OUT_STREAMS)
    return nc


OUT_STREAMS = []

_NC_CACHE = {}


def pack_s5bc(inp):
    out = np.zeros((2, 4, 128, 16, 128), np.float32)
    for l in range(2):
        for i, nm in enumerate(("b_re", "b_im")):
            b = np.asarray(inp[nm], np.float32)[l]
            for c in range(16):
                m = c % 4
                for two in range(2):
                    g = 2 * c + two
                    r0 = 32 * m + 16 * two
                    out[l, i, r0:r0 + 16, c, 64 * two:64 * two + 64] = b[g].T
        for i, nm in enumerate(("c_re", "c_im")):
            cc = np.asarray(inp[nm], np.float32)[l]
            for c in range(16):
                m = c % 4
                for two in range(2):
                    g = 2 * c + two
                    c0 = 32 * m + 16 * two
                    out[l, 2 + i, 64 * two:64 * two + 64, c, c0:c0 + 16] = cc[g].T
    return out.reshape(2, 4, 128, 2048)


def kernel(**inp):
    cfg = inp.pop("_cfg", None)
    key = repr(cfg)
    if key not in _NC_CACHE:
        _NC_CACHE[key] = build(cfg)
    nc = _NC_CACHE[key]
    f = lambda a: np.ascontiguousarray(np.asarray(a, dtype=np.float32))
    wnames = ["meta_tokens", "norm_mix", "w_in", "conv_w", "conv_b", "dt_bias", "a_log", "d_ssd", "norm_ssd",
              "lam_re", "lam_im", "log_step", "w_glu", "q_norm", "k_norm",
              "w_lift_a", "w_lift_c", "w_out", "norm_ffn", "w_up", "w_down"]
    shared = {n: f(inp[n]) for n in wnames}
    shared["d_s5"] = f(inp["d_s5"]).reshape(2, 512)
    shared["par"] = pack_params(inp)
    shared["s5bc"] = pack_s5bc(inp)
    in_maps = []
    for c in range(8):
        m = dict(shared)
        m["x_prompt"] = f(inp["x_prompt"][4 * c:4 * c + 4])
        m["x_sample"] = f(inp["x_sample"][2 * c:2 * c + 2])
        m["cache_k"] = f(inp["cache_k"][:, 2 * c:2 * c + 2]).reshape(2, 2, PAST, 512)
        m["cache_v"] = f(inp["cache_v"][:, 2 * c:2 * c + 2]).reshape(2, 2, PAST, 512)
        sc = f(inp["state_conv"][:, 2 * c:2 * c + 2]).reshape(2, 2, 3, 10, 128)
        m["state_conv"] = np.ascontiguousarray(sc.transpose(0, 1, 4, 3, 2)).reshape(2, 2, 128, 30)
        ss = f(inp["state_ssd"][:, 2 * c:2 * c + 2])
        m["state_ssd"] = np.ascontiguousarray(ss.transpose(0, 1, 2, 5, 3, 4)).reshape(2, 2, 2, 64, 512)
        for nm in ("state_s5_re", "state_s5_im"):
            a5 = f(inp[nm][:, 2 * c:2 * c + 2]).reshape(2, 2, 16, 2, 64)
            m[nm] = np.ascontiguousarray(a5.transpose(0, 1, 3, 4, 2)).reshape(2, 2, 128, 16)
        in_maps.append(m)
    ncores = (cfg or {}).get('ncores', 8)
    res = run_bass_kernel_spmd(nc, in_maps[:ncores], core_ids=list(range(ncores)))
    R = list(res.results) + [res.results[0]] * (8 - ncores)
    cat = lambda k, ax: np.concatenate([np.asarray(R[c][k], dtype=np.float32) for c in range(8)], axis=ax)
    y_p = cat("y_prompt", 0)
    y_s = cat("y_sample", 0)
    k_p = cat("k_prompt", 1).reshape(2, 32, TP, 8, 64)
    v_p = cat("v_prompt", 1).reshape(2, 32, TP, 8, 64)
    unconv = lambda a: np.ascontiguousarray(a.reshape(2, -1, 128, 10, 3).transpose(0, 1, 4, 3, 2)).reshape(2, -1, 3, 1280)
    unssd = lambda a: np.ascontiguousarray(a.reshape(2, -1, 2, 64, 8, 64).transpose(0, 1, 2, 4, 5, 3))
    uns5 = lambda a: np.ascontiguousarray(a.reshape(2, -1, 2, 64, 16).transpose(0, 1, 4, 2, 3)).reshape(2, -1, 32, 64)
    conv_p = unconv(cat("conv_prompt", 1))
    ssd_p = unssd(cat("ssd_prompt", 1))
    s5r_p = uns5(cat("s5re_prompt", 1))
    s5i_p = uns5(cat("s5im_prompt", 1))
    k_s = cat("k_sample", 1).reshape(2, 16, DSEQ, 8, 64)
    v_s = cat("v_sample", 1).reshape(2, 16, DSEQ, 8, 64)
    conv_s = unconv(cat("conv_sample", 1))
    ssd_s = unssd(cat("ssd_sample", 1))
    s5r_s = uns5(cat("s5re_sample", 1))
    s5i_s = uns5(cat("s5im_sample", 1))
    return (y_p, y_s, k_p, v_p, conv_p, ssd_p, s5r_p, s5i_p, k_s, v_s, conv_s, ssd_s, s5r_s, s5i_s)
```

```python
import math
import bisect
import numpy as np
import concourse.bass as bass
import concourse.mybir as mybir
from concourse.bass_utils import run_bass_kernel_spmd

F32 = mybir.dt.float32
F32R = mybir.dt.float32r
BF16 = mybir.dt.bfloat16
AF = mybir.ActivationFunctionType
ALU = mybir.AluOpType
AX = mybir.AxisListType

D = 1024
NMETA = 16
SEQ = 2048
TP = NMETA + SEQ
PAST = 2048
DSEQ = 64
NKMAX = 2112
INC = 7440
OFF_Z, OFF_XBC, OFF_DT, OFF_U, OFF_Q, OFF_K, OFF_V, OFF_G = 0, 1024, 2304, 2320, 2832, 3344, 3856, 4368
EPS = 1e-6
NT = 256
QS = 64
NEG = -30000.0
PCOL = {}
_c = 0
for _nm, _w in [("nmix", 8), ("nffn", 8), ("nssd", 8), ("cw", 40), ("cb", 10), ("d5", 4), ("dtb", 1), ("alog", 1),
                ("dsk", 8), ("qn", 1), ("kn", 1), ("lre", 16), ("lim", 16), ("lst", 16)]:
    PCOL[_nm] = (_c, _w)
    _c += _w
NPAR = _c


def pack_params(inp):
    par = np.zeros((2, 128, NPAR), np.float32)
    g = lambda n: np.asarray(inp[n], np.float32)

    def put(l, nm, arr):
        c0, w = PCOL[nm]
        par[l, :, c0:c0 + w] = arr

    for l in range(2):
        put(l, "nmix", g("norm_mix")[l].reshape(8, 128).T)
        put(l, "nffn", g("norm_ffn")[l].reshape(8, 128).T)
        put(l, "nssd", g("norm_ssd")[l].reshape(8, 128).T)
        cw = g("conv_w")[l].reshape(4, 10, 128)
        put(l, "cw", cw.transpose(2, 0, 1).reshape(128, 40))
        put(l, "cb", g("conv_b")[l].reshape(10, 128).T)
        put(l, "d5", g("d_s5")[l].reshape(4, 128).T)
        for nm, src in (("dtb", "dt_bias"), ("alog", "a_log")):
            col = np.zeros((128, 1), np.float32)
            col[0:16, 0] = g(src)[l]
            col[32:48, 0] = g(src)[l]
            put(l, nm, col)
        dsk = np.zeros((128, 8), np.float32)
        d = g("d_ssd")[l]
        for hh in range(2):
            dsk[64 * hh:64 * hh + 64, :] = d[hh::2][None, :]
        put(l, "dsk", dsk)
        put(l, "qn", np.tile(g("q_norm")[l], 2)[:, None])
        put(l, "kn", np.tile(g("k_norm")[l], 2)[:, None])
        for nm, src in (("lre", "lam_re"), ("lim", "lam_im")):
            a = g(src)[l].reshape(16, 2, 64)
            put(l, nm, a.transpose(1, 2, 0).reshape(128, 16))
        ls = g("log_step")[l].reshape(16, 2)
        put(l, "lst", np.repeat(ls.T[:, None, :], 64, axis=1).reshape(128, 16))
    return par
DTSZ = {F32: 4, F32R: 4, BF16: 2}


class Tile:
    def __init__(self, h, space, lo, hi):
        self.h, self.space, self.lo, self.hi = h, space, lo, hi

    def __getitem__(self, k):
        return self.h[k]

    @property
    def all(self):
        return (self.space, self.lo, self.hi)

    def r(self, lo, hi):
        return (self.space, self.lo + lo, self.lo + hi)

    def c(self, i, n=1):
        return (self.space, self.lo + i * self.cb, self.lo + (i + n) * self.cb)


def _reg(x):
    return x.all if isinstance(x, Tile) else x


class Sched:
    ENG = ["pe", "act", "dve", "pool", "sp"]

    def __init__(self, nc):
        self.nc = nc
        self.ops = []
        self.segs = {}
        self.off = 16512
        self.cnt = 0
        self.stream_n = {}
        self.psb = []

    def sb(self, shape, dtype, name="t"):
        nb = int(np.prod(shape[1:])) * DTSZ[dtype]
        off = (self.off + 31) // 32 * 32
        self.cnt += 1
        h = self.nc.alloc_sbuf_tensor_at(f"{name}{self.cnt}", list(shape), dtype, offset=off)
        self.off = off + nb
        self.peak = max(getattr(self, "peak", 0), self.off)
        assert self.off <= 16512 + 208000, ("sbuf overflow", name, self.off)
        t = Tile(h, "sb", off, off + nb)
        t.cb = (int(np.prod(shape[2:])) if len(shape) > 2 else 1) * DTSZ[dtype]
        return t

    def mark(self):
        return self.off

    def reset(self, m):
        self.off = m

    def _access(self, opi, reg, write, deps, norecord=False):
        space, lo, hi = reg
        if space not in self.segs:
            self.segs[space] = ([0], [[None, {}]])
        starts, data = self.segs[space]
        for b in (lo, hi):
            i = bisect.bisect_right(starts, b) - 1
            if starts[i] != b:
                starts.insert(i + 1, b)
                data.insert(i + 1, [data[i][0], dict(data[i][1])])
        i = bisect.bisect_left(starts, lo)
        while i < len(starts) and starts[i] < hi:
            w, rd = data[i]
            if w is not None and w != opi:
                if not write:
                    deps[w] = "raw"
                elif deps.get(w) != "raw":
                    deps[w] = "waw"
            if write:
                for r_ in rd.values():
                    if r_ != opi and r_ not in deps:
                        deps[r_] = "war"
                data[i][0] = opi
                data[i][1] = {}
            elif not norecord:
                key = self.ops[opi]["rk"]
                rd[key] = opi
            i += 1

    def op(self, eng, fn, reads=(), writes=(), stream=None):
        opi = len(self.ops)
        o = {"eng": eng, "fn": fn, "stream": stream, "deps": {}, "sig": False}
        if stream is not None:
            o["rk"] = "dma:" + stream
        else:
            o["rk"] = eng
        self.ops.append(o)
        deps = {}
        wregs = [_reg(w_) for w_ in writes]
        for r_ in reads:
            rr = _reg(r_)
            inplace = any(w[0] == rr[0] and w[1] < rr[2] and rr[1] < w[2] for w in wregs)
            self._access(opi, rr, False, deps, norecord=inplace)
        for w_ in writes:
            rg_ = _reg(w_)
            if eng == "pe" and rg_[0] == "ps":
                rg_ = ("ps", rg_[1] // 2048 * 2048, (rg_[2] + 2047) // 2048 * 2048)
            self._access(opi, rg_, True, deps)
        res = {}
        for d, kind in deps.items():
            od = self.ops[d]
            if od["stream"] is not None:
                res[d] = ("s", od["stream"], 16 * self.stream_n[od["stream"]])
            else:
                if od["eng"] == eng and stream is None:
                    if eng == "pe" or kind == "war":
                        continue
                res[d] = ("e", od["eng"], None)
                od["sig"] = True
        lw = self.__dict__.setdefault("last_waiter", {})
        if stream is not None and stream in lw:
            w = lw[stream]
            ow = self.ops[w]
            if ow["eng"] != eng and w not in res:
                if ow["stream"] is not None:
                    res[w] = ("s", ow["stream"], 16 * self.stream_n[ow["stream"]])
                else:
                    res[w] = ("e", ow["eng"], None)
                    ow["sig"] = True
        for d, (k, key, val) in res.items():
            if k == "s":
                lw[key] = opi
        o["deps"] = res
        if stream is not None:
            self.stream_n[stream] = self.stream_n.get(stream, 0) + 1
            o["sval"] = 16 * self.stream_n[stream]
        return opi

    def dma(self, q, out, in_, reads, writes, stream, **kw):
        return self.op(q, lambda e: e.dma_start(out=out, in_=in_, **kw), reads, writes, stream=stream)

    def emit(self, final_streams):
        nc = self.nc
        counts = {e: 0 for e in self.ENG}
        for o in self.ops:
            if o["stream"] is None and o["sig"]:
                counts[o["eng"]] += 1
                o["sval"] = counts[o["eng"]]
        from contextlib import ExitStack
        with ExitStack() as es:
            esem = {e: es.enter_context(nc.semaphore("e_" + e)) for e in ["pe", "act", "dve", "pool"]}
            ssem = {s: es.enter_context(nc.semaphore("s_" + s)) for s in self.stream_n}
            block = es.enter_context(nc.Block())
            per = {e: [o for o in self.ops if o["eng"] == e] for e in self.ENG}

            def run(ename, e):
                known = {}
                for o in per[ename]:
                    for d, (k, key, val) in o["deps"].items():
                        if k == "s":
                            sem, v = ssem[key], val
                        else:
                            sem, v = esem[key], self.ops[d]["sval"]
                        kk = (k, key)
                        if known.get(kk, 0) >= v:
                            continue
                        known[kk] = v
                        e.wait_ge(sem, v)
                    ins = o["fn"](e)
                    if o["stream"] is not None:
                        ins.then_inc(ssem[o["stream"]], 16)
                    elif o["sig"]:
                        ins.then_inc(esem[ename], 1)
                if ename == "sp":
                    for s in final_streams:
                        if s in self.stream_n:
                            e.wait_ge(ssem[s], 16 * self.stream_n[s])

            @block.tensor
            def _(e):
                run("pe", e)

            @block.scalar
            def _(e):
                run("act", e)

            @block.vector
            def _(e):
                run("dve", e)

            @block.gpsimd
            def _(e):
                run("pool", e)

            @block.sync
            def _(e):
                run("sp", e)


def build(cfg=None):
    cfg = cfg or {}
    n_ptiles = cfg.get("n_ptiles", SEQ // NT)
    n_pseq = cfg.get("n_pseq", 4)
    n_sseq = cfg.get("n_sseq", 2)
    nc = bass.Bass("TRN2", target_bir_lowering=False)
    S = Sched(nc)

    def din(name, shape):
        return nc.dram_tensor(name, list(shape), F32, kind="ExternalInput").ap()

    def dout(name, shape):
        return nc.dram_tensor(name, list(shape), F32, kind="ExternalOutput").ap()

    I = {}
    I["x_prompt"] = din("x_prompt", [4, SEQ, D])
    I["x_sample"] = din("x_sample", [2, DSEQ, D])
    I["cache_k"] = din("cache_k", [2, 2, PAST, 512])
    I["cache_v"] = din("cache_v", [2, 2, PAST, 512])
    I["state_conv"] = din("state_conv", [2, 2, 128, 30])
    I["state_ssd"] = din("state_ssd", [2, 2, 2, 64, 512])
    I["state_s5_re"] = din("state_s5_re", [2, 2, 128, 16])
    I["state_s5_im"] = din("state_s5_im", [2, 2, 128, 16])
    I["meta_tokens"] = din("meta_tokens", [NMETA, D])
    for nm, sh in [("norm_mix", [2, D]), ("w_in", [2, D, INC]), ("conv_w", [2, 4, 1280]), ("conv_b", [2, 1280]),
                   ("dt_bias", [2, 16]), ("a_log", [2, 16]), ("d_ssd", [2, 16]), ("norm_ssd", [2, D]),
                   ("lam_re", [2, 32, 64]), ("lam_im", [2, 32, 64]), ("log_step", [2, 32]),
                   ("s5bc", [2, 4, 128, 2048]), ("d_s5", [2, 512]), ("w_glu", [2, 512, 2048]),
                   ("q_norm", [2, 64]), ("k_norm", [2, 64]), ("w_lift_a", [2, D, D]), ("w_lift_c", [2, 512, D]),
                   ("w_out", [2, D, D]), ("norm_ffn", [2, D]), ("w_up", [2, D, 4096]), ("w_down", [2, 4096, D])]:
        I[nm] = din(nm, sh)
    O = {}
    O["y_prompt"] = dout("y_prompt", [4, SEQ, D])
    O["y_sample"] = dout("y_sample", [2, DSEQ, D])
    O["k_prompt"] = dout("k_prompt", [2, 4, TP, 512])
    O["v_prompt"] = dout("v_prompt", [2, 4, TP, 512])
    O["conv_prompt"] = dout("conv_prompt", [2, 4, 128, 30])
    O["ssd_prompt"] = dout("ssd_prompt", [2, 4, 2, 64, 512])
    O["s5re_prompt"] = dout("s5re_prompt", [2, 4, 128, 16])
    O["s5im_prompt"] = dout("s5im_prompt", [2, 4, 128, 16])
    O["k_sample"] = dout("k_sample", [2, 2, DSEQ, 512])
    O["v_sample"] = dout("v_sample", [2, 2, DSEQ, 512])
    O["conv_sample"] = dout("conv_sample", [2, 2, 128, 30])
    O["ssd_sample"] = dout("ssd_sample", [2, 2, 2, 64, 512])
    O["s5re_sample"] = dout("s5re_sample", [2, 2, 128, 16])
    O["s5im_sample"] = dout("s5im_sample", [2, 2, 128, 16])
    kTs = nc.dram_tensor("kT_scr", [6, 2, 512, NKMAX], BF16, kind="Internal").ap()
    vhs = nc.dram_tensor("vh_scr", [6, 2, NKMAX, 512], BF16, kind="Internal").ap()

    PS = []
    for b in range(8):
        h = nc.alloc_psum_tensor(f"psb{b}", [128, 512], F32)
        PS.append(Tile(h, "ps", b * 2048, (b + 1) * 2048))

    def V(fn, reads, writes):
        return S.op("dve", fn, reads, writes)

    def A(fn, reads, writes):
        return S.op("act", fn, reads, writes)

    def G(fn, reads, writes):
        return S.op("pool", fn, reads, writes)

    def MM(out, lhsT, rhs, start, stop, reads, writes, **kw):
        return S.op("pe", lambda e: e.matmul(out, lhsT=lhsT, rhs=rhs, start=start, stop=stop, **kw), reads, writes)

    def TR(out, in_, ident, reads, writes):
        return S.op("pe", lambda e: e.matmul(out, lhsT=in_, rhs=ident, start=True, stop=True), reads, writes)

    def act(out, in_, func, reads, writes, bias=None, scale=None):
        kw = {}
        if bias is not None:
            kw["bias"] = bias
        if scale is not None:
            kw["scale"] = scale
        return A(lambda e: e.activation(out=out, in_=in_, func=func, **kw), reads, writes)

    def ldpar(out, in_, writes):
        return S.dma("sp", out, in_, [], writes, "par", allow_slow_non_contiguous=True)

    iota_pc = S.sb([128, 256], F32, "iota")
    ident_f = S.sb([128, 128], F32, "identf")
    ident_b = S.sb([128, 128], BF16, "identb")
    ones_b = S.sb([128, 128], BF16, "onesb")
    bd64_b = S.sb([128, 128], BF16, "bd64")
    tri_r = S.sb([128, 128], F32R, "tri")
    ones_r = S.sb([128, 128], F32R, "onesr")
    iota_t = S.sb([128, QS + 1], F32, "iotat")
    ones_f = S.sb([128, 128], F32, "onesf")
    masks = {}
    G(lambda e: e.iota(iota_pc[:, :], [[-1, 256]], base=0, channel_multiplier=1,
                       allow_small_or_imprecise_dtypes=True), [], [iota_pc])
    G(lambda e: e.iota(iota_t[:, :], [[1, QS + 1]], base=0, channel_multiplier=0,
                       allow_small_or_imprecise_dtypes=True), [], [iota_t])
    V(lambda e: e.tensor_single_scalar(out=ident_f[:, :], in_=iota_pc[:, 0:128], scalar=0.0, op=ALU.is_equal),
      [iota_pc], [ident_f])
    V(lambda e: e.tensor_copy(out=ident_b[:, :], in_=ident_f[:, :]), [ident_f], [ident_b])
    V(lambda e: e.memset(ones_b[:, :], 1.0), [], [ones_b])
    V(lambda e: e.memset(bd64_b[:, :], 0.0), [], [bd64_b])
    V(lambda e: e.memset(bd64_b[0:64, 0:64], 1.0), [], [bd64_b])
    V(lambda e: e.memset(bd64_b[64:128, 64:128], 1.0), [], [bd64_b])
    V(lambda e: e.tensor_scalar(out=tri_r[:, :], in0=iota_pc[:, 0:128], scalar1=0.0, scalar2=-8.0,
                                op0=ALU.is_ge, op1=ALU.mult), [iota_pc], [tri_r])
    V(lambda e: e.memset(ones_f[:, :], 1.0), [], [ones_f])
    V(lambda e: e.tensor_copy(out=ones_r[:, :], in_=ones_f[:, :]), [ones_f], [ones_r])
    for off in (16, -112, -240, 0):
        m = S.sb([128, 256], F32, "mask")
        V(lambda e, m=m, off=off: e.tensor_single_scalar(out=m[:, :], in_=iota_pc[:, :], scalar=float(off),
                                                         op=ALU.is_lt), [iota_pc], [m])
        masks[off] = m

    P = []
    stage = cfg.get('stage', 99)
    I["par"] = din("par", [2, 128, NPAR])
    for l in range(2):
        pt = S.sb([128, NPAR], F32, "par")
        S.dma("sp", pt[:, :], I["par"][l], [], [pt], "par")
        p = {"_t": pt}
        for nm, (c0, w) in PCOL.items():
            p[nm] = (pt, c0, w)
        P.append(p)

    def pc(l, nm, j=0, rows=slice(0, 128)):
        pt, c0, w = P[l][nm]
        return pt[rows, c0 + j:c0 + j + 1]

    def pv(l, nm):
        pt, c0, w = P[l][nm]
        return pt[:, c0:c0 + w]

    TWO_PI = 2.0 * math.pi
    MAGIC = 12582912.0
    S5TAB = []
    for l in range(2):
        S5TAB.append((S.sb([128, 16, QS + 1], F32, "cosT"), S.sb([128, 16, QS + 1], F32, "sinT"),
                      S.sb([128, 16, QS], F32, "tbr"), S.sb([128, 16, QS], F32, "tbi"),
                      S.sb([128, 16, QS], F32, "magT")))
    S5L = [(S.sb([128, 16], F32, "lre"), S.sb([128, 16], F32, "lim"), S.sb([128, 16], F32, "lst")) for l in range(2)]
    ARENA0 = S.mark()
    S5SCR = [S.sb([128, 16], F32, "s5s") for _ in range(8)] + [S.sb([128, 16, QS + 1], F32, "s5w") for _ in range(3)]
    for l in range(2 if stage >= 2 else 0):
        p = P[l]
        lre, lim, lst = S5L[l]
        for dst, nm in ((lre, "lre"), (lim, "lim"), (lst, "lst")):
            V(lambda e, dst=dst, nm=nm, l=l: e.tensor_copy(out=dst[:, :], in_=pv(l, nm)), [P[l]["_t"]], [dst])
        step, th, mag, t0_, t1_, t2_, fre, fim, ang, w1, w2 = S5SCR
        cosT, sinT, tbr, tbi, magT = S5TAB[l]
        p.update(cosT=cosT, sinT=sinT, tbr=tbr, tbi=tbi, magT=magT, mag=mag)
        act(step[:, :], lst[:, :], AF.Exp, [lst], [step])
        V(lambda e, th=th, lim=lim, step=step: e.tensor_tensor(out=th[:, :], in0=lim[:, :], in1=step[:, :], op=ALU.mult),
          [lim, step], [th])
        V(lambda e, t0_=t0_, lre=lre, step=step: e.tensor_tensor(out=t0_[:, :], in0=lre[:, :], in1=step[:, :], op=ALU.mult),
          [lre, step], [t0_])
        act(mag[:, :], t0_[:, :], AF.Exp, [t0_], [mag])
        V(lambda e, ang=ang, th=th: e.tensor_tensor(
            out=ang[:, :, :], in0=iota_t[:, :].unsqueeze(1).broadcast_to([128, 16, QS + 1]),
            in1=th[:, :].unsqueeze(2).broadcast_to([128, 16, QS + 1]), op=ALU.mult), [iota_t, th], [ang])
        for which, outT in (("sin", sinT), ("cos", cosT)):
            addc = 0.0 if which == "sin" else 0.25
            V(lambda e, ang=ang, w1=w1, addc=addc: e.tensor_scalar(
                out=w1[:, :, :], in0=ang[:, :, :], scalar1=1.0 / TWO_PI, scalar2=addc, op0=ALU.mult, op1=ALU.add),
              [ang], [w1])
            V(lambda e, w1=w1, w2=w2: e.tensor_scalar(out=w2[:, :, :], in0=w1[:, :, :], scalar1=MAGIC, scalar2=None,
                                                     op0=ALU.add), [w1], [w2])
            V(lambda e, w2=w2: e.tensor_scalar(out=w2[:, :, :], in0=w2[:, :, :], scalar1=-MAGIC, scalar2=None,
                                               op0=ALU.add), [w2], [w2])
            V(lambda e, w1=w1, w2=w2: e.tensor_tensor(out=w1[:, :, :], in0=w1[:, :, :], in1=w2[:, :, :],
                                                     op=ALU.subtract), [w1, w2], [w1])
            V(lambda e, w1=w1: e.tensor_scalar(out=w1[:, :, :], in0=w1[:, :, :], scalar1=-0.4999, scalar2=0.4999,
                                               op0=ALU.max, op1=ALU.min), [w1], [w1])
            act(outT[:, :, :], w1[:, :, :], AF.Sin, [w1], [outT], scale=TWO_PI)
        abr, abi = t1_, t2_
        V(lambda e, abr=abr, cosT=cosT, mag=mag: e.tensor_tensor(out=abr[:, :], in0=cosT[:, :, 1], in1=mag[:, :], op=ALU.mult),
          [cosT, mag], [abr])
        V(lambda e, abi=abi, sinT=sinT, mag=mag: e.tensor_tensor(out=abi[:, :], in0=sinT[:, :, 1], in1=mag[:, :], op=ALU.mult),
          [sinT, mag], [abi])
        V(lambda e, abr=abr: e.tensor_scalar(out=abr[:, :], in0=abr[:, :], scalar1=-1.0, scalar2=None, op0=ALU.add),
          [abr], [abr])
        den = step
        V(lambda e, den=den, lre=lre: e.tensor_tensor(out=den[:, :], in0=lre[:, :], in1=lre[:, :], op=ALU.mult), [lre], [den])
        V(lambda e, t0_=t0_, lim=lim: e.tensor_tensor(out=t0_[:, :], in0=lim[:, :], in1=lim[:, :], op=ALU.mult), [lim], [t0_])
        V(lambda e, den=den, t0_=t0_: e.tensor_tensor(out=den[:, :], in0=den[:, :], in1=t0_[:, :], op=ALU.add), [den, t0_], [den])
        V(lambda e, den=den: e.reciprocal(out=den[:, :], in_=den[:, :]), [den], [den])
        V(lambda e, fre=fre, abr=abr, lre=lre: e.tensor_tensor(out=fre[:, :], in0=abr[:, :], in1=lre[:, :], op=ALU.mult), [abr, lre], [fre])
        V(lambda e, t0_=t0_, abi=abi, lim=lim: e.tensor_tensor(out=t0_[:, :], in0=abi[:, :], in1=lim[:, :], op=ALU.mult), [abi, lim], [t0_])
        V(lambda e, fre=fre, t0_=t0_: e.tensor_tensor(out=fre[:, :], in0=fre[:, :], in1=t0_[:, :], op=ALU.add), [fre, t0_], [fre])
        V(lambda e, fre=fre, den=den: e.tensor_tensor(out=fre[:, :], in0=fre[:, :], in1=den[:, :], op=ALU.mult), [fre, den], [fre])
        V(lambda e, fim=fim, abi=abi, lre=lre: e.tensor_tensor(out=fim[:, :], in0=abi[:, :], in1=lre[:, :], op=ALU.mult), [abi, lre], [fim])
        V(lambda e, t0_=t0_, abr=abr, lim=lim: e.tensor_tensor(out=t0_[:, :], in0=abr[:, :], in1=lim[:, :], op=ALU.mult), [abr, lim], [t0_])
        V(lambda e, fim=fim, t0_=t0_: e.tensor_tensor(out=fim[:, :], in0=fim[:, :], in1=t0_[:, :], op=ALU.subtract), [fim, t0_], [fim])
        V(lambda e, fim=fim, den=den: e.tensor_tensor(out=fim[:, :], in0=fim[:, :], in1=den[:, :], op=ALU.mult), [fim, den], [fim])
        frb = lambda f: f[:, :].unsqueeze(2).broadcast_to([128, 16, QS])
        V(lambda e, w1=w1, cosT=cosT, fre=fre: e.tensor_tensor(out=w1[:, :, 0:QS], in0=cosT[:, :, 0:QS], in1=frb(fre), op=ALU.mult), [cosT, fre], [w1])
        V(lambda e, w2=w2, sinT=sinT, fim=fim: e.tensor_tensor(out=w2[:, :, 0:QS], in0=sinT[:, :, 0:QS], in1=frb(fim), op=ALU.mult), [sinT, fim], [w2])
        V(lambda e, tbr=tbr, w1=w1, w2=w2: e.tensor_tensor(out=tbr[:, :, :], in0=w1[:, :, 0:QS], in1=w2[:, :, 0:QS], op=ALU.add), [w1, w2], [tbr])
        V(lambda e, w1=w1, cosT=cosT, fim=fim: e.tensor_tensor(out=w1[:, :, 0:QS], in0=cosT[:, :, 0:QS], in1=frb(fim), op=ALU.mult), [cosT, fim], [w1])
        V(lambda e, w2=w2, sinT=sinT, fre=fre: e.tensor_tensor(out=w2[:, :, 0:QS], in0=sinT[:, :, 0:QS], in1=frb(fre), op=ALU.mult), [sinT, fre], [w2])
        V(lambda e, tbi=tbi, w1=w1, w2=w2: e.tensor_tensor(out=tbi[:, :, :], in0=w1[:, :, 0:QS], in1=w2[:, :, 0:QS], op=ALU.subtract), [w1, w2], [tbi])
        V(lambda e, magT=magT, mag=mag: e.tensor_tensor(
            out=magT[:, :, :], in0=ones_f[:, 0:QS].unsqueeze(1).broadcast_to([128, 16, QS]),
            in1=mag[:, :].unsqueeze(2).broadcast_to([128, 16, QS]), op=ALU.mult), [ones_f, mag], [magT])
    OQ = cfg.get("oq", "pool")
    WG = {}
    upto = cfg.get("upto", 99)
    epsT = S.sb([128, 1], F32, "eps")
    V(lambda e: e.memset(epsT[:, :], EPS), [], [epsT])
    oneT = S.sb([128, 1], F32, "one")
    V(lambda e: e.memset(oneT[:, :], 1.0), [], [oneT])
    WDT = []
    for l in range(2):
        w_ = S.sb([128, 8, 48], BF16, "WDT")
        V(lambda e, w_=w_: e.memset(w_[:, :, :], 0.0), [], [w_])
        for c0 in (0, 32):
            S.dma("pool", w_[:, :, c0:c0 + 16], I["w_in"][l][:, OFF_DT:OFF_DT + 16].rearrange("(kc p) n -> p kc n", p=128),
                  [], [w_], "wdt")
        WDT.append(w_)

    def tt(out, in0, in1, op, reads, writes, eng="dve"):
        return S.op(eng, lambda e: e.tensor_tensor(out=out, in0=in0, in1=in1, op=op), reads, writes)

    def ts(out, in0, s1, s2, op0, op1, reads, writes, eng="dve"):
        if op1 is None:
            return S.op(eng, lambda e: e.tensor_scalar(out=out, in0=in0, scalar1=s1, scalar2=None, op0=op0), reads, writes)
        return S.op(eng, lambda e: e.tensor_scalar(out=out, in0=in0, scalar1=s1, scalar2=s2, op0=op0, op1=op1), reads, writes)

    def stt(out, in0, scalar, in1, op0, op1, reads, writes):
        return V(lambda e: e.scalar_tensor_tensor(out=out, in0=in0, scalar=scalar, in1=in1, op0=op0, op1=op1), reads, writes)

    def cp(out, in_, reads, writes, eng="dve"):
        return S.op(eng, lambda e: e.tensor_copy(out=out, in_=in_), reads, writes)

    def scan(out, d0, d1, init, reads, writes):
        return V(lambda e: e.tensor_tensor_scan(out=out, data0=d0, data1=d1, initial=init, op0=ALU.mult, op1=ALU.add), reads, writes)

    NWS = cfg.get('nws', 4)
    WSL = [S.sb([128, 4096], BF16, "wslot") for _ in range(NWS)]
    wctr = [0]

    def load_w(src_ap, kc, ncols, prt=128):
        key = (src_ap.tensor.name, int(src_ap.offset), kc, ncols, prt)
        if key not in WG:
            gi = len(WG)
            scr = nc.dram_tensor("wg%d" % gi, [prt, kc * ncols], BF16, kind="Internal").ap()
            S.dma("pool", scr.rearrange("p (kc n) -> p kc n", n=ncols), src_ap.rearrange("(kc p) n -> p kc n", p=prt),
                  [], [("wg", gi, gi + 1)], "wcast")
            WG[key] = (gi, scr)
        gi, scr = WG[key]
        i = wctr[0] % NWS
        wctr[0] += 1
        wt = WSL[i]
        view = wt[0:prt, 0:kc * ncols].rearrange("p (kc n) -> p kc n", n=ncols)
        S.dma("sp", wt[0:prt, 0:kc * ncols], scr[:, :], [("wg", gi, gi + 1)], [wt.r(0, kc * ncols * 2)], f"w{i}")
        return wt, view

    ones48 = S.sb([48, 64], F32, "ones48")
    V(lambda e: e.memset(ones48[:, :], 1.0), [], [ones48])
    LT = []
    for i in range(2):
        t = S.sb([128, 128], F32, "LT")
        V(lambda e, t=t: e.memset(t[:, :], 0.0), [], [t])
        V(lambda e, t=t: e.memset(t[0:16, :], 1.0), [], [t])
        cp(t[64:128, 0:64], ident_f[64:128, 64:128], [ident_f], [t])
        LT.append(t)
    RF = []
    for g in range(2):
        t = S.sb([128, 8, 64], F32, "RF")
        V(lambda e, t=t: e.memset(t[:, :, :], 0.0), [], [t])
        for h in range(8):
            if True:
                r = 8 * g + h
                pass
        RF.append(t)
    DEL = []
    for g in range(2):
        d_ = S.sb([48, 8], F32, "DEL")
        io = S.sb([48, 8], F32, "DELi")
        G(lambda e, io=io: e.iota(io[:, :], [[-1, 8]], base=0, channel_multiplier=1, allow_small_or_imprecise_dtypes=True), [], [io])
        ts(d_[0:32, :], io[0:32, :], float(8 * g), None, ALU.is_equal, None, [io], [d_])
        ts(d_[32:48, :], io[32:48, :], float(32 + 8 * g), None, ALU.is_equal, None, [io], [d_])
        DEL.append(d_)
        cp(RF[g][32:48, :, :], d_[32:48, :].unsqueeze(2).broadcast_to([16, 8, 64]), [d_], [RF[g]])
        ts(RF[g][64:128, :, :], iota_pc[64:128, 0:64].unsqueeze(1).broadcast_to([64, 8, 64]), 64.0, NEG, ALU.is_gt, ALU.mult,
           [iota_pc], [RF[g]])
    SHIFT = S.sb([128, 128], BF16, "shift")
    V(lambda e: e.memset(SHIFT[:, :], 0.0), [], [SHIFT])
    cp(SHIFT[0:64, 64:128], ident_b[0:64, 0:64], [ident_b], [SHIFT])
    cp(SHIFT[64:128, 64:128], ident_b[64:128, 64:128], [ident_b], [SHIFT])
    BTOK = S.sb([64, 128], BF16, "btok")
    V(lambda e: e.memset(BTOK[:, :], 0.0), [], [BTOK])
    ST_, XS_, CH_, XR_, XI_, STm, CHm, XRm, XIm = [], [], [], [], [], [], [], [], []
    for l in range(2):
        ST_.append(S.sb([128, 2, 512], F32, "ST"))
        XS_.append(S.sb([128, 2, 4, 256], BF16, "XS"))
        CH_.append(S.sb([128, 10, 3], F32, "CH"))
        XR_.append(S.sb([128, 16], F32, "XR"))
        XI_.append(S.sb([128, 16], F32, "XI"))
        STm.append(S.sb([128, 2, 512], F32, "STm"))
        CHm.append(S.sb([128, 10, 3], F32, "CHm"))
        XRm.append(S.sb([128, 16], F32, "XRm"))
        XIm.append(S.sb([128, 16], F32, "XIm"))
    acol = []
    for l in range(2):
        a_ = S.sb([48, 1], F32, "acol")
        act(a_[:, :], pc(l, "alog", 0, slice(0, 48)), AF.Exp, [P[l]["_t"]], [a_])
        ts(a_[0:32, :], a_[0:32, :], -1.0, None, ALU.mult, None, [a_], [a_])
        acol.append(a_)

    def st_to_xs(l):
        for g in range(2):
            src = ST_[l][64:128, g, :].rearrange("p (pr eo d) -> p pr eo d", pr=4, eo=2)
            dst = XS_[l][64:128, g, :, :].rearrange("p pr (blk d) -> p pr blk d", d=64)[:, :, 1::2, :]
            cp(dst, src, [ST_[l].c(g)], [XS_[l].c(g)])

    xT = S.sb([128, 8, NT], F32, "xT")
    mixT = S.sb([128, 8, NT], F32, "mixT")
    hT = S.sb([128, 8, NT], BF16, "hT")
    ARENA = S.mark()
    seqs = [dict(kind="meta", b=0, T=NMETA, slot=0)]
    for b in range(n_pseq):
        seqs.append(dict(kind="prompt", b=b, T=n_ptiles * NT, slot=b))
    for b in range(n_sseq):
        seqs.append(dict(kind="sample", b=b, T=DSEQ, slot=4 + b))
    if stage < 3:
        seqs = []
    only = cfg.get('only', ['meta', 'prompt', 'sample'])
    seqs = [q for q in seqs if q['kind'] in only]

    def rmsnorm_to(dst_bf, src_f32, ncol_name, l, N):
        mk = S.mark()
        sqb = S.sb([128, 8, N], BF16, "sqb")
        rstd = S.sb([128, N], F32, "rstd")
        for c in range(8):
            act(sqb[:, c, :], src_f32[:, c, 0:N], AF.Square, [src_f32.c(c)], [sqb.c(c)])
        bank = PS[4]
        for c in range(8):
            MM(bank[:, 0:N], ones_b[:, :], sqb[:, c, :], c == 0, c == 7, [ones_b, sqb.c(c)], [bank.r(0, 4 * N)])
        act(rstd[:, :], bank[:, 0:N], AF.Sqrt, [bank.r(0, 4 * N), epsT], [rstd], bias=epsT[:, 0:1], scale=1.0 / D)
        V(lambda e, o_=rstd[:, :]: e.reciprocal(out=o_, in_=o_), [rstd], [rstd])
        for c in range(8):
            stt(dst_bf[:, c, 0:N], src_f32[:, c, 0:N], pc(l, ncol_name, c), rstd[:, :], ALU.mult, ALU.mult,
                [src_f32.c(c), P[l]["_t"], rstd], [dst_bf.c(c)])
        S.reset(mk)

    def proj_fm(l, wsrc, kc, c0, ncols, rhs_fn, rhs_reads, N, evac, prt=128, bankset=(0, 1)):
        mi = 0
        for g0 in range(0, ncols, 512):
            gw = min(512, ncols - g0)
            wt, view = load_w(wsrc[:, c0 + g0:c0 + g0 + gw], kc, gw, prt)
            for m0 in range(0, gw, 128):
                mw = min(128, gw - m0)
                bank = PS[bankset[mi % len(bankset)]]
                for k in range(kc):
                    MM(bank[0:mw, 0:N], view[:, k, m0:m0 + mw], rhs_fn(k), k == 0, k == kc - 1,
                       [wt] + rhs_reads(k), [bank.r(0, 4 * N)])
                evac(mi, bank, mw)
                mi += 1

    for sq in seqs:
        kind, b, slot = sq["kind"], sq["b"], sq["slot"]
        tiles = [(t0, min(NT, sq["T"] - t0)) for t0 in range(0, sq["T"], NT)]
        for l in range(2):
            if kind == "prompt" and upto < 3:
                continue
            if kind == "meta":
                for t in (ST_[l], CH_[l], XR_[l], XI_[l], XS_[l]):
                    V(lambda e, a_=t.h[:]: e.memset(a_, 0.0), [], [t])
            elif kind == "prompt":
                for dst, src in ((ST_[l], STm[l]), (CH_[l], CHm[l]), (XR_[l], XRm[l]), (XI_[l], XIm[l])):
                    cp(dst.h[:], src.h[:], [src], [dst])
                st_to_xs(l)
            else:
                S.dma(OQ, CH_[l][:, :, :].rearrange("p c k -> p (c k)"), I["state_conv"][l, b], [], [CH_[l]], "stin")
                for g in range(2):
                    S.dma(OQ, ST_[l][64:128, g, :], I["state_ssd"][l, b, g], [], [ST_[l].c(g)], "stin")
                S.dma(OQ, XR_[l][:, :], I["state_s5_re"][l, b], [], [XR_[l]], "stin")
                S.dma(OQ, XI_[l][:, :], I["state_s5_im"][l, b], [], [XI_[l]], "stin")
                st_to_xs(l)
                S.reset(ARENA)
                S.dma("pool", vhs[slot, l, 0:PAST, :], I["cache_v"][l, b], [], [("vh%d_%d" % (slot, l), 0, PAST)], "vpre")
                CK = [S.sb([128, 512], F32, "CK") for _ in range(2)]
                KTP = [S.sb([128, 4, 128], BF16, "KTP") for _ in range(2)]
                for tb in range(PAST // 128):
                    ck, kt = CK[tb % 2], KTP[tb % 2]
                    S.dma(OQ, ck[:, :], I["cache_k"][l, b, 128 * tb:128 * tb + 128, :], [], [ck], "ckl%d" % (tb % 2))
                    bank = PS[7]
                    for m in range(4):
                        TR(bank[:, 128 * m:128 * m + 128], ck[:, 128 * m:128 * m + 128], ident_f[:, :], [ck, ident_f],
                           [bank.r(512 * m, 512 * m + 512)])
                    act(kt[:, :, :], bank[:, :].rearrange("p (m t) -> p m t", t=128), AF.Copy, [bank], [kt])
                    S.dma(OQ, kTs[slot, l].rearrange("(m p) t -> p m t", p=128)[:, :, 128 * tb:128 * tb + 128], kt[:, :, :],
                          [kt], [("kT%d_%d" % (slot, l), 128 * tb, 128 * tb + 128)], "kpre%d" % (tb % 2))
        for (t0, N) in tiles:
            S.reset(ARENA)
            pos0 = {"meta": 0, "prompt": NMETA + t0, "sample": PAST}[kind]
            nbk = (N + 127) // 128
            blk = [(tb, min(128, N - 128 * tb)) for tb in range(nbk)]
            if kind == "meta":
                xsrc = I["meta_tokens"]
            elif kind == "prompt":
                xsrc = I["x_prompt"][b, t0:t0 + N, :]
            else:
                xsrc = I["x_sample"][b, t0:t0 + N, :]
            mk0 = S.mark()
            x_tm = S.sb([128, nbk, D], F32, "x_tm")
            for tb, nt in blk:
                S.dma(OQ, x_tm[0:nt, tb, :], xsrc[128 * tb:128 * tb + nt, :], [], [x_tm.c(tb)], "xin")
            for c in range(8):
                bank = PS[c % 4]
                for tb, nt in blk:
                    TR(bank[:, 128 * tb:128 * tb + nt], x_tm[0:nt, tb, 128 * c:128 * c + 128], ident_f[0:nt, 0:nt],
                       [x_tm.c(tb), ident_f], [bank.r(512 * tb, 512 * tb + 4 * nt)])
                act(xT[:, c, 0:N], bank[:, 0:N], AF.Copy, [bank.r(0, 4 * N)], [xT.c(c)])
            S.reset(mk0)
            for l in range(2):
                S.reset(ARENA)
                rmsnorm_to(hT, xT, "nmix", l, N)
                hrd = lambda k: [hT.c(k)]
                hrhs = lambda k, N=N: hT[:, k, 0:N]
                if kind == "meta":
                    kdst = [O["k_prompt"][l, bb, 0:NMETA, :] for bb in range(n_pseq)]
                    vdst = [O["v_prompt"][l, bb, 0:NMETA, :] for bb in range(n_pseq)]
                    slots = list(range(n_pseq))
                elif kind == "prompt":
                    kdst = [O["k_prompt"][l, b, NMETA + t0:NMETA + t0 + N, :]]
                    vdst = [O["v_prompt"][l, b, NMETA + t0:NMETA + t0 + N, :]]
                    slots = [slot]
                else:
                    kdst = [O["k_sample"][l, b, t0:t0 + N, :]]
                    vdst = [O["v_sample"][l, b, t0:t0 + N, :]]
                    slots = [slot]
                mkA = S.mark()
                knT = S.sb([128, 4, N], F32, "knT")
                knb = S.sb([128, 4, N], BF16, "knb")
                qnb = S.sb([128, 4, N], BF16, "qnb")
                ksq = S.sb([128, N], BF16, "ksq")
                rk = S.sb([128, N], F32, "rk")

                def qk_evac(dst32, dstbf, gname):
                    def ev(m, bank, mw):
                        kd = cfg.get("kd", 99)
                        if kd < 1:
                            return
                        act(ksq[:, :], bank[:, 0:N], AF.Square, [bank.r(0, 4 * N)], [ksq])
                        if kd < 2:
                            return
                        b2 = PS[2 + m % 2]
                        MM(b2[:, 0:N], bd64_b[:, :], ksq[:, :], True, True, [bd64_b, ksq], [b2.r(0, 4 * N)])
                        if kd < 3:
                            return
                        act(rk[:, :], b2[:, 0:N], AF.Sqrt, [b2.r(0, 4 * N), epsT], [rk], bias=epsT[:, 0:1], scale=1.0 / 64)
                        if kd < 4:
                            return
                        V(lambda e, o_=rk[:, :]: e.reciprocal(out=o_, in_=o_), [rk], [rk])
                        if kd < 5:
                            return
                        if dst32 is not None:
                            stt(dst32[:, m, :], bank[:, 0:N], pc(l, gname), rk[:, :], ALU.mult, ALU.mult,
                                [bank.r(0, 4 * N), P[l]["_t"], rk], [dst32.c(m)])
                            act(dstbf[:, m, :], dst32[:, m, :], AF.Copy, [dst32.c(m)], [dstbf.c(m)])
                        else:
                            stt(dstbf[:, m, :], bank[:, 0:N], pc(l, gname), rk[:, :], ALU.mult, ALU.mult,
                                [bank.r(0, 4 * N), P[l]["_t"], rk], [dstbf.c(m)])
                    return ev
                proj_fm(l, I["w_in"][l], 8, OFF_K, 512, hrhs, hrd, N, qk_evac(knT, knb, "kn"))
                proj_fm(l, I["w_in"][l], 8, OFF_Q, 512, hrhs, hrd, N, qk_evac(None, qnb, "qn"))
                if cfg.get("kd", 99) < 7:
                    continue
                for sl in slots:
                    S.dma(OQ, kTs[sl, l].rearrange("(m p) t -> p m t", p=128)[:, :, pos0:pos0 + N], knb[:, :, :],
                          [knb], [("kT%d_%d" % (sl, l), pos0, pos0 + N)], "ktw")
                if cfg.get("kd", 99) < 8:
                    continue
                k_tm = S.sb([128, nbk, 512], F32, "k_tm")
                for tb, nt in blk:
                    bank = PS[5]
                    for m in range(4):
                        TR(bank[0:nt, 128 * m:128 * m + 128], knT[:, m, 128 * tb:128 * tb + nt], ident_f[:, :],
                           [knT.c(m), ident_f], [bank.r(512 * m, 512 * m + 512)])
                    act(k_tm[0:nt, tb, :], bank[0:nt, :], AF.Copy, [bank], [k_tm.c(tb)])
                    for dst in kdst:
                        S.dma(OQ, dst[128 * tb:128 * tb + nt, :], k_tm[0:nt, tb, :], [k_tm.c(tb)], [], "kout")
                if cfg.get("kd", 99) < 9:
                    continue
                wt, wv = load_w(I["w_in"][l][:, OFF_V:OFF_V + 512], 8, 512)
                v_tm = S.sb([128, nbk, 512], F32, "v_tm")
                v_bf = S.sb([128, nbk, 512], BF16, "v_bf")
                for tb, nt in blk:
                    bank = PS[6]
                    for kc in range(8):
                        MM(bank[0:nt, :], hT[:, kc, 128 * tb:128 * tb + nt], wv[:, kc, :], kc == 0, kc == 7,
                           [wt, hT.c(kc)], [bank])
                    act(v_tm[0:nt, tb, :], bank[0:nt, :], AF.Copy, [bank], [v_tm.c(tb)])
                    for dst in vdst:
                        S.dma(OQ, dst[128 * tb:128 * tb + nt, :], v_tm[0:nt, tb, :], [v_tm.c(tb)], [], "vout")
                    if cfg.get("kd", 99) < 10:
                        continue
                    act(v_bf[0:nt, tb, :], v_tm[0:nt, tb, :], AF.Copy, [v_tm.c(tb)], [v_bf.c(tb)])
                    if cfg.get("vh", 2) < 2:
                        continue
                    for sl in slots:
                        S.dma(OQ, vhs[sl, l, pos0 + 128 * tb:pos0 + 128 * tb + nt, :], v_bf[0:nt, tb, :],
                              [v_bf.c(tb)], [("vh%d_%d" % (sl, l), pos0 + 128 * tb, pos0 + 128 * tb + nt)], "vhw")
                if upto < 2:
                    continue
                def gates(i):
                    proj_fm(l, I["w_in"][l], 8, OFF_G + 1024 * i, 1024, hrhs, hrd, N,
                            lambda m, bank, mw: act(GT[:, m, :], bank[:, 0:N], AF.Sigmoid, [bank.r(0, 4 * N)], [GT.c(m)]))

                def mix_evac(first):
                    def ev(m, bank, mw):
                        if first:
                            tt(mixT[:, m, 0:N], GT[:, m, :], bank[:, 0:N], ALU.mult, [GT.c(m), bank.r(0, 4 * N)], [mixT.c(m)])
                        else:
                            tt(tmpA[:, :], GT[:, m, :], bank[:, 0:N], ALU.mult, [GT.c(m), bank.r(0, 4 * N)], [tmpA])
                            tt(mixT[:, m, 0:N], mixT[:, m, 0:N], tmpA[:, :], ALU.add, [mixT.c(m), tmpA], [mixT.c(m)])
                    return ev
                nk = pos0 + N
                nb = (nk + 127) // 128
                slot_r = slots[0]
                OC = S.sb([64, 8, N], BF16, "OC")
                mkC = S.mark()
                RS = S.sb([128, N], F32, "RS")
                EXs = [S.sb([128, N], F32, "EX") for _ in range(2)]
                ARGs = [S.sb([128, N], F32, "ARG") for _ in range(2)]
                LPs = [S.sb([128, N], F32, "LP") for _ in range(2)]
                WTs = [S.sb([128, N], BF16, "WT") for _ in range(2)]
                KTb = [S.sb([128, NKMAX], BF16, "KT") for _ in range(2)]
                VBb = [S.sb([128, 17, 128], BF16, "VB") for _ in range(2)]
                nfull, rem = nk // 128, nk % 128
                bctr = 0
                for pr in range(4):
                    KT, VB = KTb[pr % 2], VBb[pr % 2]
                    kname, vname = "kT%d_%d" % (slot_r, l), "vh%d_%d" % (slot_r, l)
                    S.dma(OQ, KT[:, 0:nk], kTs[slot_r, l, 128 * pr:128 * pr + 128, 0:nk], [(kname, 0, nk)], [KT], "ktl%d" % (pr % 2))
                    if nfull:
                        S.dma(OQ, VB[:, 0:nfull, :],
                              vhs[slot_r, l, 0:128 * nfull, 128 * pr:128 * pr + 128].rearrange("(b p) c -> p b c", p=128),
                              [(vname, 0, 128 * nfull)], [VB.r(0, nfull * 256)], "vbl%d" % (pr % 2))
                    if rem:
                        S.dma(OQ, VB[0:rem, nfull, :], vhs[slot_r, l, 128 * nfull:nk, 128 * pr:128 * pr + 128],
                              [(vname, 128 * nfull, nk)], [VB.r(nfull * 256, nfull * 256 + 256)], "vbl%d" % (pr % 2))
                    for hh in range(2):
                        h = 2 * pr + hh
                        R = slice(64 * hh, 64 * hh + 64)
                        V(lambda e, a_=RS[:, :]: e.memset(a_, 0.0), [], [RS])
                        PO = PS[4]
                        blocks = list(reversed(range(nb)))

                        def stage1(bI, i):
                            kb = min(128, nk - 128 * bI)
                            PZ, P2 = PS[i % 2], PS[2 + i % 2]
                            LP, EXb = LPs[i % 2], EXs[i % 2]
                            MM(PZ[0:kb, 0:N], KT[R, 128 * bI:128 * bI + kb], qnb[R, pr, 0:N], True, True,
                               [KT, qnb.c(pr)], [PZ.r(0, 4 * N)])
                            act(EXb[0:kb, :], PZ[0:kb, 0:N], AF.Exp, [PZ.r(0, 4 * N)], [EXb], scale=0.125)
                            act(LP[0:kb, :].bitcast(F32R), EXb[0:kb, :], AF.Ln, [EXb, oneT], [LP], bias=oneT[0:kb, 0:1])
                            if 128 * bI + kb - 1 >= pos0:
                                mt = masks[pos0 - 128 * bI]
                                tt(LP[0:kb, :].bitcast(F32R), LP[0:kb, :], mt[0:kb, 0:N], ALU.mult, [LP, mt], [LP])

                        def stage1b(bI, i):
                            kb = min(128, nk - 128 * bI)
                            PZ, P2 = PS[i % 2], PS[2 + i % 2]
                            LP = LPs[i % 2]
                            MM(PZ[0:kb, 0:N], tri_r[0:kb, 0:kb], LP[0:kb, :].bitcast(F32R), False, True,
                               [tri_r, LP], [PZ.r(0, 4 * N)], skip_group_check=True)
                            if bI > 0:
                                MM(P2[:, 0:N], ones_r[0:kb, :], LP[0:kb, :].bitcast(F32R), True, True, [ones_r, LP], [P2.r(0, 4 * N)])

                        def stage2(bI, i):
                            kb = min(128, nk - 128 * bI)
                            PZ, P2 = PS[i % 2], PS[2 + i % 2]
                            WT, AG = WTs[i % 2], ARGs[i % 2]
                            stt(AG[0:kb, :], PZ[0:kb, 0:N], 0.125, RS[0:kb, :], ALU.mult, ALU.subtract,
                                [PZ.r(0, 4 * N), RS], [AG])
                            if bI > 0:
                                tt(RS[:, :], RS[:, :], P2[:, 0:N], ALU.add, [RS, P2.r(0, 4 * N)], [RS])
                            act(WT[0:kb, :], AG[0:kb, :], AF.Exp, [AG], [WT])
                            if 128 * bI + kb - 1 >= pos0:
                                mt = masks[pos0 - 128 * bI]
                                tt(WT[0:kb, :], WT[0:kb, :], mt[0:kb, 0:N], ALU.mult, [WT, mt], [WT])
                            MM(PO[0:64, 0:N], VB[0:kb, bI, 64 * hh:64 * hh + 64], WT[0:kb, :], i == 0, bI == 0,
                               [VB.r(bI * 256, bI * 256 + 256), WT], [PO.r(0, 4 * N)])
                        for i, bI in enumerate(blocks):
                            stage1(bI, i)
                            stage1b(bI, i)
                            if i >= 1:
                                stage2(blocks[i - 1], i - 1)
                        stage2(blocks[-1], len(blocks) - 1)
                        act(OC[:, h, :], PO[0:64, 0:N], AF.Copy, [PO.r(0, 4 * N)], [OC.c(h)])
                S.reset(mkC)
                GT = S.sb([128, 8, N], F32, "GT")
                tmpA = S.sb([128, N], F32, "tmpA")
                gates(2)
                proj_fm(l, I["w_lift_c"][l], 8, 0, 1024, lambda h_: OC[:, h_, :], lambda h_: [OC.c(h_)], N, mix_evac(True), prt=64)
                if upto < 3:
                    continue
                S.reset(mkA)
                GT = S.sb([128, 8, N], F32, "GT")
                tmpA = S.sb([128, N], F32, "tmpA")
                Q = min(64, N)
                nch = N // Q
                xp = S.sb([128, 10, N + 3], F32, "xp")
                XBC = S.sb([128, 10, N], BF16, "XBC")
                yT = S.sb([128, 8, N], F32, "yT")
                cp(xp[:, :, 0:3], CH_[l][:, :, :], [CH_[l]], [xp])
                proj_fm(l, I["w_in"][l], 8, OFF_XBC, 1280, hrhs, hrd, N,
                        lambda m, bank, mw: act(xp[:, m, 3:3 + N], bank[:, 0:N], AF.Copy, [bank.r(0, 4 * N)], [xp.c(m)]))
                cp(CH_[l][:, :, :], xp[:, :, N:N + 3], [xp], [CH_[l]])
                for c in range(10):
                    ts(tmpA[:, :], xp[:, c, 0:N], pc(l, "cw", c), pc(l, "cb", c), ALU.mult, ALU.add, [xp.c(c), P[l]["_t"]], [tmpA])
                    for k in range(1, 4):
                        stt(tmpA[:, :], xp[:, c, k:k + N], pc(l, "cw", 10 * k + c), tmpA[:, :], ALU.mult, ALU.add,
                            [xp.c(c), P[l]["_t"], tmpA], [tmpA])
                    act(XBC[:, c, :], tmpA[:, :], AF.Silu, [tmpA], [XBC.c(c)])
                if cfg.get("sd", 99) < 2:
                    continue
                PD = PS[2]
                for k in range(8):
                    MM(PD[0:48, 0:N], WDT[l][:, k, :], hT[:, k, 0:N], k == 0, k == 7, [WDT[l], hT.c(k)], [PD.r(0, 4 * N)])
                dtT = S.sb([48, N], F32, "dtT")
                dA = S.sb([48, N], F32, "dA")
                AC = S.sb([16, N], F32, "AC")
                act(dtT[:, :], PD[0:48, 0:N], AF.Exp, [PD.r(0, 4 * N), P[l]["_t"]], [dtT], bias=pc(l, "dtb", 0, slice(0, 48)))
                act(dtT[:, :], dtT[:, :], AF.Ln, [dtT, oneT], [dtT], bias=oneT[0:48, 0:1])
                ts(dA[:, :], dtT[:, :], acol[l][:, 0:1], None, ALU.mult, None, [dtT, acol[l]], [dA])
                if cfg.get("sd", 99) < 3:
                    continue
                Ets = [S.sb([128, 8, Q], F32, "Et") for _ in range(2)]
                Wts = [S.sb([128, 8, Q], BF16, "Wt") for _ in range(2)]
                if Q < 64:
                    for w__ in Wts:
                        V(lambda e, a_=w__[:, :, :]: e.memset(a_, 0.0), [], [w__])
                dt_tm = S.sb([64, 16], F32, "dt_tm")
                coef = S.sb([64, 8], F32, "coef")
                XW = S.sb([64, 8, 64], BF16, "XW")
                ectr = 0
                for ci in range(nch):
                    cs = slice(ci * Q, ci * Q + Q)
                    LTt = LT[ci % 2]
                    scan(AC[0:16, cs], ones48[0:16, 0:Q], dA[0:16, cs], 0.0, [ones48, dA], [AC])
                    scan(LTt[32:48, 0:Q], ones48[32:48, 0:Q], dA[32:48, cs], 0.0, [ones48, dA], [LTt])
                    PT = PS[7]
                    MM(PT[0:Q, 0:16], dtT[0:16, cs], ident_f[0:16, 0:16], True, True, [dtT, ident_f], [PT.r(0, 64)])
                    act(dt_tm[0:Q, :], PT[0:Q, 0:16], AF.Copy, [PT.r(0, 64)], [dt_tm])
                    if cfg.get("sd", 99) < 4:
                        continue
                    PY = PS[3]
                    for g in range(2):
                        GR = slice(64 * g, 64 * g + 64)
                        Et, Wt = Ets[ectr % 2], Wts[ectr % 2]
                        ectr += 1
                        tt(RF[g][0:16, :, 0:Q], AC[0:16, cs].unsqueeze(1).broadcast_to([16, 8, Q]),
                           DEL[g][0:16, :].unsqueeze(2).broadcast_to([16, 8, Q]), ALU.mult, [AC, DEL[g]], [RF[g]])
                        PE_ = PS[4]
                        pev = PE_[:, 0:8 * Q].rearrange("p (h i) -> p h i", i=Q)
                        MM(pev, LTt[:, :], RF[g][:, :, 0:Q], True, True, [LTt, RF[g]], [PE_])
                        act(Et[:, :, :], pev, AF.Exp, [PE_], [Et])
                        if cfg.get("sd", 99) < 5:
                            continue
                        PG = PS[5]
                        MM(PG[0:Q, 0:Q], XBC[GR, 8, cs], XBC[GR, 9, cs], True, True, [XBC.c(8), XBC.c(9)], [PG.r(0, 256)])
                        tt(Wt[0:Q, :, :], Et[0:Q, :, :], PG[0:Q, 0:Q].unsqueeze(1).broadcast_to([Q, 8, Q]), ALU.mult,
                           [Et, PG.r(0, 256)], [Wt])
                        MM(PG[:, 128:128 + Q], SHIFT[GR, :], XBC[GR, 9, cs], True, True, [SHIFT, XBC.c(9)], [PG.r(512, 768)])
                        tt(Wt[64:128, :, :], Et[64:128, :, :], PG[64:128, 128:128 + Q].unsqueeze(1).broadcast_to([64, 8, Q]), ALU.mult,
                           [Et, PG.r(512, 768)], [Wt])
                        if cfg.get("sd", 99) < 6:
                            continue
                        PX = PS[6]
                        for pr in range(4):
                            MM(PX[0:Q, 128 * pr:128 * pr + 128], XBC[:, 4 * g + pr, cs], ident_b[:, :], True, True,
                               [XBC.c(4 * g + pr), ident_b], [PX.r(512 * pr, 512 * pr + 512)])
                        dstX = XS_[l][0:Q, g, :, :].rearrange("q pr (blk d) -> q pr blk d", d=64)[:, :, 1::2, :]
                        tt(dstX, PX[0:Q, :].rearrange("q (pr eo d) -> q pr eo d", pr=4, eo=2),
                           dt_tm[0:Q, 8 * g:8 * g + 8].rearrange("q (pr eo) -> q pr eo", eo=2).unsqueeze(3).broadcast_to([Q, 4, 2, 64]),
                           ALU.mult, [PX, dt_tm], [XS_[l].c(g)])
                        if cfg.get("sd", 99) < 7:
                            continue
                        for pr in range(4):
                            k = 4 * g + pr
                            for eo in range(2):
                                win = slice(64 + 64 * eo, 192 + 64 * eo)
                                MM(PY[:, k * Q:k * Q + Q], XS_[l][:, g, pr, win], Wt[:, 2 * pr + eo, :], eo == 0, eo == 1,
                                   [XS_[l].c(g), Wt], [PY.r(4 * k * Q, 4 * k * Q + 4 * Q)])
                        if cfg.get("sd", 99) < 8:
                            continue
                        tt(coef[0:Q, :], dt_tm[0:Q, 8 * g:8 * g + 8], Et[0:Q, :, Q - 1], ALU.mult, [dt_tm, Et], [coef])
                        tt(XW[0:Q, :, :], PX[0:Q, :].rearrange("q (h d) -> q h d", d=64),
                           coef[0:Q, :].unsqueeze(2).broadcast_to([Q, 8, 64]), ALU.mult, [PX, coef], [XW])
                        MM(PG[0:Q, 64:128], XBC[GR, 8, cs], ident_b[GR, 64 * g:64 * g + 64], True, True,
                           [XBC.c(8), ident_b], [PG.r(256, 512)])
                        cp(BTOK[0:Q, 64:128], PG[0:Q, 64:128], [PG.r(256, 512)], [BTOK])
                        PSt = PS[2]
                        MM(PSt[:, :], BTOK[0:Q, :], XW[0:Q, :, :], True, True, [BTOK, XW], [PSt])
                        stv = ST_[l][64:128, g, :].rearrange("p (h d) -> p h d", d=64)
                        tt(stv, stv, Et[64:128, :, Q - 1].unsqueeze(2).broadcast_to([64, 8, 64]), ALU.mult, [ST_[l].c(g), Et], [ST_[l].c(g)])
                        tt(ST_[l][64:128, g, :], ST_[l][64:128, g, :], PSt[64:128, :], ALU.add, [ST_[l].c(g), PSt], [ST_[l].c(g)])
                        src = ST_[l][64:128, g, :].rearrange("p (pr eo d) -> p pr eo d", pr=4, eo=2)
                        dst = XS_[l][64:128, g, :, :].rearrange("p pr (blk d) -> p pr blk d", d=64)[:, :, 1::2, :]
                        cp(dst, src, [ST_[l].c(g)], [XS_[l].c(g)])
                    if cfg.get("sd", 99) < 9:
                        continue
                    act(yT[:, :, cs], PY[:, 0:8 * Q].rearrange("p (k i) -> p k i", i=Q), AF.Copy, [PY], [yT])
                if cfg.get("sd", 99) < 10:
                    continue
                for k in range(8):
                    stt(yT[:, k, :], XBC[:, k, :], pc(l, "dsk", k), yT[:, k, :], ALU.mult, ALU.add, [XBC.c(k), P[l]["_t"], yT.c(k)], [yT.c(k)])

                def z_evac(m, bank, mw):
                    act(tmpA[:, :], bank[:, 0:N], AF.Silu, [bank.r(0, 4 * N)], [tmpA])
                    tt(yT[:, m, :], yT[:, m, :], tmpA[:, :], ALU.mult, [yT.c(m), tmpA], [yT.c(m)])
                proj_fm(l, I["w_in"][l], 8, OFF_Z, 1024, hrhs, hrd, N, z_evac)
                ysq = S.sb([128, 8, N], BF16, "ysq")
                YN = S.sb([128, 8, N], BF16, "YN")
                rg = S.sb([128, N], F32, "rg")
                for k in range(8):
                    act(ysq[:, k, :], yT[:, k, :], AF.Square, [yT.c(k)], [ysq.c(k)])
                for g in range(2):
                    bank = PS[4]
                    for k in range(4 * g, 4 * g + 4):
                        MM(bank[:, 0:N], ones_b[:, :], ysq[:, k, :], k == 4 * g, k == 4 * g + 3, [ones_b, ysq.c(k)], [bank.r(0, 4 * N)])
                    act(rg[:, :], bank[:, 0:N], AF.Sqrt, [bank.r(0, 4 * N), epsT], [rg], bias=epsT[:, 0:1], scale=1.0 / 512)
                    V(lambda e, o_=rg[:, :]: e.reciprocal(out=o_, in_=o_), [rg], [rg])
                    for k in range(4 * g, 4 * g + 4):
                        stt(YN[:, k, :], yT[:, k, :], pc(l, "nssd", k), rg[:, :], ALU.mult, ALU.mult, [yT.c(k), P[l]["_t"], rg], [YN.c(k)])
                gates(0)
                proj_fm(l, I["w_lift_a"][l], 8, 0, 1024, lambda k_: YN[:, k_, :], lambda k_: [YN.c(k_)], N, mix_evac(False))
                if upto < 4:
                    continue
                S.reset(mkA)
                GT = S.sb([128, 8, N], F32, "GT")
                tmpA = S.sb([128, N], F32, "tmpA")
                BC = S.sb([128, 4, 16, 128], BF16, "BC")
                if ("bc", l) not in WG:
                    scr_ = nc.dram_tensor("bcs%d" % l, [128, 4 * 2048], BF16, kind="Internal").ap()
                    S.dma("pool", scr_.rearrange("p (i n) -> p i n", i=4), I["s5bc"][l].rearrange("i p n -> p i n"),
                          [], [("bcs", l, l + 1)], "wcast")
                    WG[("bc", l)] = scr_
                S.dma(OQ, BC[:, :, :, :].rearrange("p i c n -> p (i c n)"), WG[("bc", l)][:, :], [("bcs", l, l + 1)], [BC], "bcl")
                uT = S.sb([128, 4, N], BF16, "uT")
                u32 = S.sb([128, 4, N], F32, "u32")

                def u_evac(m, bank, mw):
                    act(u32[:, m, :], bank[:, 0:N], AF.Copy, [bank.r(0, 4 * N)], [u32.c(m)])
                    act(uT[:, m, :], u32[:, m, :], AF.Copy, [u32.c(m)], [uT.c(m)])
                proj_fm(l, I["w_in"][l], 8, OFF_U, 512, hrhs, hrd, N, u_evac)
                A_ = S.sb([128, 8, QS], F32, "A_")
                B_ = S.sb([128, 8, QS], F32, "B_")
                C_ = S.sb([128, 8, QS], F32, "C_")
                D_ = S.sb([128, 8, QS], F32, "D_")
                CDs = [(C_, D_), (S.sb([128, 8, QS], F32, "C2"), S.sb([128, 8, QS], F32, "D2"))]
                E_ = S.sb([128, 8, QS], F32, "E_")
                F_ = S.sb([128, 8, QS], F32, "F_")
                xrb = S.sb([128, 8, QS], BF16, "xrb")
                xib = S.sb([128, 8, QS], BF16, "xib")
                WIr = S.sb([128, 8], F32, "WIr")
                WIi = S.sb([128, 8], F32, "WIi")
                t8 = S.sb([128, 8], F32, "t8")
                cosT, sinT, tbr, tbi, magT = S5TAB[l]
                XR, XI = XR_[l], XI_[l]
                nsub = (N + QS - 1) // QS
                for s_ in range(nsub):
                    ncol = min(QS, N - s_ * QS)
                    cs = slice(s_ * QS, s_ * QS + ncol)
                    for hf in range(2):
                        ch = slice(8 * hf, 8 * hf + 8)
                        Pre, Pim = PS[0], PS[1]
                        for cc in range(8):
                            c = 8 * hf + cc
                            MM(Pre[:, cc * QS:cc * QS + ncol], BC[:, 0, c, :], uT[:, c // 4, cs], True, True,
                               [BC.c(0), uT.c(c // 4)], [Pre.r(4 * cc * QS, 4 * cc * QS + 4 * ncol)])
                            MM(Pim[:, cc * QS:cc * QS + ncol], BC[:, 1, c, :], uT[:, c // 4, cs], True, True,
                               [BC.c(1), uT.c(c // 4)], [Pim.r(4 * cc * QS, 4 * cc * QS + 4 * ncol)])
                        PreV = Pre[:, :].rearrange("p (c t) -> p c t", t=QS)[:, :, 0:ncol]
                        PimV = Pim[:, :].rearrange("p (c t) -> p c t", t=QS)[:, :, 0:ncol]
                        C_, D_ = CDs[(2 * s_ + hf) % 2]
                        a_, b_, c_, d_ = A_[:, :, 0:ncol], B_[:, :, 0:ncol], C_[:, :, 0:ncol], D_[:, :, 0:ncol]
                        tr_, ti_ = tbr[:, ch, 0:ncol], tbi[:, ch, 0:ncol]
                        tt(a_, PreV, tr_, ALU.mult, [Pre, tbr], [A_])
                        tt(b_, PimV, ti_, ALU.mult, [Pim, tbi], [B_])
                        tt(a_, a_, b_, ALU.subtract, [A_, B_], [A_])
                        tt(b_, PreV, ti_, ALU.mult, [Pre, tbi], [B_])
                        tt(c_, PimV, tr_, ALU.mult, [Pim, tbr], [C_])
                        tt(b_, b_, c_, ALU.add, [B_, C_], [B_])
                        cos1, sin1 = cosT[:, ch, 1], sinT[:, ch, 1]
                        tt(WIr[:, :], cos1, XR[:, ch], ALU.mult, [cosT, XR], [WIr])
                        tt(t8[:, :], sin1, XI[:, ch], ALU.mult, [sinT, XI], [t8])
                        tt(WIr[:, :], WIr[:, :], t8[:, :], ALU.subtract, [WIr, t8], [WIr])
                        tt(WIi[:, :], sin1, XR[:, ch], ALU.mult, [sinT, XR], [WIi])
                        tt(t8[:, :], cos1, XI[:, ch], ALU.mult, [cosT, XI], [t8])
                        tt(WIi[:, :], WIi[:, :], t8[:, :], ALU.add, [WIi, t8], [WIi])
                        for cc in range(8):
                            c = 8 * hf + cc
                            scan(C_[:, cc, 0:ncol], magT[:, c, 0:ncol], A_[:, cc, 0:ncol], WIr[:, cc:cc + 1], [magT, A_, WIr], [C_.c(cc)])
                            scan(D_[:, cc, 0:ncol], magT[:, c, 0:ncol], B_[:, cc, 0:ncol], WIi[:, cc:cc + 1], [magT, B_, WIi], [D_.c(cc)])
                        cosL, sinL = cosT[:, ch, ncol - 1], sinT[:, ch, ncol - 1]
                        wrl, wil = C_[:, :, ncol - 1], D_[:, :, ncol - 1]
                        tt(XR[:, ch], cosL, wrl, ALU.mult, [cosT, C_], [XR])
                        tt(t8[:, :], sinL, wil, ALU.mult, [sinT, D_], [t8])
                        tt(XR[:, ch], XR[:, ch], t8[:, :], ALU.subtract, [XR, t8], [XR])
                        tt(XI[:, ch], sinL, wrl, ALU.mult, [sinT, C_], [XI])
                        tt(t8[:, :], cosL, wil, ALU.mult, [cosT, D_], [t8])
                        tt(XI[:, ch], XI[:, ch], t8[:, :], ALU.add, [XI, t8], [XI])
                        cos_, sin_ = cosT[:, ch, 0:ncol], sinT[:, ch, 0:ncol]
                        e_, f_ = E_[:, :, 0:ncol], F_[:, :, 0:ncol]
                        tt(e_, c_, cos_, ALU.mult, [C_, cosT], [E_])
                        tt(f_, d_, sin_, ALU.mult, [D_, sinT], [F_])
                        tt(xrb[:, :, 0:ncol], e_, f_, ALU.subtract, [E_, F_], [xrb])
                        tt(e_, c_, sin_, ALU.mult, [C_, sinT], [E_])
                        tt(f_, d_, cos_, ALU.mult, [D_, cosT], [F_])
                        stt(xib[:, :, 0:ncol], e_, -1.0, f_, ALU.mult, ALU.subtract, [E_, F_], [xib])
                        for cc in range(8):
                            c = 8 * hf + cc
                            j, m_ = c // 4, c % 4
                            bank = PS[6 + j // 2]
                            col0 = (j % 2) * N + s_ * QS
                            MM(bank[:, col0:col0 + ncol], BC[:, 2, c, :], xrb[:, cc, 0:ncol], m_ == 0, False,
                               [BC.c(2), xrb], [bank.r(4 * col0, 4 * col0 + 4 * ncol)])
                            MM(bank[:, col0:col0 + ncol], BC[:, 3, c, :], xib[:, cc, 0:ncol], False, m_ == 3,
                               [BC.c(3), xib], [bank.r(4 * col0, 4 * col0 + 4 * ncol)])
                GB = S.sb([128, 4, N], BF16, "GB")
                t5 = S.sb([128, N], F32, "t5")
                t6 = S.sb([128, N], F32, "t6")
                for j in range(4):
                    bank = PS[6 + j // 2]
                    col0 = (j % 2) * N
                    stt(t5[:, :], u32[:, j, :], pc(l, "d5", j), bank[:, col0:col0 + N], ALU.mult, ALU.add,
                        [u32.c(j), P[l]["_t"], bank.r(4 * col0, 4 * col0 + 4 * N)], [t5])
                    tt(t6[:, :], t5[:, :], t5[:, :], ALU.mult, [t5], [t6])
                    ts(t6[:, :], t6[:, :], 0.044715, 1.0, ALU.mult, ALU.add, [t6], [t6])
                    tt(t6[:, :], t6[:, :], t5[:, :], ALU.mult, [t6, t5], [t6])
                    act(t6[:, :], t6[:, :], AF.Sigmoid, [t6], [t6], scale=1.5957691216057308)
                    tt(GB[:, j, :], t5[:, :], t6[:, :], ALU.mult, [t5, t6], [GB.c(j)])
                gates(1)
                for half in range(2):
                    wtA, vA = load_w(I["w_glu"][l][:, 512 * half:512 * half + 512], 4, 512)
                    wtB, vB = load_w(I["w_glu"][l][:, 1024 + 512 * half:1024 + 512 * half + 512], 4, 512)
                    for mm_ in range(4):
                        m = 4 * half + mm_
                        P1, P2 = PS[0], PS[1]
                        for j in range(4):
                            MM(P1[:, 0:N], vA[:, j, 128 * mm_:128 * mm_ + 128], GB[:, j, :], j == 0, j == 3, [wtA, GB.c(j)], [P1.r(0, 4 * N)])
                        for j in range(4):
                            MM(P2[:, 0:N], vB[:, j, 128 * mm_:128 * mm_ + 128], GB[:, j, :], j == 0, j == 3, [wtB, GB.c(j)], [P2.r(0, 4 * N)])
                        act(t5[:, :], P2[:, 0:N], AF.Sigmoid, [P2.r(0, 4 * N)], [t5])
                        tt(t5[:, :], t5[:, :], P1[:, 0:N], ALU.mult, [t5, P1.r(0, 4 * N)], [t5])
                        tt(t5[:, :], t5[:, :], GT[:, m, :], ALU.mult, [t5, GT.c(m)], [t5])
                        tt(mixT[:, m, 0:N], mixT[:, m, 0:N], t5[:, :], ALU.add, [mixT.c(m), t5], [mixT.c(m)])
                if upto < 5:
                    continue
                S.reset(mkA)
                mixb = S.sb([128, 8, N], BF16, "mixb")
                for c in range(8):
                    act(mixb[:, c, :], mixT[:, c, 0:N], AF.Copy, [mixT.c(c)], [mixb.c(c)])

                def res_evac(m, bank, mw):
                    tt(xT[:, m, 0:N], xT[:, m, 0:N], bank[:, 0:N], ALU.add, [xT.c(m), bank.r(0, 4 * N)], [xT.c(m)])
                proj_fm(l, I["w_out"][l], 8, 0, 1024, lambda k_: mixb[:, k_, :], lambda k_: [mixb.c(k_)], N, res_evac, bankset=(0, 1, 2, 3))
                rmsnorm_to(hT, xT, "nffn", l, N)
                AT = S.sb([128, 32, N], BF16, "AT")
                t5 = S.sb([128, N], F32, "t5")

                def up_evac(m, bank, mw):
                    act(t5[:, :], bank[:, 0:N], AF.Relu, [bank.r(0, 4 * N)], [t5])
                    tt(AT[:, m, :], t5[:, :], t5[:, :], ALU.mult, [t5], [AT.c(m)])
                proj_fm(l, I["w_up"][l], 8, 0, 4096, hrhs, hrd, N, up_evac, bankset=(0, 1, 2, 3))
                for m in range(8):
                    wt, vw = load_w(I["w_down"][l][:, 128 * m:128 * m + 128], 32, 128)
                    bank = PS[m % 4]
                    for k in range(32):
                        MM(bank[:, 0:N], vw[:, k, :], AT[:, k, :], k == 0, k == 31, [wt, AT.c(k)], [bank.r(0, 4 * N)])
                    res_evac(m, bank, 128)
            if upto >= 5 and kind != "meta":
                S.reset(ARENA)
                ydst = O["y_prompt"][b, t0:t0 + N, :] if kind == "prompt" else O["y_sample"][b, t0:t0 + N, :]
                y_tm = S.sb([128, nbk, D], F32, "y_tm")
                for tb, nt in blk:
                    for hc in range(2):
                        bank = PS[5 + hc]
                        for c4 in range(4):
                            c = 4 * hc + c4
                            TR(bank[0:nt, 128 * c4:128 * c4 + 128], xT[:, c, 128 * tb:128 * tb + nt], ident_f[:, :],
                               [xT.c(c), ident_f], [bank.r(512 * c4, 512 * c4 + 512)])
                        act(y_tm[0:nt, tb, 512 * hc:512 * hc + 512], bank[0:nt, :], AF.Copy, [bank], [y_tm.c(tb)])
                    S.dma(OQ, ydst[128 * tb:128 * tb + nt, :], y_tm[0:nt, tb, :], [y_tm.c(tb)], [], "yout")
        if upto >= 3:
            for l in range(2):
                if kind == "meta":
                    for dst, src in ((STm[l], ST_[l]), (CHm[l], CH_[l]), (XRm[l], XR_[l]), (XIm[l], XI_[l])):
                        cp(dst.h[:], src.h[:], [src], [dst])
                else:
                    sfx = "prompt" if kind == "prompt" else "sample"
                    S.dma(OQ, O["conv_" + sfx][l, b], CH_[l][:, :, :].rearrange("p c k -> p (c k)"), [CH_[l]], [], "stout")
                    for g in range(2):
                        S.dma(OQ, O["ssd_" + sfx][l, b, g], ST_[l][64:128, g, :], [ST_[l].c(g)], [], "stout")
                    S.dma(OQ, O["s5re_" + sfx][l, b], XR_[l][:, :], [XR_[l]], [], "stout")
                    S.dma(OQ, O["s5im_" + sfx][l, b], XI_[l][:, :], [XI_[l]], [], "stout")
    OUT_STREAMS[:] = ["kout", "vout", "yout", "stout"]
    S.emit(OUT_STREAMS)
    return nc


OUT_STREAMS = []

_NC_CACHE = {}


def pack_s5bc(inp):
    out = np.zeros((2, 4, 128, 16, 128), np.float32)
    for l in range(2):
        for i, nm in enumerate(("b_re", "b_im")):
            b = np.asarray(inp[nm], np.float32)[l]
            for c in range(16):
                m = c % 4
                for two in range(2):
                    g = 2 * c + two
                    r0 = 32 * m + 16 * two
                    out[l, i, r0:r0 + 16, c, 64 * two:64 * two + 64] = b[g].T
        for i, nm in enumerate(("c_re", "c_im")):
            cc = np.asarray(inp[nm], np.float32)[l]
            for c in range(16):
                m = c % 4
                for two in range(2):
                    g = 2 * c + two
                    c0 = 32 * m + 16 * two
                    out[l, 2 + i, 64 * two:64 * two + 64, c, c0:c0 + 16] = cc[g].T
    return out.reshape(2, 4, 128, 2048)


def kernel(**inp):
    cfg = inp.pop("_cfg", None)
    key = repr(cfg)
    if key not in _NC_CACHE:
        _NC_CACHE[key] = build(cfg)
    nc = _NC_CACHE[key]
    f = lambda a: np.ascontiguousarray(np.asarray(a, dtype=np.float32))
    wnames = ["meta_tokens", "norm_mix", "w_in", "conv_w", "conv_b", "dt_bias", "a_log", "d_ssd", "norm_ssd",
              "lam_re", "lam_im", "log_step", "w_glu", "q_norm", "k_norm",
              "w_lift_a", "w_lift_c", "w_out", "norm_ffn", "w_up", "w_down"]
    shared = {n: f(inp[n]) for n in wnames}
    shared["d_s5"] = f(inp["d_s5"]).reshape(2, 512)
    shared["par"] = pack_params(inp)
    shared["s5bc"] = pack_s5bc(inp)
    in_maps = []
    for c in range(8):
        m = dict(shared)
        m["x_prompt"] = f(inp["x_prompt"][4 * c:4 * c + 4])
        m["x_sample"] = f(inp["x_sample"][2 * c:2 * c + 2])
        m["cache_k"] = f(inp["cache_k"][:, 2 * c:2 * c + 2]).reshape(2, 2, PAST, 512)
        m["cache_v"] = f(inp["cache_v"][:, 2 * c:2 * c + 2]).reshape(2, 2, PAST, 512)
        sc = f(inp["state_conv"][:, 2 * c:2 * c + 2]).reshape(2, 2, 3, 10, 128)
        m["state_conv"] = np.ascontiguousarray(sc.transpose(0, 1, 4, 3, 2)).reshape(2, 2, 128, 30)
        ss = f(inp["state_ssd"][:, 2 * c:2 * c + 2])
        m["state_ssd"] = np.ascontiguousarray(ss.transpose(0, 1, 2, 5, 3, 4)).reshape(2, 2, 2, 64, 512)
        for nm in ("state_s5_re", "state_s5_im"):
            a5 = f(inp[nm][:, 2 * c:2 * c + 2]).reshape(2, 2, 16, 2, 64)
            m[nm] = np.ascontiguousarray(a5.transpose(0, 1, 3, 4, 2)).reshape(2, 2, 128, 16)
        in_maps.append(m)
    ncores = (cfg or {}).get('ncores', 8)
    res = run_bass_kernel_spmd(nc, in_maps[:ncores], core_ids=list(range(ncores)))
    R = list(res.results) + [res.results[0]] * (8 - ncores)
    cat = lambda k, ax: np.concatenate([np.asarray(R[c][k], dtype=np.float32) for c in range(8)], axis=ax)
    y_p = cat("y_prompt", 0)
    y_s = cat("y_sample", 0)
    k_p = cat("k_prompt", 1).reshape(2, 32, TP, 8, 64)
    v_p = cat("v_prompt", 1).reshape(2, 32, TP, 8, 64)
    unconv = lambda a: np.ascontiguousarray(a.reshape(2, -1, 128, 10, 3).transpose(0, 1, 4, 3, 2)).reshape(2, -1, 3, 1280)
    unssd = lambda a: np.ascontiguousarray(a.reshape(2, -1, 2, 64, 8, 64).transpose(0, 1, 2, 4, 5, 3))
    uns5 = lambda a: np.ascontiguousarray(a.reshape(2, -1, 2, 64, 16).transpose(0, 1, 4, 2, 3)).reshape(2, -1, 32, 64)
    conv_p = unconv(cat("conv_prompt", 1))
    ssd_p = unssd(cat("ssd_prompt", 1))
    s5r_p = uns5(cat("s5re_prompt", 1))
    s5i_p = uns5(cat("s5im_prompt", 1))
    k_s = cat("k_sample", 1).reshape(2, 16, DSEQ, 8, 64)
    v_s = cat("v_sample", 1).reshape(2, 16, DSEQ, 8, 64)
    conv_s = unconv(cat("conv_sample", 1))
    ssd_s = unssd(cat("ssd_sample", 1))
    s5r_s = uns5(cat("s5re_sample", 1))
    s5i_s = uns5(cat("s5im_sample", 1))
    return (y_p, y_s, k_p, v_p, conv_p, ssd_p, s5r_p, s5i_p, k_s, v_s, conv_s, ssd_s, s5r_s, s5i_s)
```
